# Optimizing a Trainium2 kernel written in Bass

```python
import math
import jax, jax.numpy as jnp
from jax import lax
import numpy as np

D_MODEL = 1024
BATCH = 2
SEQ = 16384
DEPTH = 1
DEC_BATCH = 16
DEC_SEQ = 4096
PAST_LEN = 128

PLE_DIM = 256
D_FF = 2816
EPS = 1e-6
MLA_HEADS = 8
QK_NOPE = 64
QK_ROPE = 32
V_HEAD = 64
Q_LORA = 384
KV_LORA = 256
MLA_WIDTH = MLA_HEADS * V_HEAD
ROPE_BASE = 10000.0
Q_BLOCK = 128
HG_HEADS = 4
HG_DK = 128
HG_DV = 128
HG_KDIM = HG_HEADS * HG_DK
HG_WIDTH = HG_HEADS * HG_DV
CHUNK = 64
D_MIX = MLA_WIDTH + HG_WIDTH
IN_SIZES = (Q_LORA, KV_LORA, QK_ROPE, HG_KDIM, HG_WIDTH, HG_KDIM, HG_KDIM, HG_WIDTH)
D_IN = Q_LORA + KV_LORA + QK_ROPE + 3 * HG_KDIM + 2 * HG_WIDTH

kernel_name = "hybrid_mla_hgrn2_macaron_encoder"


def _rms(x, g):
    xf = x.astype(jnp.float32)
    y = xf * lax.rsqrt(jnp.mean(xf * xf, axis=-1, keepdims=True) + EPS)
    return y.astype(x.dtype) * g


def _swiglu(x, wg, wu, wd):
    return (jax.nn.silu(x @ wg) * (x @ wu)) @ wd


def _rope(x, cos, sin):
    x1, x2 = jnp.split(x, 2, axis=-1)
    return jnp.concatenate([x1 * cos - x2 * sin, x1 * sin + x2 * cos], axis=-1)


def _split_cols(u):
    out = []
    start = 0
    for n in IN_SIZES:
        out.append(u[..., start:start + n])
        start += n
    return out


def _mla(c_q, c_kv, k_r, q_norm, kv_norm, w_uq, w_uk, w_uv):
    B, S, _ = c_q.shape
    q = (_rms(c_q, q_norm) @ w_uq).reshape(B, S, MLA_HEADS, QK_NOPE + QK_ROPE)
    q_nope, q_rope = q[..., :QK_NOPE], q[..., QK_NOPE:]
    ckv = _rms(c_kv, kv_norm)
    k_nope = (ckv @ w_uk).reshape(B, S, MLA_HEADS, QK_NOPE)
    v = (ckv @ w_uv).reshape(B, S, MLA_HEADS, V_HEAD)
    inv_freq = jnp.exp(jnp.arange(0, QK_ROPE, 2, dtype=jnp.float32) * (-math.log(ROPE_BASE) / QK_ROPE))
    ang = jnp.arange(S, dtype=jnp.float32)[:, None] * inv_freq[None, :]
    cos = jnp.cos(ang).astype(c_q.dtype)
    sin = jnp.sin(ang).astype(c_q.dtype)
    q_rope = _rope(q_rope, cos[:, None, :], sin[:, None, :])
    k_r = _rope(k_r, cos, sin)
    nb = S // Q_BLOCK
    qn = q_nope.reshape(B, nb, Q_BLOCK, MLA_HEADS, QK_NOPE).transpose(1, 0, 2, 3, 4)
    qr = q_rope.reshape(B, nb, Q_BLOCK, MLA_HEADS, QK_ROPE).transpose(1, 0, 2, 3, 4)
    scale = (QK_NOPE + QK_ROPE) ** -0.5

    def block(args):
        qn_b, qr_b = args
        s = (jnp.einsum('bqhd,bkhd->bhqk', qn_b, k_nope)
             + jnp.einsum('bqhr,bkr->bhqk', qr_b, k_r))
        pr = jax.nn.softmax(s.astype(jnp.float32) * scale, axis=-1).astype(v.dtype)
        return jnp.einsum('bhqk,bkhd->bqhd', pr, v)

    o = lax.map(block, (qn, qr))
    return o.transpose(1, 0, 2, 3, 4).reshape(B, S, MLA_WIDTH)


def _hgrn_chunk_scan(q, k, v, logf):
    B, H, S, dk = q.shape
    dv = v.shape[-1]
    n = S // CHUNK

    def chunks(t):
        return t.reshape(B, H, n, CHUNK, t.shape[-1]).transpose(2, 0, 1, 3, 4)

    lower = jnp.tril(jnp.ones((CHUNK, CHUNK), dtype=bool))[:, :, None]

    def step(state, inp):
        qc, kc, vc, gc = inp
        b = jnp.cumsum(gc, axis=2)
        o_inter = jnp.einsum('bhtk,bhkv->bhtv', qc * jnp.exp(b), state)
        diff = b[:, :, :, None, :] - b[:, :, None, :, :]
        dec = jnp.exp(jnp.where(lower, diff, -jnp.inf))
        a = jnp.einsum('bhtsk,bhsk->bhts', qc[:, :, :, None, :] * dec, kc)
        o = o_inter + jnp.einsum('bhts,bhsv->bhtv', a, vc)
        b_last = b[:, :, -1:, :]
        state = (jnp.exp(b_last[:, :, 0, :])[..., None] * state
                 + jnp.einsum('bhsk,bhsv->bhkv', kc * jnp.exp(b_last - b), vc))
        return state, o

    state0 = jnp.zeros((B, H, dk, dv), jnp.float32)
    _, o = lax.scan(step, state0, (chunks(q), chunks(k), chunks(v), chunks(logf)))
    return o.transpose(1, 2, 0, 3, 4).reshape(B, H, S, dv)


def _hgrn(h_q, h_i, h_ff, h_fb, h_g, lb_f, lb_b, o_norm):
    B, S, _ = h_q.shape

    def heads(t):
        return t.astype(jnp.float32).reshape(B, S, HG_HEADS, -1).transpose(0, 2, 1, 3)

    q = heads(jax.nn.silu(h_q))
    v = heads(h_i)

    def decay(h_f, lb):
        z = h_f.astype(jnp.float32)
        lb = lb.astype(jnp.float32)
        f = lb + (1.0 - lb) * jax.nn.sigmoid(z)
        return heads((1.0 - lb) * jax.nn.sigmoid(-z)), heads(jnp.log(f))

    k_f, g_f = decay(h_ff, lb_f)
    k_b, g_b = decay(h_fb, lb_b)
    o_fwd = _hgrn_chunk_scan(q, k_f, v, g_f)
    o_bwd = jnp.flip(_hgrn_chunk_scan(jnp.flip(q, axis=2), jnp.flip(k_b, axis=2),
                                      jnp.flip(v, axis=2), jnp.flip(g_b, axis=2)), axis=2)
    o = o_fwd + o_bwd
    o = (o * lax.rsqrt(jnp.mean(o * o, axis=-1, keepdims=True) + EPS)
         * o_norm.astype(jnp.float32).reshape(HG_HEADS, 1, HG_DV))
    o = o.transpose(0, 2, 1, 3).reshape(B, S, HG_WIDTH).astype(h_q.dtype)
    return o * jax.nn.silu(h_g)


def _trunk(x, p, ffn1_norm, ffn1_wg, ffn1_wu, ffn1_wd, mix_norm, w_in, q_norm, w_uq,
           kv_norm, w_uk, w_uv, hg_lb, hg_norm, w_o, ffn2_norm, ffn2_wg, ffn2_wu, ffn2_wd,
           ple_norm, w_ple_gate, w_ple_proj, final_norm):
    lb = jnp.cumsum(jax.nn.softmax(hg_lb.astype(jnp.float32), axis=1), axis=1)
    h = x
    for l in range(DEPTH):
        h = h + 0.5 * _swiglu(_rms(h, ffn1_norm[l]), ffn1_wg[l], ffn1_wu[l], ffn1_wd[l])
        u = _rms(h, mix_norm[l]) @ w_in[l]
        c_q, c_kv, k_r, h_q, h_i, h_ff, h_fb, h_g = _split_cols(u)
        mix = jnp.concatenate([
            _mla(c_q, c_kv, k_r, q_norm[l], kv_norm[l], w_uq[l], w_uk[l], w_uv[l]),
            _hgrn(h_q, h_i, h_ff, h_fb, h_g, lb[0, l], lb[1, l], hg_norm[l]),
        ], axis=-1)
        h = h + mix @ w_o[l]
        h = h + 0.5 * _swiglu(_rms(h, ffn2_norm[l]), ffn2_wg[l], ffn2_wu[l], ffn2_wd[l])
        gate = jax.nn.sigmoid(_rms(h, ple_norm[l]) @ w_ple_gate[l])
        h = h + gate * (p[l].astype(h.dtype) @ w_ple_proj[l])
    return _rms(h, final_norm)


def setup_inputs(seed: int = 0) -> dict:
    key = jax.random.key(seed)
    ks = jax.random.split(key, 32)
    f32 = jnp.float32

    def nrm(k, shape, fan_in):
        return jax.random.normal(k, shape, f32) * (fan_in ** -0.5)

    def gain(k, shape):
        return 1.0 + 0.05 * jax.random.normal(k, shape, f32)

    L = DEPTH
    return {
        "x_prompt": jax.random.normal(ks[0], (BATCH, SEQ, D_MODEL), f32),
        "x_sample": jax.random.normal(ks[1], (DEC_BATCH, DEC_SEQ, D_MODEL), f32),
        "p_prompt": jax.random.normal(ks[2], (DEPTH, BATCH, SEQ, PLE_DIM), f32),
        "p_sample": jax.random.normal(ks[3], (DEPTH, DEC_BATCH, DEC_SEQ, PLE_DIM), f32),
        "ffn1_norm": gain(ks[4], (L, D_MODEL)),
        "ffn1_wg": nrm(ks[5], (L, D_MODEL, D_FF), D_MODEL),
        "ffn1_wu": nrm(ks[6], (L, D_MODEL, D_FF), D_MODEL),
        "ffn1_wd": nrm(ks[7], (L, D_FF, D_MODEL), D_FF),
        "mix_norm": gain(ks[8], (L, D_MODEL)),
        "w_in": nrm(ks[9], (L, D_MODEL, D_IN), D_MODEL),
        "q_norm": gain(ks[10], (L, Q_LORA)),
        "w_uq": nrm(ks[11], (L, Q_LORA, MLA_HEADS * (QK_NOPE + QK_ROPE)), Q_LORA),
        "kv_norm": gain(ks[12], (L, KV_LORA)),
        "w_uk": nrm(ks[13], (L, KV_LORA, MLA_HEADS * QK_NOPE), KV_LORA),
        "w_uv": nrm(ks[14], (L, KV_LORA, MLA_HEADS * V_HEAD), KV_LORA),
        "hg_lb": 0.5 * jax.random.normal(ks[15], (2, L + 1, HG_KDIM), f32),
        "hg_norm": gain(ks[16], (L, HG_WIDTH)),
        "w_o": nrm(ks[17], (L, D_MIX, D_MODEL), D_MIX),
        "ffn2_norm": gain(ks[18], (L, D_MODEL)),
        "ffn2_wg": nrm(ks[19], (L, D_MODEL, D_FF), D_MODEL),
        "ffn2_wu": nrm(ks[20], (L, D_MODEL, D_FF), D_MODEL),
        "ffn2_wd": nrm(ks[21], (L, D_FF, D_MODEL), D_FF),
        "ple_norm": gain(ks[22], (L, D_MODEL)),
        "w_ple_gate": nrm(ks[23], (L, D_MODEL, D_MODEL), D_MODEL),
        "w_ple_proj": nrm(ks[24], (L, PLE_DIM, D_MODEL), PLE_DIM),
        "final_norm": gain(ks[25], (D_MODEL,)),
    }


def reference(x_prompt, x_sample, p_prompt, p_sample, ffn1_norm, ffn1_wg, ffn1_wu, ffn1_wd,
              mix_norm, w_in, q_norm, w_uq, kv_norm, w_uk, w_uv, hg_lb, hg_norm, w_o,
              ffn2_norm, ffn2_wg, ffn2_wu, ffn2_wd, ple_norm, w_ple_gate, w_ple_proj, final_norm):
    w = (ffn1_norm, ffn1_wg, ffn1_wu, ffn1_wd, mix_norm, w_in, q_norm, w_uq, kv_norm, w_uk, w_uv,
         hg_lb, hg_norm, w_o, ffn2_norm, ffn2_wg, ffn2_wu, ffn2_wd, ple_norm, w_ple_gate,
         w_ple_proj, final_norm)
    y_prompt = _trunk(x_prompt, p_prompt, *w)
    y_sample = _trunk(x_sample, p_sample, *w)
    return (y_prompt, y_sample)
```

```python
import numpy as np
from contextlib import ExitStack
import concourse.bass as bass
import concourse.mybir as mybir
from concourse.bass_utils import run_bass_kernel_spmd

F32 = mybir.dt.float32
BF16 = mybir.dt.bfloat16
AF = mybir.ActivationFunctionType
ALU = mybir.AluOpType
AX = mybir.AxisListType

D = 1024
DFF = 2816
EPS = 1e-6
NFC = DFF // 128
NKC = D // 128


class Res:
    __slots__ = ("name", "w", "r")

    def __init__(self, name):
        self.name = name
        self.w = None
        self.r = {}


class Ev:
    __slots__ = ("kind", "key", "op", "count")

    def __init__(self, kind, key, op=None, count=0):
        self.kind = kind
        self.key = key
        self.op = op
        self.count = count


class Op:
    __slots__ = ("fn", "deps", "sig", "count", "dma_key", "dma_n", "ev", "inc")

    def __init__(self, fn, deps):
        self.fn = fn
        self.deps = deps
        self.sig = False
        self.count = 0
        self.dma_key = None
        self.dma_n = 0
        self.ev = None


ENGS = ("sp", "act", "dve", "pool", "pe")


class Sched:
    DMA_SEMS = {}
    DMA_CNT = {}

    def __init__(self, nc, tag):
        self.nc = nc
        self.tag = tag
        self.ops = {e: [] for e in ENGS}
        self.dma_cnt = Sched.DMA_CNT
        self.nres = 0
        self.keymap = {}

    def res(self, name=None):
        self.nres += 1
        return Res(name or f"r{self.nres}")

    def _deps(self, eng, reads, writes):
        deps = []
        for r in reads:
            if r.w is not None:
                deps.append(r.w)
        for w in writes:
            if w.w is not None:
                deps.append(w.w)
            deps.extend(w.r.values())
        out = []
        seen = set()
        for d in deps:
            if id(d) in seen:
                continue
            seen.add(id(d))
            if d.kind == "e" and d.key == "pe" and eng == "pe":
                continue
            out.append(d)
        return out

    def op(self, eng, fn, reads=(), writes=()):
        o = Op(fn, self._deps(eng, reads, writes))
        ev = Ev("e", eng, op=o)
        o.ev = ev
        self.ops[eng].append(o)
        for r in reads:
            r.r[("e", eng)] = ev
        for w in writes:
            w.w = ev
            w.r = {}
        return o

    def dma(self, queue, fn, n, key, reads=(), writes=(), inc=16):
        if key not in self.keymap:
            self.keymap[key] = f"k{len(self.keymap)}"
        key = self.keymap[key]
        o = Op(fn, self._deps(queue, reads, writes))
        o.dma_key = key
        o.dma_n = n
        o.inc = inc
        c = self.dma_cnt.get(key, 0) + n * inc
        self.dma_cnt[key] = c
        ev = Ev("d", key, op=o, count=c)
        o.ev = ev
        self.ops[queue].append(o)
        for r in reads:
            r.r[("d", key)] = ev
        for w in writes:
            w.w = ev
            w.r = {}
        return o

    def barrier(self):
        evs = []
        for e in ENGS:
            for o in reversed(self.ops[e]):
                if o.dma_key is None and o.fn is not None:
                    evs.append(o.ev)
                    break
        lastd = {}
        for e in ENGS:
            for o in self.ops[e]:
                if o.dma_key is not None:
                    lastd[o.dma_key] = o.ev
        evs.extend(lastd.values())
        for e in ENGS:
            deps = [d for d in evs if not (d.kind == "e" and d.key == e)]
            o = Op(None, deps)
            o.ev = Ev("e", e, op=o)
            self.ops[e].append(o)

    def emit(self):
        nc = self.nc
        for e in ENGS:
            for o in self.ops[e]:
                for d in o.deps:
                    if d.kind == "e":
                        d.op.sig = True
        sems = {}
        for e in ENGS:
            c = 0
            for o in self.ops[e]:
                if o.dma_key is None and o.sig:
                    assert o.fn is not None
                    c += 1
                    o.count = c
            sems[("e", e)] = nc.alloc_semaphore(f"{self.tag}_e_{e}")
        for k in self.dma_cnt:
            if k not in Sched.DMA_SEMS:
                Sched.DMA_SEMS[k] = nc.alloc_semaphore(f"d_{k}")
            sems[("d", k)] = Sched.DMA_SEMS[k]
        self.sems = sems

        def run(eng_name, eng):
            waited = {}
            for o in self.ops[eng_name]:
                for d in o.deps:
                    k = (d.kind, d.key)
                    val = d.op.count if d.kind == "e" else d.count
                    assert val > 0, (eng_name, d.kind, d.key)
                    if waited.get(k, 0) >= val:
                        continue
                    waited[k] = val
                    eng.wait_ge(sems[k], val)
                if o.fn is None:
                    continue
                ins = o.fn(eng)
                if o.dma_key is not None:
                    assert len(ins) == o.dma_n
                    for i in ins:
                        i.then_inc(sems[("d", o.dma_key)], o.inc)
                elif o.sig:
                    ins.then_inc(sems[("e", eng_name)], 1)

        with nc.Block() as block:
            @block.sync
            def _(e):
                run("sp", e)

            @block.scalar
            def _(e):
                run("act", e)

            @block.vector
            def _(e):
                run("dve", e)

            @block.gpsimd
            def _(e):
                run("pool", e)

            @block.tensor
            def _(e):
                run("pe", e)

    def release(self):
        for s in self.sems.values():
            self.nc.release_semaphore(s)


def load_weight_bf16(S, nc, w_dram, w_sb, rows_chunks, cols, gain_sb, stage, stage_res, w_res, qi=[0]):
    CW = 512
    for kc in range(rows_chunks):
        for c0 in range(0, cols, CW):
            cw = min(CW, cols - c0)
            i = qi[0] % len(stage)
            qi[0] += 1
            st, sr = stage[i], stage_res[i]
            src = w_dram[kc * 128:(kc + 1) * 128, c0:c0 + cw]
            S.dma("sp", (lambda e, st=st, src=src, cw=cw: [e.dma_start(out=st[:, 0:cw], in_=src)]),
                  1, f"wst{i}", writes=[sr])
            dst = w_sb[:, kc, c0:c0 + cw]
            if gain_sb is not None:
                g = gain_sb[:, kc:kc + 1]
                S.op("pool", (lambda e, dst=dst, st=st, cw=cw, g=g:
                              e.tensor_scalar(dst, st[:, 0:cw], g, 0.0, ALU.mult, ALU.add)),
                     reads=[sr], writes=[w_res])
            else:
                S.op("pool", (lambda e, dst=dst, st=st, cw=cw: e.tensor_copy(dst, st[:, 0:cw])),
                     reads=[sr], writes=[w_res])


def ffn_phase(nc, tag, x_d, out_d, ntiles, gain_d, wg_d, wu_d, wd_d, ident_d):
    S = Sched(nc, tag)
    with ExitStack() as es:
        def sb(name, shape, dt):
            return es.enter_context(nc.sbuf_tensor(f"{tag}_{name}", shape, dt))

        def ps(name, shape, dt):
            return es.enter_context(nc.psum_tensor(f"{tag}_{name}", shape, dt))

        wg = sb("wg", [128, NKC, DFF], BF16)
        wu = sb("wu", [128, NKC, DFF], BF16)
        wd = sb("wd", [128, NFC, D], BF16)
        gain = sb("gain", [128, NKC], F32)
        ident = sb("ident", [128, 128], BF16)
        st0 = sb("st0", [128, 512], F32)
        st1 = sb("st1", [128, 512], F32)
        xt0 = sb("xt0", [128, 4, D], F32)
        xt1 = sb("xt1", [128, 4, D], F32)
        xn = sb("xn", [128, D], BF16)
        xnT = sb("xnT", [128, NKC, 512], BF16)
        actb = sb("act", [128, NFC, 512], BF16)
        sg = sb("sg", [128, 2, 512], BF16)
        stat = sb("stat", [128, 16], F32)
        junk = sb("junk", [128, D], BF16)
        pg0 = ps("pg0", [128, 512], F32)
        pg1 = ps("pg1", [128, 512], F32)
        pu0 = ps("pu0", [128, 512], F32)
        pu1 = ps("pu1", [128, 512], F32)
        po0 = ps("po0", [128, 512], F32)
        po1 = ps("po1", [128, 512], F32)
        pt0 = ps("pt0", [128, 1024], BF16)
        pt1 = ps("pt1", [128, 1024], BF16)
        R = S.res
        r_gain, r_ident, r_w = R("gain"), R("ident"), R("w")
        r_st = [R("st0"), R("st1")]
        S.dma("sp", lambda e: [e.dma_start(out=gain[:], in_=gain_d)], 1, "c0", writes=[r_gain])
        S.dma("sp", lambda e: [e.dma_start(out=st0[:, 0:128], in_=ident_d)], 1, "wst0", writes=[r_st[0]])
        S.op("pool", lambda e: e.tensor_copy(ident[:], st0[:, 0:128]), reads=[r_st[0]], writes=[r_ident])
        qi = [1]
        r_wg, r_wu, r_wd = R("wg"), R("wu"), R("wd")
        S.op("pool", lambda e: e.tensor_copy(stat[:, 8:9], gain[:, 0:1]), reads=[r_gain], writes=[R("dummy")])
        load_weight_bf16(S, nc, wg_d, wg, NKC, DFF, gain, [st0, st1], r_st, r_wg, qi)
        load_weight_bf16(S, nc, wu_d, wu, NKC, DFF, gain, [st0, st1], r_st, r_wu, qi)
        load_weight_bf16(S, nc, wd_d, wd, NFC, D, None, [st0, st1], r_st, r_wd, qi)

        xts = [xt0, xt1]
        r_xt = [[R(f"xt{i}_{j}") for j in range(4)] for i in range(2)]
        r_xn, r_xnT, r_act = R("xn"), [R(f"xnT{k}") for k in range(NKC)], [R(f"act{f}") for f in range(NFC)]
        r_sg = [R("sg0"), R("sg1")]
        r_stat = [R(f"stat{j}") for j in range(4)]
        r_junk = R("junk")
        pgs, pus, pos, pts = [pg0, pg1], [pu0, pu1], [po0, po1], [pt0, pt1]
        r_pg, r_pu = [R("pg0"), R("pg1")], [R("pu0"), R("pu1")]
        r_po, r_pt = [R("po0"), R("po1")], [R("pt0"), R("pt1")]

        def load_tile(i):
            b = i % 2
            for j in range(4):
                src = x_d[i * 512 + j * 128: i * 512 + (j + 1) * 128, :]
                dst = xts[b][:, j, :]
                S.dma("sp", (lambda e, dst=dst, src=src: [e.dma_start(out=dst, in_=src)]), 1,
                      f"xt{b}_{j}", writes=[r_xt[b][j]])

        load_tile(0)
        nt_ctr = [0]
        for i in range(ntiles):
            b = i % 2
            xt = xts[b]
            if i + 1 < ntiles:
                load_tile(i + 1)
            for j in range(4):
                xj = xt[:, j, :]
                ss = stat[:, j:j + 1]
                rs = stat[:, 4 + j:5 + j]
                S.op("act", (lambda e, xj=xj, ss=ss: e.activation(junk[:], xj, AF.Square, accum_out=ss)),
                     reads=[r_xt[b][j]], writes=[r_junk, r_stat[j]])
                S.op("act", (lambda e, ss=ss, rs=rs: e.activation(rs, ss, AF.Sqrt, bias=EPS, scale=1.0 / D)),
                     reads=[r_stat[j]], writes=[r_stat[j]])
                S.op("dve", (lambda e, rs=rs: e.reciprocal(rs, rs)), reads=[r_stat[j]], writes=[r_stat[j]])
                S.op("dve", (lambda e, xj=xj, rs=rs: e.tensor_scalar(xn[:], xj, rs, None, ALU.mult)),
                     reads=[r_xt[b][j], r_stat[j]], writes=[r_xn])
                tb = nt_ctr[0] % 2
                nt_ctr[0] += 1
                pt = pts[tb]
                for kc in range(NKC):
                    S.op("pe", (lambda e, pt=pt, kc=kc: e.transpose(pt[:, kc * 128:(kc + 1) * 128],
                                                                   xn[:, kc * 128:(kc + 1) * 128], ident[:])),
                         reads=[r_xn, r_ident], writes=[r_pt[tb]])
                dst = xnT[:, :, j * 128:(j + 1) * 128]
                src = pt[:].rearrange("p (k t) -> p k t", k=NKC)
                S.op("act", (lambda e, dst=dst, src=src: e.copy(dst, src)),
                     reads=[r_pt[tb]], writes=r_xnT)
            for f in range(NFC):
                pb = f % 2
                for kc in range(NKC):
                    S.op("pe", (lambda e, pb=pb, f=f, kc=kc: e.matmul(
                        pgs[pb][:], wg[:, kc, f * 128:(f + 1) * 128], xnT[:, kc, :],
                        start=(kc == 0), stop=(kc == NKC - 1))),
                        reads=[r_wg, r_xnT[kc]], writes=[r_pg[pb]])
                for kc in range(NKC):
                    S.op("pe", (lambda e, pb=pb, f=f, kc=kc: e.matmul(
                        pus[pb][:], wu[:, kc, f * 128:(f + 1) * 128], xnT[:, kc, :],
                        start=(kc == 0), stop=(kc == NKC - 1))),
                        reads=[r_wu, r_xnT[kc]], writes=[r_pu[pb]])
                S.op("act", (lambda e, pb=pb: e.activation(sg[:, pb, :], pgs[pb][:], AF.Silu)),
                     reads=[r_pg[pb]], writes=[r_sg[pb]])
                S.op("dve", (lambda e, pb=pb, f=f: e.tensor_tensor(actb[:, f, :], sg[:, pb, :], pus[pb][:], ALU.mult)),
                     reads=[r_sg[pb], r_pu[pb]], writes=[r_act[f]])
            for j in range(4):
                for hh in range(2):
                    pb = (j * 2 + hh) % 2
                    for f in range(NFC):
                        S.op("pe", (lambda e, pb=pb, f=f, j=j, hh=hh: e.matmul(
                            pos[pb][:], actb[:, f, j * 128:(j + 1) * 128], wd[:, f, hh * 512:(hh + 1) * 512],
                            start=(f == 0), stop=(f == NFC - 1))),
                            reads=[r_wd, r_act[f]], writes=[r_po[pb]])
                    dst = xt[:, j, hh * 512:(hh + 1) * 512]
                    S.op("dve", (lambda e, dst=dst, pb=pb: e.scalar_tensor_tensor(
                        dst, pos[pb][:], 0.5, dst, ALU.mult, ALU.add)),
                        reads=[r_po[pb], r_xt[b][j]], writes=[r_xt[b][j]])
                dstd = out_d[i * 512 + j * 128: i * 512 + (j + 1) * 128, :]
                src = xt[:, j, :]
                S.dma("pool", (lambda e, dstd=dstd, src=src: [e.dma_start(out=dstd, in_=src)]), 1,
                      f"xo{b}_{j}", reads=[r_xt[b][j]])
        S.barrier()
        S.emit()
    return S


class Ctx:
    def __init__(self, nc, tag, es):
        self.nc, self.tag, self.es = nc, tag, es

    def sb(self, name, shape, dt):
        return self.es.enter_context(self.nc.sbuf_tensor(f"{self.tag}_{name}", shape, dt))

    def ps(self, name, shape, dt=F32):
        return self.es.enter_context(self.nc.psum_tensor(f"{self.tag}_{name}", shape, dt))


def load_w(S, w_dram, w_sb, nrc, cols, gain_sb, st, r_st, r_w, qi, rows_last=128):
    for rc in range(nrc):
        for c0 in range(0, cols, 512):
            cw = min(512, cols - c0)
            i = qi[0] % 2
            qi[0] += 1
            stt, sr = st[i], r_st[i]
            src = w_dram[rc * 128:(rc + 1) * 128, c0:c0 + cw]
            S.dma("sp", (lambda e, stt=stt, src=src, cw=cw: [e.dma_start(out=stt[:, 0:cw], in_=src)]),
                  1, f"wst{i}", writes=[sr])
            dst = w_sb[:, rc, c0:c0 + cw]
            if gain_sb is not None:
                g = gain_sb[:, rc:rc + 1]
                S.op("pool", (lambda e, dst=dst, stt=stt, cw=cw, g=g:
                              e.tensor_scalar(dst, stt[:, 0:cw], g, 0.0, ALU.mult, ALU.add)),
                     reads=[sr], writes=[r_w])
            else:
                S.op("pool", (lambda e, dst=dst, stt=stt, cw=cw: e.tensor_copy(dst, stt[:, 0:cw])),
                     reads=[sr], writes=[r_w])


def norm_transpose(S, xt, r_xt_j, j, stat, r_stat, junk, r_junk, xn, r_xn, pt, r_pt, ident, r_ident,
                   xnT, r_xnT, nfeat=D):
    nkc = nfeat // 128
    xj = xt[:, j, :]
    ss = stat[:, j:j + 1]
    rs = stat[:, 4 + j:5 + j]
    S.op("act", (lambda e: e.activation(junk[:, 0:nfeat], xj, AF.Square, accum_out=ss)),
         reads=[r_xt_j], writes=[r_junk, r_stat[j]])
    S.op("act", (lambda e: e.activation(rs, ss, AF.Sqrt, bias=EPS, scale=1.0 / nfeat)),
         reads=[r_stat[j]], writes=[r_stat[j]])
    S.op("dve", (lambda e: e.reciprocal(rs, rs)), reads=[r_stat[j]], writes=[r_stat[j]])
    S.op("dve", (lambda e: e.tensor_scalar(xn[:, 0:nfeat], xj, rs, None, ALU.mult)),
         reads=[r_xt_j, r_stat[j]], writes=[r_xn])
    for kc in range(nkc):
        S.op("pe", (lambda e, kc=kc: e.transpose(pt[:, kc * 128:(kc + 1) * 128],
                                                 xn[:, kc * 128:(kc + 1) * 128], ident[:])),
             reads=[r_xn, r_ident], writes=[r_pt])
    dst = xnT[:, 0:nkc, j * 128:(j + 1) * 128]
    src = pt[:, 0:nkc * 128].rearrange("p (k t) -> p k t", k=nkc)
    S.op("act", (lambda e: e.copy(dst, src)), reads=[r_pt], writes=r_xnT)


C_CQ, C_CKV, C_KR, C_HQ, C_HI, C_HFF, C_HFB, C_HG, C_KRS = 0, 384, 640, 672, 1184, 1696, 2208, 2720, 3232
WIN_COLS = 3264
UQ_COLS = 768 + 256


def mixer_in_phase(nc, tag, ntok, h1_d, cst, w, scr):
    S = Sched(nc, tag)
    ntiles = ntok // 512
    with ExitStack() as es:
        C = Ctx(nc, tag, es)
        R = S.res
        win = C.sb("win", [128, NKC, WIN_COLS], BF16)
        wuq = C.sb("wuq", [128, 3, UQ_COLS], BF16)
        wuk = C.sb("wuk", [128, 2, 512], BF16)
        wuv = C.sb("wuv", [128, 2, 512], BF16)
        gains = C.sb("gains", [128, 16], F32)
        lbt = C.sb("lbt", [128, 16], F32)
        lb = C.sb("lb", [128, 8], F32)
        oml = C.sb("oml", [128, 8], F32)
        ident = C.sb("ident", [128, 128], BF16)
        ones = C.sb("ones", [128, 128], BF16)
        rmask = C.sb("rmask", [128, 512], F32)
        mF = C.sb("mF", [128, 128], F32)
        mB = C.sb("mB", [128, 128], F32)
        ht = C.sb("ht", [128, 4, D], F32)
        xn = C.sb("xn", [128, D], BF16)
        junk = C.sb("junk", [128, D], BF16)
        stat = C.sb("stat", [128, 16], F32)
        xnT = C.sb("xnT", [128, NKC, 512], BF16)
        cqT = C.sb("cqT", [128, 3, 512], BF16)
        ckvT = C.sb("ckvT", [128, 2, 512], BF16)
        sqq = C.sb("sqq", [128, 2, 512], BF16)
        sqkv = C.sb("sqkv", [128, 2, 512], BF16)
        rsq = C.sb("rsq", [128, 512], F32)
        rskv = C.sb("rskv", [128, 512], F32)
        rstok = C.sb("rstok", [128, 8], F32)
        tct = C.sb("tct", [128, 512], F32)
        tst = C.sb("tst", [128, 512], F32)
        t1a = C.sb("t1a", [128, 512], F32)
        t2a = C.sb("t2a", [128, 512], F32)
        t1 = [t1a, t1a]
        t2 = [t2a, t2a]
        qout = C.sb("qout", [128, 8, 512], BF16)
        knT = C.sb("knT", [128, 4, 512], BF16)
        krp = C.sb("krp", [128, 512], BF16)
        vt = C.sb("vt", [128, 4, 512], BF16)
        qh = C.sb("qh", [128, 4, 512], F32)
        hA = [C.sb(f"hA{i}", [128, 512], F32) for i in range(2)]
        hB = [C.sb(f"hB{i}", [128, 512], F32) for i in range(2)]
        hC = [C.sb(f"hC{i}", [128, 512], F32) for i in range(2)]
        hE1 = [C.sb(f"hE1{i}", [128, 512], F32) for i in range(2)]
        hE2 = [C.sb(f"hE2{i}", [128, 512], F32) for i in range(2)]
        st = [hA[0], hB[0]]
        qpo = C.sb("qpo", [128, 8, 512], BF16)
        kpo = C.sb("kpo", [128, 8, 512], BF16)
        kppo = C.sb("kppo", [128, 8, 512], BF16)
        dco = C.sb("dco", [128, 8, 16], F32)
        vht = C.sb("vht", [128, 4, 512], BF16)
        ght = C.sb("ght", [128, 4, 512], BF16)
        ato = C.sb("ato", [128, 4, 8, 128], BF16)
        kto = C.sb("kto", [128, 4, 8, 128], BF16)
        pt = C.ps("pt", [128, 1024], BF16)
        pm = [C.ps(f"pm{i}", [128, 512]) for i in range(4)]
        pn = C.ps("pn", [128, 512])
        pa = C.ps("pa", [128, 512])
        ptk = C.ps("ptk", [128, 1024], BF16)

        r_c = R("consts")
        r_ident, r_ones = R("ident"), R("ones")

        def cdma(dst, src):
            S.dma("sp", (lambda e: [e.dma_start(out=dst, in_=src)]), 1, "c", writes=[r_c])
        cdma(gains[:, 0:8], cst["mix_norm"])
        cdma(gains[:, 8:11], cst["q_norm"])
        cdma(gains[:, 11:13], cst["kv_norm"])
        cdma(lbt[:], cst["hg_lb"])
        cdma(rmask[:], cst["rmask"])
        cdma(mF[:], cst["maskF"])
        cdma(mB[:], cst["maskB"])
        cdma(ht[:, 0, 0:128], cst["ident"])
        S.op("pool", lambda e: e.tensor_copy(ident[:], ht[:, 0, 0:128]), reads=[r_c], writes=[r_ident])
        S.op("pool", lambda e: e.memset(ones[:], 1.0), writes=[r_ones])
        lv = lbt[:].rearrange("p (d l h) -> p d l h", d=2, l=2)
        lb3 = lb[:].rearrange("p (d h) -> p d h", d=2)
        oml3 = oml[:].rearrange("p (d h) -> p d h", d=2)
        r_lb = R("lb")
        S.op("dve", lambda e: e.tensor_tensor(lb3, lv[:, :, 0, :], lv[:, :, 1, :], ALU.subtract), reads=[r_c], writes=[r_lb])
        S.op("act", lambda e: e.activation(oml[:], lb[:], AF.Sigmoid, scale=-1.0), reads=[r_lb], writes=[R("oml")])
        S.op("act", lambda e: e.activation(lb[:], lb[:], AF.Sigmoid), reads=[r_lb], writes=[r_lb])
        r_w = R("w")
        r_hA, r_hB = [R("hA0"), R("hA1")], [R("hB0"), R("hB1")]
        r_st = [r_hA[0], r_hB[0]]
        S.op("pool", lambda e: e.tensor_copy(stat[:, 15:16], gains[:, 0:1]), reads=[r_c], writes=[R("d")])
        qi = [0]
        load_w(S, w["w_in"], win, NKC, WIN_COLS, gains[:, 0:8], st, r_st, r_w, qi)
        load_w(S, w["w_uq"], wuq, 3, UQ_COLS, gains[:, 8:11], st, r_st, r_w, qi)
        load_w(S, w["w_uk"], wuk, 2, 512, gains[:, 11:13], st, r_st, r_w, qi)
        load_w(S, w["w_uv"], wuv, 2, 512, gains[:, 11:13], st, r_st, r_w, qi)

        r_ht = [R(f"ht{j}") for j in range(4)]
        r_stat = [R(f"stat{j}") for j in range(4)]
        r_junk, r_xn, r_pt = R("junk"), R("xn"), R("pt")
        r_xnT = [R(f"xnT{k}") for k in range(NKC)]
        r_pm = [R(f"pm{i}") for i in range(4)]
        r_pn, r_pa, r_ptk = R("pn"), R("pa"), R("ptk")
        r_cqT, r_ckvT, r_sqq, r_sqkv = R("cqT"), R("ckvT"), [R("sqq0"), R("sqq1")], R("sqkv")
        r_rsq, r_rskv, r_rstok = R("rsq"), R("rskv"), R("rstok")
        r_tab = R("tab")
        r_t1a, r_t2a = R("t1a"), R("t2a")
        r_t1, r_t2 = [r_t1a, r_t1a], [r_t2a, r_t2a]
        r_qout, r_knT, r_krp, r_vt, r_qh = R("qout"), R("knT"), R("krp"), R("vt"), R("qh")
        r_hC = [R("hC0"), R("hC1")]
        r_hE1, r_hE2 = [R("hE10"), R("hE11")], [R("hE20"), R("hE21")]
        r_qpo, r_kpo, r_kppo, r_dco = R("qpo"), R("kpo"), R("kppo"), R("dco")
        r_vht, r_ght, r_ato, r_kto = R("vht"), R("ght"), R("ato"), R("kto")
        S.op("pool", lambda e: e.memset(tct[:], 1.0), writes=[r_tab])
        S.op("pool", lambda e: e.memset(tst[:], 0.0), writes=[r_tab])
        pmi = [0]

        def nextpm():
            i = pmi[0] % 4
            pmi[0] += 1
            return pm[i], r_pm[i]

        def mm_fm(ps, r_ps, wsb, c0, m, xT, r_x, nkc, out_p0=0):
            for kc in range(nkc):
                S.op("pe", (lambda e, kc=kc: e.matmul(ps[out_p0:out_p0 + m, :], wsb[:, kc, c0:c0 + m], xT[:, kc, :],
                                                      start=(kc == 0), stop=(kc == nkc - 1))),
                     reads=[r_w] + r_x, writes=[r_ps])

        def mm_tm(ps, r_ps, xT, r_x, j, wsb, c0, n, nkc):
            for kc in range(nkc):
                S.op("pe", (lambda e, kc=kc: e.matmul(ps[:, 0:n], xT[:, kc, j * 128:(j + 1) * 128], wsb[:, kc, c0:c0 + n],
                                                      start=(kc == 0), stop=(kc == nkc - 1))),
                     reads=[r_w] + r_x, writes=[r_ps])

        for i in range(ntiles):
            t0 = i * 512
            S.dma("sp", (lambda e, t0=t0: [e.dma_start(out=ht[:, j, :], in_=h1_d[t0 + j * 128:t0 + (j + 1) * 128, :])
                                          for j in range(4)]), 4, "ht", writes=r_ht)
            S.dma("sp", (lambda e, t0=t0: [e.dma_start(out=tct[64:96, :], in_=cst["rope_c"][:, t0:t0 + 512]),
                                          e.dma_start(out=tst[64:96, :], in_=cst["rope_s"][:, t0:t0 + 512])]),
                  2, "tab", writes=[r_tab])
            for j in range(4):
                norm_transpose(S, ht, r_ht[j], j, stat, r_stat, junk, r_junk, xn, r_xn, pt, r_pt, ident, r_ident,
                               xnT, r_xnT)
            for c in range(3):
                ps, rp = nextpm()
                mm_fm(ps, rp, win, C_CQ + c * 128, 128, xnT, r_xnT, NKC)
                S.op("act", (lambda e, ps=ps, c=c: e.copy(cqT[:, c, :], ps[:])), reads=[rp], writes=[r_cqT])
                S.op("act", (lambda e, ps=ps, c=c: e.activation(sqq[:, c % 2, :], ps[:], AF.Square)),
                     reads=[rp], writes=[r_sqq[c % 2]])
                S.op("pe", (lambda e, c=c: e.matmul(pn[:], ones[:], sqq[:, c % 2, :], start=(c == 0), stop=(c == 2))),
                     reads=[r_ones, r_sqq[c % 2]], writes=[r_pn])
            S.op("act", lambda e: e.activation(rsq[:], pn[:], AF.Sqrt, bias=EPS, scale=1.0 / 384), reads=[r_pn], writes=[r_rsq])
            S.op("dve", lambda e: e.reciprocal(rsq[:], rsq[:]), reads=[r_rsq], writes=[r_rsq])
            for c in range(2):
                ps, rp = nextpm()
                mm_fm(ps, rp, win, C_CKV + c * 128, 128, xnT, r_xnT, NKC)
                S.op("act", (lambda e, ps=ps, c=c: e.copy(ckvT[:, c, :], ps[:])), reads=[rp], writes=[r_ckvT])
                S.op("act", (lambda e, ps=ps, c=c: e.activation(sqkv[:, c, :], ps[:], AF.Square)),
                     reads=[rp], writes=[r_sqkv])
            for c in range(2):
                S.op("pe", (lambda e, c=c: e.matmul(pn[:], ones[:], sqkv[:, c, :], start=(c == 0), stop=(c == 1))),
                     reads=[r_ones, r_sqkv], writes=[r_pn])
            S.op("act", lambda e: e.activation(rskv[:], pn[:], AF.Sqrt, bias=EPS, scale=1.0 / 256), reads=[r_pn], writes=[r_rskv])
            S.op("dve", lambda e: e.reciprocal(rskv[:], rskv[:]), reads=[r_rskv], writes=[r_rskv])
            for j in range(4):
                for c in range(2):
                    S.op("pe", (lambda e, j=j, c=c: e.matmul(pa[:, j:j + 1], sqkv[:, c, j * 128:(j + 1) * 128], ones[:, 0:1],
                                                             start=(c == 0), stop=(c == 1))),
                         reads=[r_ones, r_sqkv], writes=[r_pa])
            S.op("act", lambda e: e.activation(rstok[:, 0:4], pa[:, 0:4], AF.Sqrt, bias=EPS, scale=1.0 / 256),
                 reads=[r_pa], writes=[r_rstok])
            S.op("dve", lambda e: e.reciprocal(rstok[:, 0:4], rstok[:, 0:4]), reads=[r_rstok], writes=[r_rstok])
            ps, rp = nextpm()
            mm_fm(ps, rp, win, C_KR, 32, xnT, r_xnT, NKC, out_p0=64)
            ps2, rp2 = nextpm()
            mm_fm(ps2, rp2, win, C_KRS, 32, xnT, r_xnT, NKC, out_p0=64)
            S.op("dve", (lambda e, ps=ps: e.tensor_tensor(t1[0][64:96, :], ps[64:96, :], tct[64:96, :], ALU.mult)),
                 reads=[rp, r_tab], writes=[r_t1[0]])
            S.op("dve", (lambda e, ps2=ps2: e.tensor_tensor(t2[0][64:96, :], ps2[64:96, :], tst[64:96, :], ALU.mult)),
                 reads=[rp2, r_tab], writes=[r_t2[0]])
            S.op("pool", lambda e: e.tensor_tensor(krp[64:96, :], t1[0][64:96, :], t2[0][64:96, :], ALU.add),
                 reads=[r_t1[0], r_t2[0]], writes=[r_krp])
            S.dma("pool", (lambda e, t0=t0: [e.dma_start(out=scr["KTr"][:, t0:t0 + 512], in_=krp[64:96, :])]), 1, "s_krp",
                  reads=[r_krp])
            for h in range(8):
                b = h % 2
                ps, rp = nextpm()
                mm_fm(ps, rp, wuq, h * 96, 96, cqT, [r_cqT], 3)
                ps2, rp2 = nextpm()
                mm_fm(ps2, rp2, wuq, 768 + h * 32, 32, cqT, [r_cqT], 3, out_p0=64)
                S.op("dve", (lambda e, ps=ps, b=b: e.tensor_tensor(t1[b][0:96, :], ps[0:96, :], tct[0:96, :], ALU.mult)),
                     reads=[rp, r_tab], writes=[r_t1[b]])
                S.op("dve", (lambda e, ps2=ps2, b=b: e.tensor_tensor(t2[b][64:96, :], ps2[64:96, :], tst[64:96, :], ALU.mult)),
                     reads=[rp2, r_tab], writes=[r_t2[b]])
                S.op("pool", (lambda e, b=b: e.tensor_tensor(t1[b][64:96, :], t1[b][64:96, :], t2[b][64:96, :], ALU.add)),
                     reads=[r_t2[b]], writes=[r_t1[b]])
                S.op("pool", (lambda e, b=b, h=h: e.tensor_tensor(qout[0:96, h, :], t1[b][0:96, :], rsq[0:96, :], ALU.mult)),
                     reads=[r_t1[b], r_rsq], writes=[r_qout])
            S.dma("pool", (lambda e, t0=t0: [e.dma_start(out=scr["QT"][h, :, t0:t0 + 512], in_=qout[0:96, h, :])
                                            for h in range(8)]), 8, "s_q", reads=[r_qout])
            for a in range(4):
                ps, rp = nextpm()
                mm_fm(ps, rp, wuk, a * 128, 128, ckvT, [r_ckvT], 2)
                S.op("dve", (lambda e, ps=ps, a=a: e.tensor_tensor(knT[:, a, :], ps[:], rskv[:], ALU.mult)),
                     reads=[rp, r_rskv], writes=[r_knT])
            S.dma("pool", (lambda e, t0=t0: [e.dma_start(out=scr["KTn"][a * 128:(a + 1) * 128, t0:t0 + 512], in_=knT[:, a, :])
                                            for a in range(4)]), 4, "s_kn", reads=[r_knT])
            for j in range(4):
                ps, rp = nextpm()
                mm_tm(ps, rp, ckvT, [r_ckvT], j, wuv, 0, 512, 2)
                S.op("act", (lambda e, ps=ps, j=j: e.activation(vt[:, j, :], ps[:], AF.Copy, scale=rstok[:, j:j + 1])),
                     reads=[rp, r_rstok], writes=[r_vt])
            S.dma("pool", (lambda e, t0=t0: [e.dma_start(
                out=scr["VA"][:, t0 + j * 128:t0 + (j + 1) * 128, :].rearrange("h t d -> t h d"),
                in_=vt[:, j, :].rearrange("t (h d) -> t h d", h=8)) for j in range(4)]), 4, "s_v", reads=[r_vt])
            for h in range(4):
                ps, rp = nextpm()
                mm_fm(ps, rp, win, C_HQ + h * 128, 128, xnT, r_xnT, NKC)
                S.op("act", (lambda e, ps=ps, h=h: e.activation(qh[:, h, :], ps[:], AF.Silu)), reads=[rp], writes=[r_qh])
            for j in range(4):
                ps, rp = nextpm()
                mm_tm(ps, rp, xnT, r_xnT, j, win, C_HI, 512, NKC)
                S.op("act", (lambda e, ps=ps, j=j: e.copy(vht[:, j, :], ps[:])), reads=[rp], writes=[r_vht])
                ps, rp = nextpm()
                mm_tm(ps, rp, xnT, r_xnT, j, win, C_HG, 512, NKC)
                S.op("act", (lambda e, ps=ps, j=j: e.activation(ght[:, j, :], ps[:], AF.Silu)), reads=[rp], writes=[r_ght])
            S.dma("pool", (lambda e, t0=t0: [
                e.dma_start(out=scr["VH"][t0:t0 + 512, :].rearrange("(j p) c -> p j c", p=128), in_=vht[:]),
                e.dma_start(out=scr["GH"][t0:t0 + 512, :].rearrange("(j p) c -> p j c", p=128), in_=ght[:])]),
                2, "s_vg", reads=[r_vht, r_ght])
            for d in range(2):
                for h in range(4):
                    hd = d * 4 + h
                    b = hd % 2
                    A, B, Cc, E1, E2 = hA[b], hB[b], hC[b], hE1[b], hE2[b]
                    rA, rB, rC, rE1, rE2 = r_hA[b], r_hB[b], r_hC[b], r_hE1[b], r_hE2[b]
                    ps, rp = nextpm()
                    mm_fm(ps, rp, win, (C_HFF if d == 0 else C_HFB) + h * 128, 128, xnT, r_xnT, NKC)
                    lbs, omls = lb[:, hd:hd + 1], oml[:, hd:hd + 1]
                    S.op("act", (lambda e, ps=ps, A=A: e.activation(A[:], ps[:], AF.Sigmoid)), reads=[rp], writes=[rA])
                    S.op("act", (lambda e, ps=ps, B=B: e.activation(B[:], ps[:], AF.Sigmoid, scale=-1.0)), reads=[rp], writes=[rB])
                    S.op("pool", (lambda e, B=B, omls=omls: e.tensor_scalar(B[:], B[:], omls, 0.0, ALU.mult, ALU.add)),
                         reads=[r_lb], writes=[rB])
                    S.op("dve", (lambda e, A=A, omls=omls, lbs=lbs: e.tensor_scalar(A[:], A[:], omls, lbs, ALU.mult, ALU.add)),
                         reads=[r_lb], writes=[rA])
                    S.op("act", (lambda e, A=A: e.activation(A[:], A[:], AF.Ln)), writes=[rA])
                    S.op("dve", (lambda e, A=A, Cc=Cc: e.tensor_tensor_scan(Cc[:], rmask[:], A[:], 0.0, ALU.mult, ALU.add)),
                         reads=[rA, r_c], writes=[rC])
                    Cv = Cc[:].rearrange("p (c t) -> p c t", t=32)
                    Av = A[:].rearrange("p (c t) -> p c t", t=32)
                    if d == 0:
                        bsrc, rb = Cc, rC
                        dcol = 31
                    else:
                        S.op("pool", (lambda e, A=A, Cc=Cc: e.tensor_tensor(A[:], A[:], Cc[:], ALU.subtract)),
                             reads=[rC], writes=[rA])
                        S.op("pool", (lambda e, Av=Av, Cv=Cv: e.tensor_tensor(Av, Av, Cv[:, :, 31:32].broadcast_to([128, 16, 32]), ALU.add)),
                             reads=[rC], writes=[rA])
                        bsrc, rb = A, rA
                        dcol = 0
                    S.op("act", (lambda e, E1=E1, bsrc=bsrc: e.activation(E1[:], bsrc[:], AF.Exp)), reads=[rb], writes=[rE1])
                    S.op("act", (lambda e, E2=E2, bsrc=bsrc: e.activation(E2[:], bsrc[:], AF.Exp, scale=-1.0)), reads=[rb], writes=[rE2])
                    S.op("pool", (lambda e, E1=E1, h=h, hd=hd: e.tensor_tensor(qpo[:, hd, :], qh[:, h, :], E1[:], ALU.mult)),
                         reads=[rE1, r_qh], writes=[r_qpo])
                    S.op("dve", (lambda e, E2=E2, B=B: e.tensor_tensor(E2[:], E2[:], B[:], ALU.mult)), reads=[rB], writes=[rE2])
                    S.op("act", (lambda e, E2=E2, hd=hd: e.copy(kpo[:, hd, :], E2[:])), reads=[rE2], writes=[r_kpo])
                    E1v = E1[:].rearrange("p (c t) -> p c t", t=32)
                    E2v = E2[:].rearrange("p (c t) -> p c t", t=32)
                    S.op("dve", (lambda e, E1v=E1v, hd=hd, dcol=dcol: e.tensor_copy(dco[:, hd, :], E1v[:, :, dcol])),
                         reads=[rE1], writes=[r_dco])
                    kv = kppo[:, hd, :].rearrange("p (c t) -> p c t", t=32)
                    S.op("pool", (lambda e, E1v=E1v, E2v=E2v, kv=kv, dcol=dcol: e.tensor_tensor(
                        kv, E2v, E1v[:, :, dcol:dcol + 1].broadcast_to([128, 16, 32]), ALU.mult)),
                        reads=[rE1, rE2], writes=[r_kppo])
            nch = ntok // 32
            S.dma("pool", (lambda e, t0=t0, i=i: [
                e.dma_start(out=scr["QP"][:, :, t0:t0 + 512].rearrange("h p t -> p h t"), in_=qpo[:]),
                e.dma_start(out=scr["DC"][:, :, i * 16:(i + 1) * 16].rearrange("h p c -> p h c"), in_=dco[:])]),
                2, "s_qp", reads=[r_qpo, r_dco])
            for j in range(4):
                for d in range(2):
                    for h in range(4):
                        hd = d * 4 + h
                        S.op("pe", (lambda e, j=j, hd=hd, h=h: e.matmul(
                            pa[:, h * 128:(h + 1) * 128], kpo[:, hd, j * 128:(j + 1) * 128], qpo[:, hd, j * 128:(j + 1) * 128],
                            start=True, stop=True)), reads=[r_kpo, r_qpo], writes=[r_pa])
                    msk = (mF if d == 0 else mB)
                    S.op("dve", (lambda e, j=j, d=d, msk=msk: e.tensor_tensor(
                        ato[:, j, d * 4:(d + 1) * 4, :], pa[:].rearrange("p (h t) -> p h t", h=4),
                        msk[:].rearrange("p (o t) -> p o t", o=1).broadcast_to([128, 4, 128]), ALU.mult)),
                        reads=[r_pa, r_c], writes=[r_ato])
                for hd in range(8):
                    S.op("pe", (lambda e, j=j, hd=hd: e.transpose(ptk[:, hd * 128:(hd + 1) * 128],
                                                                   kppo[:, hd, j * 128:(j + 1) * 128], ident[:])),
                         reads=[r_kppo, r_ident], writes=[r_ptk])
                S.op("act", (lambda e, j=j: e.copy(kto[:, j, :, :], ptk[:].rearrange("p (h k) -> p h k", h=8))),
                     reads=[r_ptk], writes=[r_kto])
            S.dma("pool", (lambda e, t0=t0: [
                e.dma_start(out=scr["AT"][t0:t0 + 512, :, :].rearrange("(j p) h t -> p j h t", p=128), in_=ato[:]),
                e.dma_start(out=scr["KPT"][t0:t0 + 512, :, :].rearrange("(j p) h k -> p j h k", p=128), in_=kto[:])]),
                2, "s_at", reads=[r_ato, r_kto])
        S.barrier()
        S.emit()
    return S


def attn_phase(nc, tag, seqs, scr, xchg=None):
    S = Sched(nc, tag)
    SKMAX = max(sum(p[3] for p in sq["kp"]) for sq in seqs)
    SL = max(sq["nq"] for sq in seqs)
    scale = 96.0 ** -0.5
    with ExitStack() as es:
        C = Ctx(nc, tag, es)
        R = S.res
        kt = [C.sb(f"kt{i}", [128, SKMAX], BF16) for i in range(2)]
        vt = [C.sb(f"vt{i}", [128, SKMAX // 128, 65], BF16) for i in range(2)]
        qt = [C.sb(f"qt{i}", [128, SL], BF16) for i in range(2)]
        pT = [C.sb(f"pT{i}", [128, 512], BF16) for i in range(3)]
        onesf = C.sb("onesf", [128, 64], F32)
        rl = C.sb("rl", [128, 512], F32)
        osb = C.sb("osb", [128, 512], F32)
        obf = [C.sb(f"obf{i}", [128, 512], BF16) for i in range(2)]
        psS = [C.ps(f"psS{i}", [128, 512]) for i in range(3)]
        psO = [C.ps(f"psO{i}", [128, 512]) for i in range(2)]
        psB = C.ps("psB", [128, 512])
        r_kt, r_vt, r_qt = [R(), R()], [R(), R()], [R(), R()]
        r_pT, r_psS, r_psO = [R(), R(), R()], [R(), R(), R()], [R(), R()]
        r_ones, r_rl, r_osb, r_obf, r_psB = R(), R(), R(), [R(), R()], R()
        S.op("pool", lambda e: e.memset(onesf[:], 1.0), writes=[r_ones])
        for i in range(2):
            S.op("pool", (lambda e, i=i: e.memset(vt[i][:, :, 64:65], 1.0)), writes=[r_vt[i]])
        r_gath = R()
        r_g2, r_g3 = R(), R()
        r_gs = []
        if xchg is not None:
            n0 = xchg["n0"]
            r_xin = R()
            S.dma("sp", (lambda e: [e.dma_start(out=xchg["XK_in"][a_], in_=scr["KTn"][a_ * 128:(a_ + 1) * 128, 0:n0]) for a_ in range(4)]
                         + [e.dma_start(out=xchg["XR_in"][0:32, :], in_=scr["KTr"][:, 0:n0])]
                         + [e.dma_start(out=xchg["XV_in"][a_].rearrange("(hh t) d -> hh t d", hh=2), in_=scr["VA"][2 * a_:2 * a_ + 2, 0:n0, :])
                            for a_ in range(4)]), 9, "xin", writes=[r_xin])
            r_gs = [r_gath, r_g2, r_g3] + [R() for _ in range(6)]
            ccl = [(xchg["XK_in"][a_], xchg["XK_out"][a_]) for a_ in range(4)] + [(xchg["XR_in"], xchg["XR_out"])] + \
                  [(xchg["XV_in"][a_], xchg["XV_out"][a_]) for a_ in range(4)]
            for ci_, (cin, cout) in enumerate(ccl):
                S.dma("pool", (lambda e, cin=cin, cout=cout: [e.collective_compute(
                    "AllGather", ALU.bypass, replica_groups=xchg["groups"], ins=[cin.opt()], outs=[cout.opt()])]), 1, f"cc{ci_}",
                    reads=[r_xin], writes=[r_gs[ci_]], inc=1)
        heads = []
        for sq in seqs:
            for h in range(8):
                heads.append((sq, h))
        units = []
        obc = [0]
        for hi, (sq, h) in enumerate(heads):
            SK = sum(p[3] for p in sq["kp"])
            for qb in range(sq["nq"] // 512):
                for kc in range(SK // 128):
                    units.append((hi, qb, kc, SK // 128, obc[0] % 2, qb == sq["nq"] // 512 - 1))
                obc[0] += 1
        loaded = [-1]

        def load_head(hi):
            if hi >= len(heads) or hi <= loaded[0]:
                return
            loaded[0] = hi
            sq, h = heads[hi]
            b = hi % 2
            off = 0
            lst = []
            for (ktn, ktr, va, n) in sq["kp"]:
                lst.append((kt[b][0:64, off:off + n], ktn(h)))
                lst.append((kt[b][64:96, off:off + n], ktr))
                off += n
            dep = r_gs if sq.get("gathered") else []
            S.dma("sp", (lambda e, lst=lst: [e.dma_start(out=o, in_=i_) for o, i_ in lst]), len(lst), f"kt{b}", reads=dep, writes=[r_kt[b]])
            off = 0
            lst2 = []
            for (ktn, ktr, va, n) in sq["kp"]:
                lst2.append((vt[b][:, off // 128:(off + n) // 128, 0:64], va(h).rearrange("(c p) d -> p c d", p=128)))
                off += n
            S.dma("sp", (lambda e, lst2=lst2: [e.dma_start(out=o, in_=i_) for o, i_ in lst2]), len(lst2), f"vt{b}", reads=dep, writes=[r_vt[b]])
            q0, nq = sq["q0"], sq["nq"]
            S.dma("sp", (lambda e, b=b, h=h, q0=q0, nq=nq: [e.dma_start(out=qt[b][0:96, 0:nq], in_=scr["QT"][h, :, q0:q0 + nq])]), 1,
                  f"qt{b}", writes=[r_qt[b]])

        def qk(u):
            hi, qb, kc, nkc, ob, lastq = units[u]
            b = hi % 2
            r = u % 3
            S.op("pe", (lambda e: e.matmul(psS[r][:], kt[b][0:96, kc * 128:(kc + 1) * 128], qt[b][0:96, qb * 512:(qb + 1) * 512],
                                           start=True, stop=True)), reads=[r_kt[b], r_qt[b]], writes=[r_psS[r]])
            S.op("act", (lambda e: e.activation(pT[r][:], psS[r][:], AF.Exp, scale=scale)), reads=[r_psS[r]], writes=[r_pT[r]])

        def pv(u):
            hi, qb, kc, nkc, ob, lastq = units[u]
            b = hi % 2
            r = u % 3
            S.op("pe", (lambda e: e.matmul(psO[ob][0:65, :], vt[b][:, kc, 0:65], pT[r][:], start=(kc == 0), stop=(kc == nkc - 1))),
                 reads=[r_vt[b], r_pT[r]], writes=[r_psO[ob]])
            if kc == nkc - 1:
                sq, h = heads[hi]
                S.op("dve", (lambda e: e.reciprocal(rl[64:65, :], psO[ob][64:65, :])), reads=[r_psO[ob]], writes=[r_rl])
                S.op("pe", (lambda e: e.matmul(psB[0:64, :], onesf[64:65, 0:64], rl[64:65, :], start=True, stop=True)),
                     reads=[r_ones, r_rl], writes=[r_psB])
                S.op("dve", (lambda e: e.tensor_copy(osb[0:64, :], psO[ob][0:64, :])), reads=[r_psO[ob]], writes=[r_osb])
                S.op("dve", (lambda e: e.tensor_tensor(obf[ob][0:64, :], osb[0:64, :], psB[0:64, :], ALU.mult)),
                     reads=[r_osb, r_psB], writes=[r_obf[ob]])
                t0 = sq["q0"] + qb * 512
                S.dma("pool", (lambda e: [e.dma_start(out=scr["MIXT"][h * 64:(h + 1) * 64, t0:t0 + 512], in_=obf[ob][0:64, :])]), 1,
                      f"so{ob}", reads=[r_obf[ob]])

        load_head(0)
        load_head(1)
        n = len(units)
        for u in range(min(2, n)):
            qk(u)
        for u in range(n):
            if u + 2 < n:
                hi2 = units[u + 2][0]
                qk(u + 2)
            pv(u)
            hi, qb, kc, nkc, ob_, lastq = units[u]
            if lastq and kc == nkc - 1:
                load_head(hi + 2)
        S.barrier()
        S.emit()
    return S


def scan_phase(nc, tag, seq_lens, ntok, scr, xchg=None, cst=None):
    S = Sched(nc, tag)
    nch = ntok // 32
    with ExitStack() as es:
        C = Ctx(nc, tag, es)
        R = S.res
        NR = 3
        qpb = [[C.sb(f"qpb{d}{i}", [128, 4, 128], BF16) for i in range(NR)] for d in range(2)]
        atb = [[C.sb(f"atb{d}{i}", [128, 4, 128], BF16) for i in range(NR)] for d in range(2)]
        kpb = [[C.sb(f"kpb{d}{i}", [128, 4, 128], BF16) for i in range(NR)] for d in range(2)]
        vb = [[C.sb(f"vb{d}{i}", [128, 512], BF16) for i in range(NR)] for d in range(2)]
        r_ld = [[R() for i in range(NR)] for d in range(2)]
        dct = C.sb("dct", [128, 8, nch], F32)
        zer = C.sb("zer", [128, 512], BF16)
        SstAll = C.sb("SstAll", [128, 8, 129], F32)
        Sst = [SstAll[:, hd, 0:128] for hd in range(8)]
        Sbf = [C.sb(f"Sbf{hd}", [128, 128], BF16) for hd in range(8)]
        r_S = [R() for hd in range(8)]
        r_Sb = [R() for hd in range(8)]
        ot = [[C.sb(f"ot{d}{i}", [128, 512], F32) for i in range(2)] for d in range(2)]
        r_ot = [[R() for i in range(2)] for d in range(2)]
        psO = [[C.ps(f"psO{d}{i}", [128, 512]) for i in range(2)] for d in range(2)]
        r_psO = [[R() for i in range(2)] for d in range(2)]
        psU = [C.ps(f"psU{i}", [128, 512]) for i in range(4)]
        r_psU = [R() for i in range(4)]
        r_dc, r_z = R(), R()
        S.dma("sp", (lambda e: [e.dma_start(out=dct[:, hd, :], in_=scr["DC"][hd, :, :]) for hd in range(8)]), 8, "dc", writes=[r_dc])
        S.op("pool", lambda e: e.memset(zer[:], 0.0), writes=[r_z])
        ui = [0]
        offs = []
        a = 0
        for n in seq_lens:
            offs.append(a)
            a += n

        def run_seq(s0, SLs, mode, zero_init=True):
            NB = SLs // 128
            if zero_init:
                for hd in range(8):
                    S.op("pool", (lambda e, hd=hd: e.memset(Sst[hd], 0.0)), writes=[r_S[hd]])
                    S.op("pool", (lambda e, hd=hd: e.memset(Sbf[hd][:], 0.0)), writes=[r_Sb[hd]])

            def load(step):
                if step >= NB:
                    return
                for d in range(2):
                    blk = step if d == 0 else NB - 1 - step
                    t0 = s0 + blk * 128
                    i = step % NR
                    if mode == "full":
                        S.dma("sp", (lambda e, d=d, i=i, t0=t0: [
                            e.dma_start(out=qpb[d][i][:], in_=scr["QP"][d * 4:(d + 1) * 4, :, t0:t0 + 128].rearrange("h p t -> p h t")),
                            e.dma_start(out=atb[d][i][:], in_=scr["AT"][t0:t0 + 128, d * 4:(d + 1) * 4, :]),
                            e.dma_start(out=kpb[d][i][:], in_=scr["KPT"][t0:t0 + 128, d * 4:(d + 1) * 4, :]),
                            e.dma_start(out=vb[d][i][:], in_=scr["VH"][t0:t0 + 128, :])]), 4, f"ld{d}{i}", writes=[r_ld[d][i]])
                    else:
                        S.dma("sp", (lambda e, d=d, i=i, t0=t0: [
                            e.dma_start(out=kpb[d][i][:], in_=scr["KPT"][t0:t0 + 128, d * 4:(d + 1) * 4, :]),
                            e.dma_start(out=vb[d][i][:], in_=scr["VH"][t0:t0 + 128, :])]), 2, f"ld{d}{i}", writes=[r_ld[d][i]])
            load(0)
            load(1)
            for step in range(NB):
                load(step + 2)
                i = step % NR
                ob = step % 2
                if mode == "full":
                    for d in range(2):
                        S.op("pe", (lambda e, d=d, ob=ob: e.matmul(psO[d][ob][:], zer[:, 0:128], zer[:], start=True, stop=False,
                                                                   skip_group_check=True)), reads=[r_z], writes=[r_psO[d][ob]])
                        for h in range(4):
                            S.op("pe", (lambda e, d=d, ob=ob, h=h, i=i: e.matmul(
                                psO[d][ob][:, h * 128:(h + 1) * 128], atb[d][i][:, h, :], vb[d][i][:, h * 128:(h + 1) * 128],
                                start=False, stop=False, skip_group_check=True)), reads=[r_ld[d][i]], writes=[r_psO[d][ob]])
                for ci in range(4):
                    for d in range(2):
                        blk = step if d == 0 else NB - 1 - step
                        c = ci if d == 0 else 3 - ci
                        gch = (s0 + blk * 128) // 32 + c
                        for h in range(4):
                            hd = d * 4 + h
                            if mode == "full":
                                S.op("pe", (lambda e, d=d, ob=ob, h=h, i=i, c=c, hd=hd: e.matmul(
                                    psO[d][ob][32 * c:32 * c + 32, h * 128:(h + 1) * 128], qpb[d][i][:, h, 32 * c:32 * c + 32], Sbf[hd][:],
                                    start=False, stop=(ci == 3), skip_group_check=True, tile_position=(0, 32 * c))),
                                    reads=[r_ld[d][i], r_Sb[hd]], writes=[r_psO[d][ob]])
                            pu = ui[0] % 4
                            ui[0] += 1
                            S.op("pe", (lambda e, d=d, h=h, i=i, c=c, pu=pu: e.matmul(
                                psU[pu][:, 0:128], kpb[d][i][32 * c:32 * c + 32, h, :], vb[d][i][32 * c:32 * c + 32, h * 128:(h + 1) * 128],
                                start=True, stop=True, tile_position=(32 * c, 0))),
                                reads=[r_ld[d][i]], writes=[r_psU[pu]])
                            S.op("dve", (lambda e, hd=hd, pu=pu, gch=gch: e.scalar_tensor_tensor(
                                Sst[hd], Sst[hd], dct[:, hd, gch:gch + 1], psU[pu][:, 0:128], ALU.mult, ALU.add)),
                                reads=[r_psU[pu], r_dc], writes=[r_S[hd]])
                            if mode == "full":
                                S.op("act", (lambda e, hd=hd: e.copy(Sbf[hd][:], Sst[hd])), reads=[r_S[hd]], writes=[r_Sb[hd]])
                if mode == "full":
                    for d in range(2):
                        blk = step if d == 0 else NB - 1 - step
                        t0 = s0 + blk * 128
                        S.op("dve" if d == 0 else "act",
                             (lambda e, d=d, ob=ob: (e.tensor_copy(ot[d][ob][:], psO[d][ob][:]) if d == 0
                                                     else e.copy(ot[d][ob][:], psO[d][ob][:]))),
                             reads=[r_psO[d][ob]], writes=[r_ot[d][ob]])
                        S.dma("pool", (lambda e, d=d, ob=ob, t0=t0: [e.dma_start(out=scr["OF"][d, t0:t0 + 128, :], in_=ot[d][ob][:])]), 1,
                              f"so{d}{ob}", reads=[r_ot[d][ob]])

        if xchg is None:
            for s0, n in zip(offs, seq_lens):
                run_seq(s0, n, "full")
        else:
            n0 = seq_lens[0]
            G = C.sb("G", [128, 4, 8, 129], F32)
            rkm = C.sb("rkm", [128, 8], F32)
            tmp = C.sb("tmp", [128, 128], F32)
            r_G, r_rk, r_tmp, r_xs = R(), R(), R(), R()
            S.dma("sp", lambda e: [e.dma_start(out=rkm[:], in_=cst["rankmask"])], 1, "rk", writes=[r_rk])
            run_seq(offs[0], n0, "state")
            for hd in range(8):
                S.op("dve", (lambda e, hd=hd: e.tensor_reduce(SstAll[:, hd, 128:129], dct[:, hd, offs[0] // 32:(offs[0] + n0) // 32],
                                                              AX.X, ALU.mult)), reads=[r_dc], writes=[r_S[hd]])
            S.dma("pool", (lambda e: [e.dma_start(out=xchg["XS_in"].rearrange("(h p) c -> p h c", p=128), in_=SstAll[:])]), 1, "xs",
                  reads=r_S, writes=[r_xs])
            S.dma("pool", (lambda e: [e.collective_compute("AllGather", ALU.bypass, replica_groups=xchg["groups"],
                                                           ins=[xchg["XS_in"].opt()], outs=[xchg["XS_out"].opt()])]), 1, "ccs", reads=[r_xs], writes=[r_G], inc=1)
            for s0, n in list(zip(offs, seq_lens))[1:]:
                run_seq(s0, n, "full")
            S.dma("sp", (lambda e: [e.dma_start(out=G[:], in_=xchg["XS_out"].rearrange("(r h p) c -> p r h c", r=4, h=8))]), 1, "g",
                  reads=[r_G], writes=[r_G])
            for hd in range(8):
                S.op("pool", (lambda e, hd=hd: e.memset(Sst[hd], 0.0)), writes=[r_S[hd]])
                order = range(4) if hd < 4 else range(3, -1, -1)
                for i in order:
                    mcol = rkm[:, (0 if hd < 4 else 4) + i:(0 if hd < 4 else 4) + i + 1]
                    S.op("dve", (lambda e, hd=hd, i=i: e.scalar_tensor_tensor(tmp[:], Sst[hd], G[:, i, hd, 128:129], G[:, i, hd, 0:128],
                                                                            ALU.mult, ALU.add)), reads=[r_G, r_S[hd]], writes=[r_tmp])
                    S.op("dve", (lambda e, hd=hd: e.tensor_tensor(tmp[:], tmp[:], Sst[hd], ALU.subtract)), reads=[r_S[hd]], writes=[r_tmp])
                    S.op("dve", (lambda e, hd=hd, mcol=mcol: e.scalar_tensor_tensor(Sst[hd], tmp[:], mcol, Sst[hd], ALU.mult, ALU.add)),
                         reads=[r_tmp, r_rk], writes=[r_S[hd]])
                S.op("act", (lambda e, hd=hd: e.copy(Sbf[hd][:], Sst[hd])), reads=[r_S[hd]], writes=[r_Sb[hd]])
            run_seq(offs[0], n0, "full", zero_init=False)
        S.barrier()
        S.emit()
    return S


def outproj_phase(nc, tag, ntok, h1_d, h2_d, cst, w, scr):
    S = Sched(nc, tag)
    ntiles = ntok // 512
    with ExitStack() as es:
        C = Ctx(nc, tag, es)
        R = S.res
        wo = C.sb("wo", [128, 8, D], BF16)
        st = [C.sb("st0", [128, 512], F32), C.sb("st1", [128, 512], F32)]
        ident = C.sb("ident", [128, 128], BF16)
        onb = C.sb("onb", [128, 512], F32)
        ht = [C.sb(f"ht{i}", [128, 4, D], F32) for i in range(2)]
        mT = [C.sb(f"mT{i}", [128, 8, 512], BF16) for i in range(2)]
        of = [C.sb(f"of{i}", [128, 4, 512], F32) for i in range(2)]
        ob = [C.sb(f"ob{i}", [128, 4, 512], F32) for i in range(2)]
        gh = [C.sb(f"gh{i}", [128, 4, 512], BF16) for i in range(2)]
        osum = C.sb("osum", [128, 512], F32)
        junk = C.sb("junk", [128, 128], BF16)
        stat = C.sb("stat", [128, 8], F32)
        mh = C.sb("mh", [128, 512], BF16)
        pt = C.ps("pt", [128, 1024], BF16)
        po = [C.ps(f"po{i}", [128, 512]) for i in range(2)]
        r_w, r_st, r_c, r_ident = R(), [R(), R()], R(), R()
        S.dma("sp", lambda e: [e.dma_start(out=onb[:], in_=cst["hg_norm_b"]), e.dma_start(out=st[0][:, 0:128], in_=cst["ident"])],
              2, "c", writes=[r_c, r_st[0]])
        S.op("pool", lambda e: e.tensor_copy(ident[:], st[0][:, 0:128]), reads=[r_st[0]], writes=[r_ident])
        qi = [1]
        load_w(S, w["w_o"], wo, 8, D, None, st, r_st, r_w, qi)
        r_ld = [R(), R()]
        r_ht = [[R() for j in range(4)] for i in range(2)]
        r_mT = [[R() for j in range(4)] for i in range(2)]
        r_osum, r_junk, r_stat, r_mh, r_pt, r_po = R(), R(), R(), R(), R(), [R(), R()]

        def load(i):
            if i >= ntiles:
                return
            b = i % 2
            t0 = i * 512
            S.dma("sp", (lambda e: [e.dma_start(out=ht[b][:, j, :], in_=h1_d[t0 + j * 128:t0 + (j + 1) * 128, :]) for j in range(4)]),
                  4, f"ht{b}", writes=r_ht[b])
            S.dma("sp", (lambda e: [
                e.dma_start(out=mT[b][:, 0:4, :], in_=scr["MIXT"][0:512, t0:t0 + 512].rearrange("(c p) t -> p c t", p=128)),
                e.dma_start(out=of[b][:], in_=scr["OF"][0, t0:t0 + 512, :].rearrange("(j p) c -> p j c", p=128)),
                e.dma_start(out=ob[b][:], in_=scr["OF"][1, t0:t0 + 512, :].rearrange("(j p) c -> p j c", p=128)),
                e.dma_start(out=gh[b][:], in_=scr["GH"][t0:t0 + 512, :].rearrange("(j p) c -> p j c", p=128))]),
                4, f"ld{b}", writes=[r_ld[b]] + r_mT[b])
        load(0)

        def body(i):
            b = i % 2
            load(i + 1)
            t0 = i * 512
            for j in range(4):
                S.op("dve", (lambda e, j=j: e.tensor_tensor(osum[:], of[b][:, j, :], ob[b][:, j, :], ALU.add)),
                     reads=[r_ld[b]], writes=[r_osum])
                for h in range(4):
                    S.op("act", (lambda e, h=h: e.activation(junk[:], osum[:, h * 128:(h + 1) * 128], AF.Square,
                                                             accum_out=stat[:, h:h + 1])), reads=[r_osum], writes=[r_junk, r_stat])
                S.op("act", lambda e: e.activation(stat[:, 4:8], stat[:, 0:4], AF.Sqrt, bias=EPS, scale=1.0 / 128),
                     reads=[r_stat], writes=[r_stat])
                S.op("dve", lambda e: e.reciprocal(stat[:, 4:8], stat[:, 4:8]), reads=[r_stat], writes=[r_stat])
                ov = osum[:].rearrange("p (h v) -> p h v", h=4)
                S.op("dve", (lambda e, ov=ov: e.tensor_tensor(ov, ov, stat[:, 4:8].rearrange("p (h o) -> p h o", o=1).broadcast_to([128, 4, 128]),
                                                              ALU.mult)), reads=[r_stat], writes=[r_osum])
                S.op("pool", lambda e: e.tensor_tensor(osum[:], osum[:], onb[:], ALU.mult), reads=[r_c], writes=[r_osum])
                S.op("dve", (lambda e, j=j: e.tensor_tensor(mh[:], osum[:], gh[b][:, j, :], ALU.mult)),
                     reads=[r_osum, r_ld[b]], writes=[r_mh])
                for c in range(4):
                    S.op("pe", (lambda e, c=c: e.transpose(pt[:, c * 128:(c + 1) * 128], mh[:, c * 128:(c + 1) * 128], ident[:])),
                         reads=[r_mh, r_ident], writes=[r_pt])
                S.op("act", (lambda e, j=j: e.copy(mT[b][:, 4:8, j * 128:(j + 1) * 128],
                                                   pt[:, 0:512].rearrange("p (k t) -> p k t", k=4))),
                     reads=[r_pt], writes=[r_mT[b][j]])
                for hh in range(2):
                    pb = (j * 2 + hh) % 2
                    for kc in range(8):
                        S.op("pe", (lambda e, j=j, hh=hh, kc=kc, pb=pb: e.matmul(
                            po[pb][:], mT[b][:, kc, j * 128:(j + 1) * 128], wo[:, kc, hh * 512:(hh + 1) * 512],
                            start=(kc == 0), stop=(kc == 7))), reads=[r_w, r_mT[b][j]], writes=[r_po[pb]])
                    dst = ht[b][:, j, hh * 512:(hh + 1) * 512]
                    S.op("dve", (lambda e, dst=dst, pb=pb: e.tensor_tensor(dst, dst, po[pb][:], ALU.add)),
                         reads=[r_po[pb]], writes=[r_ht[b][j]])
                S.dma("pool", (lambda e, j=j: [e.dma_start(out=h2_d[t0 + j * 128:t0 + (j + 1) * 128, :], in_=ht[b][:, j, :])]), 1,
                      f"so{b}{j}", reads=[r_ht[b][j]])
        for i in range(ntiles):
            body(i)
        S.barrier()
        S.emit()
    return S


def ple_phase(nc, tag, ntok, h3_d, p_d, y_d, cst, w):
    S = Sched(nc, tag)
    ntiles = ntok // 512
    with ExitStack() as es:
        C = Ctx(nc, tag, es)
        R = S.res
        wg = C.sb("wg", [128, 8, D], BF16)
        wp = C.sb("wp", [128, 2, D], BF16)
        st = [C.sb("st0", [128, 512], F32), C.sb("st1", [128, 512], F32)]
        ident = C.sb("ident", [128, 128], BF16)
        gain = C.sb("gain", [128, 8], F32)
        fnb = C.sb("fnb", [128, D], F32)
        ht = [C.sb(f"ht{i}", [128, 4, D], F32) for i in range(2)]
        ptl = [C.sb(f"ptl{i}", [128, 4, 256], F32) for i in range(2)]
        xn = C.sb("xn", [128, D], BF16)
        junk = C.sb("junk", [128, D], BF16)
        stat = C.sb("stat", [128, 16], F32)
        xnT = C.sb("xnT", [128, 8, 512], BF16)
        pb16 = C.sb("pb16", [128, 256], BF16)
        pT = C.sb("pT", [128, 2, 512], BF16)
        gsb = [C.sb(f"gsb{i}", [128, 512], F32) for i in range(2)]
        pt = C.ps("pt", [128, 1024], BF16)
        pg = [C.ps(f"pg{i}", [128, 512]) for i in range(2)]
        pp = [C.ps(f"pp{i}", [128, 512]) for i in range(2)]
        r_w, r_st, r_c, r_ident = R(), [R(), R()], R(), R()
        S.dma("sp", lambda e: [e.dma_start(out=gain[:], in_=cst["ple_norm"]), e.dma_start(out=fnb[:], in_=cst["final_norm_b"]),
                               e.dma_start(out=st[0][:, 0:128], in_=cst["ident"])], 3, "c", writes=[r_c, r_st[0]])
        S.op("pool", lambda e: e.tensor_copy(ident[:], st[0][:, 0:128]), reads=[r_st[0]], writes=[r_ident])
        S.op("pool", lambda e: e.tensor_copy(stat[:, 15:16], gain[:, 0:1]), reads=[r_c], writes=[R()])
        qi = [1]
        load_w(S, w["w_ple_gate"], wg, 8, D, gain, st, r_st, r_w, qi)
        load_w(S, w["w_ple_proj"], wp, 2, D, None, st, r_st, r_w, qi)
        r_ht = [[R() for j in range(4)] for i in range(2)]
        r_pl = [R(), R()]
        r_stat = [R() for j in range(4)]
        r_junk, r_xn, r_pt = R(), R(), R()
        r_xnT = [R() for k in range(8)]
        r_pb16, r_pT, r_gsb, r_pg, r_pp = R(), R(), [R(), R()], [R(), R()], [R(), R()]

        def load(i):
            if i >= ntiles:
                return
            b = i % 2
            t0 = i * 512
            S.dma("sp", (lambda e: [e.dma_start(out=ht[b][:, j, :], in_=h3_d[t0 + j * 128:t0 + (j + 1) * 128, :]) for j in range(4)]),
                  4, f"ht{b}", writes=r_ht[b])
            S.dma("sp", (lambda e: [e.dma_start(out=ptl[b][:], in_=p_d[t0:t0 + 512, :].rearrange("(j p) c -> p j c", p=128))]),
                  1, f"pl{b}", writes=[r_pl[b]])
        load(0)

        def body(i):
            b = i % 2
            load(i + 1)
            t0 = i * 512
            for j in range(4):
                norm_transpose(S, ht[b], r_ht[b][j], j, stat, r_stat, junk, r_junk, xn, r_xn, pt, r_pt, ident, r_ident, xnT, r_xnT)
                S.op("pool", (lambda e, j=j: e.tensor_copy(pb16[:], ptl[b][:, j, :])), reads=[r_pl[b]], writes=[r_pb16])
                for c in range(2):
                    S.op("pe", (lambda e, c=c: e.transpose(pt[:, c * 128:(c + 1) * 128], pb16[:, c * 128:(c + 1) * 128], ident[:])),
                         reads=[r_pb16, r_ident], writes=[r_pt])
                S.op("act", (lambda e, j=j: e.copy(pT[:, :, j * 128:(j + 1) * 128], pt[:, 0:256].rearrange("p (k t) -> p k t", k=2))),
                     reads=[r_pt], writes=[r_pT])
            for j in range(4):
                for hh in range(2):
                    k2 = (j * 2 + hh) % 2
                    for kc in range(8):
                        S.op("pe", (lambda e, j=j, hh=hh, kc=kc, k2=k2: e.matmul(
                            pg[k2][:], xnT[:, kc, j * 128:(j + 1) * 128], wg[:, kc, hh * 512:(hh + 1) * 512],
                            start=(kc == 0), stop=(kc == 7))), reads=[r_w] + r_xnT, writes=[r_pg[k2]])
                    for kc in range(2):
                        S.op("pe", (lambda e, j=j, hh=hh, kc=kc, k2=k2: e.matmul(
                            pp[k2][:], pT[:, kc, j * 128:(j + 1) * 128], wp[:, kc, hh * 512:(hh + 1) * 512],
                            start=(kc == 0), stop=(kc == 1))), reads=[r_w, r_pT], writes=[r_pp[k2]])
                    S.op("act", (lambda e, k2=k2: e.activation(gsb[k2][:], pg[k2][:], AF.Sigmoid)), reads=[r_pg[k2]], writes=[r_gsb[k2]])
                    S.op("dve", (lambda e, k2=k2: e.tensor_tensor(gsb[k2][:], gsb[k2][:], pp[k2][:], ALU.mult)),
                         reads=[r_pp[k2]], writes=[r_gsb[k2]])
                    dst = ht[b][:, j, hh * 512:(hh + 1) * 512]
                    S.op("pool", (lambda e, dst=dst, k2=k2: e.tensor_tensor(dst, dst, gsb[k2][:], ALU.add)),
                         reads=[r_gsb[k2]], writes=[r_ht[b][j]])
                xj = ht[b][:, j, :]
                ss, rs = stat[:, 8 + j:9 + j], stat[:, 12 + j:13 + j] if j < 3 else stat[:, 7:8]
                S.op("act", (lambda e, xj=xj, ss=ss: e.activation(junk[:], xj, AF.Square, accum_out=ss)),
                     reads=[r_ht[b][j]], writes=[r_junk, r_stat[j]])
                S.op("act", (lambda e, ss=ss: e.activation(ss, ss, AF.Sqrt, bias=EPS, scale=1.0 / D)), writes=[r_stat[j]])
                S.op("dve", (lambda e, ss=ss: e.reciprocal(ss, ss)), writes=[r_stat[j]])
                S.op("dve", (lambda e, xj=xj, ss=ss: e.scalar_tensor_tensor(xj, xj, ss, fnb[:], ALU.mult, ALU.mult)),
                     reads=[r_stat[j], r_c], writes=[r_ht[b][j]])
                S.dma("pool", (lambda e, j=j, xj=xj: [e.dma_start(out=y_d[t0 + j * 128:t0 + (j + 1) * 128, :], in_=xj)]), 1,
                      f"so{b}{j}", reads=[r_ht[b][j]])
        for i in range(ntiles):
            body(i)
        S.barrier()
        S.emit()
    return S


CONST_SHAPES = {
    "ident": [128, 128], "rmask": [128, 512], "maskF": [128, 128], "maskB": [128, 128],
    "ffn1_norm": [128, 8], "mix_norm": [128, 8], "q_norm": [128, 3], "kv_norm": [128, 2], "hg_lb": [128, 16],
    "hg_norm_b": [128, 512], "ffn2_norm": [128, 8], "ple_norm": [128, 8], "final_norm_b": [128, D],
}
W_SHAPES = {
    "ffn1_wg": [D, DFF], "ffn1_wu": [D, DFF], "ffn1_wd": [DFF, D], "w_in": [D, WIN_COLS], "w_uq": [384, UQ_COLS],
    "w_uk": [256, 512], "w_uv": [256, 512], "w_o": [D, D], "ffn2_wg": [D, DFF], "ffn2_wu": [D, DFF], "ffn2_wd": [DFF, D],
    "w_ple_gate": [D, D], "w_ple_proj": [256, D],
}


def build_program(seq_lens, dbg=(), phases=None, gather=False):
    Sched.DMA_SEMS = {}
    Sched.DMA_CNT = {}
    nc = bass.Bass("TRN2", target_bir_lowering=False)
    ntok = sum(seq_lens)
    ntiles = ntok // 512

    def inp(name, shape, dt=F32):
        return nc.dram_tensor(name, shape, dt, kind="ExternalInput").ap()

    def scratch(name, shape, dt):
        kind = "ExternalOutput" if name in dbg else "Internal"
        return nc.dram_tensor(name, shape, dt, kind=kind).ap()

    x = inp("x", [ntok, D])
    p = inp("p", [ntok, 256])
    cst = {k: inp(k, v) for k, v in CONST_SHAPES.items()}
    cst["rope_c"] = inp("rope_c", [32, ntok])
    cst["rope_s"] = inp("rope_s", [32, ntok])
    w = {k: inp(k, v) for k, v in W_SHAPES.items()}
    y = nc.dram_tensor("y", [ntok, D], F32, kind="ExternalOutput").ap()
    h1 = scratch("h1", [ntok, D], F32)
    h2 = scratch("h2", [ntok, D], F32)
    h3 = scratch("h3", [ntok, D], F32)
    scr = {
        "QT": scratch("QT", [8, 96, ntok], BF16), "KTn": scratch("KTn", [512, ntok], BF16),
        "KTr": scratch("KTr", [32, ntok], BF16), "VA": scratch("VA", [8, ntok, 64], BF16),
        "QP": scratch("QP", [8, 128, ntok], BF16), "DC": scratch("DC", [8, 128, ntok // 32], F32),
        "VH": scratch("VH", [ntok, 512], BF16), "GH": scratch("GH", [ntok, 512], BF16),
        "AT": scratch("AT", [ntok, 8, 128], BF16), "KPT": scratch("KPT", [ntok, 8, 128], BF16),
        "MIXT": scratch("MIXT", [512, ntok], BF16), "OF": scratch("OF", [2, ntok, 512], F32),
    }
    def on(k):
        return phases is None or k in phases
    if on("f1"):
        ffn_phase(nc, "f1", x, h1, ntiles, cst["ffn1_norm"], w["ffn1_wg"], w["ffn1_wu"], w["ffn1_wd"], cst["ident"])
    if on("mi"):
        mixer_in_phase(nc, "mi", ntok, h1, cst, w, scr)
    seqs = []
    a = 0
    for SLs in seq_lens:
        b = a + SLs
        seqs.append(dict(q0=a, nq=SLs, kp=[((lambda h, a=a, b=b: scr["KTn"][h * 64:(h + 1) * 64, a:b]), scr["KTr"][:, a:b],
                                            (lambda h, a=a, b=b: scr["VA"][h, a:b, :]), SLs)]))
        a = b
    xchg = None
    if gather:
        n0 = seq_lens[0]
        cst["rankmask"] = inp("rankmask", [128, 8])
        xchg = dict(n0=n0, groups=[[0, 1, 2, 3], [4, 5, 6, 7]],
                    XK_in=[scratch(f"XK_in{a_}", [128, n0], BF16) for a_ in range(4)],
                    XK_out=[scratch(f"XK_out{a_}", [4 * 128, n0], BF16) for a_ in range(4)],
                    XR_in=scratch("XR_in", [128, n0], BF16), XR_out=scratch("XR_out", [4 * 128, n0], BF16),
                    XV_in=[scratch(f"XV_in{a_}", [2 * n0, 64], BF16) for a_ in range(4)],
                    XV_out=[scratch(f"XV_out{a_}", [4 * 2 * n0, 64], BF16) for a_ in range(4)],
                    XS_in=scratch("XS_in", [1024, 129], F32), XS_out=scratch("XS_out", [4096, 129], F32))
        kp = []
        for r in range(4):
            kp.append(((lambda h, r=r: xchg["XK_out"][h // 2][r * 128 + (h % 2) * 64:r * 128 + (h % 2) * 64 + 64, :]),
                       xchg["XR_out"][r * 128:r * 128 + 32, :],
                       (lambda h, r=r: xchg["XV_out"][h // 2].rearrange("(r hh t) d -> r hh t d", r=4, hh=2)[r, h % 2]),
                       n0))
        seqs[0] = dict(q0=0, nq=n0, kp=kp, gathered=True)
        seqs = seqs[1:] + seqs[0:1]
    if on("at"):
        attn_phase(nc, "at", seqs, scr, xchg=xchg)
    if on("sc"):
        scan_phase(nc, "sc", list(seq_lens), ntok, scr, xchg=xchg, cst=cst)
    if on("op"):
        outproj_phase(nc, "op", ntok, h1, h2, cst, w, scr)
    if on("f2"):
        ffn_phase(nc, "f2", h2, h3, ntiles, cst["ffn2_norm"], w["ffn2_wg"], w["ffn2_wu"], w["ffn2_wd"], cst["ident"])
    if on("pl"):
        ple_phase(nc, "pl", ntok, h3, p, y, cst, w)
    return nc


def _lay(v, nch):
    return np.ascontiguousarray(np.asarray(v, np.float32).reshape(nch, 128).T)


def host_consts(inputs):
    f = np.float32
    c = {}
    c["ident"] = np.eye(128, dtype=f)
    rm = np.ones((128, 512), f)
    rm[:, 0::32] = 0.0
    c["rmask"] = rm
    idx = np.arange(128)
    same = (idx[:, None] // 32) == (idx[None, :] // 32)
    c["maskF"] = (same & (idx[:, None] <= idx[None, :])).astype(f)
    c["maskB"] = (same & (idx[:, None] >= idx[None, :])).astype(f)
    c["ffn1_norm"] = _lay(inputs["ffn1_norm"][0], 8)
    c["mix_norm"] = _lay(inputs["mix_norm"][0], 8)
    c["q_norm"] = _lay(inputs["q_norm"][0], 3)
    c["kv_norm"] = _lay(inputs["kv_norm"][0], 2)
    lb = np.asarray(inputs["hg_lb"], f).reshape(2, 2, 4, 128)
    c["hg_lb"] = np.ascontiguousarray(lb.transpose(3, 0, 1, 2).reshape(128, 16))
    c["hg_norm_b"] = np.ascontiguousarray(np.broadcast_to(np.asarray(inputs["hg_norm"][0], f)[None, :], (128, 512)))
    c["ffn2_norm"] = _lay(inputs["ffn2_norm"][0], 8)
    c["ple_norm"] = _lay(inputs["ple_norm"][0], 8)
    c["final_norm_b"] = np.ascontiguousarray(np.broadcast_to(np.asarray(inputs["final_norm"], f)[None, :], (128, D)))
    wts = {}
    for k in ("ffn1_wg", "ffn1_wu", "ffn1_wd", "w_uk", "w_uv", "w_o", "ffn2_wg", "ffn2_wu", "ffn2_wd", "w_ple_gate", "w_ple_proj"):
        wts[k] = np.ascontiguousarray(np.asarray(inputs[k][0], f))
    win = np.asarray(inputs["w_in"][0], f)
    wts["w_in"] = np.ascontiguousarray(np.concatenate([win, win[:, 656:672], win[:, 640:656]], axis=1))
    wq = np.asarray(inputs["w_uq"][0], f)
    sw = [np.concatenate([wq[:, h * 96 + 80:h * 96 + 96], wq[:, h * 96 + 64:h * 96 + 80]], axis=1) for h in range(8)]
    wts["w_uq"] = np.ascontiguousarray(np.concatenate([wq] + sw, axis=1))
    return c, wts


def rope_tables(pos):
    inv = np.exp(np.arange(0, 32, 2, dtype=np.float32) * np.float32(-np.log(10000.0) / 32)).astype(np.float32)
    ang = (pos.astype(np.float32)[None, :] * inv[:, None]).astype(np.float32)
    cs, sn = np.cos(ang).astype(np.float32), np.sin(ang).astype(np.float32)
    return np.ascontiguousarray(np.concatenate([cs, cs], 0)), np.ascontiguousarray(np.concatenate([-sn, sn], 0))


_PROG = {}


def run_balanced(inputs):
    xp = np.asarray(inputs["x_prompt"], np.float32)
    xs = np.asarray(inputs["x_sample"], np.float32)
    pp = np.asarray(inputs["p_prompt"], np.float32)[0]
    psm = np.asarray(inputs["p_sample"], np.float32)[0]
    SP, SS = xp.shape[1], xs.shape[1]
    Q = SP // 4
    lens = (Q, SS, SS)
    c, wts = host_consts(inputs)
    key = ("bal", lens)
    if key not in _PROG:
        _PROG[key] = build_program(lens, gather=True)
    nc = _PROG[key]
    in_maps = []
    for core in range(8):
        pb, r = core // 4, core % 4
        im = {"x": np.ascontiguousarray(np.concatenate([xp[pb, r * Q:(r + 1) * Q], xs[2 * core], xs[2 * core + 1]], axis=0)),
              "p": np.ascontiguousarray(np.concatenate([pp[pb, r * Q:(r + 1) * Q], psm[2 * core], psm[2 * core + 1]], axis=0))}
        pos = np.concatenate([np.arange(r * Q, (r + 1) * Q, dtype=np.float32), np.arange(SS, dtype=np.float32),
                              np.arange(SS, dtype=np.float32)])
        im["rope_c"], im["rope_s"] = rope_tables(pos)
        rk = np.zeros((128, 8), np.float32)
        for i in range(4):
            rk[:, i] = 1.0 if i < r else 0.0
            rk[:, 4 + i] = 1.0 if i > r else 0.0
        im["rankmask"] = rk
        im.update(c)
        im.update(wts)
        in_maps.append(im)
    res = run_bass_kernel_spmd(nc, in_maps, core_ids=list(range(8)))
    y_prompt = np.empty((2, SP, D), np.float32)
    y_sample = np.empty((16, SS, D), np.float32)
    for core in range(8):
        pb, r = core // 4, core % 4
        y = np.asarray(res.results[core]["y"])
        y_prompt[pb, r * Q:(r + 1) * Q] = y[0:Q]
        y_sample[2 * core] = y[Q:Q + SS]
        y_sample[2 * core + 1] = y[Q + SS:]
    return (y_prompt, y_sample)


def kernel(**inputs):
    return run_balanced(inputs)
```

```python
import numpy as np
from contextlib import ExitStack
import concourse.bass as bass
import concourse.mybir as mybir
from concourse.bass_utils import run_bass_kernel_spmd

F32 = mybir.dt.float32
BF16 = mybir.dt.bfloat16
AF = mybir.ActivationFunctionType
ALU = mybir.AluOpType
AX = mybir.AxisListType

D = 1024
DFF = 2816
EPS = 1e-6
NFC = DFF // 128
NKC = D // 128


class Res:
    __slots__ = ("name", "w", "r")

    def __init__(self, name):
        self.name = name
        self.w = None
        self.r = {}


class Ev:
    __slots__ = ("kind", "key", "op", "count")

    def __init__(self, kind, key, op=None, count=0):
        self.kind = kind
        self.key = key
        self.op = op
        self.count = count


class Op:
    __slots__ = ("fn", "deps", "sig", "count", "dma_key", "dma_n", "ev", "inc")

    def __init__(self, fn, deps):
        self.fn = fn
        self.deps = deps
        self.sig = False
        self.count = 0
        self.dma_key = None
        self.dma_n = 0
        self.ev = None


ENGS = ("sp", "act", "dve", "pool", "pe")
FUSE_WAITS = True


class Sched:
    DMA_SEMS = {}
    DMA_CNT = {}

    def __init__(self, nc, tag):
        self.nc = nc
        self.tag = tag
        self.ops = {e: [] for e in ENGS}
        self.dma_cnt = Sched.DMA_CNT
        self.nres = 0
        self.keymap = {}

    def res(self, name=None):
        self.nres += 1
        return Res(name or f"r{self.nres}")

    def _deps(self, eng, reads, writes):
        deps = []
        for r in reads:
            if r.w is not None:
                deps.append(r.w)
        for w in writes:
            if w.w is not None:
                deps.append(w.w)
            deps.extend(w.r.values())
        out = []
        seen = set()
        for d in deps:
            if id(d) in seen:
                continue
            seen.add(id(d))
            if d.kind == "e" and d.key == "pe" and eng == "pe":
                continue
            out.append(d)
        return out

    def op(self, eng, fn, reads=(), writes=()):
        o = Op(fn, self._deps(eng, reads, writes))
        ev = Ev("e", eng, op=o)
        o.ev = ev
        self.ops[eng].append(o)
        for r in reads:
            r.r[("e", eng)] = ev
        for w in writes:
            w.w = ev
            w.r = {}
        return o

    def dma(self, queue, fn, n, key, reads=(), writes=(), inc=16):
        if key not in self.keymap:
            self.keymap[key] = f"k{len(self.keymap)}"
        key = self.keymap[key]
        o = Op(fn, self._deps(queue, reads, writes))
        o.dma_key = key
        o.dma_n = n
        o.inc = inc
        c = self.dma_cnt.get(key, 0) + n * inc
        self.dma_cnt[key] = c
        ev = Ev("d", key, op=o, count=c)
        o.ev = ev
        self.ops[queue].append(o)
        for r in reads:
            r.r[("d", key)] = ev
        for w in writes:
            w.w = ev
            w.r = {}
        return o

    def barrier(self):
        evs = []
        for e in ENGS:
            for o in reversed(self.ops[e]):
                if o.dma_key is None and o.fn is not None:
                    evs.append(o.ev)
                    break
        lastd = {}
        for e in ENGS:
            for o in self.ops[e]:
                if o.dma_key is not None:
                    lastd[o.dma_key] = o.ev
        evs.extend(lastd.values())
        for e in ENGS:
            deps = [d for d in evs if not (d.kind == "e" and d.key == e)]
            o = Op(None, deps)
            o.ev = Ev("e", e, op=o)
            self.ops[e].append(o)

    def emit(self):
        nc = self.nc
        for e in ENGS:
            for o in self.ops[e]:
                for d in o.deps:
                    if d.kind == "e":
                        d.op.sig = True
        sems = {}
        for e in ENGS:
            c = 0
            for o in self.ops[e]:
                if o.dma_key is None and o.sig:
                    assert o.fn is not None
                    c += 1
                    o.count = c
            sems[("e", e)] = nc.alloc_semaphore(f"{self.tag}_e_{e}")
        for k in self.dma_cnt:
            if k not in Sched.DMA_SEMS:
                Sched.DMA_SEMS[k] = nc.alloc_semaphore(f"d_{k}")
            sems[("d", k)] = Sched.DMA_SEMS[k]
        self.sems = sems

        def run(eng_name, eng):
            waited = {}
            for o in self.ops[eng_name]:
                need = {}
                for d in o.deps:
                    k = (d.kind, d.key)
                    val = d.op.count if d.kind == "e" else d.count
                    assert val > 0, (eng_name, d.kind, d.key)
                    if waited.get(k, 0) >= val:
                        continue
                    waited[k] = val
                    need[k] = max(need.get(k, 0), val)
                need = list(need.items())
                fuse = None
                if FUSE_WAITS and need and o.fn is not None and o.dma_key is None:
                    fuse = need.pop()
                for k, val in need:
                    eng.wait_ge(sems[k], val)
                if o.fn is None:
                    continue
                ins = o.fn(eng)
                if fuse is not None:
                    ins._wait_ge(sems[fuse[0]], fuse[1])
                if o.dma_key is not None:
                    assert len(ins) == o.dma_n
                    for i in ins:
                        i.then_inc(sems[("d", o.dma_key)], o.inc)
                elif o.sig:
                    ins.then_inc(sems[("e", eng_name)], 1)

        with nc.Block() as block:
            @block.sync
            def _(e):
                run("sp", e)

            @block.scalar
            def _(e):
                run("act", e)

            @block.vector
            def _(e):
                run("dve", e)

            @block.gpsimd
            def _(e):
                run("pool", e)

            @block.tensor
            def _(e):
                run("pe", e)

    def release(self):
        for s in self.sems.values():
            self.nc.release_semaphore(s)


def load_weight_bf16(S, nc, w_dram, w_sb, rows_chunks, cols, gain_sb, stage, stage_res, w_res, qi=[0]):
    CW = 512
    for kc in range(rows_chunks):
        for c0 in range(0, cols, CW):
            cw = min(CW, cols - c0)
            i = qi[0] % len(stage)
            qi[0] += 1
            st, sr = stage[i], stage_res[i]
            src = w_dram[kc * 128:(kc + 1) * 128, c0:c0 + cw]
            S.dma("sp", (lambda e, st=st, src=src, cw=cw: [e.dma_start(out=st[:, 0:cw], in_=src)]),
                  1, f"wst{i}", writes=[sr])
            dst = w_sb[:, kc, c0:c0 + cw]
            if gain_sb is not None:
                g = gain_sb[:, kc:kc + 1]
                S.op("pool", (lambda e, dst=dst, st=st, cw=cw, g=g:
                              e.tensor_scalar(dst, st[:, 0:cw], g, 0.0, ALU.mult, ALU.add)),
                     reads=[sr], writes=[w_res])
            else:
                S.op("pool", (lambda e, dst=dst, st=st, cw=cw: e.tensor_copy(dst, st[:, 0:cw])),
                     reads=[sr], writes=[w_res])


def ffn_phase(nc, tag, x_d, out_d, ntiles, gain_d, wg_d, wu_d, wd_d, ident_d):
    S = Sched(nc, tag)
    with ExitStack() as es:
        def sb(name, shape, dt):
            return es.enter_context(nc.sbuf_tensor(f"{tag}_{name}", shape, dt))

        def ps(name, shape, dt):
            return es.enter_context(nc.psum_tensor(f"{tag}_{name}", shape, dt))

        wg = sb("wg", [128, NKC, DFF], BF16)
        wu = sb("wu", [128, NKC, DFF], BF16)
        wd = sb("wd", [128, NFC, D], BF16)
        gain = sb("gain", [128, NKC], F32)
        ident = sb("ident", [128, 128], BF16)
        xt0 = sb("xt0", [128, 4, D], F32)
        xt1 = sb("xt1", [128, 4, D], F32)
        st0 = xt1[:, 0, 0:512]
        st1 = xt1[:, 1, 0:512]
        xn4 = sb("xn4", [128, 4, D], BF16)
        xnT = sb("xnT", [128, NKC, 512], BF16)
        actb = sb("act", [128, NFC, 512], BF16)
        sg = sb("sg", [128, 2, 512], BF16)
        stat = sb("stat", [128, 16], F32)
        junk = sb("junk", [128, D], BF16)
        pg0 = ps("pg0", [128, 512], F32)
        pg1 = ps("pg1", [128, 512], F32)
        pu0 = ps("pu0", [128, 512], F32)
        pu1 = ps("pu1", [128, 512], F32)
        po0 = ps("po0", [128, 512], F32)
        po1 = ps("po1", [128, 512], F32)
        pt0 = ps("pt0", [128, 1024], BF16)
        pt1 = ps("pt1", [128, 1024], BF16)
        R = S.res
        r_gain, r_ident, r_w = R("gain"), R("ident"), R("w")
        r_xt = [[R(f"xt{i}_{j}") for j in range(4)] for i in range(2)]
        r_st = [r_xt[1][0], r_xt[1][1]]
        S.dma("sp", lambda e: [e.dma_start(out=gain[:], in_=gain_d)], 1, "c0", writes=[r_gain])
        S.dma("sp", lambda e: [e.dma_start(out=st0[:, 0:128], in_=ident_d)], 1, "wst0", writes=[r_st[0]])
        S.op("pool", lambda e: e.tensor_copy(ident[:], st0[:, 0:128]), reads=[r_st[0]], writes=[r_ident])
        stat_dummy = None
        qi = [1]
        r_wg, r_wu, r_wd = R("wg"), R("wu"), R("wd")
        S.op("pool", lambda e: e.tensor_copy(stat[:, 8:9], gain[:, 0:1]), reads=[r_gain], writes=[R("dummy")])
        load_weight_bf16(S, nc, wg_d, wg, NKC, DFF, gain, [st0, st1], r_st, r_wg, qi)
        load_weight_bf16(S, nc, wu_d, wu, NKC, DFF, gain, [st0, st1], r_st, r_wu, qi)
        load_weight_bf16(S, nc, wd_d, wd, NFC, D, None, [st0, st1], r_st, r_wd, qi)

        xts = [xt0, xt1]
        r_xn4 = [R(f"xn{j}") for j in range(4)]
        r_xnT, r_act = [R(f"xnT{k}") for k in range(NKC)], [R(f"act{f}") for f in range(NFC)]
        r_sg = [R("sg0"), R("sg1")]
        r_stat = [R(f"stat{j}") for j in range(4)]
        r_junk = R("junk")
        pgs, pus, pos, pts = [pg0, pg1], [pu0, pu1], [po0, po1], [pt0, pt1]
        r_pg, r_pu = [R("pg0"), R("pg1")], [R("pu0"), R("pu1")]
        r_po, r_pt = [R("po0"), R("po1")], [R("pt0"), R("pt1")]

        def load_tile(i):
            b = i % 2
            for j in range(4):
                src = x_d[i * 512 + j * 128: i * 512 + (j + 1) * 128, :]
                dst = xts[b][:, j, :]
                S.dma("sp", (lambda e, dst=dst, src=src: [e.dma_start(out=dst, in_=src)]), 1,
                      f"xt{b}_{j}", writes=[r_xt[b][j]])

        load_tile(0)
        nt_ctr = [0]
        for i in range(ntiles):
            b = i % 2
            xt = xts[b]
            if i + 1 < ntiles:
                load_tile(i + 1)
            norm_transpose4(S, xt, r_xt[b], stat, r_stat, junk, xn4, r_xn4, pts, r_pt, ident, r_ident, xnT, r_xnT)
            for f in range(NFC):
                pb = f % 2
                for kc in range(NKC):
                    S.op("pe", (lambda e, pb=pb, f=f, kc=kc: e.matmul(
                        pgs[pb][:], wg[:, kc, f * 128:(f + 1) * 128], xnT[:, kc, :],
                        start=(kc == 0), stop=(kc == NKC - 1))),
                        reads=[r_wg, r_xnT[kc]], writes=[r_pg[pb]])
                for kc in range(NKC):
                    S.op("pe", (lambda e, pb=pb, f=f, kc=kc: e.matmul(
                        pus[pb][:], wu[:, kc, f * 128:(f + 1) * 128], xnT[:, kc, :],
                        start=(kc == 0), stop=(kc == NKC - 1))),
                        reads=[r_wu, r_xnT[kc]], writes=[r_pu[pb]])
                S.op("act", (lambda e, pb=pb: e.activation(sg[:, pb, :], pgs[pb][:], AF.Silu)),
                     reads=[r_pg[pb]], writes=[r_sg[pb]])
                S.op("dve", (lambda e, pb=pb, f=f: e.tensor_tensor(actb[:, f, :], sg[:, pb, :], pus[pb][:], ALU.mult)),
                     reads=[r_sg[pb], r_pu[pb]], writes=[r_act[f]])
            for j in range(4):
                for hh in range(2):
                    pb = (j * 2 + hh) % 2
                    for f in range(NFC):
                        S.op("pe", (lambda e, pb=pb, f=f, j=j, hh=hh: e.matmul(
                            pos[pb][:], actb[:, f, j * 128:(j + 1) * 128], wd[:, f, hh * 512:(hh + 1) * 512],
                            start=(f == 0), stop=(f == NFC - 1))),
                            reads=[r_wd, r_act[f]], writes=[r_po[pb]])
                    dst = xt[:, j, hh * 512:(hh + 1) * 512]
                    S.op("dve", (lambda e, dst=dst, pb=pb: e.scalar_tensor_tensor(
                        dst, pos[pb][:], 0.5, dst, ALU.mult, ALU.add)),
                        reads=[r_po[pb], r_xt[b][j]], writes=[r_xt[b][j]])
                dstd = out_d[i * 512 + j * 128: i * 512 + (j + 1) * 128, :]
                src = xt[:, j, :]
                S.dma("pool", (lambda e, dstd=dstd, src=src: [e.dma_start(out=dstd, in_=src)]), 1,
                      f"xo{b}_{j}", reads=[r_xt[b][j]])
        S.barrier()
        S.emit()
    return S


class Ctx:
    def __init__(self, nc, tag, es):
        self.nc, self.tag, self.es = nc, tag, es

    def sb(self, name, shape, dt):
        return self.es.enter_context(self.nc.sbuf_tensor(f"{self.tag}_{name}", shape, dt))

    def ps(self, name, shape, dt=F32):
        return self.es.enter_context(self.nc.psum_tensor(f"{self.tag}_{name}", shape, dt))


def load_w(S, w_dram, w_sb, nrc, cols, gain_sb, st, r_st, r_w, qi, rows_last=128):
    for rc in range(nrc):
        for c0 in range(0, cols, 512):
            cw = min(512, cols - c0)
            i = qi[0] % 2
            qi[0] += 1
            stt, sr = st[i], r_st[i]
            src = w_dram[rc * 128:(rc + 1) * 128, c0:c0 + cw]
            S.dma("sp", (lambda e, stt=stt, src=src, cw=cw: [e.dma_start(out=stt[:, 0:cw], in_=src)]),
                  1, f"wst{i}", writes=[sr])
            dst = w_sb[:, rc, c0:c0 + cw]
            if gain_sb is not None:
                g = gain_sb[:, rc:rc + 1]
                S.op("pool", (lambda e, dst=dst, stt=stt, cw=cw, g=g:
                              e.tensor_scalar(dst, stt[:, 0:cw], g, 0.0, ALU.mult, ALU.add)),
                     reads=[sr], writes=[r_w])
            else:
                S.op("pool", (lambda e, dst=dst, stt=stt, cw=cw: e.tensor_copy(dst, stt[:, 0:cw])),
                     reads=[sr], writes=[r_w])


def norm_transpose(S, xt, r_xt_j, j, stat, r_stat, junk, r_junk, xn, r_xn, pt, r_pt, ident, r_ident,
                   xnT, r_xnT, nfeat=D):
    nkc = nfeat // 128
    xj = xt[:, j, :]
    ss = stat[:, j:j + 1]
    rs = stat[:, 4 + j:5 + j]
    S.op("act", (lambda e: e.activation(junk[:, 0:nfeat], xj, AF.Square, accum_out=ss)),
         reads=[r_xt_j], writes=[r_junk, r_stat[j]])
    S.op("act", (lambda e: e.activation(rs, ss, AF.Sqrt, bias=EPS, scale=1.0 / nfeat)),
         reads=[r_stat[j]], writes=[r_stat[j]])
    S.op("dve", (lambda e: e.reciprocal(rs, rs)), reads=[r_stat[j]], writes=[r_stat[j]])
    S.op("dve", (lambda e: e.tensor_scalar(xn[:, 0:nfeat], xj, rs, None, ALU.mult)),
         reads=[r_xt_j, r_stat[j]], writes=[r_xn])
    for kc in range(nkc):
        S.op("pe", (lambda e, kc=kc: e.transpose(pt[:, kc * 128:(kc + 1) * 128],
                                                 xn[:, kc * 128:(kc + 1) * 128], ident[:])),
             reads=[r_xn, r_ident], writes=[r_pt])
    dst = xnT[:, 0:nkc, j * 128:(j + 1) * 128]
    src = pt[:, 0:nkc * 128].rearrange("p (k t) -> p k t", k=nkc)
    S.op("act", (lambda e: e.copy(dst, src)), reads=[r_pt], writes=r_xnT)


def norm_transpose4(S, xt, r_xt, stat, r_stat, junk, xn4, r_xn, pts, r_pts, ident, r_ident, xnT, r_xnT, nfeat=D):
    nkc = nfeat // 128
    r_j = [Res("junk") for _ in range(4)]
    for j in range(4):
        S.op("act", (lambda e, j=j: e.activation(junk[:, 0:nfeat], xt[:, j, :], AF.Square, accum_out=stat[:, j:j + 1])),
             reads=[r_xt[j]], writes=[r_j[j], r_stat[j]])
    for j in range(4):
        S.op("act", (lambda e, j=j: e.activation(stat[:, 4 + j:5 + j], stat[:, j:j + 1], AF.Sqrt, bias=EPS, scale=1.0 / nfeat)),
             reads=[r_stat[j]], writes=[r_stat[j]])
    for j in range(4):
        S.op("dve", (lambda e, j=j: e.reciprocal(stat[:, 4 + j:5 + j], stat[:, 4 + j:5 + j])), reads=[r_stat[j]], writes=[r_stat[j]])
    for j in range(4):
        S.op("dve" if j % 2 == 0 else "pool",
             (lambda e, j=j: e.tensor_scalar(xn4[:, j, 0:nfeat], xt[:, j, :], stat[:, 4 + j:5 + j], 0.0, ALU.mult, ALU.add)),
             reads=[r_xt[j], r_stat[j]], writes=[r_xn[j]])
    for j in range(4):
        pt, r_pt = pts[j % 2], r_pts[j % 2]
        for kc in range(nkc):
            S.op("pe", (lambda e, kc=kc, j=j, pt=pt: e.transpose(pt[:, kc * 128:(kc + 1) * 128],
                                                               xn4[:, j, kc * 128:(kc + 1) * 128], ident[:])),
                 reads=[r_xn[j], r_ident], writes=[r_pt])
        dst = xnT[:, 0:nkc, j * 128:(j + 1) * 128]
        src = pt[:, 0:nkc * 128].rearrange("p (k t) -> p k t", k=nkc)
        S.op("act", (lambda e, dst=dst, src=src: e.copy(dst, src)), reads=[r_pt], writes=r_xnT)


C_CQ, C_CKV, C_KR, C_HQ, C_HI, C_HFF, C_HFB, C_HG, C_KRS = 0, 384, 640, 672, 1184, 1696, 2208, 2720, 3232
WIN_COLS = 3264
UQ_COLS = 768 + 256


def mixer_in_phase(nc, tag, ntok, h1_d, cst, w, scr):
    S = Sched(nc, tag)
    ntiles = ntok // 512
    with ExitStack() as es:
        C = Ctx(nc, tag, es)
        R = S.res
        win = C.sb("win", [128, NKC, WIN_COLS], BF16)
        wuq = C.sb("wuq", [128, 3, UQ_COLS], BF16)
        wuk = C.sb("wuk", [128, 2, 512], BF16)
        wuv = C.sb("wuv", [128, 2, 512], BF16)
        gains = C.sb("gains", [128, 16], F32)
        lbt = C.sb("lbt", [128, 16], F32)
        lb = C.sb("lb", [128, 8], F32)
        oml = C.sb("oml", [128, 8], F32)
        ident = C.sb("ident", [128, 128], BF16)
        ones = C.sb("ones", [128, 128], BF16)
        rmask = C.sb("rmask", [128, 512], F32)
        mF = C.sb("mF", [128, 128], F32)
        mB = C.sb("mB", [128, 128], F32)
        ht = C.sb("ht", [128, 4, D], F32)
        xn = C.sb("xn", [128, D], BF16)
        junk = C.sb("junk", [128, D], BF16)
        stat = C.sb("stat", [128, 16], F32)
        xnT = C.sb("xnT", [128, NKC, 512], BF16)
        cqT = C.sb("cqT", [128, 3, 512], BF16)
        ckvT = C.sb("ckvT", [128, 2, 512], BF16)
        sqq = C.sb("sqq", [128, 2, 512], BF16)
        sqkv = C.sb("sqkv", [128, 2, 512], BF16)
        rsq = C.sb("rsq", [128, 512], F32)
        rskv = C.sb("rskv", [128, 512], F32)
        rstok = C.sb("rstok", [128, 8], F32)
        tct = C.sb("tct", [128, 512], F32)
        tst = C.sb("tst", [128, 512], F32)
        t1a = C.sb("t1a", [128, 512], F32)
        t2a = C.sb("t2a", [128, 512], F32)
        t1 = [t1a, t1a]
        t2 = [t2a, t2a]
        qout = C.sb("qout", [128, 8, 512], BF16)
        knT = C.sb("knT", [128, 4, 512], BF16)
        krp = C.sb("krp", [128, 512], BF16)
        vt = C.sb("vt", [128, 4, 512], BF16)
        qh = C.sb("qh", [128, 4, 512], F32)
        hA = [C.sb(f"hA{i}", [128, 512], F32) for i in range(2)]
        hB = [C.sb(f"hB{i}", [128, 512], F32) for i in range(2)]
        hC = [C.sb(f"hC{i}", [128, 512], F32) for i in range(2)]
        hE1 = [C.sb(f"hE1{i}", [128, 512], F32) for i in range(2)]
        hE2 = [C.sb(f"hE2{i}", [128, 512], F32) for i in range(2)]
        st = [hA[0], hB[0]]
        qpo = C.sb("qpo", [128, 8, 512], BF16)
        kpo = C.sb("kpo", [128, 8, 512], BF16)
        kppo = C.sb("kppo", [128, 8, 512], BF16)
        dco = C.sb("dco", [128, 8, 16], F32)
        vht = C.sb("vht", [128, 4, 512], BF16)
        ght = C.sb("ght", [128, 4, 512], BF16)
        ato = C.sb("ato", [128, 4, 8, 128], BF16)
        kto = C.sb("kto", [128, 4, 8, 128], BF16)
        pt = C.ps("pt", [128, 1024], BF16)
        pm = [C.ps(f"pm{i}", [128, 512]) for i in range(4)]
        pn = C.ps("pn", [128, 512])
        pa = C.ps("pa", [128, 512])
        ptk = C.ps("ptk", [128, 1024], BF16)

        r_c = R("consts")
        r_ident, r_ones = R("ident"), R("ones")

        def cdma(dst, src):
            S.dma("sp", (lambda e: [e.dma_start(out=dst, in_=src)]), 1, "c", writes=[r_c])
        cdma(gains[:, 0:8], cst["mix_norm"])
        cdma(gains[:, 8:11], cst["q_norm"])
        cdma(gains[:, 11:13], cst["kv_norm"])
        cdma(lbt[:], cst["hg_lb"])
        cdma(rmask[:], cst["rmask"])
        cdma(mF[:], cst["maskF"])
        cdma(mB[:], cst["maskB"])
        cdma(ht[:, 0, 0:128], cst["ident"])
        S.op("pool", lambda e: e.tensor_copy(ident[:], ht[:, 0, 0:128]), reads=[r_c], writes=[r_ident])
        S.op("pool", lambda e: e.memset(ones[:], 1.0), writes=[r_ones])
        lv = lbt[:].rearrange("p (d l h) -> p d l h", d=2, l=2)
        lb3 = lb[:].rearrange("p (d h) -> p d h", d=2)
        oml3 = oml[:].rearrange("p (d h) -> p d h", d=2)
        r_lb = R("lb")
        S.op("dve", lambda e: e.tensor_tensor(lb3, lv[:, :, 0, :], lv[:, :, 1, :], ALU.subtract), reads=[r_c], writes=[r_lb])
        S.op("act", lambda e: e.activation(oml[:], lb[:], AF.Sigmoid, scale=-1.0), reads=[r_lb], writes=[R("oml")])
        S.op("act", lambda e: e.activation(lb[:], lb[:], AF.Sigmoid), reads=[r_lb], writes=[r_lb])
        r_w = R("w")
        r_hA, r_hB = [R("hA0"), R("hA1")], [R("hB0"), R("hB1")]
        r_st = [r_hA[0], r_hB[0]]
        S.op("pool", lambda e: e.tensor_copy(stat[:, 15:16], gains[:, 0:1]), reads=[r_c], writes=[R("d")])
        qi = [0]
        load_w(S, w["w_in"], win, NKC, WIN_COLS, gains[:, 0:8], st, r_st, r_w, qi)
        load_w(S, w["w_uq"], wuq, 3, UQ_COLS, gains[:, 8:11], st, r_st, r_w, qi)
        load_w(S, w["w_uk"], wuk, 2, 512, gains[:, 11:13], st, r_st, r_w, qi)
        load_w(S, w["w_uv"], wuv, 2, 512, gains[:, 11:13], st, r_st, r_w, qi)

        r_ht = [R(f"ht{j}") for j in range(4)]
        r_stat = [R(f"stat{j}") for j in range(4)]
        r_junk, r_xn, r_pt = R("junk"), R("xn"), R("pt")
        r_xnT = [R(f"xnT{k}") for k in range(NKC)]
        r_pm = [R(f"pm{i}") for i in range(4)]
        r_pn, r_pa, r_ptk = R("pn"), R("pa"), R("ptk")
        r_cqT, r_ckvT, r_sqq, r_sqkv = R("cqT"), R("ckvT"), [R("sqq0"), R("sqq1")], R("sqkv")
        r_rsq, r_rskv, r_rstok = R("rsq"), R("rskv"), R("rstok")
        r_tab = R("tab")
        r_t1a, r_t2a = R("t1a"), R("t2a")
        r_t1, r_t2 = [r_t1a, r_t1a], [r_t2a, r_t2a]
        r_qout, r_knT, r_krp, r_vt, r_qh = R("qout"), R("knT"), R("krp"), R("vt"), R("qh")
        r_hC = [R("hC0"), R("hC1")]
        r_hE1, r_hE2 = [R("hE10"), R("hE11")], [R("hE20"), R("hE21")]
        r_qpo, r_kpo, r_kppo, r_dco = R("qpo"), R("kpo"), R("kppo"), R("dco")
        r_vht, r_ght, r_ato, r_kto = R("vht"), R("ght"), R("ato"), R("kto")
        S.op("pool", lambda e: e.memset(tct[:], 1.0), writes=[r_tab])
        S.op("pool", lambda e: e.memset(tst[:], 0.0), writes=[r_tab])
        pmi = [0]

        def nextpm():
            i = pmi[0] % 4
            pmi[0] += 1
            return pm[i], r_pm[i]

        def mm_fm(ps, r_ps, wsb, c0, m, xT, r_x, nkc, out_p0=0):
            for kc in range(nkc):
                S.op("pe", (lambda e, kc=kc: e.matmul(ps[out_p0:out_p0 + m, :], wsb[:, kc, c0:c0 + m], xT[:, kc, :],
                                                      start=(kc == 0), stop=(kc == nkc - 1))),
                     reads=[r_w] + r_x, writes=[r_ps])

        def mm_tm(ps, r_ps, xT, r_x, j, wsb, c0, n, nkc):
            for kc in range(nkc):
                S.op("pe", (lambda e, kc=kc: e.matmul(ps[:, 0:n], xT[:, kc, j * 128:(j + 1) * 128], wsb[:, kc, c0:c0 + n],
                                                      start=(kc == 0), stop=(kc == nkc - 1))),
                     reads=[r_w] + r_x, writes=[r_ps])

        for i in range(ntiles):
            t0 = i * 512
            S.dma("sp", (lambda e, t0=t0: [e.dma_start(out=ht[:, j, :], in_=h1_d[t0 + j * 128:t0 + (j + 1) * 128, :])
                                          for j in range(4)]), 4, "ht", writes=r_ht)
            S.dma("sp", (lambda e, t0=t0: [e.dma_start(out=tct[64:96, :], in_=cst["rope_c"][:, t0:t0 + 512]),
                                          e.dma_start(out=tst[64:96, :], in_=cst["rope_s"][:, t0:t0 + 512])]),
                  2, "tab", writes=[r_tab])
            for j in range(4):
                norm_transpose(S, ht, r_ht[j], j, stat, r_stat, junk, r_junk, xn, r_xn, pt, r_pt, ident, r_ident,
                               xnT, r_xnT)
            for c in range(3):
                ps, rp = nextpm()
                mm_fm(ps, rp, win, C_CQ + c * 128, 128, xnT, r_xnT, NKC)
                S.op("act", (lambda e, ps=ps, c=c: e.copy(cqT[:, c, :], ps[:])), reads=[rp], writes=[r_cqT])
                S.op("act", (lambda e, ps=ps, c=c: e.activation(sqq[:, c % 2, :], ps[:], AF.Square)),
                     reads=[rp], writes=[r_sqq[c % 2]])
                S.op("pe", (lambda e, c=c: e.matmul(pn[:], ones[:], sqq[:, c % 2, :], start=(c == 0), stop=(c == 2))),
                     reads=[r_ones, r_sqq[c % 2]], writes=[r_pn])
            S.op("act", lambda e: e.activation(rsq[:], pn[:], AF.Sqrt, bias=EPS, scale=1.0 / 384), reads=[r_pn], writes=[r_rsq])
            S.op("dve", lambda e: e.reciprocal(rsq[:], rsq[:]), reads=[r_rsq], writes=[r_rsq])
            for c in range(2):
                ps, rp = nextpm()
                mm_fm(ps, rp, win, C_CKV + c * 128, 128, xnT, r_xnT, NKC)
                S.op("act", (lambda e, ps=ps, c=c: e.copy(ckvT[:, c, :], ps[:])), reads=[rp], writes=[r_ckvT])
                S.op("act", (lambda e, ps=ps, c=c: e.activation(sqkv[:, c, :], ps[:], AF.Square)),
                     reads=[rp], writes=[r_sqkv])
            for c in range(2):
                S.op("pe", (lambda e, c=c: e.matmul(pn[:], ones[:], sqkv[:, c, :], start=(c == 0), stop=(c == 1))),
                     reads=[r_ones, r_sqkv], writes=[r_pn])
            S.op("act", lambda e: e.activation(rskv[:], pn[:], AF.Sqrt, bias=EPS, scale=1.0 / 256), reads=[r_pn], writes=[r_rskv])
            S.op("dve", lambda e: e.reciprocal(rskv[:], rskv[:]), reads=[r_rskv], writes=[r_rskv])
            for j in range(4):
                for c in range(2):
                    S.op("pe", (lambda e, j=j, c=c: e.matmul(pa[:, j:j + 1], sqkv[:, c, j * 128:(j + 1) * 128], ones[:, 0:1],
                                                             start=(c == 0), stop=(c == 1))),
                         reads=[r_ones, r_sqkv], writes=[r_pa])
            S.op("act", lambda e: e.activation(rstok[:, 0:4], pa[:, 0:4], AF.Sqrt, bias=EPS, scale=1.0 / 256),
                 reads=[r_pa], writes=[r_rstok])
            S.op("dve", lambda e: e.reciprocal(rstok[:, 0:4], rstok[:, 0:4]), reads=[r_rstok], writes=[r_rstok])
            ps, rp = nextpm()
            mm_fm(ps, rp, win, C_KR, 32, xnT, r_xnT, NKC, out_p0=64)
            ps2, rp2 = nextpm()
            mm_fm(ps2, rp2, win, C_KRS, 32, xnT, r_xnT, NKC, out_p0=64)
            S.op("dve", (lambda e, ps=ps: e.tensor_tensor(t1[0][64:96, :], ps[64:96, :], tct[64:96, :], ALU.mult)),
                 reads=[rp, r_tab], writes=[r_t1[0]])
            S.op("dve", (lambda e, ps2=ps2: e.tensor_tensor(t2[0][64:96, :], ps2[64:96, :], tst[64:96, :], ALU.mult)),
                 reads=[rp2, r_tab], writes=[r_t2[0]])
            S.op("pool", lambda e: e.tensor_tensor(krp[64:96, :], t1[0][64:96, :], t2[0][64:96, :], ALU.add),
                 reads=[r_t1[0], r_t2[0]], writes=[r_krp])
            S.dma("pool", (lambda e, t0=t0: [e.dma_start(out=scr["KTr"][:, t0:t0 + 512], in_=krp[64:96, :])]), 1, "s_krp",
                  reads=[r_krp])
            for h in range(8):
                b = h % 2
                ps, rp = nextpm()
                mm_fm(ps, rp, wuq, h * 96, 96, cqT, [r_cqT], 3)
                ps2, rp2 = nextpm()
                mm_fm(ps2, rp2, wuq, 768 + h * 32, 32, cqT, [r_cqT], 3, out_p0=64)
                S.op("dve", (lambda e, ps=ps, b=b: e.tensor_tensor(t1[b][0:96, :], ps[0:96, :], tct[0:96, :], ALU.mult)),
                     reads=[rp, r_tab], writes=[r_t1[b]])
                S.op("dve", (lambda e, ps2=ps2, b=b: e.tensor_tensor(t2[b][64:96, :], ps2[64:96, :], tst[64:96, :], ALU.mult)),
                     reads=[rp2, r_tab], writes=[r_t2[b]])
                S.op("pool", (lambda e, b=b: e.tensor_tensor(t1[b][64:96, :], t1[b][64:96, :], t2[b][64:96, :], ALU.add)),
                     reads=[r_t2[b]], writes=[r_t1[b]])
                S.op("pool", (lambda e, b=b, h=h: e.tensor_tensor(qout[0:96, h, :], t1[b][0:96, :], rsq[0:96, :], ALU.mult)),
                     reads=[r_t1[b], r_rsq], writes=[r_qout])
            S.dma("pool", (lambda e, t0=t0: [e.dma_start(out=scr["QT"][h, :, t0:t0 + 512], in_=qout[0:96, h, :])
                                            for h in range(8)]), 8, "s_q", reads=[r_qout])
            for a in range(4):
                ps, rp = nextpm()
                mm_fm(ps, rp, wuk, a * 128, 128, ckvT, [r_ckvT], 2)
                S.op("dve", (lambda e, ps=ps, a=a: e.tensor_tensor(knT[:, a, :], ps[:], rskv[:], ALU.mult)),
                     reads=[rp, r_rskv], writes=[r_knT])
            S.dma("pool", (lambda e, t0=t0: [e.dma_start(out=scr["KTn"][a * 128:(a + 1) * 128, t0:t0 + 512], in_=knT[:, a, :])
                                            for a in range(4)]), 4, "s_kn", reads=[r_knT])
            for j in range(4):
                ps, rp = nextpm()
                mm_tm(ps, rp, ckvT, [r_ckvT], j, wuv, 0, 512, 2)
                S.op("act", (lambda e, ps=ps, j=j: e.activation(vt[:, j, :], ps[:], AF.Copy, scale=rstok[:, j:j + 1])),
                     reads=[rp, r_rstok], writes=[r_vt])
            S.dma("pool", (lambda e, t0=t0: [e.dma_start(
                out=scr["VA"][:, t0 + j * 128:t0 + (j + 1) * 128, :].rearrange("h t d -> t h d"),
                in_=vt[:, j, :].rearrange("t (h d) -> t h d", h=8)) for j in range(4)]), 4, "s_v", reads=[r_vt])
            for h in range(4):
                ps, rp = nextpm()
                mm_fm(ps, rp, win, C_HQ + h * 128, 128, xnT, r_xnT, NKC)
                S.op("act", (lambda e, ps=ps, h=h: e.activation(qh[:, h, :], ps[:], AF.Silu)), reads=[rp], writes=[r_qh])
            for j in range(4):
                ps, rp = nextpm()
                mm_tm(ps, rp, xnT, r_xnT, j, win, C_HI, 512, NKC)
                S.op("act", (lambda e, ps=ps, j=j: e.copy(vht[:, j, :], ps[:])), reads=[rp], writes=[r_vht])
                ps, rp = nextpm()
                mm_tm(ps, rp, xnT, r_xnT, j, win, C_HG, 512, NKC)
                S.op("act", (lambda e, ps=ps, j=j: e.activation(ght[:, j, :], ps[:], AF.Silu)), reads=[rp], writes=[r_ght])
            S.dma("pool", (lambda e, t0=t0: [
                e.dma_start(out=scr["VH"][t0:t0 + 512, :].rearrange("(j p) c -> p j c", p=128), in_=vht[:]),
                e.dma_start(out=scr["GH"][t0:t0 + 512, :].rearrange("(j p) c -> p j c", p=128), in_=ght[:])]),
                2, "s_vg", reads=[r_vht, r_ght])
            for d in range(2):
                for h in range(4):
                    hd = d * 4 + h
                    b = hd % 2
                    A, B, Cc, E1, E2 = hA[b], hB[b], hC[b], hE1[b], hE2[b]
                    rA, rB, rC, rE1, rE2 = r_hA[b], r_hB[b], r_hC[b], r_hE1[b], r_hE2[b]
                    ps, rp = nextpm()
                    mm_fm(ps, rp, win, (C_HFF if d == 0 else C_HFB) + h * 128, 128, xnT, r_xnT, NKC)
                    lbs, omls = lb[:, hd:hd + 1], oml[:, hd:hd + 1]
                    S.op("act", (lambda e, ps=ps, A=A: e.activation(A[:], ps[:], AF.Sigmoid)), reads=[rp], writes=[rA])
                    S.op("act", (lambda e, ps=ps, B=B: e.activation(B[:], ps[:], AF.Sigmoid, scale=-1.0)), reads=[rp], writes=[rB])
                    S.op("pool", (lambda e, B=B, omls=omls: e.tensor_scalar(B[:], B[:], omls, 0.0, ALU.mult, ALU.add)),
                         reads=[r_lb], writes=[rB])
                    S.op("dve", (lambda e, A=A, omls=omls, lbs=lbs: e.tensor_scalar(A[:], A[:], omls, lbs, ALU.mult, ALU.add)),
                         reads=[r_lb], writes=[rA])
                    S.op("act", (lambda e, A=A: e.activation(A[:], A[:], AF.Ln)), writes=[rA])
                    S.op("dve", (lambda e, A=A, Cc=Cc: e.tensor_tensor_scan(Cc[:], rmask[:], A[:], 0.0, ALU.mult, ALU.add)),
                         reads=[rA, r_c], writes=[rC])
                    Cv = Cc[:].rearrange("p (c t) -> p c t", t=32)
                    Av = A[:].rearrange("p (c t) -> p c t", t=32)
                    if d == 0:
                        bsrc, rb = Cc, rC
                        dcol = 31
                    else:
                        S.op("pool", (lambda e, A=A, Cc=Cc: e.tensor_tensor(A[:], A[:], Cc[:], ALU.subtract)),
                             reads=[rC], writes=[rA])
                        S.op("pool", (lambda e, Av=Av, Cv=Cv: e.tensor_tensor(Av, Av, Cv[:, :, 31:32].broadcast_to([128, 16, 32]), ALU.add)),
                             reads=[rC], writes=[rA])
                        bsrc, rb = A, rA
                        dcol = 0
                    S.op("act", (lambda e, E1=E1, bsrc=bsrc: e.activation(E1[:], bsrc[:], AF.Exp)), reads=[rb], writes=[rE1])
                    S.op("act", (lambda e, E2=E2, bsrc=bsrc: e.activation(E2[:], bsrc[:], AF.Exp, scale=-1.0)), reads=[rb], writes=[rE2])
                    S.op("pool", (lambda e, E1=E1, h=h, hd=hd: e.tensor_tensor(qpo[:, hd, :], qh[:, h, :], E1[:], ALU.mult)),
                         reads=[rE1, r_qh], writes=[r_qpo])
                    S.op("dve", (lambda e, E2=E2, B=B: e.tensor_tensor(E2[:], E2[:], B[:], ALU.mult)), reads=[rB], writes=[rE2])
                    S.op("act", (lambda e, E2=E2, hd=hd: e.copy(kpo[:, hd, :], E2[:])), reads=[rE2], writes=[r_kpo])
                    E1v = E1[:].rearrange("p (c t) -> p c t", t=32)
                    E2v = E2[:].rearrange("p (c t) -> p c t", t=32)
                    S.op("dve", (lambda e, E1v=E1v, hd=hd, dcol=dcol: e.tensor_copy(dco[:, hd, :], E1v[:, :, dcol])),
                         reads=[rE1], writes=[r_dco])
                    kv = kppo[:, hd, :].rearrange("p (c t) -> p c t", t=32)
                    S.op("pool", (lambda e, E1v=E1v, E2v=E2v, kv=kv, dcol=dcol: e.tensor_tensor(
                        kv, E2v, E1v[:, :, dcol:dcol + 1].broadcast_to([128, 16, 32]), ALU.mult)),
                        reads=[rE1, rE2], writes=[r_kppo])
            nch = ntok // 32
            S.dma("pool", (lambda e, t0=t0, i=i: [
                e.dma_start(out=scr["QP"][:, :, t0:t0 + 512].rearrange("h p t -> p h t"), in_=qpo[:]),
                e.dma_start(out=scr["DC"][:, :, i * 16:(i + 1) * 16].rearrange("h p c -> p h c"), in_=dco[:])]),
                2, "s_qp", reads=[r_qpo, r_dco])
            for j in range(4):
                for d in range(2):
                    for h in range(4):
                        hd = d * 4 + h
                        S.op("pe", (lambda e, j=j, hd=hd, h=h: e.matmul(
                            pa[:, h * 128:(h + 1) * 128], kpo[:, hd, j * 128:(j + 1) * 128], qpo[:, hd, j * 128:(j + 1) * 128],
                            start=True, stop=True)), reads=[r_kpo, r_qpo], writes=[r_pa])
                    msk = (mF if d == 0 else mB)
                    S.op("dve", (lambda e, j=j, d=d, msk=msk: e.tensor_tensor(
                        ato[:, j, d * 4:(d + 1) * 4, :], pa[:].rearrange("p (h t) -> p h t", h=4),
                        msk[:].rearrange("p (o t) -> p o t", o=1).broadcast_to([128, 4, 128]), ALU.mult)),
                        reads=[r_pa, r_c], writes=[r_ato])
                for hd in range(8):
                    S.op("pe", (lambda e, j=j, hd=hd: e.transpose(ptk[:, hd * 128:(hd + 1) * 128],
                                                                   kppo[:, hd, j * 128:(j + 1) * 128], ident[:])),
                         reads=[r_kppo, r_ident], writes=[r_ptk])
                S.op("act", (lambda e, j=j: e.copy(kto[:, j, :, :], ptk[:].rearrange("p (h k) -> p h k", h=8))),
                     reads=[r_ptk], writes=[r_kto])
            S.dma("pool", (lambda e, t0=t0: [
                e.dma_start(out=scr["AT"][t0:t0 + 512, :, :].rearrange("(j p) h t -> p j h t", p=128), in_=ato[:]),
                e.dma_start(out=scr["KPT"][t0:t0 + 512, :, :].rearrange("(j p) h k -> p j h k", p=128), in_=kto[:])]),
                2, "s_at", reads=[r_ato, r_kto])
        S.barrier()
        S.emit()
    return S


def attn_phase(nc, tag, seqs, scr, xchg=None):
    S = Sched(nc, tag)
    SKMAX = max(sum(p[3] for p in sq["kp"]) for sq in seqs)
    SL = max(sq["nq"] for sq in seqs)
    scale = 96.0 ** -0.5
    with ExitStack() as es:
        C = Ctx(nc, tag, es)
        R = S.res
        kt = [C.sb(f"kt{i}", [128, SKMAX], BF16) for i in range(2)]
        vt = [C.sb(f"vt{i}", [128, SKMAX // 128, 65], BF16) for i in range(2)]
        qt = [C.sb(f"qt{i}", [128, SL], BF16) for i in range(2)]
        pT = [C.sb(f"pT{i}", [128, 1024], BF16) for i in range(3)]
        onesf = C.sb("onesf", [128, 64], F32)
        rl = C.sb("rl", [128, 512], F32)
        osb = C.sb("osb", [128, 512], F32)
        obf = [C.sb(f"obf{i}", [128, 512], BF16) for i in range(2)]
        psS = [C.ps(f"psS{i}", [128, 1024]) for i in range(2)]
        psO = [C.ps(f"psO{i}", [128, 512]) for i in range(2)]
        psB = C.ps("psB", [128, 512])
        r_kt, r_vt, r_qt = [R(), R()], [R(), R()], [R(), R()]
        r_pT, r_psS, r_psO = [R(), R(), R()], [R(), R(), R()], [R(), R()]
        r_ones, r_rl, r_osb, r_obf, r_psB = R(), R(), R(), [R(), R()], R()
        S.op("pool", lambda e: e.memset(onesf[:], 1.0), writes=[r_ones])
        for i in range(2):
            S.op("pool", (lambda e, i=i: e.memset(vt[i][:, :, 64:65], 1.0)), writes=[r_vt[i]])
        r_gath = R()
        r_g2, r_g3 = R(), R()
        r_gs = []
        if xchg is not None:
            n0 = xchg["n0"]
            r_xin = R()
            S.dma("sp", (lambda e: [e.dma_start(out=xchg["XK_in"][a_], in_=scr["KTn"][a_ * 128:(a_ + 1) * 128, 0:n0]) for a_ in range(4)]
                         + [e.dma_start(out=xchg["XR_in"][0:32, :], in_=scr["KTr"][:, 0:n0])]
                         + [e.dma_start(out=xchg["XV_in"][a_].rearrange("(hh t) d -> hh t d", hh=2), in_=scr["VA"][2 * a_:2 * a_ + 2, 0:n0, :])
                            for a_ in range(4)]), 9, "xin", writes=[r_xin])
            r_gs = [r_gath, r_g2, r_g3] + [R() for _ in range(6)]
            ccl = [(xchg["XK_in"][a_], xchg["XK_out"][a_]) for a_ in range(4)] + [(xchg["XR_in"], xchg["XR_out"])] + \
                  [(xchg["XV_in"][a_], xchg["XV_out"][a_]) for a_ in range(4)]
            for ci_, (cin, cout) in enumerate(ccl):
                S.dma("pool", (lambda e, cin=cin, cout=cout: [e.collective_compute(
                    "AllGather", ALU.bypass, replica_groups=xchg["groups"], ins=[cin.opt()], outs=[cout.opt()])]), 1, f"cc{ci_}",
                    reads=[r_xin], writes=[r_gs[ci_]], inc=1)
        heads = []
        for sq in seqs:
            for h in range(8):
                heads.append((sq, h))
        units = []
        obc = [0]
        for hi, (sq, h) in enumerate(heads):
            SK = sum(p[3] for p in sq["kp"])
            for qb in range(sq["nq"] // 512):
                for kc in range(SK // 256):
                    units.append((hi, qb, kc, SK // 256, obc[0] % 2, qb == sq["nq"] // 512 - 1))
                obc[0] += 1
        loaded = [-1]

        def load_head(hi):
            if hi >= len(heads) or hi <= loaded[0]:
                return
            loaded[0] = hi
            sq, h = heads[hi]
            b = hi % 2
            off = 0
            lst = []
            for (ktn, ktr, va, n) in sq["kp"]:
                lst.append((kt[b][0:64, off:off + n], ktn(h)))
                lst.append((kt[b][64:96, off:off + n], ktr))
                off += n
            dep = r_gs if sq.get("gathered") else []
            S.dma("sp", (lambda e, lst=lst: [e.dma_start(out=o, in_=i_) for o, i_ in lst]), len(lst), f"kt{b}", reads=dep, writes=[r_kt[b]])
            off = 0
            lst2 = []
            for (ktn, ktr, va, n) in sq["kp"]:
                lst2.append((vt[b][:, off // 128:(off + n) // 128, 0:64], va(h).rearrange("(c p) d -> p c d", p=128)))
                off += n
            S.dma("sp", (lambda e, lst2=lst2: [e.dma_start(out=o, in_=i_) for o, i_ in lst2]), len(lst2), f"vt{b}", reads=dep, writes=[r_vt[b]])
            q0, nq = sq["q0"], sq["nq"]
            S.dma("sp", (lambda e, b=b, h=h, q0=q0, nq=nq: [e.dma_start(out=qt[b][0:96, 0:nq], in_=scr["QT"][h, :, q0:q0 + nq])]), 1,
                  f"qt{b}", writes=[r_qt[b]])

        def qk(u):
            hi, qb, kc, nkc, ob, lastq = units[u]
            b = hi % 2
            r = u % 2
            r3 = u % 3
            for t in range(2):
                S.op("pe", (lambda e, t=t: e.matmul(psS[r][:, t * 512:(t + 1) * 512], kt[b][0:96, (2 * kc + t) * 128:(2 * kc + t + 1) * 128],
                                                    qt[b][0:96, qb * 512:(qb + 1) * 512], start=True, stop=True)),
                     reads=[r_kt[b], r_qt[b]], writes=[r_psS[r]])
            S.op("act", (lambda e: e.activation(pT[r3][:], psS[r][:], AF.Exp, scale=scale)), reads=[r_psS[r]], writes=[r_pT[r3]])

        def pv(u):
            hi, qb, kc, nkc, ob, lastq = units[u]
            b = hi % 2
            r3 = u % 3
            for t in range(2):
                S.op("pe", (lambda e, t=t: e.matmul(psO[ob][0:65, :], vt[b][:, 2 * kc + t, 0:65], pT[r3][:, t * 512:(t + 1) * 512],
                                                    start=(kc == 0 and t == 0), stop=(kc == nkc - 1 and t == 1))),
                     reads=[r_vt[b], r_pT[r3]], writes=[r_psO[ob]])
            if kc == nkc - 1:
                sq, h = heads[hi]
                S.op("dve", (lambda e: e.reciprocal(rl[64:65, :], psO[ob][64:65, :])), reads=[r_psO[ob]], writes=[r_rl])
                S.op("pe", (lambda e: e.matmul(psB[0:64, :], onesf[64:65, 0:64], rl[64:65, :], start=True, stop=True)),
                     reads=[r_ones, r_rl], writes=[r_psB])
                S.op("dve", (lambda e: e.tensor_copy(osb[0:64, :], psO[ob][0:64, :])), reads=[r_psO[ob]], writes=[r_osb])
                S.op("dve", (lambda e: e.tensor_tensor(obf[ob][0:64, :], osb[0:64, :], psB[0:64, :], ALU.mult)),
                     reads=[r_osb, r_psB], writes=[r_obf[ob]])
                t0 = sq["q0"] + qb * 512
                S.dma("pool", (lambda e: [e.dma_start(out=scr["MIXT"][h * 64:(h + 1) * 64, t0:t0 + 512], in_=obf[ob][0:64, :])]), 1,
                      f"so{ob}", reads=[r_obf[ob]])

        load_head(0)
        load_head(1)
        n = len(units)
        qk(0)
        for u in range(n):
            if u + 1 < n:
                qk(u + 1)
            pv(u)
            hi, qb, kc, nkc, ob_, lastq = units[u]
            if lastq and kc == nkc - 1:
                load_head(hi + 2)
        S.barrier()
        S.emit()
    return S


def scan_phase(nc, tag, seq_lens, ntok, scr, xchg=None, cst=None):
    S = Sched(nc, tag)
    nch = ntok // 32
    with ExitStack() as es:
        C = Ctx(nc, tag, es)
        R = S.res
        NR = 3
        qpb = [[C.sb(f"qpb{d}{i}", [128, 4, 128], BF16) for i in range(NR)] for d in range(2)]
        atb = [[C.sb(f"atb{d}{i}", [128, 4, 128], BF16) for i in range(NR)] for d in range(2)]
        kpb = [[C.sb(f"kpb{d}{i}", [128, 4, 128], BF16) for i in range(NR)] for d in range(2)]
        vb = [[C.sb(f"vb{d}{i}", [128, 512], BF16) for i in range(NR)] for d in range(2)]
        r_ld = [[R() for i in range(NR)] for d in range(2)]
        dct = C.sb("dct", [128, 8, nch], F32)
        zer = C.sb("zer", [128, 512], BF16)
        SstAll = C.sb("SstAll", [128, 8, 129], F32)
        Sst = [SstAll[:, hd, 0:128] for hd in range(8)]
        Sbf = [C.sb(f"Sbf{hd}", [128, 128], BF16) for hd in range(8)]
        r_S = [R() for hd in range(8)]
        r_Sb = [R() for hd in range(8)]
        ot = [[C.sb(f"ot{d}{i}", [128, 512], F32) for i in range(2)] for d in range(2)]
        r_ot = [[R() for i in range(2)] for d in range(2)]
        psO = [[C.ps(f"psO{d}{i}", [128, 512]) for i in range(2)] for d in range(2)]
        r_psO = [[R() for i in range(2)] for d in range(2)]
        psU = [C.ps(f"psU{i}", [128, 512]) for i in range(4)]
        r_psU = [R() for i in range(4)]
        r_dc, r_z = R(), R()
        S.dma("sp", (lambda e: [e.dma_start(out=dct[:, hd, :], in_=scr["DC"][hd, :, :]) for hd in range(8)]), 8, "dc", writes=[r_dc])
        S.op("pool", lambda e: e.memset(zer[:], 0.0), writes=[r_z])
        ui = [0]
        offs = []
        a = 0
        for n in seq_lens:
            offs.append(a)
            a += n

        def run_seq(s0, SLs, mode, zero_init=True):
            NB = SLs // 128
            if zero_init:
                for hd in range(8):
                    S.op("pool", (lambda e, hd=hd: e.memset(Sst[hd], 0.0)), writes=[r_S[hd]])
                    S.op("pool", (lambda e, hd=hd: e.memset(Sbf[hd][:], 0.0)), writes=[r_Sb[hd]])

            def load(step):
                if step >= NB:
                    return
                for d in range(2):
                    blk = step if d == 0 else NB - 1 - step
                    t0 = s0 + blk * 128
                    i = step % NR
                    if mode == "full":
                        S.dma("sp", (lambda e, d=d, i=i, t0=t0: [
                            e.dma_start(out=qpb[d][i][:], in_=scr["QP"][d * 4:(d + 1) * 4, :, t0:t0 + 128].rearrange("h p t -> p h t")),
                            e.dma_start(out=atb[d][i][:], in_=scr["AT"][t0:t0 + 128, d * 4:(d + 1) * 4, :]),
                            e.dma_start(out=kpb[d][i][:], in_=scr["KPT"][t0:t0 + 128, d * 4:(d + 1) * 4, :]),
                            e.dma_start(out=vb[d][i][:], in_=scr["VH"][t0:t0 + 128, :])]), 4, f"ld{d}{i}", writes=[r_ld[d][i]])
                    else:
                        S.dma("sp", (lambda e, d=d, i=i, t0=t0: [
                            e.dma_start(out=kpb[d][i][:], in_=scr["KPT"][t0:t0 + 128, d * 4:(d + 1) * 4, :]),
                            e.dma_start(out=vb[d][i][:], in_=scr["VH"][t0:t0 + 128, :])]), 2, f"ld{d}{i}", writes=[r_ld[d][i]])
            load(0)
            load(1)
            for step in range(NB):
                load(step + 2)
                i = step % NR
                ob = step % 2
                if mode == "full":
                    for d in range(2):
                        S.op("pe", (lambda e, d=d, ob=ob: e.matmul(psO[d][ob][:], zer[:, 0:128], zer[:], start=True, stop=False,
                                                                   skip_group_check=True)), reads=[r_z], writes=[r_psO[d][ob]])
                        for h in range(4):
                            S.op("pe", (lambda e, d=d, ob=ob, h=h, i=i: e.matmul(
                                psO[d][ob][:, h * 128:(h + 1) * 128], atb[d][i][:, h, :], vb[d][i][:, h * 128:(h + 1) * 128],
                                start=False, stop=False, skip_group_check=True)), reads=[r_ld[d][i]], writes=[r_psO[d][ob]])
                for ci in range(4):
                    for d in range(2):
                        blk = step if d == 0 else NB - 1 - step
                        c = ci if d == 0 else 3 - ci
                        gch = (s0 + blk * 128) // 32 + c
                        for h in range(4):
                            hd = d * 4 + h
                            if mode == "full":
                                S.op("pe", (lambda e, d=d, ob=ob, h=h, i=i, c=c, hd=hd: e.matmul(
                                    psO[d][ob][32 * c:32 * c + 32, h * 128:(h + 1) * 128], qpb[d][i][:, h, 32 * c:32 * c + 32], Sbf[hd][:],
                                    start=False, stop=(ci == 3), skip_group_check=True, tile_position=(0, 32 * c))),
                                    reads=[r_ld[d][i], r_Sb[hd]], writes=[r_psO[d][ob]])
                            pu = ui[0] % 4
                            ui[0] += 1
                            S.op("pe", (lambda e, d=d, h=h, i=i, c=c, pu=pu: e.matmul(
                                psU[pu][:, 0:128], kpb[d][i][32 * c:32 * c + 32, h, :], vb[d][i][32 * c:32 * c + 32, h * 128:(h + 1) * 128],
                                start=True, stop=True, tile_position=(32 * c, 0))),
                                reads=[r_ld[d][i]], writes=[r_psU[pu]])
                            S.op("dve", (lambda e, hd=hd, pu=pu, gch=gch: e.scalar_tensor_tensor(
                                Sst[hd], Sst[hd], dct[:, hd, gch:gch + 1], psU[pu][:, 0:128], ALU.mult, ALU.add)),
                                reads=[r_psU[pu], r_dc], writes=[r_S[hd]])
                            if mode == "full":
                                S.op("act", (lambda e, hd=hd: e.copy(Sbf[hd][:], Sst[hd])), reads=[r_S[hd]], writes=[r_Sb[hd]])
                if mode == "full":
                    for d in range(2):
                        blk = step if d == 0 else NB - 1 - step
                        t0 = s0 + blk * 128
                        S.op("dve" if d == 0 else "act",
                             (lambda e, d=d, ob=ob: (e.tensor_copy(ot[d][ob][:], psO[d][ob][:]) if d == 0
                                                     else e.copy(ot[d][ob][:], psO[d][ob][:]))),
                             reads=[r_psO[d][ob]], writes=[r_ot[d][ob]])
                        S.dma("pool", (lambda e, d=d, ob=ob, t0=t0: [e.dma_start(out=scr["OF"][d, t0:t0 + 128, :], in_=ot[d][ob][:])]), 1,
                              f"so{d}{ob}", reads=[r_ot[d][ob]])

        if xchg is None:
            for s0, n in zip(offs, seq_lens):
                run_seq(s0, n, "full")
        else:
            n0 = seq_lens[0]
            G = C.sb("G", [128, 4, 8, 129], F32)
            rkm = C.sb("rkm", [128, 8], F32)
            tmp = C.sb("tmp", [128, 128], F32)
            r_G, r_rk, r_tmp, r_xs = R(), R(), R(), R()
            S.dma("sp", lambda e: [e.dma_start(out=rkm[:], in_=cst["rankmask"])], 1, "rk", writes=[r_rk])
            run_seq(offs[0], n0, "state")
            for hd in range(8):
                S.op("dve", (lambda e, hd=hd: e.tensor_reduce(SstAll[:, hd, 128:129], dct[:, hd, offs[0] // 32:(offs[0] + n0) // 32],
                                                              AX.X, ALU.mult)), reads=[r_dc], writes=[r_S[hd]])
            S.dma("pool", (lambda e: [e.dma_start(out=xchg["XS_in"].rearrange("(h p) c -> p h c", p=128), in_=SstAll[:])]), 1, "xs",
                  reads=r_S, writes=[r_xs])
            S.dma("pool", (lambda e: [e.collective_compute("AllGather", ALU.bypass, replica_groups=xchg["groups"],
                                                           ins=[xchg["XS_in"].opt()], outs=[xchg["XS_out"].opt()])]), 1, "ccs", reads=[r_xs], writes=[r_G], inc=1)
            for s0, n in list(zip(offs, seq_lens))[1:]:
                run_seq(s0, n, "full")
            S.dma("sp", (lambda e: [e.dma_start(out=G[:], in_=xchg["XS_out"].rearrange("(r h p) c -> p r h c", r=4, h=8))]), 1, "g",
                  reads=[r_G], writes=[r_G])
            for hd in range(8):
                S.op("pool", (lambda e, hd=hd: e.memset(Sst[hd], 0.0)), writes=[r_S[hd]])
                order = range(4) if hd < 4 else range(3, -1, -1)
                for i in order:
                    mcol = rkm[:, (0 if hd < 4 else 4) + i:(0 if hd < 4 else 4) + i + 1]
                    S.op("dve", (lambda e, hd=hd, i=i: e.scalar_tensor_tensor(tmp[:], Sst[hd], G[:, i, hd, 128:129], G[:, i, hd, 0:128],
                                                                            ALU.mult, ALU.add)), reads=[r_G, r_S[hd]], writes=[r_tmp])
                    S.op("dve", (lambda e, hd=hd: e.tensor_tensor(tmp[:], tmp[:], Sst[hd], ALU.subtract)), reads=[r_S[hd]], writes=[r_tmp])
                    S.op("dve", (lambda e, hd=hd, mcol=mcol: e.scalar_tensor_tensor(Sst[hd], tmp[:], mcol, Sst[hd], ALU.mult, ALU.add)),
                         reads=[r_tmp, r_rk], writes=[r_S[hd]])
                S.op("act", (lambda e, hd=hd: e.copy(Sbf[hd][:], Sst[hd])), reads=[r_S[hd]], writes=[r_Sb[hd]])
            run_seq(offs[0], n0, "full", zero_init=False)
        S.barrier()
        S.emit()
    return S


def outproj_phase(nc, tag, ntok, h1_d, h2_d, cst, w, scr):
    S = Sched(nc, tag)
    ntiles = ntok // 512
    with ExitStack() as es:
        C = Ctx(nc, tag, es)
        R = S.res
        wo = C.sb("wo", [128, 8, D], BF16)
        st = [C.sb("st0", [128, 512], F32), C.sb("st1", [128, 512], F32)]
        ident = C.sb("ident", [128, 128], BF16)
        onb = C.sb("onb", [128, 512], F32)
        ht = [C.sb(f"ht{i}", [128, 4, D], F32) for i in range(2)]
        mT = [C.sb(f"mT{i}", [128, 8, 512], BF16) for i in range(2)]
        of = [C.sb(f"of{i}", [128, 4, 512], F32) for i in range(2)]
        ob = [C.sb(f"ob{i}", [128, 4, 512], F32) for i in range(2)]
        gh = [C.sb(f"gh{i}", [128, 4, 512], BF16) for i in range(2)]
        osum4 = C.sb("osum4", [128, 4, 512], F32)
        junk = C.sb("junk", [128, 128], BF16)
        stat4 = C.sb("stat4", [128, 40], F32)
        mh4 = C.sb("mh4", [128, 4, 512], BF16)
        pt = C.ps("pt", [128, 1024], BF16)
        pt2 = C.ps("pt2", [128, 1024], BF16)
        r_pts = [R(), R()]
        po = [C.ps(f"po{i}", [128, 512]) for i in range(2)]
        r_w, r_st, r_c, r_ident = R(), [R(), R()], R(), R()
        S.dma("sp", lambda e: [e.dma_start(out=onb[:], in_=cst["hg_norm_b"]), e.dma_start(out=st[0][:, 0:128], in_=cst["ident"])],
              2, "c", writes=[r_c, r_st[0]])
        S.op("pool", lambda e: e.tensor_copy(ident[:], st[0][:, 0:128]), reads=[r_st[0]], writes=[r_ident])
        qi = [1]
        load_w(S, w["w_o"], wo, 8, D, None, st, r_st, r_w, qi)
        r_ld = [R(), R()]
        r_ht = [[R() for j in range(4)] for i in range(2)]
        r_mT = [[R() for j in range(4)] for i in range(2)]
        r_osum, r_junk, r_stat, r_mh, r_pt, r_po = R(), R(), R(), R(), R(), [R(), R()]

        def load(i):
            if i >= ntiles:
                return
            b = i % 2
            t0 = i * 512
            S.dma("sp", (lambda e: [e.dma_start(out=ht[b][:, j, :], in_=h1_d[t0 + j * 128:t0 + (j + 1) * 128, :]) for j in range(4)]),
                  4, f"ht{b}", writes=r_ht[b])
            S.dma("sp", (lambda e: [
                e.dma_start(out=mT[b][:, 0:4, :], in_=scr["MIXT"][0:512, t0:t0 + 512].rearrange("(c p) t -> p c t", p=128)),
                e.dma_start(out=of[b][:], in_=scr["OF"][0, t0:t0 + 512, :].rearrange("(j p) c -> p j c", p=128)),
                e.dma_start(out=ob[b][:], in_=scr["OF"][1, t0:t0 + 512, :].rearrange("(j p) c -> p j c", p=128)),
                e.dma_start(out=gh[b][:], in_=scr["GH"][t0:t0 + 512, :].rearrange("(j p) c -> p j c", p=128))]),
                4, f"ld{b}", writes=[r_ld[b]] + r_mT[b])
        load(0)

        def body(i):
            b = i % 2
            load(i + 1)
            t0 = i * 512
            r_os = [R() for j in range(4)]
            r_mhj = [R() for j in range(4)]
            r_stj = [R() for j in range(4)]
            r_jk = [Res("junk") for _ in range(16)]
            for j in range(4):
                S.op("dve" if j % 2 == 0 else "pool",
                     (lambda e, j=j: e.tensor_tensor(osum4[:, j, :], of[b][:, j, :], ob[b][:, j, :], ALU.add)),
                     reads=[r_ld[b], r_osum], writes=[r_os[j]])
            for j in range(4):
                for h in range(4):
                    S.op("act", (lambda e, h=h, j=j: e.activation(junk[:], osum4[:, j, h * 128:(h + 1) * 128], AF.Square,
                                                                  accum_out=stat4[:, j * 8 + h:j * 8 + h + 1])),
                         reads=[r_os[j]], writes=[r_jk[j * 4 + h], r_stj[j]])
            for j in range(4):
                S.op("act", (lambda e, j=j: e.activation(stat4[:, j * 8 + 4:j * 8 + 8], stat4[:, j * 8:j * 8 + 4], AF.Sqrt, bias=EPS, scale=1.0 / 128)),
                     reads=[r_stj[j]], writes=[r_stj[j]])
            for j in range(4):
                S.op("dve", (lambda e, j=j: e.reciprocal(stat4[:, j * 8 + 4:j * 8 + 8], stat4[:, j * 8 + 4:j * 8 + 8])), reads=[r_stj[j]], writes=[r_stj[j]])
            for j in range(4):
                ov = osum4[:, j, :].rearrange("p (h v) -> p h v", h=4)
                S.op("dve", (lambda e, ov=ov, j=j: e.tensor_tensor(
                    ov, ov, stat4[:, j * 8 + 4:j * 8 + 8].rearrange("p (h o) -> p h o", o=1).broadcast_to([128, 4, 128]), ALU.mult)),
                    reads=[r_stj[j]], writes=[r_os[j]])
            for j in range(4):
                S.op("pool", (lambda e, j=j: e.tensor_tensor(osum4[:, j, :], osum4[:, j, :], onb[:], ALU.mult)), reads=[r_c], writes=[r_os[j]])
            for j in range(4):
                S.op("dve", (lambda e, j=j: e.tensor_tensor(mh4[:, j, :], osum4[:, j, :], gh[b][:, j, :], ALU.mult)),
                     reads=[r_os[j], r_ld[b], r_mh], writes=[r_mhj[j]])
            for j in range(4):
                ptj, r_ptj = [pt, pt2][j % 2], r_pts[j % 2]
                for c in range(4):
                    S.op("pe", (lambda e, c=c, j=j, ptj=ptj: e.transpose(ptj[:, c * 128:(c + 1) * 128], mh4[:, j, c * 128:(c + 1) * 128], ident[:])),
                         reads=[r_mhj[j], r_ident], writes=[r_ptj])
                S.op("act", (lambda e, j=j, ptj=ptj: e.copy(mT[b][:, 4:8, j * 128:(j + 1) * 128],
                                                            ptj[:, 0:512].rearrange("p (k t) -> p k t", k=4))),
                     reads=[r_ptj], writes=[r_mT[b][j]])
            S.op("pool", (lambda e: e.memset(stat4[:, 32:33], 0.0)), writes=r_os + r_mhj + [r_osum, r_mh])
            for j in range(4):
                for hh in range(2):
                    pb = (j * 2 + hh) % 2
                    for kc in range(8):
                        S.op("pe", (lambda e, j=j, hh=hh, kc=kc, pb=pb: e.matmul(
                            po[pb][:], mT[b][:, kc, j * 128:(j + 1) * 128], wo[:, kc, hh * 512:(hh + 1) * 512],
                            start=(kc == 0), stop=(kc == 7))), reads=[r_w, r_mT[b][j]], writes=[r_po[pb]])
                    dst = ht[b][:, j, hh * 512:(hh + 1) * 512]
                    S.op("dve", (lambda e, dst=dst, pb=pb: e.tensor_tensor(dst, dst, po[pb][:], ALU.add)),
                         reads=[r_po[pb]], writes=[r_ht[b][j]])
                S.dma("pool", (lambda e, j=j: [e.dma_start(out=h2_d[t0 + j * 128:t0 + (j + 1) * 128, :], in_=ht[b][:, j, :])]), 1,
                      f"so{b}{j}", reads=[r_ht[b][j]])
        for i in range(ntiles):
            body(i)
        S.barrier()
        S.emit()
    return S


def ple_phase(nc, tag, ntok, h3_d, p_d, y_d, cst, w):
    S = Sched(nc, tag)
    ntiles = ntok // 512
    with ExitStack() as es:
        C = Ctx(nc, tag, es)
        R = S.res
        wg = C.sb("wg", [128, 8, D], BF16)
        wp = C.sb("wp", [128, 2, D], BF16)
        st = [C.sb("st0", [128, 512], F32), C.sb("st1", [128, 512], F32)]
        ident = C.sb("ident", [128, 128], BF16)
        gain = C.sb("gain", [128, 8], F32)
        fnb = C.sb("fnb", [128, D], F32)
        ht = [C.sb(f"ht{i}", [128, 4, D], F32) for i in range(2)]
        ptl = [C.sb(f"ptl{i}", [128, 4, 256], F32) for i in range(2)]
        xn4 = C.sb("xn4", [128, 4, D], BF16)
        junk = C.sb("junk", [128, D], BF16)
        stat = C.sb("stat", [128, 16], F32)
        xnT = C.sb("xnT", [128, 8, 512], BF16)
        pb16 = C.sb("pb16", [128, 4, 256], BF16)
        pT = C.sb("pT", [128, 2, 512], BF16)
        gsb = [C.sb(f"gsb{i}", [128, 512], F32) for i in range(2)]
        pt = C.ps("pt", [128, 1024], BF16)
        pt2 = C.ps("pt2", [128, 1024], BF16)
        pg = [C.ps(f"pg{i}", [128, 512]) for i in range(2)]
        pp = [C.ps(f"pp{i}", [128, 512]) for i in range(2)]
        r_w, r_st, r_c, r_ident = R(), [R(), R()], R(), R()
        S.dma("sp", lambda e: [e.dma_start(out=gain[:], in_=cst["ple_norm"]), e.dma_start(out=fnb[:], in_=cst["final_norm_b"]),
                               e.dma_start(out=st[0][:, 0:128], in_=cst["ident"])], 3, "c", writes=[r_c, r_st[0]])
        S.op("pool", lambda e: e.tensor_copy(ident[:], st[0][:, 0:128]), reads=[r_st[0]], writes=[r_ident])
        S.op("pool", lambda e: e.tensor_copy(stat[:, 15:16], gain[:, 0:1]), reads=[r_c], writes=[R()])
        qi = [1]
        load_w(S, w["w_ple_gate"], wg, 8, D, gain, st, r_st, r_w, qi)
        load_w(S, w["w_ple_proj"], wp, 2, D, None, st, r_st, r_w, qi)
        r_ht = [[R() for j in range(4)] for i in range(2)]
        r_pl = [R(), R()]
        r_stat = [R() for j in range(4)]
        r_junk, r_pt = R(), R()
        r_xn4 = [R() for j in range(4)]
        r_pts = [R(), R()]
        r_xnT = [R() for k in range(8)]
        r_pb16, r_pT, r_gsb, r_pg, r_pp = [R() for j in range(4)], R(), [R(), R()], [R(), R()], [R(), R()]

        def load(i):
            if i >= ntiles:
                return
            b = i % 2
            t0 = i * 512
            S.dma("sp", (lambda e: [e.dma_start(out=ht[b][:, j, :], in_=h3_d[t0 + j * 128:t0 + (j + 1) * 128, :]) for j in range(4)]),
                  4, f"ht{b}", writes=r_ht[b])
            S.dma("sp", (lambda e: [e.dma_start(out=ptl[b][:], in_=p_d[t0:t0 + 512, :].rearrange("(j p) c -> p j c", p=128))]),
                  1, f"pl{b}", writes=[r_pl[b]])
        load(0)

        def body(i):
            b = i % 2
            load(i + 1)
            t0 = i * 512
            norm_transpose4(S, ht[b], r_ht[b], stat, r_stat, junk, xn4, r_xn4, [pt, pt2], r_pts, ident, r_ident, xnT, r_xnT)
            for j in range(4):
                S.op("pool", (lambda e, j=j: e.tensor_copy(pb16[:, j, :], ptl[b][:, j, :])), reads=[r_pl[b]], writes=[r_pb16[j]])
            for j in range(4):
                ptj, r_ptj = [pt, pt2][j % 2], r_pts[j % 2]
                for c in range(2):
                    S.op("pe", (lambda e, c=c, j=j, ptj=ptj: e.transpose(ptj[:, c * 128:(c + 1) * 128], pb16[:, j, c * 128:(c + 1) * 128], ident[:])),
                         reads=[r_pb16[j], r_ident], writes=[r_ptj])
                S.op("act", (lambda e, j=j, ptj=ptj: e.copy(pT[:, :, j * 128:(j + 1) * 128], ptj[:, 0:256].rearrange("p (k t) -> p k t", k=2))),
                     reads=[r_ptj], writes=[r_pT])
            for j in range(4):
                for hh in range(2):
                    k2 = (j * 2 + hh) % 2
                    for kc in range(8):
                        S.op("pe", (lambda e, j=j, hh=hh, kc=kc, k2=k2: e.matmul(
                            pg[k2][:], xnT[:, kc, j * 128:(j + 1) * 128], wg[:, kc, hh * 512:(hh + 1) * 512],
                            start=(kc == 0), stop=(kc == 7))), reads=[r_w] + r_xnT, writes=[r_pg[k2]])
                    for kc in range(2):
                        S.op("pe", (lambda e, j=j, hh=hh, kc=kc, k2=k2: e.matmul(
                            pp[k2][:], pT[:, kc, j * 128:(j + 1) * 128], wp[:, kc, hh * 512:(hh + 1) * 512],
                            start=(kc == 0), stop=(kc == 1))), reads=[r_w, r_pT], writes=[r_pp[k2]])
                    S.op("act", (lambda e, k2=k2: e.activation(gsb[k2][:], pg[k2][:], AF.Sigmoid)), reads=[r_pg[k2]], writes=[r_gsb[k2]])
                    S.op("dve", (lambda e, k2=k2: e.tensor_tensor(gsb[k2][:], gsb[k2][:], pp[k2][:], ALU.mult)),
                         reads=[r_pp[k2]], writes=[r_gsb[k2]])
                    dst = ht[b][:, j, hh * 512:(hh + 1) * 512]
                    S.op("pool", (lambda e, dst=dst, k2=k2: e.tensor_tensor(dst, dst, gsb[k2][:], ALU.add)),
                         reads=[r_gsb[k2]], writes=[r_ht[b][j]])
            r_j2 = [Res("junk") for _ in range(4)]
            for j in range(4):
                S.op("act", (lambda e, j=j: e.activation(junk[:], ht[b][:, j, :], AF.Square, accum_out=stat[:, 8 + j:9 + j])),
                     reads=[r_ht[b][j]], writes=[r_j2[j], r_stat[j]])
            for j in range(4):
                S.op("act", (lambda e, j=j: e.activation(stat[:, 8 + j:9 + j], stat[:, 8 + j:9 + j], AF.Sqrt, bias=EPS, scale=1.0 / D)),
                     writes=[r_stat[j]])
            for j in range(4):
                S.op("dve", (lambda e, j=j: e.reciprocal(stat[:, 8 + j:9 + j], stat[:, 8 + j:9 + j])), writes=[r_stat[j]])
            for j in range(4):
                S.op("dve", (lambda e, j=j: e.scalar_tensor_tensor(ht[b][:, j, :], ht[b][:, j, :], stat[:, 8 + j:9 + j], fnb[:], ALU.mult, ALU.mult)),
                     reads=[r_stat[j], r_c], writes=[r_ht[b][j]])
                S.dma("pool", (lambda e, j=j: [e.dma_start(out=y_d[t0 + j * 128:t0 + (j + 1) * 128, :], in_=ht[b][:, j, :])]), 1,
                      f"so{b}{j}", reads=[r_ht[b][j]])
        for i in range(ntiles):
            body(i)
        S.barrier()
        S.emit()
    return S


CONST_SHAPES = {
    "ident": [128, 128], "rmask": [128, 512], "maskF": [128, 128], "maskB": [128, 128],
    "ffn1_norm": [128, 8], "mix_norm": [128, 8], "q_norm": [128, 3], "kv_norm": [128, 2], "hg_lb": [128, 16],
    "hg_norm_b": [128, 512], "ffn2_norm": [128, 8], "ple_norm": [128, 8], "final_norm_b": [128, D],
}
W_SHAPES = {
    "ffn1_wg": [D, DFF], "ffn1_wu": [D, DFF], "ffn1_wd": [DFF, D], "w_in": [D, WIN_COLS], "w_uq": [384, UQ_COLS],
    "w_uk": [256, 512], "w_uv": [256, 512], "w_o": [D, D], "ffn2_wg": [D, DFF], "ffn2_wu": [D, DFF], "ffn2_wd": [DFF, D],
    "w_ple_gate": [D, D], "w_ple_proj": [256, D],
}


def build_program(seq_lens, dbg=(), phases=None, gather=False):
    Sched.DMA_SEMS = {}
    Sched.DMA_CNT = {}
    nc = bass.Bass("TRN2", target_bir_lowering=False)
    ntok = sum(seq_lens)
    ntiles = ntok // 512

    def inp(name, shape, dt=F32):
        return nc.dram_tensor(name, shape, dt, kind="ExternalInput").ap()

    def scratch(name, shape, dt):
        kind = "ExternalOutput" if name in dbg else "Internal"
        return nc.dram_tensor(name, shape, dt, kind=kind).ap()

    x = inp("x", [ntok, D])
    p = inp("p", [ntok, 256])
    cst = {k: inp(k, v) for k, v in CONST_SHAPES.items()}
    cst["rope_c"] = inp("rope_c", [32, ntok])
    cst["rope_s"] = inp("rope_s", [32, ntok])
    w = {k: inp(k, v) for k, v in W_SHAPES.items()}
    y = nc.dram_tensor("y", [ntok, D], F32, kind="ExternalOutput").ap()
    h1 = scratch("h1", [ntok, D], F32)
    h2 = scratch("h2", [ntok, D], F32)
    h3 = scratch("h3", [ntok, D], F32)
    scr = {
        "QT": scratch("QT", [8, 96, ntok], BF16), "KTn": scratch("KTn", [512, ntok], BF16),
        "KTr": scratch("KTr", [32, ntok], BF16), "VA": scratch("VA", [8, ntok, 64], BF16),
        "QP": scratch("QP", [8, 128, ntok], BF16), "DC": scratch("DC", [8, 128, ntok // 32], F32),
        "VH": scratch("VH", [ntok, 512], BF16), "GH": scratch("GH", [ntok, 512], BF16),
        "AT": scratch("AT", [ntok, 8, 128], BF16), "KPT": scratch("KPT", [ntok, 8, 128], BF16),
        "MIXT": scratch("MIXT", [512, ntok], BF16), "OF": scratch("OF", [2, ntok, 512], F32),
    }
    def on(k):
        return phases is None or k in phases
    if on("f1"):
        ffn_phase(nc, "f1", x, h1, ntiles, cst["ffn1_norm"], w["ffn1_wg"], w["ffn1_wu"], w["ffn1_wd"], cst["ident"])
    if on("mi"):
        mixer_in_phase(nc, "mi", ntok, h1, cst, w, scr)
    seqs = []
    a = 0
    for SLs in seq_lens:
        b = a + SLs
        seqs.append(dict(q0=a, nq=SLs, kp=[((lambda h, a=a, b=b: scr["KTn"][h * 64:(h + 1) * 64, a:b]), scr["KTr"][:, a:b],
                                            (lambda h, a=a, b=b: scr["VA"][h, a:b, :]), SLs)]))
        a = b
    xchg = None
    if gather:
        n0 = seq_lens[0]
        cst["rankmask"] = inp("rankmask", [128, 8])
        xchg = dict(n0=n0, groups=[[0, 1, 2, 3], [4, 5, 6, 7]],
                    XK_in=[scratch(f"XK_in{a_}", [128, n0], BF16) for a_ in range(4)],
                    XK_out=[scratch(f"XK_out{a_}", [4 * 128, n0], BF16) for a_ in range(4)],
                    XR_in=scratch("XR_in", [128, n0], BF16), XR_out=scratch("XR_out", [4 * 128, n0], BF16),
                    XV_in=[scratch(f"XV_in{a_}", [2 * n0, 64], BF16) for a_ in range(4)],
                    XV_out=[scratch(f"XV_out{a_}", [4 * 2 * n0, 64], BF16) for a_ in range(4)],
                    XS_in=scratch("XS_in", [1024, 129], F32), XS_out=scratch("XS_out", [4096, 129], F32))
        kp = []
        for r in range(4):
            kp.append(((lambda h, r=r: xchg["XK_out"][h // 2][r * 128 + (h % 2) * 64:r * 128 + (h % 2) * 64 + 64, :]),
                       xchg["XR_out"][r * 128:r * 128 + 32, :],
                       (lambda h, r=r: xchg["XV_out"][h // 2].rearrange("(r hh t) d -> r hh t d", r=4, hh=2)[r, h % 2]),
                       n0))
        seqs[0] = dict(q0=0, nq=n0, kp=kp, gathered=True)
        seqs = seqs[1:] + seqs[0:1]
    if on("at"):
        attn_phase(nc, "at", seqs, scr, xchg=xchg)
    if on("sc"):
        scan_phase(nc, "sc", list(seq_lens), ntok, scr, xchg=xchg, cst=cst)
    if on("op"):
        outproj_phase(nc, "op", ntok, h1, h2, cst, w, scr)
    if on("f2"):
        ffn_phase(nc, "f2", h2, h3, ntiles, cst["ffn2_norm"], w["ffn2_wg"], w["ffn2_wu"], w["ffn2_wd"], cst["ident"])
    if on("pl"):
        ple_phase(nc, "pl", ntok, h3, p, y, cst, w)
    return nc


def _lay(v, nch):
    return np.ascontiguousarray(np.asarray(v, np.float32).reshape(nch, 128).T)


def host_consts(inputs):
    f = np.float32
    c = {}
    c["ident"] = np.eye(128, dtype=f)
    rm = np.ones((128, 512), f)
    rm[:, 0::32] = 0.0
    c["rmask"] = rm
    idx = np.arange(128)
    same = (idx[:, None] // 32) == (idx[None, :] // 32)
    c["maskF"] = (same & (idx[:, None] <= idx[None, :])).astype(f)
    c["maskB"] = (same & (idx[:, None] >= idx[None, :])).astype(f)
    c["ffn1_norm"] = _lay(inputs["ffn1_norm"][0], 8)
    c["mix_norm"] = _lay(inputs["mix_norm"][0], 8)
    c["q_norm"] = _lay(inputs["q_norm"][0], 3)
    c["kv_norm"] = _lay(inputs["kv_norm"][0], 2)
    lb = np.asarray(inputs["hg_lb"], f).reshape(2, 2, 4, 128)
    c["hg_lb"] = np.ascontiguousarray(lb.transpose(3, 0, 1, 2).reshape(128, 16))
    c["hg_norm_b"] = np.ascontiguousarray(np.broadcast_to(np.asarray(inputs["hg_norm"][0], f)[None, :], (128, 512)))
    c["ffn2_norm"] = _lay(inputs["ffn2_norm"][0], 8)
    c["ple_norm"] = _lay(inputs["ple_norm"][0], 8)
    c["final_norm_b"] = np.ascontiguousarray(np.broadcast_to(np.asarray(inputs["final_norm"], f)[None, :], (128, D)))
    wts = {}
    for k in ("ffn1_wg", "ffn1_wu", "ffn1_wd", "w_uk", "w_uv", "w_o", "ffn2_wg", "ffn2_wu", "ffn2_wd", "w_ple_gate", "w_ple_proj"):
        wts[k] = np.ascontiguousarray(np.asarray(inputs[k][0], f))
    win = np.asarray(inputs["w_in"][0], f)
    wts["w_in"] = np.ascontiguousarray(np.concatenate([win, win[:, 656:672], win[:, 640:656]], axis=1))
    wq = np.asarray(inputs["w_uq"][0], f)
    sw = [np.concatenate([wq[:, h * 96 + 80:h * 96 + 96], wq[:, h * 96 + 64:h * 96 + 80]], axis=1) for h in range(8)]
    wts["w_uq"] = np.ascontiguousarray(np.concatenate([wq] + sw, axis=1))
    return c, wts


def rope_tables(pos):
    inv = np.exp(np.arange(0, 32, 2, dtype=np.float32) * np.float32(-np.log(10000.0) / 32)).astype(np.float32)
    ang = (pos.astype(np.float32)[None, :] * inv[:, None]).astype(np.float32)
    cs, sn = np.cos(ang).astype(np.float32), np.sin(ang).astype(np.float32)
    return np.ascontiguousarray(np.concatenate([cs, cs], 0)), np.ascontiguousarray(np.concatenate([-sn, sn], 0))


_PROG = {}


def run_balanced(inputs):
    xp = np.asarray(inputs["x_prompt"], np.float32)
    xs = np.asarray(inputs["x_sample"], np.float32)
    pp = np.asarray(inputs["p_prompt"], np.float32)[0]
    psm = np.asarray(inputs["p_sample"], np.float32)[0]
    SP, SS = xp.shape[1], xs.shape[1]
    Q = SP // 4
    lens = (Q, SS, SS)
    c, wts = host_consts(inputs)
    key = ("bal", lens)
    if key not in _PROG:
        _PROG[key] = build_program(lens, gather=True)
    nc = _PROG[key]
    in_maps = []
    for core in range(8):
        pb, r = core // 4, core % 4
        im = {"x": np.ascontiguousarray(np.concatenate([xp[pb, r * Q:(r + 1) * Q], xs[2 * core], xs[2 * core + 1]], axis=0)),
              "p": np.ascontiguousarray(np.concatenate([pp[pb, r * Q:(r + 1) * Q], psm[2 * core], psm[2 * core + 1]], axis=0))}
        pos = np.concatenate([np.arange(r * Q, (r + 1) * Q, dtype=np.float32), np.arange(SS, dtype=np.float32),
                              np.arange(SS, dtype=np.float32)])
        im["rope_c"], im["rope_s"] = rope_tables(pos)
        rk = np.zeros((128, 8), np.float32)
        for i in range(4):
            rk[:, i] = 1.0 if i < r else 0.0
            rk[:, 4 + i] = 1.0 if i > r else 0.0
        im["rankmask"] = rk
        im.update(c)
        im.update(wts)
        in_maps.append(im)
    res = run_bass_kernel_spmd(nc, in_maps, core_ids=list(range(8)))
    y_prompt = np.empty((2, SP, D), np.float32)
    y_sample = np.empty((16, SS, D), np.float32)
    for core in range(8):
        pb, r = core // 4, core % 4
        y = np.asarray(res.results[core]["y"])
        y_prompt[pb, r * Q:(r + 1) * Q] = y[0:Q]
        y_sample[2 * core] = y[Q:Q + SS]
        y_sample[2 * core + 1] = y[Q + SS:]
    return (y_prompt, y_sample)


def kernel(**inputs):
    return run_balanced(inputs)
```

```python
import numpy as np
from contextlib import ExitStack
import concourse.bass as bass
import concourse.mybir as mybir
from concourse.bass_utils import run_bass_kernel_spmd

F32 = mybir.dt.float32
BF16 = mybir.dt.bfloat16
AF = mybir.ActivationFunctionType
ALU = mybir.AluOpType
AX = mybir.AxisListType

D = 1024
DFF = 2816
EPS = 1e-6
NFC = DFF // 128
NKC = D // 128


class Res:
    __slots__ = ("name", "w", "r")

    def __init__(self, name):
        self.name = name
        self.w = None
        self.r = {}


class Ev:
    __slots__ = ("kind", "key", "op", "count")

    def __init__(self, kind, key, op=None, count=0):
        self.kind = kind
        self.key = key
        self.op = op
        self.count = count


class Op:
    __slots__ = ("fn", "deps", "sig", "count", "dma_key", "dma_n", "ev", "inc")

    def __init__(self, fn, deps):
        self.fn = fn
        self.deps = deps
        self.sig = False
        self.count = 0
        self.dma_key = None
        self.dma_n = 0
        self.ev = None


ENGS = ("sp", "act", "dve", "pool", "pe")
FUSE_WAITS = True


class Sched:
    DMA_SEMS = {}
    DMA_CNT = {}

    def __init__(self, nc, tag):
        self.nc = nc
        self.tag = tag
        self.ops = {e: [] for e in ENGS}
        self.dma_cnt = Sched.DMA_CNT
        self.nres = 0
        self.keymap = {}

    def res(self, name=None):
        self.nres += 1
        return Res(name or f"r{self.nres}")

    def _deps(self, eng, reads, writes):
        deps = []
        for r in reads:
            if r.w is not None:
                deps.append(r.w)
        for w in writes:
            if w.w is not None:
                deps.append(w.w)
            deps.extend(w.r.values())
        out = []
        seen = set()
        for d in deps:
            if id(d) in seen:
                continue
            seen.add(id(d))
            if d.kind == "e" and d.key == "pe" and eng == "pe":
                continue
            out.append(d)
        return out

    def op(self, eng, fn, reads=(), writes=()):
        o = Op(fn, self._deps(eng, reads, writes))
        ev = Ev("e", eng, op=o)
        o.ev = ev
        self.ops[eng].append(o)
        for r in reads:
            r.r[("e", eng)] = ev
        for w in writes:
            w.w = ev
            w.r = {}
        return o

    def dma(self, queue, fn, n, key, reads=(), writes=(), inc=16):
        if key not in self.keymap:
            self.keymap[key] = f"k{len(self.keymap)}"
        key = self.keymap[key]
        o = Op(fn, self._deps(queue, reads, writes))
        o.dma_key = key
        o.dma_n = n
        o.inc = inc
        c = self.dma_cnt.get(key, 0) + n * inc
        self.dma_cnt[key] = c
        ev = Ev("d", key, op=o, count=c)
        o.ev = ev
        self.ops[queue].append(o)
        for r in reads:
            r.r[("d", key)] = ev
        for w in writes:
            w.w = ev
            w.r = {}
        return o

    def barrier(self):
        evs = []
        for e in ENGS:
            for o in reversed(self.ops[e]):
                if o.dma_key is None and o.fn is not None:
                    evs.append(o.ev)
                    break
        lastd = {}
        for e in ENGS:
            for o in self.ops[e]:
                if o.dma_key is not None:
                    lastd[o.dma_key] = o.ev
        evs.extend(lastd.values())
        for e in ENGS:
            deps = [d for d in evs if not (d.kind == "e" and d.key == e)]
            o = Op(None, deps)
            o.ev = Ev("e", e, op=o)
            self.ops[e].append(o)

    def emit(self):
        nc = self.nc
        for e in ENGS:
            for o in self.ops[e]:
                for d in o.deps:
                    if d.kind == "e":
                        d.op.sig = True
        sems = {}
        for e in ENGS:
            c = 0
            for o in self.ops[e]:
                if o.dma_key is None and o.sig:
                    assert o.fn is not None
                    c += 1
                    o.count = c
            sems[("e", e)] = nc.alloc_semaphore(f"{self.tag}_e_{e}")
        for k in self.dma_cnt:
            if k not in Sched.DMA_SEMS:
                Sched.DMA_SEMS[k] = nc.alloc_semaphore(f"d_{k}")
            sems[("d", k)] = Sched.DMA_SEMS[k]
        self.sems = sems

        def run(eng_name, eng):
            waited = {}
            for o in self.ops[eng_name]:
                need = {}
                for d in o.deps:
                    k = (d.kind, d.key)
                    val = d.op.count if d.kind == "e" else d.count
                    assert val > 0, (eng_name, d.kind, d.key)
                    if waited.get(k, 0) >= val:
                        continue
                    waited[k] = val
                    need[k] = max(need.get(k, 0), val)
                need = list(need.items())
                fuse = None
                if FUSE_WAITS and need and o.fn is not None and o.dma_key is None:
                    fuse = need.pop()
                for k, val in need:
                    eng.wait_ge(sems[k], val)
                if o.fn is None:
                    continue
                ins = o.fn(eng)
                if fuse is not None:
                    ins._wait_ge(sems[fuse[0]], fuse[1])
                if o.dma_key is not None:
                    assert len(ins) == o.dma_n
                    for i in ins:
                        i.then_inc(sems[("d", o.dma_key)], o.inc)
                elif o.sig:
                    ins.then_inc(sems[("e", eng_name)], 1)

        with nc.Block() as block:
            @block.sync
            def _(e):
                run("sp", e)

            @block.scalar
            def _(e):
                run("act", e)

            @block.vector
            def _(e):
                run("dve", e)

            @block.gpsimd
            def _(e):
                run("pool", e)

            @block.tensor
            def _(e):
                run("pe", e)

    def release(self):
        for s in self.sems.values():
            self.nc.release_semaphore(s)


def load_weight_bf16(S, nc, w_dram, w_sb, rows_chunks, cols, gain_sb, stage, stage_res, w_res, qi=[0]):
    CW = 512
    for kc in range(rows_chunks):
        for c0 in range(0, cols, CW):
            cw = min(CW, cols - c0)
            i = qi[0] % len(stage)
            qi[0] += 1
            st, sr = stage[i], stage_res[i]
            src = w_dram[kc * 128:(kc + 1) * 128, c0:c0 + cw]
            S.dma("sp", (lambda e, st=st, src=src, cw=cw: [e.dma_start(out=st[:, 0:cw], in_=src)]),
                  1, f"wst{i}", writes=[sr])
            dst = w_sb[:, kc, c0:c0 + cw]
            if gain_sb is not None:
                g = gain_sb[:, kc:kc + 1]
                S.op("pool", (lambda e, dst=dst, st=st, cw=cw, g=g:
                              e.tensor_scalar(dst, st[:, 0:cw], g, 0.0, ALU.mult, ALU.add)),
                     reads=[sr], writes=[w_res])
            else:
                S.op("pool", (lambda e, dst=dst, st=st, cw=cw: e.tensor_copy(dst, st[:, 0:cw])),
                     reads=[sr], writes=[w_res])


def ffn_phase(nc, tag, x_d, out_d, ntiles, gain_d, wg_d, wu_d, wd_d, ident_d):
    S = Sched(nc, tag)
    with ExitStack() as es:
        def sb(name, shape, dt):
            return es.enter_context(nc.sbuf_tensor(f"{tag}_{name}", shape, dt))

        def ps(name, shape, dt):
            return es.enter_context(nc.psum_tensor(f"{tag}_{name}", shape, dt))

        wg = sb("wg", [128, NKC, DFF], BF16)
        wu = sb("wu", [128, NKC, DFF], BF16)
        wd = sb("wd", [128, NFC, D], BF16)
        gain = sb("gain", [128, NKC], F32)
        ident = sb("ident", [128, 128], BF16)
        xt0 = sb("xt0", [128, 4, D], F32)
        xt1 = sb("xt1", [128, 4, D], F32)
        st0 = xt1[:, 0, 0:512]
        st1 = xt1[:, 1, 0:512]
        xn4 = sb("xn4", [128, 4, D], BF16)
        xnT = sb("xnT", [128, NKC, 512], BF16)
        actb = sb("act", [128, NFC, 512], BF16)
        sg = sb("sg", [128, 2, 512], BF16)
        stat = sb("stat", [128, 16], F32)
        junk = sb("junk", [128, D], BF16)
        pg0 = ps("pg0", [128, 512], F32)
        pg1 = ps("pg1", [128, 512], F32)
        pu0 = ps("pu0", [128, 512], F32)
        pu1 = ps("pu1", [128, 512], F32)
        po0 = ps("po0", [128, 512], F32)
        po1 = ps("po1", [128, 512], F32)
        pt0 = ps("pt0", [128, 1024], BF16)
        pt1 = ps("pt1", [128, 1024], BF16)
        R = S.res
        r_gain, r_ident, r_w = R("gain"), R("ident"), R("w")
        r_xt = [[R(f"xt{i}_{j}") for j in range(4)] for i in range(2)]
        r_st = [r_xt[1][0], r_xt[1][1]]
        S.dma("sp", lambda e: [e.dma_start(out=gain[:], in_=gain_d)], 1, "c0", writes=[r_gain])
        S.dma("sp", lambda e: [e.dma_start(out=st0[:, 0:128], in_=ident_d)], 1, "wst0", writes=[r_st[0]])
        S.op("pool", lambda e: e.tensor_copy(ident[:], st0[:, 0:128]), reads=[r_st[0]], writes=[r_ident])
        stat_dummy = None
        qi = [1]
        r_wg, r_wu, r_wd = R("wg"), R("wu"), R("wd")
        S.op("pool", lambda e: e.tensor_copy(stat[:, 8:9], gain[:, 0:1]), reads=[r_gain], writes=[R("dummy")])
        load_weight_bf16(S, nc, wg_d, wg, NKC, DFF, gain, [st0, st1], r_st, r_wg, qi)
        load_weight_bf16(S, nc, wu_d, wu, NKC, DFF, gain, [st0, st1], r_st, r_wu, qi)
        load_weight_bf16(S, nc, wd_d, wd, NFC, D, None, [st0, st1], r_st, r_wd, qi)

        xts = [xt0, xt1]
        r_xn4 = [R(f"xn{j}") for j in range(4)]
        r_xnT, r_act = [R(f"xnT{k}") for k in range(NKC)], [R(f"act{f}") for f in range(NFC)]
        r_sg = [R("sg0"), R("sg1")]
        r_stat = [R(f"stat{j}") for j in range(4)]
        r_junk = R("junk")
        pgs, pus, pos, pts = [pg0, pg1], [pu0, pu1], [po0, po1], [pt0, pt1]
        r_pg, r_pu = [R("pg0"), R("pg1")], [R("pu0"), R("pu1")]
        r_po, r_pt = [R("po0"), R("po1")], [R("pt0"), R("pt1")]

        def load_tile(i):
            b = i % 2
            for j in range(4):
                src = x_d[i * 512 + j * 128: i * 512 + (j + 1) * 128, :]
                dst = xts[b][:, j, :]
                S.dma("sp", (lambda e, dst=dst, src=src: [e.dma_start(out=dst, in_=src)]), 1,
                      f"xt{b}_{j}", writes=[r_xt[b][j]])

        load_tile(0)
        nt_ctr = [0]
        for i in range(ntiles):
            b = i % 2
            xt = xts[b]
            if i + 1 < ntiles:
                load_tile(i + 1)
            norm_transpose4(S, xt, r_xt[b], stat, r_stat, junk, xn4, r_xn4, pts, r_pt, ident, r_ident, xnT, r_xnT)
            for f in range(NFC):
                pb = f % 2
                for kc in range(NKC):
                    S.op("pe", (lambda e, pb=pb, f=f, kc=kc: e.matmul(
                        pgs[pb][:], wg[:, kc, f * 128:(f + 1) * 128], xnT[:, kc, :],
                        start=(kc == 0), stop=(kc == NKC - 1))),
                        reads=[r_wg, r_xnT[kc]], writes=[r_pg[pb]])
                for kc in range(NKC):
                    S.op("pe", (lambda e, pb=pb, f=f, kc=kc: e.matmul(
                        pus[pb][:], wu[:, kc, f * 128:(f + 1) * 128], xnT[:, kc, :],
                        start=(kc == 0), stop=(kc == NKC - 1))),
                        reads=[r_wu, r_xnT[kc]], writes=[r_pu[pb]])
                S.op("act", (lambda e, pb=pb: e.activation(sg[:, pb, :], pgs[pb][:], AF.Silu)),
                     reads=[r_pg[pb]], writes=[r_sg[pb]])
                S.op("dve", (lambda e, pb=pb, f=f: e.tensor_tensor(actb[:, f, :], sg[:, pb, :], pus[pb][:], ALU.mult)),
                     reads=[r_sg[pb], r_pu[pb]], writes=[r_act[f]])
            for j in range(4):
                for hh in range(2):
                    pb = (j * 2 + hh) % 2
                    for f in range(NFC):
                        S.op("pe", (lambda e, pb=pb, f=f, j=j, hh=hh: e.matmul(
                            pos[pb][:], actb[:, f, j * 128:(j + 1) * 128], wd[:, f, hh * 512:(hh + 1) * 512],
                            start=(f == 0), stop=(f == NFC - 1))),
                            reads=[r_wd, r_act[f]], writes=[r_po[pb]])
                    dst = xt[:, j, hh * 512:(hh + 1) * 512]
                    S.op("dve", (lambda e, dst=dst, pb=pb: e.scalar_tensor_tensor(
                        dst, pos[pb][:], 0.5, dst, ALU.mult, ALU.add)),
                        reads=[r_po[pb], r_xt[b][j]], writes=[r_xt[b][j]])
                dstd = out_d[i * 512 + j * 128: i * 512 + (j + 1) * 128, :]
                src = xt[:, j, :]
                S.dma("pool", (lambda e, dstd=dstd, src=src: [e.dma_start(out=dstd, in_=src)]), 1,
                      f"xo{b}_{j}", reads=[r_xt[b][j]])
        S.barrier()
        S.emit()
    return S


class Ctx:
    def __init__(self, nc, tag, es):
        self.nc, self.tag, self.es = nc, tag, es

    def sb(self, name, shape, dt):
        return self.es.enter_context(self.nc.sbuf_tensor(f"{self.tag}_{name}", shape, dt))

    def ps(self, name, shape, dt=F32):
        return self.es.enter_context(self.nc.psum_tensor(f"{self.tag}_{name}", shape, dt))


def load_w(S, w_dram, w_sb, nrc, cols, gain_sb, st, r_st, r_w, qi, rows_last=128):
    for rc in range(nrc):
        for c0 in range(0, cols, 512):
            cw = min(512, cols - c0)
            i = qi[0] % 2
            qi[0] += 1
            stt, sr = st[i], r_st[i]
            src = w_dram[rc * 128:(rc + 1) * 128, c0:c0 + cw]
            S.dma("sp", (lambda e, stt=stt, src=src, cw=cw: [e.dma_start(out=stt[:, 0:cw], in_=src)]),
                  1, f"wst{i}", writes=[sr])
            dst = w_sb[:, rc, c0:c0 + cw]
            if gain_sb is not None:
                g = gain_sb[:, rc:rc + 1]
                S.op("pool", (lambda e, dst=dst, stt=stt, cw=cw, g=g:
                              e.tensor_scalar(dst, stt[:, 0:cw], g, 0.0, ALU.mult, ALU.add)),
                     reads=[sr], writes=[r_w])
            else:
                S.op("pool", (lambda e, dst=dst, stt=stt, cw=cw: e.tensor_copy(dst, stt[:, 0:cw])),
                     reads=[sr], writes=[r_w])


def norm_transpose(S, xt, r_xt_j, j, stat, r_stat, junk, r_junk, xn, r_xn, pt, r_pt, ident, r_ident,
                   xnT, r_xnT, nfeat=D):
    nkc = nfeat // 128
    xj = xt[:, j, :]
    ss = stat[:, j:j + 1]
    rs = stat[:, 4 + j:5 + j]
    S.op("act", (lambda e: e.activation(junk[:, 0:nfeat], xj, AF.Square, accum_out=ss)),
         reads=[r_xt_j], writes=[r_junk, r_stat[j]])
    S.op("act", (lambda e: e.activation(rs, ss, AF.Sqrt, bias=EPS, scale=1.0 / nfeat)),
         reads=[r_stat[j]], writes=[r_stat[j]])
    S.op("dve", (lambda e: e.reciprocal(rs, rs)), reads=[r_stat[j]], writes=[r_stat[j]])
    S.op("dve", (lambda e: e.tensor_scalar(xn[:, 0:nfeat], xj, rs, None, ALU.mult)),
         reads=[r_xt_j, r_stat[j]], writes=[r_xn])
    for kc in range(nkc):
        S.op("pe", (lambda e, kc=kc: e.transpose(pt[:, kc * 128:(kc + 1) * 128],
                                                 xn[:, kc * 128:(kc + 1) * 128], ident[:])),
             reads=[r_xn, r_ident], writes=[r_pt])
    dst = xnT[:, 0:nkc, j * 128:(j + 1) * 128]
    src = pt[:, 0:nkc * 128].rearrange("p (k t) -> p k t", k=nkc)
    S.op("act", (lambda e: e.copy(dst, src)), reads=[r_pt], writes=r_xnT)


def norm_transpose4(S, xt, r_xt, stat, r_stat, junk, xn4, r_xn, pts, r_pts, ident, r_ident, xnT, r_xnT, nfeat=D):
    nkc = nfeat // 128
    r_j = [Res("junk") for _ in range(4)]
    for j in range(4):
        S.op("act", (lambda e, j=j: e.activation(junk[:, 0:nfeat], xt[:, j, :], AF.Square, accum_out=stat[:, j:j + 1])),
             reads=[r_xt[j]], writes=[r_j[j], r_stat[j]])
    for j in range(4):
        S.op("act", (lambda e, j=j: e.activation(stat[:, 4 + j:5 + j], stat[:, j:j + 1], AF.Sqrt, bias=EPS, scale=1.0 / nfeat)),
             reads=[r_stat[j]], writes=[r_stat[j]])
    for j in range(4):
        S.op("dve", (lambda e, j=j: e.reciprocal(stat[:, 4 + j:5 + j], stat[:, 4 + j:5 + j])), reads=[r_stat[j]], writes=[r_stat[j]])
    for j in range(4):
        S.op("dve" if j % 2 == 0 else "pool",
             (lambda e, j=j: e.tensor_scalar(xn4[:, j, 0:nfeat], xt[:, j, :], stat[:, 4 + j:5 + j], 0.0, ALU.mult, ALU.add)),
             reads=[r_xt[j], r_stat[j]], writes=[r_xn[j]])
    for j in range(4):
        pt, r_pt = pts[j % 2], r_pts[j % 2]
        for kc in range(nkc):
            S.op("pe", (lambda e, kc=kc, j=j, pt=pt: e.transpose(pt[:, kc * 128:(kc + 1) * 128],
                                                               xn4[:, j, kc * 128:(kc + 1) * 128], ident[:])),
                 reads=[r_xn[j], r_ident], writes=[r_pt])
        dst = xnT[:, 0:nkc, j * 128:(j + 1) * 128]
        src = pt[:, 0:nkc * 128].rearrange("p (k t) -> p k t", k=nkc)
        S.op("act", (lambda e, dst=dst, src=src: e.copy(dst, src)), reads=[r_pt], writes=r_xnT)


C_CQ, C_CKV, C_KR, C_HQ, C_HI, C_HFF, C_HFB, C_HG, C_KRS = 0, 384, 640, 672, 1184, 1696, 2208, 2720, 3232
WIN_COLS = 3264
UQ_COLS = 768 + 256


def mixer_in_phase(nc, tag, ntok, h1_d, cst, w, scr):
    S = Sched(nc, tag)
    ntiles = ntok // 512
    with ExitStack() as es:
        C = Ctx(nc, tag, es)
        R = S.res
        win = C.sb("win", [128, NKC, WIN_COLS], BF16)
        wuq = C.sb("wuq", [128, 3, UQ_COLS], BF16)
        wuk = C.sb("wuk", [128, 2, 512], BF16)
        wuv = C.sb("wuv", [128, 2, 512], BF16)
        gains = C.sb("gains", [128, 16], F32)
        lbt = C.sb("lbt", [128, 16], F32)
        lb = C.sb("lb", [128, 8], F32)
        oml = C.sb("oml", [128, 8], F32)
        ident = C.sb("ident", [128, 128], BF16)
        ones = C.sb("ones", [128, 128], BF16)
        rmask = C.sb("rmask", [128, 512], F32)
        mF = C.sb("mF", [128, 128], F32)
        mB = C.sb("mB", [128, 128], F32)
        ht = C.sb("ht", [128, 4, D], F32)
        xn = C.sb("xn", [128, D], BF16)
        junk = C.sb("junk", [128, D], BF16)
        stat = C.sb("stat", [128, 16], F32)
        xnT = C.sb("xnT", [128, NKC, 512], BF16)
        cqT = C.sb("cqT", [128, 3, 512], BF16)
        ckvT = C.sb("ckvT", [128, 2, 512], BF16)
        sqq = C.sb("sqq", [128, 2, 512], BF16)
        sqkv = C.sb("sqkv", [128, 2, 512], BF16)
        rsq = C.sb("rsq", [128, 512], F32)
        rskv = C.sb("rskv", [128, 512], F32)
        rstok = C.sb("rstok", [128, 8], F32)
        tct = C.sb("tct", [128, 512], F32)
        tst = C.sb("tst", [128, 512], F32)
        t1a = C.sb("t1a", [128, 512], F32)
        t2a = C.sb("t2a", [128, 512], F32)
        t1 = [t1a, t1a]
        t2 = [t2a, t2a]
        qout = C.sb("qout", [128, 8, 512], BF16)
        knT = C.sb("knT", [128, 4, 512], BF16)
        krp = C.sb("krp", [128, 512], BF16)
        vt = C.sb("vt", [128, 4, 512], BF16)
        qh = C.sb("qh", [128, 4, 512], F32)
        hA = [C.sb(f"hA{i}", [128, 512], F32) for i in range(2)]
        hB = [C.sb(f"hB{i}", [128, 512], F32) for i in range(2)]
        hC = [C.sb(f"hC{i}", [128, 512], F32) for i in range(2)]
        hE1 = [C.sb(f"hE1{i}", [128, 512], F32) for i in range(2)]
        hE2 = [C.sb(f"hE2{i}", [128, 512], F32) for i in range(2)]
        st = [hA[0], hB[0]]
        qpo = C.sb("qpo", [128, 8, 512], BF16)
        kpo = C.sb("kpo", [128, 8, 512], BF16)
        kppo = C.sb("kppo", [128, 8, 512], BF16)
        dco = C.sb("dco", [128, 8, 16], F32)
        vht = C.sb("vht", [128, 4, 512], BF16)
        ght = C.sb("ght", [128, 4, 512], BF16)
        ato = C.sb("ato", [128, 4, 8, 128], BF16)
        kto = C.sb("kto", [128, 4, 8, 128], BF16)
        pt = C.ps("pt", [128, 1024], BF16)
        pm = [C.ps(f"pm{i}", [128, 512]) for i in range(4)]
        pn = C.ps("pn", [128, 512])
        pa = C.ps("pa", [128, 512])
        ptk = C.ps("ptk", [128, 1024], BF16)

        r_c = R("consts")
        r_ident, r_ones = R("ident"), R("ones")

        def cdma(dst, src):
            S.dma("sp", (lambda e: [e.dma_start(out=dst, in_=src)]), 1, "c", writes=[r_c])
        cdma(gains[:, 0:8], cst["mix_norm"])
        cdma(gains[:, 8:11], cst["q_norm"])
        cdma(gains[:, 11:13], cst["kv_norm"])
        cdma(lbt[:], cst["hg_lb"])
        cdma(rmask[:], cst["rmask"])
        cdma(mF[:], cst["maskF"])
        cdma(mB[:], cst["maskB"])
        cdma(ht[:, 0, 0:128], cst["ident"])
        S.op("pool", lambda e: e.tensor_copy(ident[:], ht[:, 0, 0:128]), reads=[r_c], writes=[r_ident])
        S.op("pool", lambda e: e.memset(ones[:], 1.0), writes=[r_ones])
        lv = lbt[:].rearrange("p (d l h) -> p d l h", d=2, l=2)
        lb3 = lb[:].rearrange("p (d h) -> p d h", d=2)
        oml3 = oml[:].rearrange("p (d h) -> p d h", d=2)
        r_lb = R("lb")
        S.op("dve", lambda e: e.tensor_tensor(lb3, lv[:, :, 0, :], lv[:, :, 1, :], ALU.subtract), reads=[r_c], writes=[r_lb])
        S.op("act", lambda e: e.activation(oml[:], lb[:], AF.Sigmoid, scale=-1.0), reads=[r_lb], writes=[R("oml")])
        S.op("act", lambda e: e.activation(lb[:], lb[:], AF.Sigmoid), reads=[r_lb], writes=[r_lb])
        r_w = R("w")
        r_hA, r_hB = [R("hA0"), R("hA1")], [R("hB0"), R("hB1")]
        r_st = [r_hA[0], r_hB[0]]
        S.op("pool", lambda e: e.tensor_copy(stat[:, 15:16], gains[:, 0:1]), reads=[r_c], writes=[R("d")])
        qi = [0]
        load_w(S, w["w_in"], win, NKC, WIN_COLS, gains[:, 0:8], st, r_st, r_w, qi)
        load_w(S, w["w_uq"], wuq, 3, UQ_COLS, gains[:, 8:11], st, r_st, r_w, qi)
        load_w(S, w["w_uk"], wuk, 2, 512, gains[:, 11:13], st, r_st, r_w, qi)
        load_w(S, w["w_uv"], wuv, 2, 512, gains[:, 11:13], st, r_st, r_w, qi)

        r_ht = [R(f"ht{j}") for j in range(4)]
        r_stat = [R(f"stat{j}") for j in range(4)]
        r_junk, r_xn, r_pt = R("junk"), R("xn"), R("pt")
        r_xnT = [R(f"xnT{k}") for k in range(NKC)]
        r_pm = [R(f"pm{i}") for i in range(4)]
        r_pn, r_pa, r_ptk = R("pn"), R("pa"), R("ptk")
        r_cqT, r_ckvT, r_sqq, r_sqkv = R("cqT"), R("ckvT"), [R("sqq0"), R("sqq1")], R("sqkv")
        r_rsq, r_rskv, r_rstok = R("rsq"), R("rskv"), R("rstok")
        r_tab = R("tab")
        r_t1a, r_t2a = R("t1a"), R("t2a")
        r_t1, r_t2 = [r_t1a, r_t1a], [r_t2a, r_t2a]
        r_qout, r_knT, r_krp, r_vt, r_qh = R("qout"), R("knT"), R("krp"), R("vt"), R("qh")
        r_hC = [R("hC0"), R("hC1")]
        r_hE1, r_hE2 = [R("hE10"), R("hE11")], [R("hE20"), R("hE21")]
        r_qpo, r_kpo, r_kppo, r_dco = R("qpo"), R("kpo"), R("kppo"), R("dco")
        r_vht, r_ght, r_ato, r_kto = R("vht"), R("ght"), R("ato"), R("kto")
        S.op("pool", lambda e: e.memset(tct[:], 1.0), writes=[r_tab])
        S.op("pool", lambda e: e.memset(tst[:], 0.0), writes=[r_tab])
        pmi = [0]

        def nextpm():
            i = pmi[0] % 4
            pmi[0] += 1
            return pm[i], r_pm[i]

        def mm_fm(ps, r_ps, wsb, c0, m, xT, r_x, nkc, out_p0=0):
            for kc in range(nkc):
                S.op("pe", (lambda e, kc=kc: e.matmul(ps[out_p0:out_p0 + m, :], wsb[:, kc, c0:c0 + m], xT[:, kc, :],
                                                      start=(kc == 0), stop=(kc == nkc - 1))),
                     reads=[r_w] + r_x, writes=[r_ps])

        def mm_tm(ps, r_ps, xT, r_x, j, wsb, c0, n, nkc):
            for kc in range(nkc):
                S.op("pe", (lambda e, kc=kc: e.matmul(ps[:, 0:n], xT[:, kc, j * 128:(j + 1) * 128], wsb[:, kc, c0:c0 + n],
                                                      start=(kc == 0), stop=(kc == nkc - 1))),
                     reads=[r_w] + r_x, writes=[r_ps])

        for i in range(ntiles):
            t0 = i * 512
            S.dma("sp", (lambda e, t0=t0: [e.dma_start(out=ht[:, j, :], in_=h1_d[t0 + j * 128:t0 + (j + 1) * 128, :])
                                          for j in range(4)]), 4, "ht", writes=r_ht)
            S.dma("sp", (lambda e, t0=t0: [e.dma_start(out=tct[64:96, :], in_=cst["rope_c"][:, t0:t0 + 512]),
                                          e.dma_start(out=tst[64:96, :], in_=cst["rope_s"][:, t0:t0 + 512])]),
                  2, "tab", writes=[r_tab])
            for j in range(4):
                norm_transpose(S, ht, r_ht[j], j, stat, r_stat, junk, r_junk, xn, r_xn, pt, r_pt, ident, r_ident,
                               xnT, r_xnT)
            for c in range(3):
                ps, rp = nextpm()
                mm_fm(ps, rp, win, C_CQ + c * 128, 128, xnT, r_xnT, NKC)
                S.op("act", (lambda e, ps=ps, c=c: e.copy(cqT[:, c, :], ps[:])), reads=[rp], writes=[r_cqT])
                S.op("act", (lambda e, ps=ps, c=c: e.activation(sqq[:, c % 2, :], ps[:], AF.Square)),
                     reads=[rp], writes=[r_sqq[c % 2]])
                S.op("pe", (lambda e, c=c: e.matmul(pn[:], ones[:], sqq[:, c % 2, :], start=(c == 0), stop=(c == 2))),
                     reads=[r_ones, r_sqq[c % 2]], writes=[r_pn])
            S.op("act", lambda e: e.activation(rsq[:], pn[:], AF.Sqrt, bias=EPS, scale=1.0 / 384), reads=[r_pn], writes=[r_rsq])
            S.op("dve", lambda e: e.reciprocal(rsq[:], rsq[:]), reads=[r_rsq], writes=[r_rsq])
            for c in range(2):
                ps, rp = nextpm()
                mm_fm(ps, rp, win, C_CKV + c * 128, 128, xnT, r_xnT, NKC)
                S.op("act", (lambda e, ps=ps, c=c: e.copy(ckvT[:, c, :], ps[:])), reads=[rp], writes=[r_ckvT])
                S.op("act", (lambda e, ps=ps, c=c: e.activation(sqkv[:, c, :], ps[:], AF.Square)),
                     reads=[rp], writes=[r_sqkv])
            for c in range(2):
                S.op("pe", (lambda e, c=c: e.matmul(pn[:], ones[:], sqkv[:, c, :], start=(c == 0), stop=(c == 1))),
                     reads=[r_ones, r_sqkv], writes=[r_pn])
            S.op("act", lambda e: e.activation(rskv[:], pn[:], AF.Sqrt, bias=EPS, scale=1.0 / 256), reads=[r_pn], writes=[r_rskv])
            S.op("dve", lambda e: e.reciprocal(rskv[:], rskv[:]), reads=[r_rskv], writes=[r_rskv])
            for j in range(4):
                for c in range(2):
                    S.op("pe", (lambda e, j=j, c=c: e.matmul(pa[:, j:j + 1], sqkv[:, c, j * 128:(j + 1) * 128], ones[:, 0:1],
                                                             start=(c == 0), stop=(c == 1))),
                         reads=[r_ones, r_sqkv], writes=[r_pa])
            S.op("act", lambda e: e.activation(rstok[:, 0:4], pa[:, 0:4], AF.Sqrt, bias=EPS, scale=1.0 / 256),
                 reads=[r_pa], writes=[r_rstok])
            S.op("dve", lambda e: e.reciprocal(rstok[:, 0:4], rstok[:, 0:4]), reads=[r_rstok], writes=[r_rstok])
            ps, rp = nextpm()
            mm_fm(ps, rp, win, C_KR, 32, xnT, r_xnT, NKC, out_p0=64)
            ps2, rp2 = nextpm()
            mm_fm(ps2, rp2, win, C_KRS, 32, xnT, r_xnT, NKC, out_p0=64)
            S.op("dve", (lambda e, ps=ps: e.tensor_tensor(t1[0][64:96, :], ps[64:96, :], tct[64:96, :], ALU.mult)),
                 reads=[rp, r_tab], writes=[r_t1[0]])
            S.op("dve", (lambda e, ps2=ps2: e.tensor_tensor(t2[0][64:96, :], ps2[64:96, :], tst[64:96, :], ALU.mult)),
                 reads=[rp2, r_tab], writes=[r_t2[0]])
            S.op("pool", lambda e: e.tensor_tensor(krp[64:96, :], t1[0][64:96, :], t2[0][64:96, :], ALU.add),
                 reads=[r_t1[0], r_t2[0]], writes=[r_krp])
            S.dma("pool", (lambda e, t0=t0: [e.dma_start(out=scr["KTr"][:, t0:t0 + 512], in_=krp[64:96, :])]), 1, "s_krp",
                  reads=[r_krp])
            for h in range(8):
                b = h % 2
                ps, rp = nextpm()
                mm_fm(ps, rp, wuq, h * 96, 96, cqT, [r_cqT], 3)
                ps2, rp2 = nextpm()
                mm_fm(ps2, rp2, wuq, 768 + h * 32, 32, cqT, [r_cqT], 3, out_p0=64)
                S.op("dve", (lambda e, ps=ps, b=b: e.tensor_tensor(t1[b][0:96, :], ps[0:96, :], tct[0:96, :], ALU.mult)),
                     reads=[rp, r_tab], writes=[r_t1[b]])
                S.op("dve", (lambda e, ps2=ps2, b=b: e.tensor_tensor(t2[b][64:96, :], ps2[64:96, :], tst[64:96, :], ALU.mult)),
                     reads=[rp2, r_tab], writes=[r_t2[b]])
                S.op("pool", (lambda e, b=b: e.tensor_tensor(t1[b][64:96, :], t1[b][64:96, :], t2[b][64:96, :], ALU.add)),
                     reads=[r_t2[b]], writes=[r_t1[b]])
                S.op("pool", (lambda e, b=b, h=h: e.tensor_tensor(qout[0:96, h, :], t1[b][0:96, :], rsq[0:96, :], ALU.mult)),
                     reads=[r_t1[b], r_rsq], writes=[r_qout])
            S.dma("pool", (lambda e, t0=t0: [e.dma_start(out=scr["QT"][h, :, t0:t0 + 512], in_=qout[0:96, h, :])
                                            for h in range(8)]), 8, "s_q", reads=[r_qout])
            for a in range(4):
                ps, rp = nextpm()
                mm_fm(ps, rp, wuk, a * 128, 128, ckvT, [r_ckvT], 2)
                S.op("dve", (lambda e, ps=ps, a=a: e.tensor_tensor(knT[:, a, :], ps[:], rskv[:], ALU.mult)),
                     reads=[rp, r_rskv], writes=[r_knT])
            S.dma("pool", (lambda e, t0=t0: [e.dma_start(out=scr["KTn"][a * 128:(a + 1) * 128, t0:t0 + 512], in_=knT[:, a, :])
                                            for a in range(4)]), 4, "s_kn", reads=[r_knT])
            for j in range(4):
                ps, rp = nextpm()
                mm_tm(ps, rp, ckvT, [r_ckvT], j, wuv, 0, 512, 2)
                S.op("act", (lambda e, ps=ps, j=j: e.activation(vt[:, j, :], ps[:], AF.Copy, scale=rstok[:, j:j + 1])),
                     reads=[rp, r_rstok], writes=[r_vt])
            S.dma("pool", (lambda e, t0=t0: [e.dma_start(
                out=scr["VA"][:, t0 + j * 128:t0 + (j + 1) * 128, :].rearrange("h t d -> t h d"),
                in_=vt[:, j, :].rearrange("t (h d) -> t h d", h=8)) for j in range(4)]), 4, "s_v", reads=[r_vt])
            for h in range(4):
                ps, rp = nextpm()
                mm_fm(ps, rp, win, C_HQ + h * 128, 128, xnT, r_xnT, NKC)
                S.op("act", (lambda e, ps=ps, h=h: e.activation(qh[:, h, :], ps[:], AF.Silu)), reads=[rp], writes=[r_qh])
            for j in range(4):
                ps, rp = nextpm()
                mm_tm(ps, rp, xnT, r_xnT, j, win, C_HI, 512, NKC)
                S.op("act", (lambda e, ps=ps, j=j: e.copy(vht[:, j, :], ps[:])), reads=[rp], writes=[r_vht])
                ps, rp = nextpm()
                mm_tm(ps, rp, xnT, r_xnT, j, win, C_HG, 512, NKC)
                S.op("act", (lambda e, ps=ps, j=j: e.activation(ght[:, j, :], ps[:], AF.Silu)), reads=[rp], writes=[r_ght])
            S.dma("pool", (lambda e, t0=t0: [
                e.dma_start(out=scr["VH"][t0:t0 + 512, :].rearrange("(j p) c -> p j c", p=128), in_=vht[:]),
                e.dma_start(out=scr["GH"][t0:t0 + 512, :].rearrange("(j p) c -> p j c", p=128), in_=ght[:])]),
                2, "s_vg", reads=[r_vht, r_ght])
            for d in range(2):
                for h in range(4):
                    hd = d * 4 + h
                    b = hd % 2
                    A, B, Cc, E1, E2 = hA[b], hB[b], hC[b], hE1[b], hE2[b]
                    rA, rB, rC, rE1, rE2 = r_hA[b], r_hB[b], r_hC[b], r_hE1[b], r_hE2[b]
                    ps, rp = nextpm()
                    mm_fm(ps, rp, win, (C_HFF if d == 0 else C_HFB) + h * 128, 128, xnT, r_xnT, NKC)
                    lbs, omls = lb[:, hd:hd + 1], oml[:, hd:hd + 1]
                    S.op("act", (lambda e, ps=ps, A=A: e.activation(A[:], ps[:], AF.Sigmoid)), reads=[rp], writes=[rA])
                    S.op("act", (lambda e, ps=ps, B=B: e.activation(B[:], ps[:], AF.Sigmoid, scale=-1.0)), reads=[rp], writes=[rB])
                    S.op("pool", (lambda e, B=B, omls=omls: e.tensor_scalar(B[:], B[:], omls, 0.0, ALU.mult, ALU.add)),
                         reads=[r_lb], writes=[rB])
                    S.op("dve", (lambda e, A=A, omls=omls, lbs=lbs: e.tensor_scalar(A[:], A[:], omls, lbs, ALU.mult, ALU.add)),
                         reads=[r_lb], writes=[rA])
                    S.op("act", (lambda e, A=A: e.activation(A[:], A[:], AF.Ln)), writes=[rA])
                    S.op("dve", (lambda e, A=A, Cc=Cc: e.tensor_tensor_scan(Cc[:], rmask[:], A[:], 0.0, ALU.mult, ALU.add)),
                         reads=[rA, r_c], writes=[rC])
                    Cv = Cc[:].rearrange("p (c t) -> p c t", t=32)
                    Av = A[:].rearrange("p (c t) -> p c t", t=32)
                    if d == 0:
                        bsrc, rb = Cc, rC
                        dcol = 31
                    else:
                        S.op("pool", (lambda e, A=A, Cc=Cc: e.tensor_tensor(A[:], A[:], Cc[:], ALU.subtract)),
                             reads=[rC], writes=[rA])
                        S.op("pool", (lambda e, Av=Av, Cv=Cv: e.tensor_tensor(Av, Av, Cv[:, :, 31:32].broadcast_to([128, 16, 32]), ALU.add)),
                             reads=[rC], writes=[rA])
                        bsrc, rb = A, rA
                        dcol = 0
                    S.op("act", (lambda e, E1=E1, bsrc=bsrc: e.activation(E1[:], bsrc[:], AF.Exp)), reads=[rb], writes=[rE1])
                    S.op("act", (lambda e, E2=E2, bsrc=bsrc: e.activation(E2[:], bsrc[:], AF.Exp, scale=-1.0)), reads=[rb], writes=[rE2])
                    S.op("pool", (lambda e, E1=E1, h=h, hd=hd: e.tensor_tensor(qpo[:, hd, :], qh[:, h, :], E1[:], ALU.mult)),
                         reads=[rE1, r_qh], writes=[r_qpo])
                    S.op("dve", (lambda e, E2=E2, B=B: e.tensor_tensor(E2[:], E2[:], B[:], ALU.mult)), reads=[rB], writes=[rE2])
                    S.op("act", (lambda e, E2=E2, hd=hd: e.copy(kpo[:, hd, :], E2[:])), reads=[rE2], writes=[r_kpo])
                    E1v = E1[:].rearrange("p (c t) -> p c t", t=32)
                    E2v = E2[:].rearrange("p (c t) -> p c t", t=32)
                    S.op("dve", (lambda e, E1v=E1v, hd=hd, dcol=dcol: e.tensor_copy(dco[:, hd, :], E1v[:, :, dcol])),
                         reads=[rE1], writes=[r_dco])
                    kv = kppo[:, hd, :].rearrange("p (c t) -> p c t", t=32)
                    S.op("pool", (lambda e, E1v=E1v, E2v=E2v, kv=kv, dcol=dcol: e.tensor_tensor(
                        kv, E2v, E1v[:, :, dcol:dcol + 1].broadcast_to([128, 16, 32]), ALU.mult)),
                        reads=[rE1, rE2], writes=[r_kppo])
            nch = ntok // 32
            S.dma("pool", (lambda e, t0=t0, i=i: [
                e.dma_start(out=scr["QP"][:, :, t0:t0 + 512].rearrange("h p t -> p h t"), in_=qpo[:]),
                e.dma_start(out=scr["DC"][:, :, i * 16:(i + 1) * 16].rearrange("h p c -> p h c"), in_=dco[:])]),
                2, "s_qp", reads=[r_qpo, r_dco])
            for j in range(4):
                for d in range(2):
                    for h in range(4):
                        hd = d * 4 + h
                        S.op("pe", (lambda e, j=j, hd=hd, h=h: e.matmul(
                            pa[:, h * 128:(h + 1) * 128], kpo[:, hd, j * 128:(j + 1) * 128], qpo[:, hd, j * 128:(j + 1) * 128],
                            start=True, stop=True)), reads=[r_kpo, r_qpo], writes=[r_pa])
                    msk = (mF if d == 0 else mB)
                    S.op("dve", (lambda e, j=j, d=d, msk=msk: e.tensor_tensor(
                        ato[:, j, d * 4:(d + 1) * 4, :], pa[:].rearrange("p (h t) -> p h t", h=4),
                        msk[:].rearrange("p (o t) -> p o t", o=1).broadcast_to([128, 4, 128]), ALU.mult)),
                        reads=[r_pa, r_c], writes=[r_ato])
                for hd in range(8):
                    S.op("pe", (lambda e, j=j, hd=hd: e.transpose(ptk[:, hd * 128:(hd + 1) * 128],
                                                                   kppo[:, hd, j * 128:(j + 1) * 128], ident[:])),
                         reads=[r_kppo, r_ident], writes=[r_ptk])
                S.op("act", (lambda e, j=j: e.copy(kto[:, j, :, :], ptk[:].rearrange("p (h k) -> p h k", h=8))),
                     reads=[r_ptk], writes=[r_kto])
            S.dma("pool", (lambda e, t0=t0: [
                e.dma_start(out=scr["AT"][t0:t0 + 512, :, :].rearrange("(j p) h t -> p j h t", p=128), in_=ato[:]),
                e.dma_start(out=scr["KPT"][t0:t0 + 512, :, :].rearrange("(j p) h k -> p j h k", p=128), in_=kto[:])]),
                2, "s_at", reads=[r_ato, r_kto])
        S.barrier()
        S.emit()
    return S


def attn_phase(nc, tag, seqs, scr, xchg=None):
    S = Sched(nc, tag)
    SKMAX = max(sum(p[3] for p in sq["kp"]) for sq in seqs)
    SL = max(sq["nq"] for sq in seqs)
    scale = 96.0 ** -0.5
    with ExitStack() as es:
        C = Ctx(nc, tag, es)
        R = S.res
        kt = [C.sb(f"kt{i}", [128, SKMAX], BF16) for i in range(2)]
        vt = [C.sb(f"vt{i}", [128, SKMAX // 128, 65], BF16) for i in range(2)]
        qt = [C.sb(f"qt{i}", [128, SL], BF16) for i in range(2)]
        pT = [C.sb(f"pT{i}", [128, 1024], BF16) for i in range(3)]
        onesf = C.sb("onesf", [128, 64], F32)
        rl = C.sb("rl", [128, 512], F32)
        osb = C.sb("osb", [128, 512], F32)
        obf = [C.sb(f"obf{i}", [128, 512], BF16) for i in range(2)]
        psS = [C.ps(f"psS{i}", [128, 1024]) for i in range(2)]
        psO = [C.ps(f"psO{i}", [128, 512]) for i in range(2)]
        psB = C.ps("psB", [128, 512])
        r_kt, r_vt, r_qt = [R(), R()], [R(), R()], [R(), R()]
        r_pT, r_psS, r_psO = [R(), R(), R()], [R(), R(), R()], [R(), R()]
        r_ones, r_rl, r_osb, r_obf, r_psB = R(), R(), R(), [R(), R()], R()
        S.op("pool", lambda e: e.memset(onesf[:], 1.0), writes=[r_ones])
        for i in range(2):
            S.op("pool", (lambda e, i=i: e.memset(vt[i][:, :, 64:65], 1.0)), writes=[r_vt[i]])
        r_gath = R()
        r_g2, r_g3 = R(), R()
        r_gs = []
        if xchg is not None:
            n0 = xchg["n0"]
            r_xin = R()
            S.dma("sp", (lambda e: [e.dma_start(out=xchg["XK_in"][a_], in_=scr["KTn"][a_ * 128:(a_ + 1) * 128, 0:n0]) for a_ in range(4)]
                         + [e.dma_start(out=xchg["XR_in"][0:32, :], in_=scr["KTr"][:, 0:n0])]
                         + [e.dma_start(out=xchg["XV_in"][a_].rearrange("(hh t) d -> hh t d", hh=2), in_=scr["VA"][2 * a_:2 * a_ + 2, 0:n0, :])
                            for a_ in range(4)]), 9, "xin", writes=[r_xin])
            r_gs = [r_gath, r_g2, r_g3] + [R() for _ in range(6)]
            ccl = [(xchg["XK_in"][a_], xchg["XK_out"][a_]) for a_ in range(4)] + [(xchg["XR_in"], xchg["XR_out"])] + \
                  [(xchg["XV_in"][a_], xchg["XV_out"][a_]) for a_ in range(4)]
            for ci_, (cin, cout) in enumerate(ccl):
                S.dma("pool", (lambda e, cin=cin, cout=cout: [e.collective_compute(
                    "AllGather", ALU.bypass, replica_groups=xchg["groups"], ins=[cin.opt()], outs=[cout.opt()])]), 1, f"cc{ci_}",
                    reads=[r_xin], writes=[r_gs[ci_]], inc=1)
        heads = []
        for sq in seqs:
            for h in range(8):
                heads.append((sq, h))
        units = []
        obc = [0]
        for hi, (sq, h) in enumerate(heads):
            SK = sum(p[3] for p in sq["kp"])
            for qb in range(sq["nq"] // 512):
                for kc in range(SK // 256):
                    units.append((hi, qb, kc, SK // 256, obc[0] % 2, qb == sq["nq"] // 512 - 1))
                obc[0] += 1
        loaded = [-1]
        pending = []

        def load_head(hi):
            if hi >= len(heads) or hi <= loaded[0]:
                return
            loaded[0] = hi
            sq, h = heads[hi]
            b = hi % 2
            off = 0
            lst = []
            for (ktn, ktr, va, n) in sq["kp"]:
                lst.append((kt[b][0:64, off:off + n], ktn(h)))
                lst.append((kt[b][64:96, off:off + n], ktr))
                off += n
            dep = r_gs if sq.get("gathered") else []
            S.dma("sp", (lambda e, lst=lst: [e.dma_start(out=o, in_=i_) for o, i_ in lst]), len(lst), f"kt{b}", reads=dep, writes=[r_kt[b]])
            off = 0
            lst2 = []
            for (ktn, ktr, va, n) in sq["kp"]:
                lst2.append((vt[b][:, off // 128:(off + n) // 128, 0:64], va(h).rearrange("(c p) d -> p c d", p=128)))
                off += n
            S.dma("sp", (lambda e, lst2=lst2: [e.dma_start(out=o, in_=i_) for o, i_ in lst2]), len(lst2), f"vt{b}", reads=dep, writes=[r_vt[b]])
            q0, nq = sq["q0"], sq["nq"]
            S.dma("sp", (lambda e, b=b, h=h, q0=q0, nq=nq: [e.dma_start(out=qt[b][0:96, 0:nq], in_=scr["QT"][h, :, q0:q0 + nq])]), 1,
                  f"qt{b}", writes=[r_qt[b]])

        def qk(u):
            hi, qb, kc, nkc, ob, lastq = units[u]
            b = hi % 2
            r = u % 2
            r3 = u % 3
            for t in range(2):
                S.op("pe", (lambda e, t=t: e.matmul(psS[r][:, t * 512:(t + 1) * 512], kt[b][0:96, (2 * kc + t) * 128:(2 * kc + t + 1) * 128],
                                                    qt[b][0:96, qb * 512:(qb + 1) * 512], start=True, stop=True)),
                     reads=[r_kt[b], r_qt[b]], writes=[r_psS[r]])
            S.op("act", (lambda e: e.activation(pT[r3][:], psS[r][:], AF.Exp, scale=scale)), reads=[r_psS[r]], writes=[r_pT[r3]])

        def pv(u):
            hi, qb, kc, nkc, ob, lastq = units[u]
            b = hi % 2
            r3 = u % 3
            for t in range(2):
                S.op("pe", (lambda e, t=t: e.matmul(psO[ob][0:65, :], vt[b][:, 2 * kc + t, 0:65], pT[r3][:, t * 512:(t + 1) * 512],
                                                    start=(kc == 0 and t == 0), stop=(kc == nkc - 1 and t == 1))),
                     reads=[r_vt[b], r_pT[r3]], writes=[r_psO[ob]])
            if kc == nkc - 1:
                sq, h = heads[hi]
                S.op("dve", (lambda e: e.reciprocal(rl[64:65, :], psO[ob][64:65, :])), reads=[r_psO[ob]], writes=[r_rl])
                S.op("dve", (lambda e: e.tensor_copy(osb[0:64, :], psO[ob][0:64, :])), reads=[r_psO[ob]], writes=[r_osb])
                t0 = sq["q0"] + qb * 512

                def fin():
                    S.op("pe", (lambda e: e.matmul(psB[0:64, :], onesf[64:65, 0:64], rl[64:65, :], start=True, stop=True)),
                         reads=[r_ones, r_rl], writes=[r_psB])
                    S.op("dve", (lambda e: e.tensor_tensor(obf[ob][0:64, :], osb[0:64, :], psB[0:64, :], ALU.mult)),
                         reads=[r_osb, r_psB], writes=[r_obf[ob]])
                    S.dma("pool", (lambda e: [e.dma_start(out=scr["MIXT"][h * 64:(h + 1) * 64, t0:t0 + 512], in_=obf[ob][0:64, :])]), 1,
                          f"so{ob}", reads=[r_obf[ob]])
                pending.append([4, fin])

        load_head(0)
        load_head(1)
        n = len(units)
        qk(0)
        for u in range(n):
            if u + 1 < n:
                qk(u + 1)
            for pnd in list(pending):
                pnd[0] -= 1
                if pnd[0] <= 0:
                    pnd[1]()
                    pending.remove(pnd)
            pv(u)
            hi, qb, kc, nkc, ob_, lastq = units[u]
            if lastq and kc == nkc - 1:
                load_head(hi + 2)
        for pnd in pending:
            pnd[1]()
        S.barrier()
        S.emit()
    return S


def scan_phase(nc, tag, seq_lens, ntok, scr, xchg=None, cst=None):
    S = Sched(nc, tag)
    nch = ntok // 32
    with ExitStack() as es:
        C = Ctx(nc, tag, es)
        R = S.res
        NR = 3
        qpb = [[C.sb(f"qpb{d}{i}", [128, 4, 128], BF16) for i in range(NR)] for d in range(2)]
        atb = [[C.sb(f"atb{d}{i}", [128, 4, 128], BF16) for i in range(NR)] for d in range(2)]
        kpb = [[C.sb(f"kpb{d}{i}", [128, 4, 128], BF16) for i in range(NR)] for d in range(2)]
        vb = [[C.sb(f"vb{d}{i}", [128, 512], BF16) for i in range(NR)] for d in range(2)]
        r_ld = [[R() for i in range(NR)] for d in range(2)]
        dct = C.sb("dct", [128, 8, nch], F32)
        zer = C.sb("zer", [128, 512], BF16)
        SstAll = C.sb("SstAll", [128, 8, 129], F32)
        Sst = [SstAll[:, hd, 0:128] for hd in range(8)]
        Sbf = [C.sb(f"Sbf{hd}", [128, 128], BF16) for hd in range(8)]
        r_S = [R() for hd in range(8)]
        r_Sb = [R() for hd in range(8)]
        ot = [[C.sb(f"ot{d}{i}", [128, 512], F32) for i in range(2)] for d in range(2)]
        r_ot = [[R() for i in range(2)] for d in range(2)]
        psO = [[C.ps(f"psO{d}{i}", [128, 512]) for i in range(2)] for d in range(2)]
        r_psO = [[R() for i in range(2)] for d in range(2)]
        psU = [C.ps(f"psU{i}", [128, 512]) for i in range(4)]
        r_psU = [R() for i in range(4)]
        r_dc, r_z = R(), R()
        S.dma("sp", (lambda e: [e.dma_start(out=dct[:, hd, :], in_=scr["DC"][hd, :, :]) for hd in range(8)]), 8, "dc", writes=[r_dc])
        S.op("pool", lambda e: e.memset(zer[:], 0.0), writes=[r_z])
        ui = [0]
        offs = []
        a = 0
        for n in seq_lens:
            offs.append(a)
            a += n

        def run_seq(s0, SLs, mode, zero_init=True):
            NB = SLs // 128
            if zero_init:
                for hd in range(8):
                    S.op("pool", (lambda e, hd=hd: e.memset(Sst[hd], 0.0)), writes=[r_S[hd]])
                    S.op("pool", (lambda e, hd=hd: e.memset(Sbf[hd][:], 0.0)), writes=[r_Sb[hd]])

            def load(step):
                if step >= NB:
                    return
                for d in range(2):
                    blk = step if d == 0 else NB - 1 - step
                    t0 = s0 + blk * 128
                    i = step % NR
                    if mode == "full":
                        S.dma("sp", (lambda e, d=d, i=i, t0=t0: [
                            e.dma_start(out=qpb[d][i][:], in_=scr["QP"][d * 4:(d + 1) * 4, :, t0:t0 + 128].rearrange("h p t -> p h t")),
                            e.dma_start(out=atb[d][i][:], in_=scr["AT"][t0:t0 + 128, d * 4:(d + 1) * 4, :]),
                            e.dma_start(out=kpb[d][i][:], in_=scr["KPT"][t0:t0 + 128, d * 4:(d + 1) * 4, :]),
                            e.dma_start(out=vb[d][i][:], in_=scr["VH"][t0:t0 + 128, :])]), 4, f"ld{d}{i}", writes=[r_ld[d][i]])
                    else:
                        S.dma("sp", (lambda e, d=d, i=i, t0=t0: [
                            e.dma_start(out=kpb[d][i][:], in_=scr["KPT"][t0:t0 + 128, d * 4:(d + 1) * 4, :]),
                            e.dma_start(out=vb[d][i][:], in_=scr["VH"][t0:t0 + 128, :])]), 2, f"ld{d}{i}", writes=[r_ld[d][i]])
            load(0)
            load(1)
            for step in range(NB):
                load(step + 2)
                i = step % NR
                ob = step % 2
                if mode == "full":
                    for d in range(2):
                        S.op("pe", (lambda e, d=d, ob=ob: e.matmul(psO[d][ob][:], zer[:, 0:128], zer[:], start=True, stop=False,
                                                                   skip_group_check=True)), reads=[r_z], writes=[r_psO[d][ob]])
                        for h in range(4):
                            S.op("pe", (lambda e, d=d, ob=ob, h=h, i=i: e.matmul(
                                psO[d][ob][:, h * 128:(h + 1) * 128], atb[d][i][:, h, :], vb[d][i][:, h * 128:(h + 1) * 128],
                                start=False, stop=False, skip_group_check=True)), reads=[r_ld[d][i]], writes=[r_psO[d][ob]])
                for ci in range(4):
                    for d in range(2):
                        blk = step if d == 0 else NB - 1 - step
                        c = ci if d == 0 else 3 - ci
                        gch = (s0 + blk * 128) // 32 + c
                        for h in range(4):
                            hd = d * 4 + h
                            if mode == "full":
                                S.op("pe", (lambda e, d=d, ob=ob, h=h, i=i, c=c, hd=hd: e.matmul(
                                    psO[d][ob][32 * c:32 * c + 32, h * 128:(h + 1) * 128], qpb[d][i][:, h, 32 * c:32 * c + 32], Sbf[hd][:],
                                    start=False, stop=(ci == 3), skip_group_check=True, tile_position=(0, 32 * c))),
                                    reads=[r_ld[d][i], r_Sb[hd]], writes=[r_psO[d][ob]])
                            pu = ui[0] % 4
                            ui[0] += 1
                            S.op("pe", (lambda e, d=d, h=h, i=i, c=c, pu=pu: e.matmul(
                                psU[pu][:, 0:128], kpb[d][i][32 * c:32 * c + 32, h, :], vb[d][i][32 * c:32 * c + 32, h * 128:(h + 1) * 128],
                                start=True, stop=True, tile_position=(32 * c, 0))),
                                reads=[r_ld[d][i]], writes=[r_psU[pu]])
                            S.op("dve", (lambda e, hd=hd, pu=pu, gch=gch: e.scalar_tensor_tensor(
                                Sst[hd], Sst[hd], dct[:, hd, gch:gch + 1], psU[pu][:, 0:128], ALU.mult, ALU.add)),
                                reads=[r_psU[pu], r_dc], writes=[r_S[hd]])
                            if mode == "full":
                                S.op("act", (lambda e, hd=hd: e.copy(Sbf[hd][:], Sst[hd])), reads=[r_S[hd]], writes=[r_Sb[hd]])
                if mode == "full":
                    for d in range(2):
                        blk = step if d == 0 else NB - 1 - step
                        t0 = s0 + blk * 128
                        S.op("dve" if d == 0 else "act",
                             (lambda e, d=d, ob=ob: (e.tensor_copy(ot[d][ob][:], psO[d][ob][:]) if d == 0
                                                     else e.copy(ot[d][ob][:], psO[d][ob][:]))),
                             reads=[r_psO[d][ob]], writes=[r_ot[d][ob]])
                        S.dma("pool", (lambda e, d=d, ob=ob, t0=t0: [e.dma_start(out=scr["OF"][d, t0:t0 + 128, :], in_=ot[d][ob][:])]), 1,
                              f"so{d}{ob}", reads=[r_ot[d][ob]])

        if xchg is None:
            for s0, n in zip(offs, seq_lens):
                run_seq(s0, n, "full")
        else:
            n0 = seq_lens[0]
            G = C.sb("G", [128, 4, 8, 129], F32)
            rkm = C.sb("rkm", [128, 8], F32)
            tmp = C.sb("tmp", [128, 128], F32)
            r_G, r_rk, r_tmp, r_xs = R(), R(), R(), R()
            S.dma("sp", lambda e: [e.dma_start(out=rkm[:], in_=cst["rankmask"])], 1, "rk", writes=[r_rk])
            run_seq(offs[0], n0, "state")
            for hd in range(8):
                S.op("dve", (lambda e, hd=hd: e.tensor_reduce(SstAll[:, hd, 128:129], dct[:, hd, offs[0] // 32:(offs[0] + n0) // 32],
                                                              AX.X, ALU.mult)), reads=[r_dc], writes=[r_S[hd]])
            S.dma("pool", (lambda e: [e.dma_start(out=xchg["XS_in"].rearrange("(h p) c -> p h c", p=128), in_=SstAll[:])]), 1, "xs",
                  reads=r_S, writes=[r_xs])
            S.dma("pool", (lambda e: [e.collective_compute("AllGather", ALU.bypass, replica_groups=xchg["groups"],
                                                           ins=[xchg["XS_in"].opt()], outs=[xchg["XS_out"].opt()])]), 1, "ccs", reads=[r_xs], writes=[r_G], inc=1)
            for s0, n in list(zip(offs, seq_lens))[1:]:
                run_seq(s0, n, "full")
            S.dma("sp", (lambda e: [e.dma_start(out=G[:], in_=xchg["XS_out"].rearrange("(r h p) c -> p r h c", r=4, h=8))]), 1, "g",
                  reads=[r_G], writes=[r_G])
            for hd in range(8):
                S.op("pool", (lambda e, hd=hd: e.memset(Sst[hd], 0.0)), writes=[r_S[hd]])
                order = range(4) if hd < 4 else range(3, -1, -1)
                for i in order:
                    mcol = rkm[:, (0 if hd < 4 else 4) + i:(0 if hd < 4 else 4) + i + 1]
                    S.op("dve", (lambda e, hd=hd, i=i: e.scalar_tensor_tensor(tmp[:], Sst[hd], G[:, i, hd, 128:129], G[:, i, hd, 0:128],
                                                                            ALU.mult, ALU.add)), reads=[r_G, r_S[hd]], writes=[r_tmp])
                    S.op("dve", (lambda e, hd=hd: e.tensor_tensor(tmp[:], tmp[:], Sst[hd], ALU.subtract)), reads=[r_S[hd]], writes=[r_tmp])
                    S.op("dve", (lambda e, hd=hd, mcol=mcol: e.scalar_tensor_tensor(Sst[hd], tmp[:], mcol, Sst[hd], ALU.mult, ALU.add)),
                         reads=[r_tmp, r_rk], writes=[r_S[hd]])
                S.op("act", (lambda e, hd=hd: e.copy(Sbf[hd][:], Sst[hd])), reads=[r_S[hd]], writes=[r_Sb[hd]])
            run_seq(offs[0], n0, "full", zero_init=False)
        S.barrier()
        S.emit()
    return S


def outproj_phase(nc, tag, ntok, h1_d, h2_d, cst, w, scr):
    S = Sched(nc, tag)
    ntiles = ntok // 512
    with ExitStack() as es:
        C = Ctx(nc, tag, es)
        R = S.res
        wo = C.sb("wo", [128, 8, D], BF16)
        st = [C.sb("st0", [128, 512], F32), C.sb("st1", [128, 512], F32)]
        ident = C.sb("ident", [128, 128], BF16)
        onb = C.sb("onb", [128, 512], F32)
        ht = [C.sb(f"ht{i}", [128, 4, D], F32) for i in range(2)]
        mT = [C.sb(f"mT{i}", [128, 8, 512], BF16) for i in range(2)]
        of = [C.sb(f"of{i}", [128, 4, 512], F32) for i in range(2)]
        ob = [C.sb(f"ob{i}", [128, 4, 512], F32) for i in range(2)]
        gh = [C.sb(f"gh{i}", [128, 4, 512], BF16) for i in range(2)]
        osum4 = C.sb("osum4", [128, 4, 512], F32)
        junk = C.sb("junk", [128, 128], BF16)
        stat4 = C.sb("stat4", [128, 40], F32)
        mh4 = C.sb("mh4", [128, 4, 512], BF16)
        pt = C.ps("pt", [128, 1024], BF16)
        pt2 = C.ps("pt2", [128, 1024], BF16)
        r_pts = [R(), R()]
        po = [C.ps(f"po{i}", [128, 512]) for i in range(2)]
        r_w, r_st, r_c, r_ident = R(), [R(), R()], R(), R()
        S.dma("sp", lambda e: [e.dma_start(out=onb[:], in_=cst["hg_norm_b"]), e.dma_start(out=st[0][:, 0:128], in_=cst["ident"])],
              2, "c", writes=[r_c, r_st[0]])
        S.op("pool", lambda e: e.tensor_copy(ident[:], st[0][:, 0:128]), reads=[r_st[0]], writes=[r_ident])
        qi = [1]
        load_w(S, w["w_o"], wo, 8, D, None, st, r_st, r_w, qi)
        r_ld = [R(), R()]
        r_ht = [[R() for j in range(4)] for i in range(2)]
        r_mT = [[R() for j in range(4)] for i in range(2)]
        r_osum, r_junk, r_stat, r_mh, r_pt, r_po = R(), R(), R(), R(), R(), [R(), R()]

        def load(i):
            if i >= ntiles:
                return
            b = i % 2
            t0 = i * 512
            S.dma("sp", (lambda e: [e.dma_start(out=ht[b][:, j, :], in_=h1_d[t0 + j * 128:t0 + (j + 1) * 128, :]) for j in range(4)]),
                  4, f"ht{b}", writes=r_ht[b])
            S.dma("sp", (lambda e: [
                e.dma_start(out=mT[b][:, 0:4, :], in_=scr["MIXT"][0:512, t0:t0 + 512].rearrange("(c p) t -> p c t", p=128)),
                e.dma_start(out=of[b][:], in_=scr["OF"][0, t0:t0 + 512, :].rearrange("(j p) c -> p j c", p=128)),
                e.dma_start(out=ob[b][:], in_=scr["OF"][1, t0:t0 + 512, :].rearrange("(j p) c -> p j c", p=128)),
                e.dma_start(out=gh[b][:], in_=scr["GH"][t0:t0 + 512, :].rearrange("(j p) c -> p j c", p=128))]),
                4, f"ld{b}", writes=[r_ld[b]] + r_mT[b])
        load(0)

        def body(i):
            b = i % 2
            load(i + 1)
            t0 = i * 512
            r_os = [R() for j in range(4)]
            r_mhj = [R() for j in range(4)]
            r_stj = [R() for j in range(4)]
            r_jk = [Res("junk") for _ in range(16)]
            for j in range(4):
                S.op("dve" if j % 2 == 0 else "pool",
                     (lambda e, j=j: e.tensor_tensor(osum4[:, j, :], of[b][:, j, :], ob[b][:, j, :], ALU.add)),
                     reads=[r_ld[b], r_osum], writes=[r_os[j]])
            for j in range(4):
                for h in range(4):
                    S.op("act", (lambda e, h=h, j=j: e.activation(junk[:], osum4[:, j, h * 128:(h + 1) * 128], AF.Square,
                                                                  accum_out=stat4[:, j * 8 + h:j * 8 + h + 1])),
                         reads=[r_os[j]], writes=[r_jk[j * 4 + h], r_stj[j]])
            for j in range(4):
                S.op("act", (lambda e, j=j: e.activation(stat4[:, j * 8 + 4:j * 8 + 8], stat4[:, j * 8:j * 8 + 4], AF.Sqrt, bias=EPS, scale=1.0 / 128)),
                     reads=[r_stj[j]], writes=[r_stj[j]])
            for j in range(4):
                S.op("dve", (lambda e, j=j: e.reciprocal(stat4[:, j * 8 + 4:j * 8 + 8], stat4[:, j * 8 + 4:j * 8 + 8])), reads=[r_stj[j]], writes=[r_stj[j]])
            for j in range(4):
                ov = osum4[:, j, :].rearrange("p (h v) -> p h v", h=4)
                S.op("dve", (lambda e, ov=ov, j=j: e.tensor_tensor(
                    ov, ov, stat4[:, j * 8 + 4:j * 8 + 8].rearrange("p (h o) -> p h o", o=1).broadcast_to([128, 4, 128]), ALU.mult)),
                    reads=[r_stj[j]], writes=[r_os[j]])
            for j in range(4):
                S.op("pool", (lambda e, j=j: e.tensor_tensor(osum4[:, j, :], osum4[:, j, :], onb[:], ALU.mult)), reads=[r_c], writes=[r_os[j]])
            for j in range(4):
                S.op("dve", (lambda e, j=j: e.tensor_tensor(mh4[:, j, :], osum4[:, j, :], gh[b][:, j, :], ALU.mult)),
                     reads=[r_os[j], r_ld[b], r_mh], writes=[r_mhj[j]])
            for j in range(4):
                ptj, r_ptj = [pt, pt2][j % 2], r_pts[j % 2]
                for c in range(4):
                    S.op("pe", (lambda e, c=c, j=j, ptj=ptj: e.transpose(ptj[:, c * 128:(c + 1) * 128], mh4[:, j, c * 128:(c + 1) * 128], ident[:])),
                         reads=[r_mhj[j], r_ident], writes=[r_ptj])
                S.op("act", (lambda e, j=j, ptj=ptj: e.copy(mT[b][:, 4:8, j * 128:(j + 1) * 128],
                                                            ptj[:, 0:512].rearrange("p (k t) -> p k t", k=4))),
                     reads=[r_ptj], writes=[r_mT[b][j]])
            S.op("pool", (lambda e: e.memset(stat4[:, 32:33], 0.0)), writes=r_os + r_mhj + [r_osum, r_mh])
            for j in range(4):
                for hh in range(2):
                    pb = (j * 2 + hh) % 2
                    for kc in range(8):
                        S.op("pe", (lambda e, j=j, hh=hh, kc=kc, pb=pb: e.matmul(
                            po[pb][:], mT[b][:, kc, j * 128:(j + 1) * 128], wo[:, kc, hh * 512:(hh + 1) * 512],
                            start=(kc == 0), stop=(kc == 7))), reads=[r_w, r_mT[b][j]], writes=[r_po[pb]])
                    dst = ht[b][:, j, hh * 512:(hh + 1) * 512]
                    S.op("dve", (lambda e, dst=dst, pb=pb: e.tensor_tensor(dst, dst, po[pb][:], ALU.add)),
                         reads=[r_po[pb]], writes=[r_ht[b][j]])
                S.dma("pool", (lambda e, j=j: [e.dma_start(out=h2_d[t0 + j * 128:t0 + (j + 1) * 128, :], in_=ht[b][:, j, :])]), 1,
                      f"so{b}{j}", reads=[r_ht[b][j]])
        for i in range(ntiles):
            body(i)
        S.barrier()
        S.emit()
    return S


def ple_phase(nc, tag, ntok, h3_d, p_d, y_d, cst, w):
    S = Sched(nc, tag)
    ntiles = ntok // 512
    with ExitStack() as es:
        C = Ctx(nc, tag, es)
        R = S.res
        wg = C.sb("wg", [128, 8, D], BF16)
        wp = C.sb("wp", [128, 2, D], BF16)
        st = [C.sb("st0", [128, 512], F32), C.sb("st1", [128, 512], F32)]
        ident = C.sb("ident", [128, 128], BF16)
        gain = C.sb("gain", [128, 8], F32)
        fnb = C.sb("fnb", [128, D], F32)
        ht = [C.sb(f"ht{i}", [128, 4, D], F32) for i in range(2)]
        ptl = [C.sb(f"ptl{i}", [128, 4, 256], F32) for i in range(2)]
        xn4 = C.sb("xn4", [128, 4, D], BF16)
        junk = C.sb("junk", [128, D], BF16)
        stat = C.sb("stat", [128, 16], F32)
        xnT = C.sb("xnT", [128, 8, 512], BF16)
        pb16 = C.sb("pb16", [128, 4, 256], BF16)
        pT = C.sb("pT", [128, 2, 512], BF16)
        gsb = [C.sb(f"gsb{i}", [128, 512], F32) for i in range(2)]
        pt = C.ps("pt", [128, 1024], BF16)
        pt2 = C.ps("pt2", [128, 1024], BF16)
        pg = [C.ps(f"pg{i}", [128, 512]) for i in range(2)]
        pp = [C.ps(f"pp{i}", [128, 512]) for i in range(2)]
        r_w, r_st, r_c, r_ident = R(), [R(), R()], R(), R()
        S.dma("sp", lambda e: [e.dma_start(out=gain[:], in_=cst["ple_norm"]), e.dma_start(out=fnb[:], in_=cst["final_norm_b"]),
                               e.dma_start(out=st[0][:, 0:128], in_=cst["ident"])], 3, "c", writes=[r_c, r_st[0]])
        S.op("pool", lambda e: e.tensor_copy(ident[:], st[0][:, 0:128]), reads=[r_st[0]], writes=[r_ident])
        S.op("pool", lambda e: e.tensor_copy(stat[:, 15:16], gain[:, 0:1]), reads=[r_c], writes=[R()])
        qi = [1]
        load_w(S, w["w_ple_gate"], wg, 8, D, gain, st, r_st, r_w, qi)
        load_w(S, w["w_ple_proj"], wp, 2, D, None, st, r_st, r_w, qi)
        r_ht = [[R() for j in range(4)] for i in range(2)]
        r_pl = [R(), R()]
        r_stat = [R() for j in range(4)]
        r_junk, r_pt = R(), R()
        r_xn4 = [R() for j in range(4)]
        r_pts = [R(), R()]
        r_xnT = [R() for k in range(8)]
        r_pb16, r_pT, r_gsb, r_pg, r_pp = [R() for j in range(4)], R(), [R(), R()], [R(), R()], [R(), R()]

        def load(i):
            if i >= ntiles:
                return
            b = i % 2
            t0 = i * 512
            S.dma("sp", (lambda e: [e.dma_start(out=ht[b][:, j, :], in_=h3_d[t0 + j * 128:t0 + (j + 1) * 128, :]) for j in range(4)]),
                  4, f"ht{b}", writes=r_ht[b])
            S.dma("sp", (lambda e: [e.dma_start(out=ptl[b][:], in_=p_d[t0:t0 + 512, :].rearrange("(j p) c -> p j c", p=128))]),
                  1, f"pl{b}", writes=[r_pl[b]])
        load(0)

        def body(i):
            b = i % 2
            load(i + 1)
            t0 = i * 512
            norm_transpose4(S, ht[b], r_ht[b], stat, r_stat, junk, xn4, r_xn4, [pt, pt2], r_pts, ident, r_ident, xnT, r_xnT)
            for j in range(4):
                S.op("pool", (lambda e, j=j: e.tensor_copy(pb16[:, j, :], ptl[b][:, j, :])), reads=[r_pl[b]], writes=[r_pb16[j]])
            for j in range(4):
                ptj, r_ptj = [pt, pt2][j % 2], r_pts[j % 2]
                for c in range(2):
                    S.op("pe", (lambda e, c=c, j=j, ptj=ptj: e.transpose(ptj[:, c * 128:(c + 1) * 128], pb16[:, j, c * 128:(c + 1) * 128], ident[:])),
                         reads=[r_pb16[j], r_ident], writes=[r_ptj])
                S.op("act", (lambda e, j=j, ptj=ptj: e.copy(pT[:, :, j * 128:(j + 1) * 128], ptj[:, 0:256].rearrange("p (k t) -> p k t", k=2))),
                     reads=[r_ptj], writes=[r_pT])
            for j in range(4):
                for hh in range(2):
                    k2 = (j * 2 + hh) % 2
                    for kc in range(8):
                        S.op("pe", (lambda e, j=j, hh=hh, kc=kc, k2=k2: e.matmul(
                            pg[k2][:], xnT[:, kc, j * 128:(j + 1) * 128], wg[:, kc, hh * 512:(hh + 1) * 512],
                            start=(kc == 0), stop=(kc == 7))), reads=[r_w] + r_xnT, writes=[r_pg[k2]])
                    for kc in range(2):
                        S.op("pe", (lambda e, j=j, hh=hh, kc=kc, k2=k2: e.matmul(
                            pp[k2][:], pT[:, kc, j * 128:(j + 1) * 128], wp[:, kc, hh * 512:(hh + 1) * 512],
                            start=(kc == 0), stop=(kc == 1))), reads=[r_w, r_pT], writes=[r_pp[k2]])
                    S.op("act", (lambda e, k2=k2: e.activation(gsb[k2][:], pg[k2][:], AF.Sigmoid)), reads=[r_pg[k2]], writes=[r_gsb[k2]])
                    S.op("dve", (lambda e, k2=k2: e.tensor_tensor(gsb[k2][:], gsb[k2][:], pp[k2][:], ALU.mult)),
                         reads=[r_pp[k2]], writes=[r_gsb[k2]])
                    dst = ht[b][:, j, hh * 512:(hh + 1) * 512]
                    S.op("pool", (lambda e, dst=dst, k2=k2: e.tensor_tensor(dst, dst, gsb[k2][:], ALU.add)),
                         reads=[r_gsb[k2]], writes=[r_ht[b][j]])
            r_j2 = [Res("junk") for _ in range(4)]
            for j in range(4):
                S.op("act", (lambda e, j=j: e.activation(junk[:], ht[b][:, j, :], AF.Square, accum_out=stat[:, 8 + j:9 + j])),
                     reads=[r_ht[b][j]], writes=[r_j2[j], r_stat[j]])
            for j in range(4):
                S.op("act", (lambda e, j=j: e.activation(stat[:, 8 + j:9 + j], stat[:, 8 + j:9 + j], AF.Sqrt, bias=EPS, scale=1.0 / D)),
                     writes=[r_stat[j]])
            for j in range(4):
                S.op("dve", (lambda e, j=j: e.reciprocal(stat[:, 8 + j:9 + j], stat[:, 8 + j:9 + j])), writes=[r_stat[j]])
            for j in range(4):
                S.op("dve", (lambda e, j=j: e.scalar_tensor_tensor(ht[b][:, j, :], ht[b][:, j, :], stat[:, 8 + j:9 + j], fnb[:], ALU.mult, ALU.mult)),
                     reads=[r_stat[j], r_c], writes=[r_ht[b][j]])
                S.dma("pool", (lambda e, j=j: [e.dma_start(out=y_d[t0 + j * 128:t0 + (j + 1) * 128, :], in_=ht[b][:, j, :])]), 1,
                      f"so{b}{j}", reads=[r_ht[b][j]])
        for i in range(ntiles):
            body(i)
        S.barrier()
        S.emit()
    return S


CONST_SHAPES = {
    "ident": [128, 128], "rmask": [128, 512], "maskF": [128, 128], "maskB": [128, 128],
    "ffn1_norm": [128, 8], "mix_norm": [128, 8], "q_norm": [128, 3], "kv_norm": [128, 2], "hg_lb": [128, 16],
    "hg_norm_b": [128, 512], "ffn2_norm": [128, 8], "ple_norm": [128, 8], "final_norm_b": [128, D],
}
W_SHAPES = {
    "ffn1_wg": [D, DFF], "ffn1_wu": [D, DFF], "ffn1_wd": [DFF, D], "w_in": [D, WIN_COLS], "w_uq": [384, UQ_COLS],
    "w_uk": [256, 512], "w_uv": [256, 512], "w_o": [D, D], "ffn2_wg": [D, DFF], "ffn2_wu": [D, DFF], "ffn2_wd": [DFF, D],
    "w_ple_gate": [D, D], "w_ple_proj": [256, D],
}


def build_program(seq_lens, dbg=(), phases=None, gather=False):
    Sched.DMA_SEMS = {}
    Sched.DMA_CNT = {}
    nc = bass.Bass("TRN2", target_bir_lowering=False)
    ntok = sum(seq_lens)
    ntiles = ntok // 512

    def inp(name, shape, dt=F32):
        return nc.dram_tensor(name, shape, dt, kind="ExternalInput").ap()

    def scratch(name, shape, dt):
        kind = "ExternalOutput" if name in dbg else "Internal"
        return nc.dram_tensor(name, shape, dt, kind=kind).ap()

    x = inp("x", [ntok, D])
    p = inp("p", [ntok, 256])
    cst = {k: inp(k, v) for k, v in CONST_SHAPES.items()}
    cst["rope_c"] = inp("rope_c", [32, ntok])
    cst["rope_s"] = inp("rope_s", [32, ntok])
    w = {k: inp(k, v) for k, v in W_SHAPES.items()}
    y = nc.dram_tensor("y", [ntok, D], F32, kind="ExternalOutput").ap()
    h1 = scratch("h1", [ntok, D], F32)
    h2 = scratch("h2", [ntok, D], F32)
    h3 = scratch("h3", [ntok, D], F32)
    scr = {
        "QT": scratch("QT", [8, 96, ntok], BF16), "KTn": scratch("KTn", [512, ntok], BF16),
        "KTr": scratch("KTr", [32, ntok], BF16), "VA": scratch("VA", [8, ntok, 64], BF16),
        "QP": scratch("QP", [8, 128, ntok], BF16), "DC": scratch("DC", [8, 128, ntok // 32], F32),
        "VH": scratch("VH", [ntok, 512], BF16), "GH": scratch("GH", [ntok, 512], BF16),
        "AT": scratch("AT", [ntok, 8, 128], BF16), "KPT": scratch("KPT", [ntok, 8, 128], BF16),
        "MIXT": scratch("MIXT", [512, ntok], BF16), "OF": scratch("OF", [2, ntok, 512], F32),
    }
    def on(k):
        return phases is None or k in phases
    if on("f1"):
        ffn_phase(nc, "f1", x, h1, ntiles, cst["ffn1_norm"], w["ffn1_wg"], w["ffn1_wu"], w["ffn1_wd"], cst["ident"])
    if on("mi"):
        mixer_in_phase(nc, "mi", ntok, h1, cst, w, scr)
    seqs = []
    a = 0
    for SLs in seq_lens:
        b = a + SLs
        seqs.append(dict(q0=a, nq=SLs, kp=[((lambda h, a=a, b=b: scr["KTn"][h * 64:(h + 1) * 64, a:b]), scr["KTr"][:, a:b],
                                            (lambda h, a=a, b=b: scr["VA"][h, a:b, :]), SLs)]))
        a = b
    xchg = None
    if gather:
        n0 = seq_lens[0]
        cst["rankmask"] = inp("rankmask", [128, 8])
        xchg = dict(n0=n0, groups=[[0, 1, 2, 3], [4, 5, 6, 7]],
                    XK_in=[scratch(f"XK_in{a_}", [128, n0], BF16) for a_ in range(4)],
                    XK_out=[scratch(f"XK_out{a_}", [4 * 128, n0], BF16) for a_ in range(4)],
                    XR_in=scratch("XR_in", [128, n0], BF16), XR_out=scratch("XR_out", [4 * 128, n0], BF16),
                    XV_in=[scratch(f"XV_in{a_}", [2 * n0, 64], BF16) for a_ in range(4)],
                    XV_out=[scratch(f"XV_out{a_}", [4 * 2 * n0, 64], BF16) for a_ in range(4)],
                    XS_in=scratch("XS_in", [1024, 129], F32), XS_out=scratch("XS_out", [4096, 129], F32))
        kp = []
        for r in range(4):
            kp.append(((lambda h, r=r: xchg["XK_out"][h // 2][r * 128 + (h % 2) * 64:r * 128 + (h % 2) * 64 + 64, :]),
                       xchg["XR_out"][r * 128:r * 128 + 32, :],
                       (lambda h, r=r: xchg["XV_out"][h // 2].rearrange("(r hh t) d -> r hh t d", r=4, hh=2)[r, h % 2]),
                       n0))
        seqs[0] = dict(q0=0, nq=n0, kp=kp, gathered=True)
        seqs = seqs[1:] + seqs[0:1]
    if on("at"):
        attn_phase(nc, "at", seqs, scr, xchg=xchg)
    if on("sc"):
        scan_phase(nc, "sc", list(seq_lens), ntok, scr, xchg=xchg, cst=cst)
    if on("op"):
        outproj_phase(nc, "op", ntok, h1, h2, cst, w, scr)
    if on("f2"):
        ffn_phase(nc, "f2", h2, h3, ntiles, cst["ffn2_norm"], w["ffn2_wg"], w["ffn2_wu"], w["ffn2_wd"], cst["ident"])
    if on("pl"):
        ple_phase(nc, "pl", ntok, h3, p, y, cst, w)
    return nc


def _lay(v, nch):
    return np.ascontiguousarray(np.asarray(v, np.float32).reshape(nch, 128).T)


def host_consts(inputs):
    f = np.float32
    c = {}
    c["ident"] = np.eye(128, dtype=f)
    rm = np.ones((128, 512), f)
    rm[:, 0::32] = 0.0
    c["rmask"] = rm
    idx = np.arange(128)
    same = (idx[:, None] // 32) == (idx[None, :] // 32)
    c["maskF"] = (same & (idx[:, None] <= idx[None, :])).astype(f)
    c["maskB"] = (same & (idx[:, None] >= idx[None, :])).astype(f)
    c["ffn1_norm"] = _lay(inputs["ffn1_norm"][0], 8)
    c["mix_norm"] = _lay(inputs["mix_norm"][0], 8)
    c["q_norm"] = _lay(inputs["q_norm"][0], 3)
    c["kv_norm"] = _lay(inputs["kv_norm"][0], 2)
    lb = np.asarray(inputs["hg_lb"], f).reshape(2, 2, 4, 128)
    c["hg_lb"] = np.ascontiguousarray(lb.transpose(3, 0, 1, 2).reshape(128, 16))
    c["hg_norm_b"] = np.ascontiguousarray(np.broadcast_to(np.asarray(inputs["hg_norm"][0], f)[None, :], (128, 512)))
    c["ffn2_norm"] = _lay(inputs["ffn2_norm"][0], 8)
    c["ple_norm"] = _lay(inputs["ple_norm"][0], 8)
    c["final_norm_b"] = np.ascontiguousarray(np.broadcast_to(np.asarray(inputs["final_norm"], f)[None, :], (128, D)))
    wts = {}
    for k in ("ffn1_wg", "ffn1_wu", "ffn1_wd", "w_uk", "w_uv", "w_o", "ffn2_wg", "ffn2_wu", "ffn2_wd", "w_ple_gate", "w_ple_proj"):
        wts[k] = np.ascontiguousarray(np.asarray(inputs[k][0], f))
    win = np.asarray(inputs["w_in"][0], f)
    wts["w_in"] = np.ascontiguousarray(np.concatenate([win, win[:, 656:672], win[:, 640:656]], axis=1))
    wq = np.asarray(inputs["w_uq"][0], f)
    sw = [np.concatenate([wq[:, h * 96 + 80:h * 96 + 96], wq[:, h * 96 + 64:h * 96 + 80]], axis=1) for h in range(8)]
    wts["w_uq"] = np.ascontiguousarray(np.concatenate([wq] + sw, axis=1))
    return c, wts


def rope_tables(pos):
    inv = np.exp(np.arange(0, 32, 2, dtype=np.float32) * np.float32(-np.log(10000.0) / 32)).astype(np.float32)
    ang = (pos.astype(np.float32)[None, :] * inv[:, None]).astype(np.float32)
    cs, sn = np.cos(ang).astype(np.float32), np.sin(ang).astype(np.float32)
    return np.ascontiguousarray(np.concatenate([cs, cs], 0)), np.ascontiguousarray(np.concatenate([-sn, sn], 0))


_PROG = {}


def run_balanced(inputs):
    xp = np.asarray(inputs["x_prompt"], np.float32)
    xs = np.asarray(inputs["x_sample"], np.float32)
    pp = np.asarray(inputs["p_prompt"], np.float32)[0]
    psm = np.asarray(inputs["p_sample"], np.float32)[0]
    SP, SS = xp.shape[1], xs.shape[1]
    Q = SP // 4
    lens = (Q, SS, SS)
    c, wts = host_consts(inputs)
    key = ("bal", lens)
    if key not in _PROG:
        _PROG[key] = build_program(lens, gather=True)
    nc = _PROG[key]
    in_maps = []
    for core in range(8):
        pb, r = core // 4, core % 4
        im = {"x": np.ascontiguousarray(np.concatenate([xp[pb, r * Q:(r + 1) * Q], xs[2 * core], xs[2 * core + 1]], axis=0)),
              "p": np.ascontiguousarray(np.concatenate([pp[pb, r * Q:(r + 1) * Q], psm[2 * core], psm[2 * core + 1]], axis=0))}
        pos = np.concatenate([np.arange(r * Q, (r + 1) * Q, dtype=np.float32), np.arange(SS, dtype=np.float32),
                              np.arange(SS, dtype=np.float32)])
        im["rope_c"], im["rope_s"] = rope_tables(pos)
        rk = np.zeros((128, 8), np.float32)
        for i in range(4):
            rk[:, i] = 1.0 if i < r else 0.0
            rk[:, 4 + i] = 1.0 if i > r else 0.0
        im["rankmask"] = rk
        im.update(c)
        im.update(wts)
        in_maps.append(im)
    res = run_bass_kernel_spmd(nc, in_maps, core_ids=list(range(8)))
    y_prompt = np.empty((2, SP, D), np.float32)
    y_sample = np.empty((16, SS, D), np.float32)
    for core in range(8):
        pb, r = core // 4, core % 4
        y = np.asarray(res.results[core]["y"])
        y_prompt[pb, r * Q:(r + 1) * Q] = y[0:Q]
        y_sample[2 * core] = y[Q:Q + SS]
        y_sample[2 * core + 1] = y[Q + SS:]
    return (y_prompt, y_sample)


def kernel(**inputs):
    return run_balanced(inputs)
```

```python
import numpy as np
from contextlib import ExitStack
import concourse.bass as bass
import concourse.mybir as mybir
from concourse.bass_utils import run_bass_kernel_spmd

F32 = mybir.dt.float32
BF16 = mybir.dt.bfloat16
AF = mybir.ActivationFunctionType
ALU = mybir.AluOpType
AX = mybir.AxisListType

D = 1024
DFF = 2816
EPS = 1e-6
NFC = DFF // 128
NKC = D // 128


class Res:
    __slots__ = ("name", "w", "r")

    def __init__(self, name):
        self.name = name
        self.w = None
        self.r = {}


class Ev:
    __slots__ = ("kind", "key", "op", "count")

    def __init__(self, kind, key, op=None, count=0):
        self.kind = kind
        self.key = key
        self.op = op
        self.count = count


class Op:
    __slots__ = ("fn", "deps", "sig", "count", "dma_key", "dma_n", "ev", "inc")

    def __init__(self, fn, deps):
        self.fn = fn
        self.deps = deps
        self.sig = False
        self.count = 0
        self.dma_key = None
        self.dma_n = 0
        self.ev = None


ENGS = ("sp", "act", "dve", "pool", "pe")
FUSE_WAITS = True


class Sched:
    DMA_SEMS = {}
    DMA_CNT = {}

    def __init__(self, nc, tag):
        self.nc = nc
        self.tag = tag
        self.ops = {e: [] for e in ENGS}
        self.dma_cnt = Sched.DMA_CNT
        self.nres = 0
        self.keymap = {}

    def res(self, name=None):
        self.nres += 1
        return Res(name or f"r{self.nres}")

    def _deps(self, eng, reads, writes):
        deps = []
        for r in reads:
            if r.w is not None:
                deps.append(r.w)
        for w in writes:
            if w.w is not None:
                deps.append(w.w)
            deps.extend(w.r.values())
        out = []
        seen = set()
        for d in deps:
            if id(d) in seen:
                continue
            seen.add(id(d))
            if d.kind == "e" and d.key == "pe" and eng == "pe":
                continue
            out.append(d)
        return out

    def op(self, eng, fn, reads=(), writes=()):
        o = Op(fn, self._deps(eng, reads, writes))
        ev = Ev("e", eng, op=o)
        o.ev = ev
        self.ops[eng].append(o)
        for r in reads:
            r.r[("e", eng)] = ev
        for w in writes:
            w.w = ev
            w.r = {}
        return o

    def dma(self, queue, fn, n, key, reads=(), writes=(), inc=16):
        if key not in self.keymap:
            self.keymap[key] = f"k{len(self.keymap)}"
        key = self.keymap[key]
        o = Op(fn, self._deps(queue, reads, writes))
        o.dma_key = key
        o.dma_n = n
        o.inc = inc
        c = self.dma_cnt.get(key, 0) + n * inc
        self.dma_cnt[key] = c
        ev = Ev("d", key, op=o, count=c)
        o.ev = ev
        self.ops[queue].append(o)
        for r in reads:
            r.r[("d", key)] = ev
        for w in writes:
            w.w = ev
            w.r = {}
        return o

    def barrier(self):
        evs = []
        for e in ENGS:
            for o in reversed(self.ops[e]):
                if o.dma_key is None and o.fn is not None:
                    evs.append(o.ev)
                    break
        lastd = {}
        for e in ENGS:
            for o in self.ops[e]:
                if o.dma_key is not None:
                    lastd[o.dma_key] = o.ev
        evs.extend(lastd.values())
        for e in ENGS:
            deps = [d for d in evs if not (d.kind == "e" and d.key == e)]
            o = Op(None, deps)
            o.ev = Ev("e", e, op=o)
            self.ops[e].append(o)

    def emit(self):
        nc = self.nc
        for e in ENGS:
            for o in self.ops[e]:
                for d in o.deps:
                    if d.kind == "e":
                        d.op.sig = True
        sems = {}
        for e in ENGS:
            c = 0
            for o in self.ops[e]:
                if o.dma_key is None and o.sig:
                    assert o.fn is not None
                    c += 1
                    o.count = c
            sems[("e", e)] = nc.alloc_semaphore(f"{self.tag}_e_{e}")
        for k in self.dma_cnt:
            if k not in Sched.DMA_SEMS:
                Sched.DMA_SEMS[k] = nc.alloc_semaphore(f"d_{k}")
            sems[("d", k)] = Sched.DMA_SEMS[k]
        self.sems = sems

        def run(eng_name, eng):
            waited = {}
            for o in self.ops[eng_name]:
                need = {}
                for d in o.deps:
                    k = (d.kind, d.key)
                    val = d.op.count if d.kind == "e" else d.count
                    assert val > 0, (eng_name, d.kind, d.key)
                    if waited.get(k, 0) >= val:
                        continue
                    waited[k] = val
                    need[k] = max(need.get(k, 0), val)
                need = list(need.items())
                fuse = None
                if FUSE_WAITS and need and o.fn is not None and o.dma_key is None:
                    fuse = need.pop()
                for k, val in need:
                    eng.wait_ge(sems[k], val)
                if o.fn is None:
                    continue
                ins = o.fn(eng)
                if fuse is not None:
                    ins._wait_ge(sems[fuse[0]], fuse[1])
                if o.dma_key is not None:
                    assert len(ins) == o.dma_n
                    for i in ins:
                        i.then_inc(sems[("d", o.dma_key)], o.inc)
                elif o.sig:
                    ins.then_inc(sems[("e", eng_name)], 1)

        with nc.Block() as block:
            @block.sync
            def _(e):
                run("sp", e)

            @block.scalar
            def _(e):
                run("act", e)

            @block.vector
            def _(e):
                run("dve", e)

            @block.gpsimd
            def _(e):
                run("pool", e)

            @block.tensor
            def _(e):
                run("pe", e)

    def release(self):
        for s in self.sems.values():
            self.nc.release_semaphore(s)


def load_weight_bf16(S, nc, w_dram, w_sb, rows_chunks, cols, gain_sb, stage, stage_res, w_res, qi=[0]):
    CW = 512
    for kc in range(rows_chunks):
        for c0 in range(0, cols, CW):
            cw = min(CW, cols - c0)
            i = qi[0] % len(stage)
            qi[0] += 1
            st, sr = stage[i], stage_res[i]
            src = w_dram[kc * 128:(kc + 1) * 128, c0:c0 + cw]
            S.dma("sp", (lambda e, st=st, src=src, cw=cw: [e.dma_start(out=st[:, 0:cw], in_=src)]),
                  1, f"wst{i}", writes=[sr])
            dst = w_sb[:, kc, c0:c0 + cw]
            if gain_sb is not None:
                g = gain_sb[:, kc:kc + 1]
                S.op("pool", (lambda e, dst=dst, st=st, cw=cw, g=g:
                              e.tensor_scalar(dst, st[:, 0:cw], g, 0.0, ALU.mult, ALU.add)),
                     reads=[sr], writes=[w_res])
            else:
                S.op("pool", (lambda e, dst=dst, st=st, cw=cw: e.tensor_copy(dst, st[:, 0:cw])),
                     reads=[sr], writes=[w_res])


def ffn_phase(nc, tag, x_d, out_d, ntiles, gain_d, wg_d, wu_d, wd_d, ident_d):
    S = Sched(nc, tag)
    with ExitStack() as es:
        def sb(name, shape, dt):
            return es.enter_context(nc.sbuf_tensor(f"{tag}_{name}", shape, dt))

        def ps(name, shape, dt):
            return es.enter_context(nc.psum_tensor(f"{tag}_{name}", shape, dt))

        wg = sb("wg", [128, NKC, DFF], BF16)
        wu = sb("wu", [128, NKC, DFF], BF16)
        wd = sb("wd", [128, NFC, D], BF16)
        gain = sb("gain", [128, NKC], F32)
        ident = sb("ident", [128, 128], BF16)
        xt0 = sb("xt0", [128, 4, D], F32)
        xt1 = sb("xt1", [128, 4, D], F32)
        st0 = xt1[:, 0, 0:512]
        st1 = xt1[:, 1, 0:512]
        xn4 = sb("xn4", [128, 4, D], BF16)
        xnT = sb("xnT", [128, NKC, 512], BF16)
        actb = sb("act", [128, NFC, 512], BF16)
        sg = sb("sg", [128, 2, 512], BF16)
        stat = sb("stat", [128, 16], F32)
        junk = sb("junk", [128, D], BF16)
        pg0 = ps("pg0", [128, 512], F32)
        pg1 = ps("pg1", [128, 512], F32)
        pu0 = ps("pu0", [128, 512], F32)
        pu1 = ps("pu1", [128, 512], F32)
        po0 = ps("po0", [128, 512], F32)
        po1 = ps("po1", [128, 512], F32)
        pt0 = ps("pt0", [128, 1024], BF16)
        pt1 = ps("pt1", [128, 1024], BF16)
        R = S.res
        r_gain, r_ident, r_w = R("gain"), R("ident"), R("w")
        r_xt = [[R(f"xt{i}_{j}") for j in range(4)] for i in range(2)]
        r_st = [r_xt[1][0], r_xt[1][1]]
        S.dma("sp", lambda e: [e.dma_start(out=gain[:], in_=gain_d)], 1, "c0", writes=[r_gain])
        S.dma("sp", lambda e: [e.dma_start(out=st0[:, 0:128], in_=ident_d)], 1, "wst0", writes=[r_st[0]])
        S.op("pool", lambda e: e.tensor_copy(ident[:], st0[:, 0:128]), reads=[r_st[0]], writes=[r_ident])
        stat_dummy = None
        qi = [1]
        r_wg, r_wu, r_wd = R("wg"), R("wu"), R("wd")
        S.op("pool", lambda e: e.tensor_copy(stat[:, 8:9], gain[:, 0:1]), reads=[r_gain], writes=[R("dummy")])
        load_weight_bf16(S, nc, wg_d, wg, NKC, DFF, gain, [st0, st1], r_st, r_wg, qi)
        load_weight_bf16(S, nc, wu_d, wu, NKC, DFF, gain, [st0, st1], r_st, r_wu, qi)
        load_weight_bf16(S, nc, wd_d, wd, NFC, D, None, [st0, st1], r_st, r_wd, qi)

        xts = [xt0, xt1]
        r_xn4 = [R(f"xn{j}") for j in range(4)]
        r_xnT, r_act = [R(f"xnT{k}") for k in range(NKC)], [R(f"act{f}") for f in range(NFC)]
        r_sg = [R("sg0"), R("sg1")]
        r_stat = [R(f"stat{j}") for j in range(4)]
        r_junk = R("junk")
        pgs, pus, pos, pts = [pg0, pg1], [pu0, pu1], [po0, po1], [pt0, pt1]
        r_pg, r_pu = [R("pg0"), R("pg1")], [R("pu0"), R("pu1")]
        r_po, r_pt = [R("po0"), R("po1")], [R("pt0"), R("pt1")]

        def load_tile(i):
            b = i % 2
            for j in range(4):
                src = x_d[i * 512 + j * 128: i * 512 + (j + 1) * 128, :]
                dst = xts[b][:, j, :]
                S.dma("sp", (lambda e, dst=dst, src=src: [e.dma_start(out=dst, in_=src)]), 1,
                      f"xt{b}_{j}", writes=[r_xt[b][j]])

        load_tile(0)
        nt_ctr = [0]
        for i in range(ntiles):
            b = i % 2
            xt = xts[b]
            if i + 1 < ntiles:
                load_tile(i + 1)
            norm_transpose4(S, xt, r_xt[b], stat, r_stat, junk, xn4, r_xn4, pts, r_pt, ident, r_ident, xnT, r_xnT)
            for f in range(NFC):
                pb = f % 2
                for kc in range(NKC):
                    S.op("pe", (lambda e, pb=pb, f=f, kc=kc: e.matmul(
                        pgs[pb][:], wg[:, kc, f * 128:(f + 1) * 128], xnT[:, kc, :],
                        start=(kc == 0), stop=(kc == NKC - 1))),
                        reads=[r_wg, r_xnT[kc]], writes=[r_pg[pb]])
                for kc in range(NKC):
                    S.op("pe", (lambda e, pb=pb, f=f, kc=kc: e.matmul(
                        pus[pb][:], wu[:, kc, f * 128:(f + 1) * 128], xnT[:, kc, :],
                        start=(kc == 0), stop=(kc == NKC - 1))),
                        reads=[r_wu, r_xnT[kc]], writes=[r_pu[pb]])
                S.op("act", (lambda e, pb=pb: e.activation(sg[:, pb, :], pgs[pb][:], AF.Silu)),
                     reads=[r_pg[pb]], writes=[r_sg[pb]])
                S.op("dve", (lambda e, pb=pb, f=f: e.tensor_tensor(actb[:, f, :], sg[:, pb, :], pus[pb][:], ALU.mult)),
                     reads=[r_sg[pb], r_pu[pb]], writes=[r_act[f]])
            for j in range(4):
                for hh in range(2):
                    pb = (j * 2 + hh) % 2
                    for f in range(NFC):
                        S.op("pe", (lambda e, pb=pb, f=f, j=j, hh=hh: e.matmul(
                            pos[pb][:], actb[:, f, j * 128:(j + 1) * 128], wd[:, f, hh * 512:(hh + 1) * 512],
                            start=(f == 0), stop=(f == NFC - 1))),
                            reads=[r_wd, r_act[f]], writes=[r_po[pb]])
                    dst = xt[:, j, hh * 512:(hh + 1) * 512]
                    S.op("dve", (lambda e, dst=dst, pb=pb: e.scalar_tensor_tensor(
                        dst, pos[pb][:], 0.5, dst, ALU.mult, ALU.add)),
                        reads=[r_po[pb], r_xt[b][j]], writes=[r_xt[b][j]])
                dstd = out_d[i * 512 + j * 128: i * 512 + (j + 1) * 128, :]
                src = xt[:, j, :]
                S.dma("pool", (lambda e, dstd=dstd, src=src: [e.dma_start(out=dstd, in_=src)]), 1,
                      f"xo{b}_{j}", reads=[r_xt[b][j]])
        S.barrier()
        S.emit()
    return S


class Ctx:
    def __init__(self, nc, tag, es):
        self.nc, self.tag, self.es = nc, tag, es

    def sb(self, name, shape, dt):
        return self.es.enter_context(self.nc.sbuf_tensor(f"{self.tag}_{name}", shape, dt))

    def ps(self, name, shape, dt=F32):
        return self.es.enter_context(self.nc.psum_tensor(f"{self.tag}_{name}", shape, dt))


def load_w(S, w_dram, w_sb, nrc, cols, gain_sb, st, r_st, r_w, qi, rows_last=128):
    for rc in range(nrc):
        for c0 in range(0, cols, 512):
            cw = min(512, cols - c0)
            i = qi[0] % 2
            qi[0] += 1
            stt, sr = st[i], r_st[i]
            src = w_dram[rc * 128:(rc + 1) * 128, c0:c0 + cw]
            S.dma("sp", (lambda e, stt=stt, src=src, cw=cw: [e.dma_start(out=stt[:, 0:cw], in_=src)]),
                  1, f"wst{i}", writes=[sr])
            dst = w_sb[:, rc, c0:c0 + cw]
            if gain_sb is not None:
                g = gain_sb[:, rc:rc + 1]
                S.op("pool", (lambda e, dst=dst, stt=stt, cw=cw, g=g:
                              e.tensor_scalar(dst, stt[:, 0:cw], g, 0.0, ALU.mult, ALU.add)),
                     reads=[sr], writes=[r_w])
            else:
                S.op("pool", (lambda e, dst=dst, stt=stt, cw=cw: e.tensor_copy(dst, stt[:, 0:cw])),
                     reads=[sr], writes=[r_w])


def norm_transpose(S, xt, r_xt_j, j, stat, r_stat, junk, r_junk, xn, r_xn, pt, r_pt, ident, r_ident,
                   xnT, r_xnT, nfeat=D):
    nkc = nfeat // 128
    xj = xt[:, j, :]
    ss = stat[:, j:j + 1]
    rs = stat[:, 4 + j:5 + j]
    S.op("act", (lambda e: e.activation(junk[:, 0:nfeat], xj, AF.Square, accum_out=ss)),
         reads=[r_xt_j], writes=[r_junk, r_stat[j]])
    S.op("act", (lambda e: e.activation(rs, ss, AF.Sqrt, bias=EPS, scale=1.0 / nfeat)),
         reads=[r_stat[j]], writes=[r_stat[j]])
    S.op("dve", (lambda e: e.reciprocal(rs, rs)), reads=[r_stat[j]], writes=[r_stat[j]])
    S.op("dve", (lambda e: e.tensor_scalar(xn[:, 0:nfeat], xj, rs, None, ALU.mult)),
         reads=[r_xt_j, r_stat[j]], writes=[r_xn])
    for kc in range(nkc):
        S.op("pe", (lambda e, kc=kc: e.transpose(pt[:, kc * 128:(kc + 1) * 128],
                                                 xn[:, kc * 128:(kc + 1) * 128], ident[:])),
             reads=[r_xn, r_ident], writes=[r_pt])
    dst = xnT[:, 0:nkc, j * 128:(j + 1) * 128]
    src = pt[:, 0:nkc * 128].rearrange("p (k t) -> p k t", k=nkc)
    S.op("act", (lambda e: e.copy(dst, src)), reads=[r_pt], writes=r_xnT)


def norm_transpose4(S, xt, r_xt, stat, r_stat, junk, xn4, r_xn, pts, r_pts, ident, r_ident, xnT, r_xnT, nfeat=D):
    nkc = nfeat // 128
    r_j = [Res("junk") for _ in range(4)]
    for j in range(4):
        S.op("act", (lambda e, j=j: e.activation(junk[:, 0:nfeat], xt[:, j, :], AF.Square, accum_out=stat[:, j:j + 1])),
             reads=[r_xt[j]], writes=[r_j[j], r_stat[j]])
    for j in range(4):
        S.op("act", (lambda e, j=j: e.activation(stat[:, 4 + j:5 + j], stat[:, j:j + 1], AF.Sqrt, bias=EPS, scale=1.0 / nfeat)),
             reads=[r_stat[j]], writes=[r_stat[j]])
    for j in range(4):
        S.op("dve", (lambda e, j=j: e.reciprocal(stat[:, 4 + j:5 + j], stat[:, 4 + j:5 + j])), reads=[r_stat[j]], writes=[r_stat[j]])
    for j in range(4):
        S.op("dve" if j % 2 == 0 else "pool",
             (lambda e, j=j: e.tensor_scalar(xn4[:, j, 0:nfeat], xt[:, j, :], stat[:, 4 + j:5 + j], 0.0, ALU.mult, ALU.add)),
             reads=[r_xt[j], r_stat[j]], writes=[r_xn[j]])
    for j in range(4):
        pt, r_pt = pts[j % 2], r_pts[j % 2]
        for kc in range(nkc):
            S.op("pe", (lambda e, kc=kc, j=j, pt=pt: e.transpose(pt[:, kc * 128:(kc + 1) * 128],
                                                               xn4[:, j, kc * 128:(kc + 1) * 128], ident[:])),
                 reads=[r_xn[j], r_ident], writes=[r_pt])
        dst = xnT[:, 0:nkc, j * 128:(j + 1) * 128]
        src = pt[:, 0:nkc * 128].rearrange("p (k t) -> p k t", k=nkc)
        S.op("act", (lambda e, dst=dst, src=src: e.copy(dst, src)), reads=[r_pt], writes=r_xnT)


C_CQ, C_CKV, C_KR, C_HQ, C_HI, C_HFF, C_HFB, C_HG, C_KRS = 0, 384, 640, 672, 1184, 1696, 2208, 2720, 3232
WIN_COLS = 3264
UQ_COLS = 768 + 256


def mixer_in_phase(nc, tag, ntok, h1_d, cst, w, scr):
    S = Sched(nc, tag)
    ntiles = ntok // 512
    with ExitStack() as es:
        C = Ctx(nc, tag, es)
        R = S.res
        win = C.sb("win", [128, NKC, WIN_COLS], BF16)
        wuq = C.sb("wuq", [128, 3, UQ_COLS], BF16)
        wuk = C.sb("wuk", [128, 2, 512], BF16)
        wuv = C.sb("wuv", [128, 2, 512], BF16)
        gains = C.sb("gains", [128, 16], F32)
        lbt = C.sb("lbt", [128, 16], F32)
        lb = C.sb("lb", [128, 8], F32)
        oml = C.sb("oml", [128, 8], F32)
        ident = C.sb("ident", [128, 128], BF16)
        ones = C.sb("ones", [128, 128], BF16)
        rmask = C.sb("rmask", [128, 512], F32)
        mF = C.sb("mF", [128, 128], F32)
        mB = C.sb("mB", [128, 128], F32)
        ht = C.sb("ht", [128, 4, D], F32)
        xn = C.sb("xn", [128, D], BF16)
        junk = C.sb("junk", [128, D], BF16)
        stat = C.sb("stat", [128, 16], F32)
        xnT = C.sb("xnT", [128, NKC, 512], BF16)
        cqT = C.sb("cqT", [128, 3, 512], BF16)
        ckvT = C.sb("ckvT", [128, 2, 512], BF16)
        sqq = C.sb("sqq", [128, 2, 512], BF16)
        sqkv = C.sb("sqkv", [128, 2, 512], BF16)
        rsq = C.sb("rsq", [128, 512], F32)
        rskv = C.sb("rskv", [128, 512], F32)
        rstok = C.sb("rstok", [128, 8], F32)
        tct = C.sb("tct", [128, 512], F32)
        tst = C.sb("tst", [128, 512], F32)
        t1a = C.sb("t1a", [128, 512], F32)
        t2a = C.sb("t2a", [128, 512], F32)
        t1 = [t1a, t1a]
        t2 = [t2a, t2a]
        qout = C.sb("qout", [128, 8, 512], BF16)
        knT = C.sb("knT", [128, 4, 512], BF16)
        krp = C.sb("krp", [128, 512], BF16)
        vt = C.sb("vt", [128, 4, 512], BF16)
        qh = C.sb("qh", [128, 4, 512], F32)
        hA = [C.sb(f"hA{i}", [128, 512], F32) for i in range(2)]
        hB = [C.sb(f"hB{i}", [128, 512], F32) for i in range(2)]
        hC = [C.sb(f"hC{i}", [128, 512], F32) for i in range(2)]
        hE1 = [C.sb(f"hE1{i}", [128, 512], F32) for i in range(2)]
        hE2 = [C.sb(f"hE2{i}", [128, 512], F32) for i in range(2)]
        st = [hA[0], hB[0]]
        qpo = C.sb("qpo", [128, 8, 512], BF16)
        kpo = C.sb("kpo", [128, 8, 512], BF16)
        kppo = C.sb("kppo", [128, 8, 512], BF16)
        dco = C.sb("dco", [128, 8, 16], F32)
        vht = C.sb("vht", [128, 4, 512], BF16)
        ght = C.sb("ght", [128, 4, 512], BF16)
        ato = C.sb("ato", [128, 4, 8, 128], BF16)
        kto = C.sb("kto", [128, 4, 8, 128], BF16)
        pt = C.ps("pt", [128, 1024], BF16)
        pm = [C.ps(f"pm{i}", [128, 512]) for i in range(4)]
        pn = C.ps("pn", [128, 512])
        pa = C.ps("pa", [128, 512])
        ptk = C.ps("ptk", [128, 1024], BF16)

        r_c = R("consts")
        r_ident, r_ones = R("ident"), R("ones")

        def cdma(dst, src):
            S.dma("sp", (lambda e: [e.dma_start(out=dst, in_=src)]), 1, "c", writes=[r_c])
        cdma(gains[:, 0:8], cst["mix_norm"])
        cdma(gains[:, 8:11], cst["q_norm"])
        cdma(gains[:, 11:13], cst["kv_norm"])
        cdma(lbt[:], cst["hg_lb"])
        cdma(rmask[:], cst["rmask"])
        cdma(mF[:], cst["maskF"])
        cdma(mB[:], cst["maskB"])
        cdma(ht[:, 0, 0:128], cst["ident"])
        S.op("pool", lambda e: e.tensor_copy(ident[:], ht[:, 0, 0:128]), reads=[r_c], writes=[r_ident])
        S.op("pool", lambda e: e.memset(ones[:], 1.0), writes=[r_ones])
        lv = lbt[:].rearrange("p (d l h) -> p d l h", d=2, l=2)
        lb3 = lb[:].rearrange("p (d h) -> p d h", d=2)
        oml3 = oml[:].rearrange("p (d h) -> p d h", d=2)
        r_lb = R("lb")
        S.op("dve", lambda e: e.tensor_tensor(lb3, lv[:, :, 0, :], lv[:, :, 1, :], ALU.subtract), reads=[r_c], writes=[r_lb])
        S.op("act", lambda e: e.activation(oml[:], lb[:], AF.Sigmoid, scale=-1.0), reads=[r_lb], writes=[R("oml")])
        S.op("act", lambda e: e.activation(lb[:], lb[:], AF.Sigmoid), reads=[r_lb], writes=[r_lb])
        r_w = R("w")
        r_hA, r_hB = [R("hA0"), R("hA1")], [R("hB0"), R("hB1")]
        r_st = [r_hA[0], r_hB[0]]
        S.op("pool", lambda e: e.tensor_copy(stat[:, 15:16], gains[:, 0:1]), reads=[r_c], writes=[R("d")])
        qi = [0]
        load_w(S, w["w_in"], win, NKC, WIN_COLS, gains[:, 0:8], st, r_st, r_w, qi)
        load_w(S, w["w_uq"], wuq, 3, UQ_COLS, gains[:, 8:11], st, r_st, r_w, qi)
        load_w(S, w["w_uk"], wuk, 2, 512, gains[:, 11:13], st, r_st, r_w, qi)
        load_w(S, w["w_uv"], wuv, 2, 512, gains[:, 11:13], st, r_st, r_w, qi)

        r_ht = [R(f"ht{j}") for j in range(4)]
        r_stat = [R(f"stat{j}") for j in range(4)]
        r_junk, r_xn, r_pt = R("junk"), R("xn"), R("pt")
        r_xnT = [R(f"xnT{k}") for k in range(NKC)]
        r_pm = [R(f"pm{i}") for i in range(4)]
        r_pn, r_pa, r_ptk = R("pn"), R("pa"), R("ptk")
        r_cqT, r_ckvT, r_sqq, r_sqkv = R("cqT"), R("ckvT"), [R("sqq0"), R("sqq1")], R("sqkv")
        r_rsq, r_rskv, r_rstok = R("rsq"), R("rskv"), R("rstok")
        r_tab = R("tab")
        r_t1a, r_t2a = R("t1a"), R("t2a")
        r_t1, r_t2 = [r_t1a, r_t1a], [r_t2a, r_t2a]
        r_qout, r_knT, r_krp, r_vt, r_qh = R("qout"), R("knT"), R("krp"), R("vt"), R("qh")
        r_hC = [R("hC0"), R("hC1")]
        r_hE1, r_hE2 = [R("hE10"), R("hE11")], [R("hE20"), R("hE21")]
        r_qpo, r_kpo, r_kppo, r_dco = R("qpo"), R("kpo"), R("kppo"), R("dco")
        r_vht, r_ght, r_ato, r_kto = R("vht"), R("ght"), R("ato"), R("kto")
        S.op("pool", lambda e: e.memset(tct[:], 1.0), writes=[r_tab])
        S.op("pool", lambda e: e.memset(tst[:], 0.0), writes=[r_tab])
        pmi = [0]

        def nextpm():
            i = pmi[0] % 4
            pmi[0] += 1
            return pm[i], r_pm[i]

        def mm_fm(ps, r_ps, wsb, c0, m, xT, r_x, nkc, out_p0=0):
            for kc in range(nkc):
                S.op("pe", (lambda e, kc=kc: e.matmul(ps[out_p0:out_p0 + m, :], wsb[:, kc, c0:c0 + m], xT[:, kc, :],
                                                      start=(kc == 0), stop=(kc == nkc - 1))),
                     reads=[r_w] + r_x, writes=[r_ps])

        def mm_tm(ps, r_ps, xT, r_x, j, wsb, c0, n, nkc):
            for kc in range(nkc):
                S.op("pe", (lambda e, kc=kc: e.matmul(ps[:, 0:n], xT[:, kc, j * 128:(j + 1) * 128], wsb[:, kc, c0:c0 + n],
                                                      start=(kc == 0), stop=(kc == nkc - 1))),
                     reads=[r_w] + r_x, writes=[r_ps])

        for i in range(ntiles):
            t0 = i * 512
            S.dma("sp", (lambda e, t0=t0: [e.dma_start(out=ht[:, j, :], in_=h1_d[t0 + j * 128:t0 + (j + 1) * 128, :])
                                          for j in range(4)]), 4, "ht", writes=r_ht)
            S.dma("sp", (lambda e, t0=t0: [e.dma_start(out=tct[64:96, :], in_=cst["rope_c"][:, t0:t0 + 512]),
                                          e.dma_start(out=tst[64:96, :], in_=cst["rope_s"][:, t0:t0 + 512])]),
                  2, "tab", writes=[r_tab])
            for j in range(4):
                norm_transpose(S, ht, r_ht[j], j, stat, r_stat, junk, r_junk, xn, r_xn, pt, r_pt, ident, r_ident,
                               xnT, r_xnT)
            for c in range(3):
                ps, rp = nextpm()
                mm_fm(ps, rp, win, C_CQ + c * 128, 128, xnT, r_xnT, NKC)
                S.op("act", (lambda e, ps=ps, c=c: e.copy(cqT[:, c, :], ps[:])), reads=[rp], writes=[r_cqT])
                S.op("act", (lambda e, ps=ps, c=c: e.activation(sqq[:, c % 2, :], ps[:], AF.Square)),
                     reads=[rp], writes=[r_sqq[c % 2]])
                S.op("pe", (lambda e, c=c: e.matmul(pn[:], ones[:], sqq[:, c % 2, :], start=(c == 0), stop=(c == 2))),
                     reads=[r_ones, r_sqq[c % 2]], writes=[r_pn])
            S.op("act", lambda e: e.activation(rsq[:], pn[:], AF.Sqrt, bias=EPS, scale=1.0 / 384), reads=[r_pn], writes=[r_rsq])
            S.op("dve", lambda e: e.reciprocal(rsq[:], rsq[:]), reads=[r_rsq], writes=[r_rsq])
            for c in range(2):
                ps, rp = nextpm()
                mm_fm(ps, rp, win, C_CKV + c * 128, 128, xnT, r_xnT, NKC)
                S.op("act", (lambda e, ps=ps, c=c: e.copy(ckvT[:, c, :], ps[:])), reads=[rp], writes=[r_ckvT])
                S.op("act", (lambda e, ps=ps, c=c: e.activation(sqkv[:, c, :], ps[:], AF.Square)),
                     reads=[rp], writes=[r_sqkv])
            for c in range(2):
                S.op("pe", (lambda e, c=c: e.matmul(pn[:], ones[:], sqkv[:, c, :], start=(c == 0), stop=(c == 1))),
                     reads=[r_ones, r_sqkv], writes=[r_pn])
            S.op("act", lambda e: e.activation(rskv[:], pn[:], AF.Sqrt, bias=EPS, scale=1.0 / 256), reads=[r_pn], writes=[r_rskv])
            S.op("dve", lambda e: e.reciprocal(rskv[:], rskv[:]), reads=[r_rskv], writes=[r_rskv])
            for j in range(4):
                for c in range(2):
                    S.op("pe", (lambda e, j=j, c=c: e.matmul(pa[:, j:j + 1], sqkv[:, c, j * 128:(j + 1) * 128], ones[:, 0:1],
                                                             start=(c == 0), stop=(c == 1))),
                         reads=[r_ones, r_sqkv], writes=[r_pa])
            S.op("act", lambda e: e.activation(rstok[:, 0:4], pa[:, 0:4], AF.Sqrt, bias=EPS, scale=1.0 / 256),
                 reads=[r_pa], writes=[r_rstok])
            S.op("dve", lambda e: e.reciprocal(rstok[:, 0:4], rstok[:, 0:4]), reads=[r_rstok], writes=[r_rstok])
            ps, rp = nextpm()
            mm_fm(ps, rp, win, C_KR, 32, xnT, r_xnT, NKC, out_p0=64)
            ps2, rp2 = nextpm()
            mm_fm(ps2, rp2, win, C_KRS, 32, xnT, r_xnT, NKC, out_p0=64)
            S.op("dve", (lambda e, ps=ps: e.tensor_tensor(t1[0][64:96, :], ps[64:96, :], tct[64:96, :], ALU.mult)),
                 reads=[rp, r_tab], writes=[r_t1[0]])
            S.op("dve", (lambda e, ps2=ps2: e.tensor_tensor(t2[0][64:96, :], ps2[64:96, :], tst[64:96, :], ALU.mult)),
                 reads=[rp2, r_tab], writes=[r_t2[0]])
            S.op("pool", lambda e: e.tensor_tensor(krp[64:96, :], t1[0][64:96, :], t2[0][64:96, :], ALU.add),
                 reads=[r_t1[0], r_t2[0]], writes=[r_krp])
            S.dma("pool", (lambda e, t0=t0: [e.dma_start(out=scr["KTr"][:, t0:t0 + 512], in_=krp[64:96, :])]), 1, "s_krp",
                  reads=[r_krp])
            for h in range(8):
                b = h % 2
                ps, rp = nextpm()
                mm_fm(ps, rp, wuq, h * 96, 96, cqT, [r_cqT], 3)
                ps2, rp2 = nextpm()
                mm_fm(ps2, rp2, wuq, 768 + h * 32, 32, cqT, [r_cqT], 3, out_p0=64)
                S.op("dve", (lambda e, ps=ps, b=b: e.tensor_tensor(t1[b][0:96, :], ps[0:96, :], tct[0:96, :], ALU.mult)),
                     reads=[rp, r_tab], writes=[r_t1[b]])
                S.op("dve", (lambda e, ps2=ps2, b=b: e.tensor_tensor(t2[b][64:96, :], ps2[64:96, :], tst[64:96, :], ALU.mult)),
                     reads=[rp2, r_tab], writes=[r_t2[b]])
                S.op("pool", (lambda e, b=b: e.tensor_tensor(t1[b][64:96, :], t1[b][64:96, :], t2[b][64:96, :], ALU.add)),
                     reads=[r_t2[b]], writes=[r_t1[b]])
                S.op("pool", (lambda e, b=b, h=h: e.tensor_tensor(qout[0:96, h, :], t1[b][0:96, :], rsq[0:96, :], ALU.mult)),
                     reads=[r_t1[b], r_rsq], writes=[r_qout])
            S.dma("pool", (lambda e, t0=t0: [e.dma_start(out=scr["QT"][h, :, t0:t0 + 512], in_=qout[0:96, h, :])
                                            for h in range(8)]), 8, "s_q", reads=[r_qout])
            for a in range(4):
                ps, rp = nextpm()
                mm_fm(ps, rp, wuk, a * 128, 128, ckvT, [r_ckvT], 2)
                S.op("dve", (lambda e, ps=ps, a=a: e.tensor_tensor(knT[:, a, :], ps[:], rskv[:], ALU.mult)),
                     reads=[rp, r_rskv], writes=[r_knT])
            S.dma("pool", (lambda e, t0=t0: [e.dma_start(out=scr["KTn"][a * 128:(a + 1) * 128, t0:t0 + 512], in_=knT[:, a, :])
                                            for a in range(4)]), 4, "s_kn", reads=[r_knT])
            for j in range(4):
                ps, rp = nextpm()
                mm_tm(ps, rp, ckvT, [r_ckvT], j, wuv, 0, 512, 2)
                S.op("act", (lambda e, ps=ps, j=j: e.activation(vt[:, j, :], ps[:], AF.Copy, scale=rstok[:, j:j + 1])),
                     reads=[rp, r_rstok], writes=[r_vt])
            S.dma("pool", (lambda e, t0=t0: [e.dma_start(
                out=scr["VA"][:, t0 + j * 128:t0 + (j + 1) * 128, :].rearrange("h t d -> t h d"),
                in_=vt[:, j, :].rearrange("t (h d) -> t h d", h=8)) for j in range(4)]), 4, "s_v", reads=[r_vt])
            for h in range(4):
                ps, rp = nextpm()
                mm_fm(ps, rp, win, C_HQ + h * 128, 128, xnT, r_xnT, NKC)
                S.op("act", (lambda e, ps=ps, h=h: e.activation(qh[:, h, :], ps[:], AF.Silu)), reads=[rp], writes=[r_qh])
            for j in range(4):
                ps, rp = nextpm()
                mm_tm(ps, rp, xnT, r_xnT, j, win, C_HI, 512, NKC)
                S.op("act", (lambda e, ps=ps, j=j: e.copy(vht[:, j, :], ps[:])), reads=[rp], writes=[r_vht])
                ps, rp = nextpm()
                mm_tm(ps, rp, xnT, r_xnT, j, win, C_HG, 512, NKC)
                S.op("act", (lambda e, ps=ps, j=j: e.activation(ght[:, j, :], ps[:], AF.Silu)), reads=[rp], writes=[r_ght])
            S.dma("pool", (lambda e, t0=t0: [
                e.dma_start(out=scr["VH"][t0:t0 + 512, :].rearrange("(j p) c -> p j c", p=128), in_=vht[:]),
                e.dma_start(out=scr["GH"][t0:t0 + 512, :].rearrange("(j p) c -> p j c", p=128), in_=ght[:])]),
                2, "s_vg", reads=[r_vht, r_ght])
            for d in range(2):
                for h in range(4):
                    hd = d * 4 + h
                    b = hd % 2
                    A, B, Cc, E1, E2 = hA[b], hB[b], hC[b], hE1[b], hE2[b]
                    rA, rB, rC, rE1, rE2 = r_hA[b], r_hB[b], r_hC[b], r_hE1[b], r_hE2[b]
                    ps, rp = nextpm()
                    mm_fm(ps, rp, win, (C_HFF if d == 0 else C_HFB) + h * 128, 128, xnT, r_xnT, NKC)
                    lbs, omls = lb[:, hd:hd + 1], oml[:, hd:hd + 1]
                    S.op("act", (lambda e, ps=ps, A=A: e.activation(A[:], ps[:], AF.Sigmoid)), reads=[rp], writes=[rA])
                    S.op("act", (lambda e, ps=ps, B=B: e.activation(B[:], ps[:], AF.Sigmoid, scale=-1.0)), reads=[rp], writes=[rB])
                    S.op("pool", (lambda e, B=B, omls=omls: e.tensor_scalar(B[:], B[:], omls, 0.0, ALU.mult, ALU.add)),
                         reads=[r_lb], writes=[rB])
                    S.op("dve", (lambda e, A=A, omls=omls, lbs=lbs: e.tensor_scalar(A[:], A[:], omls, lbs, ALU.mult, ALU.add)),
                         reads=[r_lb], writes=[rA])
                    S.op("act", (lambda e, A=A: e.activation(A[:], A[:], AF.Ln)), writes=[rA])
                    S.op("dve", (lambda e, A=A, Cc=Cc: e.tensor_tensor_scan(Cc[:], rmask[:], A[:], 0.0, ALU.mult, ALU.add)),
                         reads=[rA, r_c], writes=[rC])
                    Cv = Cc[:].rearrange("p (c t) -> p c t", t=32)
                    Av = A[:].rearrange("p (c t) -> p c t", t=32)
                    if d == 0:
                        bsrc, rb = Cc, rC
                        dcol = 31
                    else:
                        S.op("pool", (lambda e, A=A, Cc=Cc: e.tensor_tensor(A[:], A[:], Cc[:], ALU.subtract)),
                             reads=[rC], writes=[rA])
                        S.op("pool", (lambda e, Av=Av, Cv=Cv: e.tensor_tensor(Av, Av, Cv[:, :, 31:32].broadcast_to([128, 16, 32]), ALU.add)),
                             reads=[rC], writes=[rA])
                        bsrc, rb = A, rA
                        dcol = 0
                    S.op("act", (lambda e, E1=E1, bsrc=bsrc: e.activation(E1[:], bsrc[:], AF.Exp)), reads=[rb], writes=[rE1])
                    S.op("act", (lambda e, E2=E2, bsrc=bsrc: e.activation(E2[:], bsrc[:], AF.Exp, scale=-1.0)), reads=[rb], writes=[rE2])
                    S.op("pool", (lambda e, E1=E1, h=h, hd=hd: e.tensor_tensor(qpo[:, hd, :], qh[:, h, :], E1[:], ALU.mult)),
                         reads=[rE1, r_qh], writes=[r_qpo])
                    S.op("dve", (lambda e, E2=E2, B=B: e.tensor_tensor(E2[:], E2[:], B[:], ALU.mult)), reads=[rB], writes=[rE2])
                    S.op("act", (lambda e, E2=E2, hd=hd: e.copy(kpo[:, hd, :], E2[:])), reads=[rE2], writes=[r_kpo])
                    E1v = E1[:].rearrange("p (c t) -> p c t", t=32)
                    E2v = E2[:].rearrange("p (c t) -> p c t", t=32)
                    S.op("dve", (lambda e, E1v=E1v, hd=hd, dcol=dcol: e.tensor_copy(dco[:, hd, :], E1v[:, :, dcol])),
                         reads=[rE1], writes=[r_dco])
                    kv = kppo[:, hd, :].rearrange("p (c t) -> p c t", t=32)
                    S.op("pool", (lambda e, E1v=E1v, E2v=E2v, kv=kv, dcol=dcol: e.tensor_tensor(
                        kv, E2v, E1v[:, :, dcol:dcol + 1].broadcast_to([128, 16, 32]), ALU.mult)),
                        reads=[rE1, rE2], writes=[r_kppo])
            nch = ntok // 32
            S.dma("pool", (lambda e, t0=t0, i=i: [
                e.dma_start(out=scr["QP"][:, :, t0:t0 + 512].rearrange("h p t -> p h t"), in_=qpo[:]),
                e.dma_start(out=scr["DC"][:, :, i * 16:(i + 1) * 16].rearrange("h p c -> p h c"), in_=dco[:])]),
                2, "s_qp", reads=[r_qpo, r_dco])
            for j in range(4):
                for d in range(2):
                    for h in range(4):
                        hd = d * 4 + h
                        S.op("pe", (lambda e, j=j, hd=hd, h=h: e.matmul(
                            pa[:, h * 128:(h + 1) * 128], kpo[:, hd, j * 128:(j + 1) * 128], qpo[:, hd, j * 128:(j + 1) * 128],
                            start=True, stop=True)), reads=[r_kpo, r_qpo], writes=[r_pa])
                    msk = (mF if d == 0 else mB)
                    S.op("dve", (lambda e, j=j, d=d, msk=msk: e.tensor_tensor(
                        ato[:, j, d * 4:(d + 1) * 4, :], pa[:].rearrange("p (h t) -> p h t", h=4),
                        msk[:].rearrange("p (o t) -> p o t", o=1).broadcast_to([128, 4, 128]), ALU.mult)),
                        reads=[r_pa, r_c], writes=[r_ato])
                for hd in range(8):
                    S.op("pe", (lambda e, j=j, hd=hd: e.transpose(ptk[:, hd * 128:(hd + 1) * 128],
                                                                   kppo[:, hd, j * 128:(j + 1) * 128], ident[:])),
                         reads=[r_kppo, r_ident], writes=[r_ptk])
                S.op("act", (lambda e, j=j: e.copy(kto[:, j, :, :], ptk[:].rearrange("p (h k) -> p h k", h=8))),
                     reads=[r_ptk], writes=[r_kto])
            S.dma("pool", (lambda e, t0=t0: [
                e.dma_start(out=scr["AT"][t0:t0 + 512, :, :].rearrange("(j p) h t -> p j h t", p=128), in_=ato[:]),
                e.dma_start(out=scr["KPT"][t0:t0 + 512, :, :].rearrange("(j p) h k -> p j h k", p=128), in_=kto[:])]),
                2, "s_at", reads=[r_ato, r_kto])
        S.barrier()
        S.emit()
    return S


def attn_phase(nc, tag, seqs, scr, xchg=None):
    S = Sched(nc, tag)
    SKMAX = max(sum(p[3] for p in sq["kp"]) for sq in seqs)
    SL = max(sq["nq"] for sq in seqs)
    scale = 96.0 ** -0.5
    with ExitStack() as es:
        C = Ctx(nc, tag, es)
        R = S.res
        kt = [C.sb(f"kt{i}", [128, SKMAX], BF16) for i in range(2)]
        vt = [C.sb(f"vt{i}", [128, SKMAX // 128, 65], BF16) for i in range(2)]
        qt = [C.sb(f"qt{i}", [128, SL], BF16) for i in range(2)]
        pT = [C.sb(f"pT{i}", [128, 1024], BF16) for i in range(3)]
        onesf = C.sb("onesf", [128, 64], F32)
        rl = C.sb("rl", [128, 512], F32)
        osb = C.sb("osb", [128, 512], F32)
        obf = [C.sb(f"obf{i}", [128, 512], BF16) for i in range(2)]
        psS = [C.ps(f"psS{i}", [128, 1024]) for i in range(2)]
        psO = [C.ps(f"psO{i}", [128, 512]) for i in range(2)]
        psB = C.ps("psB", [128, 512])
        r_kt, r_vt, r_qt = [R(), R()], [R(), R()], [R(), R()]
        r_pT, r_psS, r_psO = [R(), R(), R()], [R(), R(), R()], [R(), R()]
        r_ones, r_rl, r_osb, r_obf, r_psB = R(), R(), R(), [R(), R()], R()
        S.op("pool", lambda e: e.memset(onesf[:], 1.0), writes=[r_ones])
        for i in range(2):
            S.op("pool", (lambda e, i=i: e.memset(vt[i][:, :, 64:65], 1.0)), writes=[r_vt[i]])
        r_gath = R()
        r_g2, r_g3 = R(), R()
        r_gs = []
        if xchg is not None:
            n0 = xchg["n0"]
            r_xin = R()
            S.dma("sp", (lambda e: [e.dma_start(out=xchg["XK_in"][a_], in_=scr["KTn"][a_ * 128:(a_ + 1) * 128, 0:n0]) for a_ in range(4)]
                         + [e.dma_start(out=xchg["XR_in"][0:32, :], in_=scr["KTr"][:, 0:n0])]
                         + [e.dma_start(out=xchg["XV_in"][a_].rearrange("(hh t) d -> hh t d", hh=2), in_=scr["VA"][2 * a_:2 * a_ + 2, 0:n0, :])
                            for a_ in range(4)]), 9, "xin", writes=[r_xin])
            r_gs = [r_gath, r_g2, r_g3] + [R() for _ in range(6)]
            ccl = [(xchg["XK_in"][a_], xchg["XK_out"][a_]) for a_ in range(4)] + [(xchg["XR_in"], xchg["XR_out"])] + \
                  [(xchg["XV_in"][a_], xchg["XV_out"][a_]) for a_ in range(4)]
            for ci_, (cin, cout) in enumerate(ccl):
                S.dma("pool", (lambda e, cin=cin, cout=cout: [e.collective_compute(
                    "AllGather", ALU.bypass, replica_groups=xchg["groups"], ins=[cin.opt()], outs=[cout.opt()])]), 1, f"cc{ci_}",
                    reads=[r_xin], writes=[r_gs[ci_]], inc=1)
        heads = []
        for sq in seqs:
            for h in range(8):
                heads.append((sq, h))
        units = []
        obc = [0]
        for hi, (sq, h) in enumerate(heads):
            SK = sum(p[3] for p in sq["kp"])
            for qb in range(sq["nq"] // 512):
                for kc in range(SK // 256):
                    units.append((hi, qb, kc, SK // 256, obc[0] % 2, qb == sq["nq"] // 512 - 1))
                obc[0] += 1
        loaded = [-1]
        pending = []

        def load_head(hi):
            if hi >= len(heads) or hi <= loaded[0]:
                return
            loaded[0] = hi
            sq, h = heads[hi]
            b = hi % 2
            off = 0
            lst = []
            for (ktn, ktr, va, n) in sq["kp"]:
                lst.append((kt[b][0:64, off:off + n], ktn(h)))
                lst.append((kt[b][64:96, off:off + n], ktr))
                off += n
            dep = r_gs if sq.get("gathered") else []
            S.dma("sp", (lambda e, lst=lst: [e.dma_start(out=o, in_=i_) for o, i_ in lst]), len(lst), f"kt{b}", reads=dep, writes=[r_kt[b]])
            off = 0
            lst2 = []
            for (ktn, ktr, va, n) in sq["kp"]:
                lst2.append((vt[b][:, off // 128:(off + n) // 128, 0:64], va(h).rearrange("(c p) d -> p c d", p=128)))
                off += n
            S.dma("sp", (lambda e, lst2=lst2: [e.dma_start(out=o, in_=i_) for o, i_ in lst2]), len(lst2), f"vt{b}", reads=dep, writes=[r_vt[b]])
            q0, nq = sq["q0"], sq["nq"]
            S.dma("sp", (lambda e, b=b, h=h, q0=q0, nq=nq: [e.dma_start(out=qt[b][0:96, 0:nq], in_=scr["QT"][h, :, q0:q0 + nq])]), 1,
                  f"qt{b}", writes=[r_qt[b]])

        def qk(u):
            hi, qb, kc, nkc, ob, lastq = units[u]
            b = hi % 2
            r = u % 2
            r3 = u % 3
            for t in range(2):
                S.op("pe", (lambda e, t=t: e.matmul(psS[r][:, t * 512:(t + 1) * 512], kt[b][0:96, (2 * kc + t) * 128:(2 * kc + t + 1) * 128],
                                                    qt[b][0:96, qb * 512:(qb + 1) * 512], start=True, stop=True)),
                     reads=[r_kt[b], r_qt[b]], writes=[r_psS[r]])
            S.op("act", (lambda e: e.activation(pT[r3][:], psS[r][:], AF.Exp, scale=scale)), reads=[r_psS[r]], writes=[r_pT[r3]])

        def pv(u):
            hi, qb, kc, nkc, ob, lastq = units[u]
            b = hi % 2
            r3 = u % 3
            for t in range(2):
                S.op("pe", (lambda e, t=t: e.matmul(psO[ob][0:65, :], vt[b][:, 2 * kc + t, 0:65], pT[r3][:, t * 512:(t + 1) * 512],
                                                    start=(kc == 0 and t == 0), stop=(kc == nkc - 1 and t == 1))),
                     reads=[r_vt[b], r_pT[r3]], writes=[r_psO[ob]])
            if kc == nkc - 1:
                sq, h = heads[hi]
                S.op("dve", (lambda e: e.reciprocal(rl[64:65, :], psO[ob][64:65, :])), reads=[r_psO[ob]], writes=[r_rl])
                S.op("dve", (lambda e: e.tensor_copy(osb[0:64, :], psO[ob][0:64, :])), reads=[r_psO[ob]], writes=[r_osb])
                t0 = sq["q0"] + qb * 512

                def fin():
                    S.op("pe", (lambda e: e.matmul(psB[0:64, :], onesf[64:65, 0:64], rl[64:65, :], start=True, stop=True)),
                         reads=[r_ones, r_rl], writes=[r_psB])
                    S.op("dve", (lambda e: e.tensor_tensor(obf[ob][0:64, :], osb[0:64, :], psB[0:64, :], ALU.mult)),
                         reads=[r_osb, r_psB], writes=[r_obf[ob]])
                    S.dma("pool", (lambda e: [e.dma_start(out=scr["MIXT"][h * 64:(h + 1) * 64, t0:t0 + 512], in_=obf[ob][0:64, :])]), 1,
                          f"so{ob}", reads=[r_obf[ob]])
                pending.append([4, fin])

        load_head(0)
        load_head(1)
        n = len(units)
        qk(0)
        for u in range(n):
            if u + 1 < n:
                qk(u + 1)
            for pnd in list(pending):
                pnd[0] -= 1
                if pnd[0] <= 0:
                    pnd[1]()
                    pending.remove(pnd)
            pv(u)
            hi, qb, kc, nkc, ob_, lastq = units[u]
            if lastq and kc == nkc - 1:
                load_head(hi + 2)
        for pnd in pending:
            pnd[1]()
        S.barrier()
        S.emit()
    return S


def scan_phase(nc, tag, seq_lens, ntok, scr, xchg=None, cst=None):
    S = Sched(nc, tag)
    nch = ntok // 32
    with ExitStack() as es:
        C = Ctx(nc, tag, es)
        R = S.res
        NR = 3
        qpb = [[C.sb(f"qpb{d}{i}", [128, 4, 128], BF16) for i in range(NR)] for d in range(2)]
        atb = [[C.sb(f"atb{d}{i}", [128, 4, 128], BF16) for i in range(NR)] for d in range(2)]
        kpb = [[C.sb(f"kpb{d}{i}", [128, 4, 128], BF16) for i in range(NR)] for d in range(2)]
        vb = [[C.sb(f"vb{d}{i}", [128, 512], BF16) for i in range(NR)] for d in range(2)]
        r_ld = [[R() for i in range(NR)] for d in range(2)]
        dct = C.sb("dct", [128, 8, nch], F32)
        zer = C.sb("zer", [128, 512], BF16)
        SstAll = C.sb("SstAll", [128, 8, 129], F32)
        Sst = [SstAll[:, hd, 0:128] for hd in range(8)]
        Sbf = [C.sb(f"Sbf{hd}", [128, 128], BF16) for hd in range(8)]
        r_S = [R() for hd in range(8)]
        r_Sb = [R() for hd in range(8)]
        ot = [[C.sb(f"ot{d}{i}", [128, 512], F32) for i in range(2)] for d in range(2)]
        r_ot = [[R() for i in range(2)] for d in range(2)]
        psO = [[C.ps(f"psO{d}{i}", [128, 512]) for i in range(2)] for d in range(2)]
        r_psO = [[R() for i in range(2)] for d in range(2)]
        psU = [C.ps(f"psU{i}", [128, 512]) for i in range(4)]
        r_psU = [R() for i in range(4)]
        r_dc, r_z = R(), R()
        S.dma("sp", (lambda e: [e.dma_start(out=dct[:, hd, :], in_=scr["DC"][hd, :, :]) for hd in range(8)]), 8, "dc", writes=[r_dc])
        S.op("pool", lambda e: e.memset(zer[:], 0.0), writes=[r_z])
        ui = [0]
        offs = []
        a = 0
        for n in seq_lens:
            offs.append(a)
            a += n

        def run_seq(s0, SLs, mode, zero_init=True):
            NB = SLs // 128
            if zero_init:
                for hd in range(8):
                    S.op("pool", (lambda e, hd=hd: e.memset(Sst[hd], 0.0)), writes=[r_S[hd]])
                    S.op("pool", (lambda e, hd=hd: e.memset(Sbf[hd][:], 0.0)), writes=[r_Sb[hd]])

            def load(step):
                if step >= NB:
                    return
                for d in range(2):
                    blk = step if d == 0 else NB - 1 - step
                    t0 = s0 + blk * 128
                    i = step % NR
                    if mode == "full":
                        S.dma("sp", (lambda e, d=d, i=i, t0=t0: [
                            e.dma_start(out=qpb[d][i][:], in_=scr["QP"][d * 4:(d + 1) * 4, :, t0:t0 + 128].rearrange("h p t -> p h t")),
                            e.dma_start(out=atb[d][i][:], in_=scr["AT"][t0:t0 + 128, d * 4:(d + 1) * 4, :]),
                            e.dma_start(out=kpb[d][i][:], in_=scr["KPT"][t0:t0 + 128, d * 4:(d + 1) * 4, :]),
                            e.dma_start(out=vb[d][i][:], in_=scr["VH"][t0:t0 + 128, :])]), 4, f"ld{d}{i}", writes=[r_ld[d][i]])
                    else:
                        S.dma("sp", (lambda e, d=d, i=i, t0=t0: [
                            e.dma_start(out=kpb[d][i][:], in_=scr["KPT"][t0:t0 + 128, d * 4:(d + 1) * 4, :]),
                            e.dma_start(out=vb[d][i][:], in_=scr["VH"][t0:t0 + 128, :])]), 2, f"ld{d}{i}", writes=[r_ld[d][i]])
            load(0)
            load(1)
            for step in range(NB):
                load(step + 2)
                i = step % NR
                ob = step % 2
                if mode == "full":
                    for d in range(2):
                        S.op("pe", (lambda e, d=d, ob=ob: e.matmul(psO[d][ob][:], zer[:, 0:128], zer[:], start=True, stop=False,
                                                                   skip_group_check=True)), reads=[r_z], writes=[r_psO[d][ob]])
                        for h in range(4):
                            S.op("pe", (lambda e, d=d, ob=ob, h=h, i=i: e.matmul(
                                psO[d][ob][:, h * 128:(h + 1) * 128], atb[d][i][:, h, :], vb[d][i][:, h * 128:(h + 1) * 128],
                                start=False, stop=False, skip_group_check=True)), reads=[r_ld[d][i]], writes=[r_psO[d][ob]])
                for ci in range(4):
                    for d in range(2):
                        blk = step if d == 0 else NB - 1 - step
                        c = ci if d == 0 else 3 - ci
                        gch = (s0 + blk * 128) // 32 + c
                        for h in range(4):
                            hd = d * 4 + h
                            if mode == "full":
                                S.op("pe", (lambda e, d=d, ob=ob, h=h, i=i, c=c, hd=hd: e.matmul(
                                    psO[d][ob][32 * c:32 * c + 32, h * 128:(h + 1) * 128], qpb[d][i][:, h, 32 * c:32 * c + 32], Sbf[hd][:],
                                    start=False, stop=(ci == 3), skip_group_check=True, tile_position=(0, 32 * c))),
                                    reads=[r_ld[d][i], r_Sb[hd]], writes=[r_psO[d][ob]])
                            pu = ui[0] % 4
                            ui[0] += 1
                            S.op("pe", (lambda e, d=d, h=h, i=i, c=c, pu=pu: e.matmul(
                                psU[pu][:, 0:128], kpb[d][i][32 * c:32 * c + 32, h, :], vb[d][i][32 * c:32 * c + 32, h * 128:(h + 1) * 128],
                                start=True, stop=True, tile_position=(32 * c, 0))),
                                reads=[r_ld[d][i]], writes=[r_psU[pu]])
                            S.op("dve", (lambda e, hd=hd, pu=pu, gch=gch: e.scalar_tensor_tensor(
                                Sst[hd], Sst[hd], dct[:, hd, gch:gch + 1], psU[pu][:, 0:128], ALU.mult, ALU.add)),
                                reads=[r_psU[pu], r_dc], writes=[r_S[hd]])
                            if mode == "full":
                                S.op("act", (lambda e, hd=hd: e.copy(Sbf[hd][:], Sst[hd])), reads=[r_S[hd]], writes=[r_Sb[hd]])
                if mode == "full":
                    for d in range(2):
                        blk = step if d == 0 else NB - 1 - step
                        t0 = s0 + blk * 128
                        S.op("dve" if d == 0 else "act",
                             (lambda e, d=d, ob=ob: (e.tensor_copy(ot[d][ob][:], psO[d][ob][:]) if d == 0
                                                     else e.copy(ot[d][ob][:], psO[d][ob][:]))),
                             reads=[r_psO[d][ob]], writes=[r_ot[d][ob]])
                        S.dma("pool", (lambda e, d=d, ob=ob, t0=t0: [e.dma_start(out=scr["OF"][d, t0:t0 + 128, :], in_=ot[d][ob][:])]), 1,
                              f"so{d}{ob}", reads=[r_ot[d][ob]])

        if xchg is None:
            for s0, n in zip(offs, seq_lens):
                run_seq(s0, n, "full")
        else:
            n0 = seq_lens[0]
            G = C.sb("G", [128, 4, 8, 129], F32)
            rkm = C.sb("rkm", [128, 8], F32)
            tmp = C.sb("tmp", [128, 128], F32)
            r_G, r_rk, r_tmp, r_xs = R(), R(), R(), R()
            S.dma("sp", lambda e: [e.dma_start(out=rkm[:], in_=cst["rankmask"])], 1, "rk", writes=[r_rk])
            run_seq(offs[0], n0, "state")
            for hd in range(8):
                S.op("dve", (lambda e, hd=hd: e.tensor_reduce(SstAll[:, hd, 128:129], dct[:, hd, offs[0] // 32:(offs[0] + n0) // 32],
                                                              AX.X, ALU.mult)), reads=[r_dc], writes=[r_S[hd]])
            S.dma("pool", (lambda e: [e.dma_start(out=xchg["XS_in"].rearrange("(h p) c -> p h c", p=128), in_=SstAll[:])]), 1, "xs",
                  reads=r_S, writes=[r_xs])
            S.dma("pool", (lambda e: [e.collective_compute("AllGather", ALU.bypass, replica_groups=xchg["groups"],
                                                           ins=[xchg["XS_in"].opt()], outs=[xchg["XS_out"].opt()])]), 1, "ccs", reads=[r_xs], writes=[r_G], inc=1)
            for s0, n in list(zip(offs, seq_lens))[1:]:
                run_seq(s0, n, "full")
            S.dma("sp", (lambda e: [e.dma_start(out=G[:], in_=xchg["XS_out"].rearrange("(r h p) c -> p r h c", r=4, h=8))]), 1, "g",
                  reads=[r_G], writes=[r_G])
            for hd in range(8):
                S.op("pool", (lambda e, hd=hd: e.memset(Sst[hd], 0.0)), writes=[r_S[hd]])
                order = range(4) if hd < 4 else range(3, -1, -1)
                for i in order:
                    mcol = rkm[:, (0 if hd < 4 else 4) + i:(0 if hd < 4 else 4) + i + 1]
                    S.op("dve", (lambda e, hd=hd, i=i: e.scalar_tensor_tensor(tmp[:], Sst[hd], G[:, i, hd, 128:129], G[:, i, hd, 0:128],
                                                                            ALU.mult, ALU.add)), reads=[r_G, r_S[hd]], writes=[r_tmp])
                    S.op("dve", (lambda e, hd=hd: e.tensor_tensor(tmp[:], tmp[:], Sst[hd], ALU.subtract)), reads=[r_S[hd]], writes=[r_tmp])
                    S.op("dve", (lambda e, hd=hd, mcol=mcol: e.scalar_tensor_tensor(Sst[hd], tmp[:], mcol, Sst[hd], ALU.mult, ALU.add)),
                         reads=[r_tmp, r_rk], writes=[r_S[hd]])
                S.op("act", (lambda e, hd=hd: e.copy(Sbf[hd][:], Sst[hd])), reads=[r_S[hd]], writes=[r_Sb[hd]])
            run_seq(offs[0], n0, "full", zero_init=False)
        S.barrier()
        S.emit()
    return S


def outproj_phase(nc, tag, ntok, h1_d, h2_d, cst, w, scr):
    S = Sched(nc, tag)
    ntiles = ntok // 512
    with ExitStack() as es:
        C = Ctx(nc, tag, es)
        R = S.res
        wo = C.sb("wo", [128, 8, D], BF16)
        st = [C.sb("st0", [128, 512], F32), C.sb("st1", [128, 512], F32)]
        ident = C.sb("ident", [128, 128], BF16)
        onb = C.sb("onb", [128, 512], F32)
        ht = [C.sb(f"ht{i}", [128, 4, D], F32) for i in range(2)]
        mT = [C.sb(f"mT{i}", [128, 8, 512], BF16) for i in range(2)]
        of = [C.sb(f"of{i}", [128, 4, 512], F32) for i in range(2)]
        ob = [C.sb(f"ob{i}", [128, 4, 512], F32) for i in range(2)]
        gh = [C.sb(f"gh{i}", [128, 4, 512], BF16) for i in range(2)]
        osum4 = C.sb("osum4", [128, 4, 512], F32)
        junk = C.sb("junk", [128, 128], BF16)
        stat4 = C.sb("stat4", [128, 40], F32)
        mh4 = C.sb("mh4", [128, 4, 512], BF16)
        pt = C.ps("pt", [128, 1024], BF16)
        pt2 = C.ps("pt2", [128, 1024], BF16)
        r_pts = [R(), R()]
        po = [C.ps(f"po{i}", [128, 512]) for i in range(2)]
        r_w, r_st, r_c, r_ident = R(), [R(), R()], R(), R()
        S.dma("sp", lambda e: [e.dma_start(out=onb[:], in_=cst["hg_norm_b"]), e.dma_start(out=st[0][:, 0:128], in_=cst["ident"])],
              2, "c", writes=[r_c, r_st[0]])
        S.op("pool", lambda e: e.tensor_copy(ident[:], st[0][:, 0:128]), reads=[r_st[0]], writes=[r_ident])
        qi = [1]
        load_w(S, w["w_o"], wo, 8, D, None, st, r_st, r_w, qi)
        r_ld = [R(), R()]
        r_ht = [[R() for j in range(4)] for i in range(2)]
        r_mT = [[R() for j in range(4)] for i in range(2)]
        r_osum, r_junk, r_stat, r_mh, r_pt, r_po = R(), R(), R(), R(), R(), [R(), R()]

        def load(i):
            if i >= ntiles:
                return
            b = i % 2
            t0 = i * 512
            S.dma("sp", (lambda e: [e.dma_start(out=ht[b][:, j, :], in_=h1_d[t0 + j * 128:t0 + (j + 1) * 128, :]) for j in range(4)]),
                  4, f"ht{b}", writes=r_ht[b])
            S.dma("sp", (lambda e: [
                e.dma_start(out=mT[b][:, 0:4, :], in_=scr["MIXT"][0:512, t0:t0 + 512].rearrange("(c p) t -> p c t", p=128)),
                e.dma_start(out=of[b][:], in_=scr["OF"][0, t0:t0 + 512, :].rearrange("(j p) c -> p j c", p=128)),
                e.dma_start(out=ob[b][:], in_=scr["OF"][1, t0:t0 + 512, :].rearrange("(j p) c -> p j c", p=128)),
                e.dma_start(out=gh[b][:], in_=scr["GH"][t0:t0 + 512, :].rearrange("(j p) c -> p j c", p=128))]),
                4, f"ld{b}", writes=[r_ld[b]] + r_mT[b])
        load(0)

        def body(i):
            b = i % 2
            load(i + 1)
            t0 = i * 512
            r_os = [R() for j in range(4)]
            r_mhj = [R() for j in range(4)]
            r_stj = [R() for j in range(4)]
            r_jk = [Res("junk") for _ in range(16)]
            for j in range(4):
                S.op("dve" if j % 2 == 0 else "pool",
                     (lambda e, j=j: e.tensor_tensor(osum4[:, j, :], of[b][:, j, :], ob[b][:, j, :], ALU.add)),
                     reads=[r_ld[b], r_osum], writes=[r_os[j]])
            for j in range(4):
                for h in range(4):
                    S.op("act", (lambda e, h=h, j=j: e.activation(junk[:], osum4[:, j, h * 128:(h + 1) * 128], AF.Square,
                                                                  accum_out=stat4[:, j * 8 + h:j * 8 + h + 1])),
                         reads=[r_os[j]], writes=[r_jk[j * 4 + h], r_stj[j]])
            for j in range(4):
                S.op("act", (lambda e, j=j: e.activation(stat4[:, j * 8 + 4:j * 8 + 8], stat4[:, j * 8:j * 8 + 4], AF.Sqrt, bias=EPS, scale=1.0 / 128)),
                     reads=[r_stj[j]], writes=[r_stj[j]])
            for j in range(4):
                S.op("dve", (lambda e, j=j: e.reciprocal(stat4[:, j * 8 + 4:j * 8 + 8], stat4[:, j * 8 + 4:j * 8 + 8])), reads=[r_stj[j]], writes=[r_stj[j]])
            for j in range(4):
                ov = osum4[:, j, :].rearrange("p (h v) -> p h v", h=4)
                S.op("dve", (lambda e, ov=ov, j=j: e.tensor_tensor(
                    ov, ov, stat4[:, j * 8 + 4:j * 8 + 8].rearrange("p (h o) -> p h o", o=1).broadcast_to([128, 4, 128]), ALU.mult)),
                    reads=[r_stj[j]], writes=[r_os[j]])
            for j in range(4):
                S.op("pool", (lambda e, j=j: e.tensor_tensor(osum4[:, j, :], osum4[:, j, :], onb[:], ALU.mult)), reads=[r_c], writes=[r_os[j]])
            for j in range(4):
                S.op("dve", (lambda e, j=j: e.tensor_tensor(mh4[:, j, :], osum4[:, j, :], gh[b][:, j, :], ALU.mult)),
                     reads=[r_os[j], r_ld[b], r_mh], writes=[r_mhj[j]])
            for j in range(4):
                ptj, r_ptj = [pt, pt2][j % 2], r_pts[j % 2]
                for c in range(4):
                    S.op("pe", (lambda e, c=c, j=j, ptj=ptj: e.transpose(ptj[:, c * 128:(c + 1) * 128], mh4[:, j, c * 128:(c + 1) * 128], ident[:])),
                         reads=[r_mhj[j], r_ident], writes=[r_ptj])
                S.op("act", (lambda e, j=j, ptj=ptj: e.copy(mT[b][:, 4:8, j * 128:(j + 1) * 128],
                                                            ptj[:, 0:512].rearrange("p (k t) -> p k t", k=4))),
                     reads=[r_ptj], writes=[r_mT[b][j]])
            S.op("pool", (lambda e: e.memset(stat4[:, 32:33], 0.0)), writes=r_os + r_mhj + [r_osum, r_mh])
            for j in range(4):
                for hh in range(2):
                    pb = (j * 2 + hh) % 2
                    for kc in range(8):
                        S.op("pe", (lambda e, j=j, hh=hh, kc=kc, pb=pb: e.matmul(
                            po[pb][:], mT[b][:, kc, j * 128:(j + 1) * 128], wo[:, kc, hh * 512:(hh + 1) * 512],
                            start=(kc == 0), stop=(kc == 7))), reads=[r_w, r_mT[b][j]], writes=[r_po[pb]])
                    dst = ht[b][:, j, hh * 512:(hh + 1) * 512]
                    S.op("dve", (lambda e, dst=dst, pb=pb: e.tensor_tensor(dst, dst, po[pb][:], ALU.add)),
                         reads=[r_po[pb]], writes=[r_ht[b][j]])
                S.dma("pool", (lambda e, j=j: [e.dma_start(out=h2_d[t0 + j * 128:t0 + (j + 1) * 128, :], in_=ht[b][:, j, :])]), 1,
                      f"so{b}{j}", reads=[r_ht[b][j]])
        for i in range(ntiles):
            body(i)
        S.barrier()
        S.emit()
    return S


def ple_phase(nc, tag, ntok, h3_d, p_d, y_d, cst, w):
    S = Sched(nc, tag)
    ntiles = ntok // 512
    with ExitStack() as es:
        C = Ctx(nc, tag, es)
        R = S.res
        wg = C.sb("wg", [128, 8, D], BF16)
        wp = C.sb("wp", [128, 2, D], BF16)
        st = [C.sb("st0", [128, 512], F32), C.sb("st1", [128, 512], F32)]
        ident = C.sb("ident", [128, 128], BF16)
        gain = C.sb("gain", [128, 8], F32)
        fnb = C.sb("fnb", [128, D], F32)
        ht = [C.sb(f"ht{i}", [128, 4, D], F32) for i in range(2)]
        ptl = [C.sb(f"ptl{i}", [128, 4, 256], F32) for i in range(2)]
        xn4 = C.sb("xn4", [128, 4, D], BF16)
        junk = C.sb("junk", [128, D], BF16)
        stat = C.sb("stat", [128, 16], F32)
        xnT = C.sb("xnT", [128, 8, 512], BF16)
        pb16 = C.sb("pb16", [128, 4, 256], BF16)
        pT = C.sb("pT", [128, 2, 512], BF16)
        gsb = [C.sb(f"gsb{i}", [128, 512], F32) for i in range(2)]
        pt = C.ps("pt", [128, 1024], BF16)
        pt2 = C.ps("pt2", [128, 1024], BF16)
        pg = [C.ps(f"pg{i}", [128, 512]) for i in range(2)]
        pp = [C.ps(f"pp{i}", [128, 512]) for i in range(2)]
        r_w, r_st, r_c, r_ident = R(), [R(), R()], R(), R()
        S.dma("sp", lambda e: [e.dma_start(out=gain[:], in_=cst["ple_norm"]), e.dma_start(out=fnb[:], in_=cst["final_norm_b"]),
                               e.dma_start(out=st[0][:, 0:128], in_=cst["ident"])], 3, "c", writes=[r_c, r_st[0]])
        S.op("pool", lambda e: e.tensor_copy(ident[:], st[0][:, 0:128]), reads=[r_st[0]], writes=[r_ident])
        S.op("pool", lambda e: e.tensor_copy(stat[:, 15:16], gain[:, 0:1]), reads=[r_c], writes=[R()])
        qi = [1]
        load_w(S, w["w_ple_gate"], wg, 8, D, gain, st, r_st, r_w, qi)
        load_w(S, w["w_ple_proj"], wp, 2, D, None, st, r_st, r_w, qi)
        r_ht = [[R() for j in range(4)] for i in range(2)]
        r_pl = [R(), R()]
        r_stat = [R() for j in range(4)]
        r_junk, r_pt = R(), R()
        r_xn4 = [R() for j in range(4)]
        r_pts = [R(), R()]
        r_xnT = [R() for k in range(8)]
        r_pb16, r_pT, r_gsb, r_pg, r_pp = [R() for j in range(4)], R(), [R(), R()], [R(), R()], [R(), R()]

        def load(i):
            if i >= ntiles:
                return
            b = i % 2
            t0 = i * 512
            S.dma("sp", (lambda e: [e.dma_start(out=ht[b][:, j, :], in_=h3_d[t0 + j * 128:t0 + (j + 1) * 128, :]) for j in range(4)]),
                  4, f"ht{b}", writes=r_ht[b])
            S.dma("sp", (lambda e: [e.dma_start(out=ptl[b][:], in_=p_d[t0:t0 + 512, :].rearrange("(j p) c -> p j c", p=128))]),
                  1, f"pl{b}", writes=[r_pl[b]])
        load(0)

        def body(i):
            b = i % 2
            load(i + 1)
            t0 = i * 512
            norm_transpose4(S, ht[b], r_ht[b], stat, r_stat, junk, xn4, r_xn4, [pt, pt2], r_pts, ident, r_ident, xnT, r_xnT)
            for j in range(4):
                S.op("pool", (lambda e, j=j: e.tensor_copy(pb16[:, j, :], ptl[b][:, j, :])), reads=[r_pl[b]], writes=[r_pb16[j]])
            for j in range(4):
                ptj, r_ptj = [pt, pt2][j % 2], r_pts[j % 2]
                for c in range(2):
                    S.op("pe", (lambda e, c=c, j=j, ptj=ptj: e.transpose(ptj[:, c * 128:(c + 1) * 128], pb16[:, j, c * 128:(c + 1) * 128], ident[:])),
                         reads=[r_pb16[j], r_ident], writes=[r_ptj])
                S.op("act", (lambda e, j=j, ptj=ptj: e.copy(pT[:, :, j * 128:(j + 1) * 128], ptj[:, 0:256].rearrange("p (k t) -> p k t", k=2))),
                     reads=[r_ptj], writes=[r_pT])
            for j in range(4):
                for hh in range(2):
                    k2 = (j * 2 + hh) % 2
                    for kc in range(8):
                        S.op("pe", (lambda e, j=j, hh=hh, kc=kc, k2=k2: e.matmul(
                            pg[k2][:], xnT[:, kc, j * 128:(j + 1) * 128], wg[:, kc, hh * 512:(hh + 1) * 512],
                            start=(kc == 0), stop=(kc == 7))), reads=[r_w] + r_xnT, writes=[r_pg[k2]])
                    for kc in range(2):
                        S.op("pe", (lambda e, j=j, hh=hh, kc=kc, k2=k2: e.matmul(
                            pp[k2][:], pT[:, kc, j * 128:(j + 1) * 128], wp[:, kc, hh * 512:(hh + 1) * 512],
                            start=(kc == 0), stop=(kc == 1))), reads=[r_w, r_pT], writes=[r_pp[k2]])
                    S.op("act", (lambda e, k2=k2: e.activation(gsb[k2][:], pg[k2][:], AF.Sigmoid)), reads=[r_pg[k2]], writes=[r_gsb[k2]])
                    S.op("dve", (lambda e, k2=k2: e.tensor_tensor(gsb[k2][:], gsb[k2][:], pp[k2][:], ALU.mult)),
                         reads=[r_pp[k2]], writes=[r_gsb[k2]])
                    dst = ht[b][:, j, hh * 512:(hh + 1) * 512]
                    S.op("pool", (lambda e, dst=dst, k2=k2: e.tensor_tensor(dst, dst, gsb[k2][:], ALU.add)),
                         reads=[r_gsb[k2]], writes=[r_ht[b][j]])
            r_j2 = [Res("junk") for _ in range(4)]
            for j in range(4):
                S.op("act", (lambda e, j=j: e.activation(junk[:], ht[b][:, j, :], AF.Square, accum_out=stat[:, 8 + j:9 + j])),
                     reads=[r_ht[b][j]], writes=[r_j2[j], r_stat[j]])
            for j in range(4):
                S.op("act", (lambda e, j=j: e.activation(stat[:, 8 + j:9 + j], stat[:, 8 + j:9 + j], AF.Sqrt, bias=EPS, scale=1.0 / D)),
                     writes=[r_stat[j]])
            for j in range(4):
                S.op("dve", (lambda e, j=j: e.reciprocal(stat[:, 8 + j:9 + j], stat[:, 8 + j:9 + j])), writes=[r_stat[j]])
            for j in range(4):
                S.op("dve", (lambda e, j=j: e.scalar_tensor_tensor(ht[b][:, j, :], ht[b][:, j, :], stat[:, 8 + j:9 + j], fnb[:], ALU.mult, ALU.mult)),
                     reads=[r_stat[j], r_c], writes=[r_ht[b][j]])
                S.dma("pool", (lambda e, j=j: [e.dma_start(out=y_d[t0 + j * 128:t0 + (j + 1) * 128, :], in_=ht[b][:, j, :])]), 1,
                      f"so{b}{j}", reads=[r_ht[b][j]])
        for i in range(ntiles):
            body(i)
        S.barrier()
        S.emit()
    return S


CONST_SHAPES = {
    "ident": [128, 128], "rmask": [128, 512], "maskF": [128, 128], "maskB": [128, 128],
    "ffn1_norm": [128, 8], "mix_norm": [128, 8], "q_norm": [128, 3], "kv_norm": [128, 2], "hg_lb": [128, 16],
    "hg_norm_b": [128, 512], "ffn2_norm": [128, 8], "ple_norm": [128, 8], "final_norm_b": [128, D],
}
W_SHAPES = {
    "ffn1_wg": [D, DFF], "ffn1_wu": [D, DFF], "ffn1_wd": [DFF, D], "w_in": [D, WIN_COLS], "w_uq": [384, UQ_COLS],
    "w_uk": [256, 512], "w_uv": [256, 512], "w_o": [D, D], "ffn2_wg": [D, DFF], "ffn2_wu": [D, DFF], "ffn2_wd": [DFF, D],
    "w_ple_gate": [D, D], "w_ple_proj": [256, D],
}


def build_program(seq_lens, dbg=(), phases=None, gather=False):
    Sched.DMA_SEMS = {}
    Sched.DMA_CNT = {}
    nc = bass.Bass("TRN2", target_bir_lowering=False)
    ntok = sum(seq_lens)
    ntiles = ntok // 512

    def inp(name, shape, dt=F32):
        return nc.dram_tensor(name, shape, dt, kind="ExternalInput").ap()

    def scratch(name, shape, dt):
        kind = "ExternalOutput" if name in dbg else "Internal"
        return nc.dram_tensor(name, shape, dt, kind=kind).ap()

    x = inp("x", [ntok, D])
    p = inp("p", [ntok, 256])
    cst = {k: inp(k, v) for k, v in CONST_SHAPES.items()}
    cst["rope_c"] = inp("rope_c", [32, ntok])
    cst["rope_s"] = inp("rope_s", [32, ntok])
    w = {k: inp(k, v) for k, v in W_SHAPES.items()}
    y = nc.dram_tensor("y", [ntok, D], F32, kind="ExternalOutput").ap()
    h1 = scratch("h1", [ntok, D], F32)
    h2 = scratch("h2", [ntok, D], F32)
    h3 = scratch("h3", [ntok, D], F32)
    scr = {
        "QT": scratch("QT", [8, 96, ntok], BF16), "KTn": scratch("KTn", [512, ntok], BF16),
        "KTr": scratch("KTr", [32, ntok], BF16), "VA": scratch("VA", [8, ntok, 64], BF16),
        "QP": scratch("QP", [8, 128, ntok], BF16), "DC": scratch("DC", [8, 128, ntok // 32], F32),
        "VH": scratch("VH", [ntok, 512], BF16), "GH": scratch("GH", [ntok, 512], BF16),
        "AT": scratch("AT", [ntok, 8, 128], BF16), "KPT": scratch("KPT", [ntok, 8, 128], BF16),
        "MIXT": scratch("MIXT", [512, ntok], BF16), "OF": scratch("OF", [2, ntok, 512], F32),
    }
    def on(k):
        return phases is None or k in phases
    if on("f1"):
        ffn_phase(nc, "f1", x, h1, ntiles, cst["ffn1_norm"], w["ffn1_wg"], w["ffn1_wu"], w["ffn1_wd"], cst["ident"])
    if on("mi"):
        mixer_phase2(nc, "ma", "a", ntok, h1, cst, w, scr)
        mixer_phase2(nc, "mb", "b", ntok, h1, cst, w, scr)
    seqs = []
    a = 0
    for SLs in seq_lens:
        b = a + SLs
        seqs.append(dict(q0=a, nq=SLs, kp=[((lambda h, a=a, b=b: scr["KTn"][h * 64:(h + 1) * 64, a:b]), scr["KTr"][:, a:b],
                                            (lambda h, a=a, b=b: scr["VA"][h, a:b, :]), SLs)]))
        a = b
    xchg = None
    if gather:
        n0 = seq_lens[0]
        cst["rankmask"] = inp("rankmask", [128, 8])
        xchg = dict(n0=n0, groups=[[0, 1, 2, 3], [4, 5, 6, 7]],
                    XK_in=[scratch(f"XK_in{a_}", [128, n0], BF16) for a_ in range(4)],
                    XK_out=[scratch(f"XK_out{a_}", [4 * 128, n0], BF16) for a_ in range(4)],
                    XR_in=scratch("XR_in", [128, n0], BF16), XR_out=scratch("XR_out", [4 * 128, n0], BF16),
                    XV_in=[scratch(f"XV_in{a_}", [2 * n0, 64], BF16) for a_ in range(4)],
                    XV_out=[scratch(f"XV_out{a_}", [4 * 2 * n0, 64], BF16) for a_ in range(4)],
                    XS_in=scratch("XS_in", [1024, 129], F32), XS_out=scratch("XS_out", [4096, 129], F32))
        kp = []
        for r in range(4):
            kp.append(((lambda h, r=r: xchg["XK_out"][h // 2][r * 128 + (h % 2) * 64:r * 128 + (h % 2) * 64 + 64, :]),
                       xchg["XR_out"][r * 128:r * 128 + 32, :],
                       (lambda h, r=r: xchg["XV_out"][h // 2].rearrange("(r hh t) d -> r hh t d", r=4, hh=2)[r, h % 2]),
                       n0))
        seqs[0] = dict(q0=0, nq=n0, kp=kp, gathered=True)
        seqs = seqs[1:] + seqs[0:1]
    if on("at"):
        attn_phase(nc, "at", seqs, scr, xchg=xchg)
    if on("sc"):
        scan_phase(nc, "sc", list(seq_lens), ntok, scr, xchg=xchg, cst=cst)
    if on("op"):
        outproj_phase(nc, "op", ntok, h1, h2, cst, w, scr)
    if on("f2"):
        ffn_phase(nc, "f2", h2, h3, ntiles, cst["ffn2_norm"], w["ffn2_wg"], w["ffn2_wu"], w["ffn2_wd"], cst["ident"])
    if on("pl"):
        ple_phase(nc, "pl", ntok, h3, p, y, cst, w)
    return nc


def _lay(v, nch):
    return np.ascontiguousarray(np.asarray(v, np.float32).reshape(nch, 128).T)


def host_consts(inputs):
    f = np.float32
    c = {}
    c["ident"] = np.eye(128, dtype=f)
    rm = np.ones((128, 512), f)
    rm[:, 0::32] = 0.0
    c["rmask"] = rm
    idx = np.arange(128)
    same = (idx[:, None] // 32) == (idx[None, :] // 32)
    c["maskF"] = (same & (idx[:, None] <= idx[None, :])).astype(f)
    c["maskB"] = (same & (idx[:, None] >= idx[None, :])).astype(f)
    c["ffn1_norm"] = _lay(inputs["ffn1_norm"][0], 8)
    c["mix_norm"] = _lay(inputs["mix_norm"][0], 8)
    c["q_norm"] = _lay(inputs["q_norm"][0], 3)
    c["kv_norm"] = _lay(inputs["kv_norm"][0], 2)
    lb = np.asarray(inputs["hg_lb"], f).reshape(2, 2, 4, 128)
    c["hg_lb"] = np.ascontiguousarray(lb.transpose(3, 0, 1, 2).reshape(128, 16))
    c["hg_norm_b"] = np.ascontiguousarray(np.broadcast_to(np.asarray(inputs["hg_norm"][0], f)[None, :], (128, 512)))
    c["ffn2_norm"] = _lay(inputs["ffn2_norm"][0], 8)
    c["ple_norm"] = _lay(inputs["ple_norm"][0], 8)
    c["final_norm_b"] = np.ascontiguousarray(np.broadcast_to(np.asarray(inputs["final_norm"], f)[None, :], (128, D)))
    wts = {}
    for k in ("ffn1_wg", "ffn1_wu", "ffn1_wd", "w_uk", "w_uv", "w_o", "ffn2_wg", "ffn2_wu", "ffn2_wd", "w_ple_gate", "w_ple_proj"):
        wts[k] = np.ascontiguousarray(np.asarray(inputs[k][0], f))
    win = np.asarray(inputs["w_in"][0], f)
    wts["w_in"] = np.ascontiguousarray(np.concatenate([win, win[:, 656:672], win[:, 640:656]], axis=1))
    wq = np.asarray(inputs["w_uq"][0], f)
    sw = [np.concatenate([wq[:, h * 96 + 80:h * 96 + 96], wq[:, h * 96 + 64:h * 96 + 80]], axis=1) for h in range(8)]
    wts["w_uq"] = np.ascontiguousarray(np.concatenate([wq] + sw, axis=1))
    return c, wts


def rope_tables(pos):
    inv = np.exp(np.arange(0, 32, 2, dtype=np.float32) * np.float32(-np.log(10000.0) / 32)).astype(np.float32)
    ang = (pos.astype(np.float32)[None, :] * inv[:, None]).astype(np.float32)
    cs, sn = np.cos(ang).astype(np.float32), np.sin(ang).astype(np.float32)
    return np.ascontiguousarray(np.concatenate([cs, cs], 0)), np.ascontiguousarray(np.concatenate([-sn, sn], 0))


_PROG = {}


def run_balanced(inputs):
    xp = np.asarray(inputs["x_prompt"], np.float32)
    xs = np.asarray(inputs["x_sample"], np.float32)
    pp = np.asarray(inputs["p_prompt"], np.float32)[0]
    psm = np.asarray(inputs["p_sample"], np.float32)[0]
    SP, SS = xp.shape[1], xs.shape[1]
    Q = SP // 4
    lens = (Q, SS, SS)
    c, wts = host_consts(inputs)
    key = ("bal", lens)
    if key not in _PROG:
        _PROG[key] = build_program(lens, gather=True)
    nc = _PROG[key]
    in_maps = []
    for core in range(8):
        pb, r = core // 4, core % 4
        im = {"x": np.ascontiguousarray(np.concatenate([xp[pb, r * Q:(r + 1) * Q], xs[2 * core], xs[2 * core + 1]], axis=0)),
              "p": np.ascontiguousarray(np.concatenate([pp[pb, r * Q:(r + 1) * Q], psm[2 * core], psm[2 * core + 1]], axis=0))}
        pos = np.concatenate([np.arange(r * Q, (r + 1) * Q, dtype=np.float32), np.arange(SS, dtype=np.float32),
                              np.arange(SS, dtype=np.float32)])
        im["rope_c"], im["rope_s"] = rope_tables(pos)
        rk = np.zeros((128, 8), np.float32)
        for i in range(4):
            rk[:, i] = 1.0 if i < r else 0.0
            rk[:, 4 + i] = 1.0 if i > r else 0.0
        im["rankmask"] = rk
        im.update(c)
        im.update(wts)
        in_maps.append(im)
    res = run_bass_kernel_spmd(nc, in_maps, core_ids=list(range(8)))
    y_prompt = np.empty((2, SP, D), np.float32)
    y_sample = np.empty((16, SS, D), np.float32)
    for core in range(8):
        pb, r = core // 4, core % 4
        y = np.asarray(res.results[core]["y"])
        y_prompt[pb, r * Q:(r + 1) * Q] = y[0:Q]
        y_sample[2 * core] = y[Q:Q + SS]
        y_sample[2 * core + 1] = y[Q + SS:]
    return (y_prompt, y_sample)


def kernel(**inputs):
    return run_balanced(inputs)


def _drive(gens):
    alive = list(gens)
    while alive:
        for g in list(alive):
            try:
                next(g)
            except StopIteration:
                alive.remove(g)


def mixer_phase2(nc, tag, part, ntok, h1_d, cst, w, scr):
    S = Sched(nc, tag)
    ntiles = ntok // 512
    with ExitStack() as es:
        C = Ctx(nc, tag, es)
        R = S.res
        gains = C.sb("gains", [128, 16], F32)
        ident = C.sb("ident", [128, 128], BF16)
        ht = C.sb("ht", [128, 4, D], F32)
        xn4 = C.sb("xn4", [128, 4, D], BF16)
        junk = C.sb("junk", [128, D], BF16)
        stat = C.sb("stat", [128, 16], F32)
        pt = C.ps("pt", [128, 1024], BF16)
        pt2 = C.ps("pt2", [128, 1024], BF16)
        pm = [C.ps(f"pm{i}", [128, 512]) for i in range(3)]
        pa = C.ps("pa", [128, 512])
        r_c, r_ident, r_w = R(), R(), R()
        r_ht = [R() for j in range(4)]
        r_stat = [R() for j in range(4)]
        r_xn4 = [R() for j in range(4)]
        r_pts = [R(), R()]
        r_pm = [R() for i in range(3)]
        r_pa = R()
        st = [ht[:, 0, 0:512], ht[:, 1, 0:512]]
        r_st = [r_ht[0], r_ht[1]]
        pmi = [0]

        def nextpm():
            i = pmi[0] % 3
            pmi[0] += 1
            return pm[i], r_pm[i]

        def cdma(dst, src):
            S.dma("sp", (lambda e: [e.dma_start(out=dst, in_=src)]), 1, "c", writes=[r_c])

        def mm_fm(ps, r_ps, wsb, c0, m, xT, r_x, nkc, out_p0=0):
            for kc in range(nkc):
                S.op("pe", (lambda e, kc=kc: e.matmul(ps[out_p0:out_p0 + m, :], wsb[:, kc, c0:c0 + m], xT[:, kc, :],
                                                      start=(kc == 0), stop=(kc == nkc - 1))),
                     reads=[r_w] + r_x, writes=[r_ps])

        def mm_tm(ps, r_ps, xT, r_x, j, wsb, c0, n, nkc):
            for kc in range(nkc):
                S.op("pe", (lambda e, kc=kc: e.matmul(ps[:, 0:n], xT[:, kc, j * 128:(j + 1) * 128], wsb[:, kc, c0:c0 + n],
                                                      start=(kc == 0), stop=(kc == nkc - 1))),
                     reads=[r_w] + r_x, writes=[r_ps])

        cdma(gains[:, 0:8], cst["mix_norm"])
        cdma(gains[:, 8:11], cst["q_norm"])
        cdma(gains[:, 11:13], cst["kv_norm"])
        S.dma("sp", (lambda e: [e.dma_start(out=ht[:, 2, 0:128], in_=cst["ident"])]), 1, "c2", writes=[r_ht[2]])
        S.op("pool", lambda e: e.tensor_copy(ident[:], ht[:, 2, 0:128]), reads=[r_ht[2]], writes=[r_ident])
        S.op("pool", lambda e: e.tensor_copy(stat[:, 15:16], gains[:, 0:1]), reads=[r_c], writes=[R()])
        qi = [0]

        def head(i, xnT, r_xnT):
            t0 = i * 512
            S.dma("sp", (lambda e: [e.dma_start(out=ht[:, j, :], in_=h1_d[t0 + j * 128:t0 + (j + 1) * 128, :])
                                    for j in range(4)]), 4, "ht", writes=r_ht)
            norm_transpose4(S, ht, r_ht, stat, r_stat, junk, xn4, r_xn4, [pt, pt2], r_pts, ident, r_ident, xnT, r_xnT)

        if part == "a":
            wa = C.sb("wa", [128, NKC, 704], BF16)
            wuq = C.sb("wuq", [128, 3, UQ_COLS], BF16)
            wuk = C.sb("wuk", [128, 2, 512], BF16)
            wuv = C.sb("wuv", [128, 2, 512], BF16)
            ones = C.sb("ones", [128, 128], BF16)
            r_ones = R()
            S.op("pool", lambda e: e.memset(ones[:], 1.0), writes=[r_ones])
            load_w(S, w["w_in"][:, 0:672], wa[:, :, 0:672], NKC, 672, gains[:, 0:8], st, r_st, r_w, qi)
            load_w(S, w["w_in"][:, C_KRS:C_KRS + 32], wa[:, :, 672:704], NKC, 32, gains[:, 0:8], st, r_st, r_w, qi)
            load_w(S, w["w_uq"], wuq, 3, UQ_COLS, gains[:, 8:11], st, r_st, r_w, qi)
            load_w(S, w["w_uk"], wuk, 2, 512, gains[:, 11:13], st, r_st, r_w, qi)
            load_w(S, w["w_uv"], wuv, 2, 512, gains[:, 11:13], st, r_st, r_w, qi)

            def make_set(k):
                pn = C.ps(f"pn{k}", [128, 512])
                r_pn = R()
                xnT = C.sb(f"xnT{k}", [128, NKC, 512], BF16)
                cqT = C.sb(f"cqT{k}", [128, 3, 512], BF16)
                ckvT = C.sb(f"ckvT{k}", [128, 2, 512], BF16)
                sqq = C.sb(f"sqq{k}", [128, 2, 512], BF16)
                sqkv = C.sb(f"sqkv{k}", [128, 2, 512], BF16)
                rsq = C.sb(f"rsq{k}", [128, 512], F32)
                rskv = C.sb(f"rskv{k}", [128, 512], F32)
                rstok = C.sb(f"rstok{k}", [128, 8], F32)
                tct = C.sb(f"tct{k}", [128, 512], F32)
                tst = C.sb(f"tst{k}", [128, 512], F32)
                t1 = [C.sb(f"t1{k}{i}", [128, 512], F32) for i in range(2)]
                t2 = C.sb(f"t2{k}", [128, 512], F32)
                qout = C.sb(f"qout{k}", [128, 8, 512], BF16)
                knT = C.sb(f"knT{k}", [128, 4, 512], BF16)
                krp = C.sb(f"krp{k}", [128, 512], BF16)
                vt = C.sb(f"vt{k}", [128, 4, 512], BF16)
                r_xnT = [R() for _ in range(NKC)]
                r_cqT, r_ckvT, r_sqq, r_sqkv = R(), R(), [R(), R()], R()
                r_rsq, r_rskv, r_rstok, r_tab = R(), R(), R(), R()
                r_t1, r_t2 = [R(), R()], R()
                r_qout, r_knT, r_krp, r_vt = R(), R(), R(), R()
                S.op("pool", lambda e: e.memset(tct[:], 1.0), writes=[r_tab])
                S.op("pool", lambda e: e.memset(tst[:], 0.0), writes=[r_tab])

                def tile(i):
                    t0 = i * 512
                    S.dma("sp", (lambda e: [e.dma_start(out=tct[64:96, :], in_=cst["rope_c"][:, t0:t0 + 512]),
                                            e.dma_start(out=tst[64:96, :], in_=cst["rope_s"][:, t0:t0 + 512])]),
                          2, f"tab{k}", writes=[r_tab])
                    head(i, xnT, r_xnT)
                    yield
                    for c in range(3):
                        ps, rp = nextpm()
                        mm_fm(ps, rp, wa, C_CQ + c * 128, 128, xnT, r_xnT, NKC)
                        S.op("act", (lambda e, ps=ps, c=c: e.copy(cqT[:, c, :], ps[:])), reads=[rp], writes=[r_cqT])
                        S.op("act", (lambda e, ps=ps, c=c: e.activation(sqq[:, c % 2, :], ps[:], AF.Square)),
                             reads=[rp], writes=[r_sqq[c % 2]])
                        S.op("pe", (lambda e, c=c: e.matmul(pn[:], ones[:], sqq[:, c % 2, :], start=(c == 0), stop=(c == 2))),
                             reads=[r_ones, r_sqq[c % 2]], writes=[r_pn])
                        yield
                    S.op("act", lambda e: e.activation(rsq[:], pn[:], AF.Sqrt, bias=EPS, scale=1.0 / 384), reads=[r_pn], writes=[r_rsq])
                    S.op("dve", lambda e: e.reciprocal(rsq[:], rsq[:]), reads=[r_rsq], writes=[r_rsq])
                    yield
                    for c in range(2):
                        ps, rp = nextpm()
                        mm_fm(ps, rp, wa, C_CKV + c * 128, 128, xnT, r_xnT, NKC)
                        S.op("act", (lambda e, ps=ps, c=c: e.copy(ckvT[:, c, :], ps[:])), reads=[rp], writes=[r_ckvT])
                        S.op("act", (lambda e, ps=ps, c=c: e.activation(sqkv[:, c, :], ps[:], AF.Square)),
                             reads=[rp], writes=[r_sqkv])
                        yield
                    for c in range(2):
                        S.op("pe", (lambda e, c=c: e.matmul(pn[:], ones[:], sqkv[:, c, :], start=(c == 0), stop=(c == 1))),
                             reads=[r_ones, r_sqkv], writes=[r_pn])
                    S.op("act", lambda e: e.activation(rskv[:], pn[:], AF.Sqrt, bias=EPS, scale=1.0 / 256), reads=[r_pn], writes=[r_rskv])
                    S.op("dve", lambda e: e.reciprocal(rskv[:], rskv[:]), reads=[r_rskv], writes=[r_rskv])
                    for j in range(4):
                        for c in range(2):
                            S.op("pe", (lambda e, j=j, c=c: e.matmul(pa[:, j:j + 1], sqkv[:, c, j * 128:(j + 1) * 128], ones[:, 0:1],
                                                                     start=(c == 0), stop=(c == 1))),
                                 reads=[r_ones, r_sqkv], writes=[r_pa])
                    S.op("act", lambda e: e.activation(rstok[:, 0:4], pa[:, 0:4], AF.Sqrt, bias=EPS, scale=1.0 / 256),
                         reads=[r_pa], writes=[r_rstok])
                    S.op("dve", lambda e: e.reciprocal(rstok[:, 0:4], rstok[:, 0:4]), reads=[r_rstok], writes=[r_rstok])
                    yield
                    ps, rp = nextpm()
                    mm_fm(ps, rp, wa, C_KR, 32, xnT, r_xnT, NKC, out_p0=64)
                    ps2, rp2 = nextpm()
                    mm_fm(ps2, rp2, wa, 672, 32, xnT, r_xnT, NKC, out_p0=64)
                    S.op("dve", (lambda e, ps=ps: e.tensor_tensor(t1[0][64:96, :], ps[64:96, :], tct[64:96, :], ALU.mult)),
                         reads=[rp, r_tab], writes=[r_t1[0]])
                    S.op("dve", (lambda e, ps2=ps2: e.tensor_tensor(t2[64:96, :], ps2[64:96, :], tst[64:96, :], ALU.mult)),
                         reads=[rp2, r_tab], writes=[r_t2])
                    S.op("pool", lambda e: e.tensor_tensor(krp[64:96, :], t1[0][64:96, :], t2[64:96, :], ALU.add),
                         reads=[r_t1[0], r_t2], writes=[r_krp])
                    S.dma("pool", (lambda e: [e.dma_start(out=scr["KTr"][:, t0:t0 + 512], in_=krp[64:96, :])]), 1, f"s_krp{k}",
                          reads=[r_krp])
                    yield
                    for h in range(8):
                        b = h % 2
                        ps, rp = nextpm()
                        mm_fm(ps, rp, wuq, h * 96, 96, cqT, [r_cqT], 3)
                        ps2, rp2 = nextpm()
                        mm_fm(ps2, rp2, wuq, 768 + h * 32, 32, cqT, [r_cqT], 3, out_p0=64)
                        S.op("dve", (lambda e, ps=ps, b=b: e.tensor_tensor(t1[b][0:96, :], ps[0:96, :], tct[0:96, :], ALU.mult)),
                             reads=[rp, r_tab], writes=[r_t1[b]])
                        S.op("dve", (lambda e, ps2=ps2: e.tensor_tensor(t2[64:96, :], ps2[64:96, :], tst[64:96, :], ALU.mult)),
                             reads=[rp2, r_tab], writes=[r_t2])
                        S.op("pool", (lambda e, b=b: e.tensor_tensor(t1[b][64:96, :], t1[b][64:96, :], t2[64:96, :], ALU.add)),
                             reads=[r_t2], writes=[r_t1[b]])
                        S.op("pool", (lambda e, b=b, h=h: e.tensor_tensor(qout[0:96, h, :], t1[b][0:96, :], rsq[0:96, :], ALU.mult)),
                             reads=[r_t1[b], r_rsq], writes=[r_qout])
                        yield
                    S.dma("pool", (lambda e: [e.dma_start(out=scr["QT"][h, :, t0:t0 + 512], in_=qout[0:96, h, :])
                                              for h in range(8)]), 8, f"s_q{k}", reads=[r_qout])
                    for a in range(4):
                        ps, rp = nextpm()
                        mm_fm(ps, rp, wuk, a * 128, 128, ckvT, [r_ckvT], 2)
                        S.op("dve", (lambda e, ps=ps, a=a: e.tensor_tensor(knT[:, a, :], ps[:], rskv[:], ALU.mult)),
                             reads=[rp, r_rskv], writes=[r_knT])
                        yield
                    S.dma("pool", (lambda e: [e.dma_start(out=scr["KTn"][a * 128:(a + 1) * 128, t0:t0 + 512], in_=knT[:, a, :])
                                              for a in range(4)]), 4, f"s_kn{k}", reads=[r_knT])
                    for j in range(4):
                        ps, rp = nextpm()
                        mm_tm(ps, rp, ckvT, [r_ckvT], j, wuv, 0, 512, 2)
                        S.op("act", (lambda e, ps=ps, j=j: e.activation(vt[:, j, :], ps[:], AF.Copy, scale=rstok[:, j:j + 1])),
                             reads=[rp, r_rstok], writes=[r_vt])
                        yield
                    S.dma("pool", (lambda e: [e.dma_start(
                        out=scr["VA"][:, t0 + j * 128:t0 + (j + 1) * 128, :].rearrange("h t d -> t h d"),
                        in_=vt[:, j, :].rearrange("t (h d) -> t h d", h=8)) for j in range(4)]), 4, f"s_v{k}", reads=[r_vt])
                return tile
        else:
            wb = C.sb("wb", [128, NKC, 2560], BF16)
            lbt = C.sb("lbt", [128, 16], F32)
            lb = C.sb("lb", [128, 8], F32)
            oml = C.sb("oml", [128, 8], F32)
            rmask = C.sb("rmask", [128, 512], F32)
            mF = C.sb("mF", [128, 128], F32)
            mB = C.sb("mB", [128, 128], F32)
            ato = C.sb("ato", [128, 4, 8, 128], BF16)
            kto = C.sb("kto", [128, 4, 8, 128], BF16)
            ptk = C.ps("ptk", [128, 1024], BF16)
            r_ptk, r_ato, r_kto, r_lb = R(), R(), R(), R()
            cdma(lbt[:], cst["hg_lb"])
            cdma(rmask[:], cst["rmask"])
            cdma(mF[:], cst["maskF"])
            cdma(mB[:], cst["maskB"])
            lv = lbt[:].rearrange("p (d l h) -> p d l h", d=2, l=2)
            lb3 = lb[:].rearrange("p (d h) -> p d h", d=2)
            S.op("dve", lambda e: e.tensor_tensor(lb3, lv[:, :, 0, :], lv[:, :, 1, :], ALU.subtract), reads=[r_c], writes=[r_lb])
            S.op("act", lambda e: e.activation(oml[:], lb[:], AF.Sigmoid, scale=-1.0), reads=[r_lb], writes=[R()])
            S.op("act", lambda e: e.activation(lb[:], lb[:], AF.Sigmoid), reads=[r_lb], writes=[r_lb])
            load_w(S, w["w_in"][:, 672:3232], wb, NKC, 2560, gains[:, 0:8], st, r_st, r_w, qi)
            B_HQ, B_HI, B_HFF, B_HFB, B_HG = 0, 512, 1024, 1536, 2048

            def make_set(k):
                xnT = C.sb(f"xnT{k}", [128, NKC, 512], BF16)
                qh = C.sb(f"qh{k}", [128, 4, 512], F32)
                A = C.sb(f"hA{k}", [128, 512], F32)
                B = C.sb(f"hB{k}", [128, 512], F32)
                Cc = C.sb(f"hC{k}", [128, 512], F32)
                E1 = C.sb(f"hE1{k}", [128, 512], F32)
                E2 = C.sb(f"hE2{k}", [128, 512], F32)
                qpo = C.sb(f"qpo{k}", [128, 8, 512], BF16)
                kpo = C.sb(f"kpo{k}", [128, 8, 512], BF16)
                kppo = C.sb(f"kppo{k}", [128, 8, 512], BF16)
                dco = C.sb(f"dco{k}", [128, 8, 16], F32)
                vht = C.sb(f"vht{k}", [128, 4, 512], BF16)
                ght = C.sb(f"ght{k}", [128, 4, 512], BF16)
                r_xnT = [R() for _ in range(NKC)]
                r_qh, rA, rB, rC, rE1, rE2 = R(), R(), R(), R(), R(), R()
                r_qpo, r_kpo, r_kppo, r_dco, r_vht, r_ght = R(), R(), R(), R(), R(), R()

                def tile(i):
                    t0 = i * 512
                    head(i, xnT, r_xnT)
                    yield
                    for h in range(4):
                        ps, rp = nextpm()
                        mm_fm(ps, rp, wb, B_HQ + h * 128, 128, xnT, r_xnT, NKC)
                        S.op("act", (lambda e, ps=ps, h=h: e.activation(qh[:, h, :], ps[:], AF.Silu)), reads=[rp], writes=[r_qh])
                        yield
                    for j in range(4):
                        ps, rp = nextpm()
                        mm_tm(ps, rp, xnT, r_xnT, j, wb, B_HI, 512, NKC)
                        S.op("act", (lambda e, ps=ps, j=j: e.copy(vht[:, j, :], ps[:])), reads=[rp], writes=[r_vht])
                        yield
                        ps, rp = nextpm()
                        mm_tm(ps, rp, xnT, r_xnT, j, wb, B_HG, 512, NKC)
                        S.op("act", (lambda e, ps=ps, j=j: e.activation(ght[:, j, :], ps[:], AF.Silu)), reads=[rp], writes=[r_ght])
                        yield
                    S.dma("pool", (lambda e: [
                        e.dma_start(out=scr["VH"][t0:t0 + 512, :].rearrange("(j p) c -> p j c", p=128), in_=vht[:]),
                        e.dma_start(out=scr["GH"][t0:t0 + 512, :].rearrange("(j p) c -> p j c", p=128), in_=ght[:])]),
                        2, f"s_vg{k}", reads=[r_vht, r_ght])
                    for d in range(2):
                        for h in range(4):
                            hd = d * 4 + h
                            ps, rp = nextpm()
                            mm_fm(ps, rp, wb, (B_HFF if d == 0 else B_HFB) + h * 128, 128, xnT, r_xnT, NKC)
                            lbs, omls = lb[:, hd:hd + 1], oml[:, hd:hd + 1]
                            S.op("act", (lambda e, ps=ps: e.activation(A[:], ps[:], AF.Sigmoid)), reads=[rp], writes=[rA])
                            S.op("act", (lambda e, ps=ps: e.activation(B[:], ps[:], AF.Sigmoid, scale=-1.0)), reads=[rp], writes=[rB])
                            yield
                            S.op("pool", (lambda e, omls=omls: e.tensor_scalar(B[:], B[:], omls, 0.0, ALU.mult, ALU.add)),
                                 reads=[r_lb], writes=[rB])
                            S.op("dve", (lambda e, omls=omls, lbs=lbs: e.tensor_scalar(A[:], A[:], omls, lbs, ALU.mult, ALU.add)),
                                 reads=[r_lb], writes=[rA])
                            S.op("act", (lambda e: e.activation(A[:], A[:], AF.Ln)), writes=[rA])
                            yield
                            S.op("dve", (lambda e: e.tensor_tensor_scan(Cc[:], rmask[:], A[:], 0.0, ALU.mult, ALU.add)),
                                 reads=[rA, r_c], writes=[rC])
                            Cv = Cc[:].rearrange("p (c t) -> p c t", t=32)
                            Av = A[:].rearrange("p (c t) -> p c t", t=32)
                            if d == 0:
                                bsrc, rb, dcol = Cc, rC, 31
                            else:
                                S.op("pool", (lambda e: e.tensor_tensor(A[:], A[:], Cc[:], ALU.subtract)), reads=[rC], writes=[rA])
                                S.op("pool", (lambda e, Av=Av, Cv=Cv: e.tensor_tensor(
                                    Av, Av, Cv[:, :, 31:32].broadcast_to([128, 16, 32]), ALU.add)), reads=[rC], writes=[rA])
                                bsrc, rb, dcol = A, rA, 0
                            yield
                            S.op("act", (lambda e, bsrc=bsrc: e.activation(E1[:], bsrc[:], AF.Exp)), reads=[rb], writes=[rE1])
                            S.op("act", (lambda e, bsrc=bsrc: e.activation(E2[:], bsrc[:], AF.Exp, scale=-1.0)), reads=[rb], writes=[rE2])
                            yield
                            S.op("pool", (lambda e, h=h, hd=hd: e.tensor_tensor(qpo[:, hd, :], qh[:, h, :], E1[:], ALU.mult)),
                                 reads=[rE1, r_qh], writes=[r_qpo])
                            S.op("dve", (lambda e: e.tensor_tensor(E2[:], E2[:], B[:], ALU.mult)), reads=[rB], writes=[rE2])
                            yield
                            S.op("act", (lambda e, hd=hd: e.copy(kpo[:, hd, :], E2[:])), reads=[rE2], writes=[r_kpo])
                            E1v = E1[:].rearrange("p (c t) -> p c t", t=32)
                            E2v = E2[:].rearrange("p (c t) -> p c t", t=32)
                            S.op("dve", (lambda e, E1v=E1v, hd=hd, dcol=dcol: e.tensor_copy(dco[:, hd, :], E1v[:, :, dcol])),
                                 reads=[rE1], writes=[r_dco])
                            kv = kppo[:, hd, :].rearrange("p (c t) -> p c t", t=32)
                            S.op("pool", (lambda e, E1v=E1v, E2v=E2v, kv=kv, dcol=dcol: e.tensor_tensor(
                                kv, E2v, E1v[:, :, dcol:dcol + 1].broadcast_to([128, 16, 32]), ALU.mult)),
                                reads=[rE1, rE2], writes=[r_kppo])
                            yield
                    S.dma("pool", (lambda e: [
                        e.dma_start(out=scr["QP"][:, :, t0:t0 + 512].rearrange("h p t -> p h t"), in_=qpo[:]),
                        e.dma_start(out=scr["DC"][:, :, i * 16:(i + 1) * 16].rearrange("h p c -> p h c"), in_=dco[:])]),
                        2, f"s_qp{k}", reads=[r_qpo, r_dco])
                    for j in range(4):
                        for d in range(2):
                            for h in range(4):
                                hd = d * 4 + h
                                S.op("pe", (lambda e, j=j, hd=hd, h=h: e.matmul(
                                    pa[:, h * 128:(h + 1) * 128], kpo[:, hd, j * 128:(j + 1) * 128], qpo[:, hd, j * 128:(j + 1) * 128],
                                    start=True, stop=True)), reads=[r_kpo, r_qpo], writes=[r_pa])
                            msk = (mF if d == 0 else mB)
                            S.op("dve", (lambda e, j=j, d=d, msk=msk: e.tensor_tensor(
                                ato[:, j, d * 4:(d + 1) * 4, :], pa[:].rearrange("p (h t) -> p h t", h=4),
                                msk[:].rearrange("p (o t) -> p o t", o=1).broadcast_to([128, 4, 128]), ALU.mult)),
                                reads=[r_pa, r_c], writes=[r_ato])
                        for hd in range(8):
                            S.op("pe", (lambda e, j=j, hd=hd: e.transpose(ptk[:, hd * 128:(hd + 1) * 128],
                                                                           kppo[:, hd, j * 128:(j + 1) * 128], ident[:])),
                                 reads=[r_kppo, r_ident], writes=[r_ptk])
                        S.op("act", (lambda e, j=j: e.copy(kto[:, j, :, :], ptk[:].rearrange("p (h k) -> p h k", h=8))),
                             reads=[r_ptk], writes=[r_kto])
                    S.dma("pool", (lambda e: [
                        e.dma_start(out=scr["AT"][t0:t0 + 512, :, :].rearrange("(j p) h t -> p j h t", p=128), in_=ato[:]),
                        e.dma_start(out=scr["KPT"][t0:t0 + 512, :, :].rearrange("(j p) h k -> p j h k", p=128), in_=kto[:])]),
                        2, "s_at", reads=[r_ato, r_kto])
                return tile

        tiles = [make_set(0), make_set(1)]
        for i0 in range(0, ntiles, 2):
            gens = [tiles[0](i0)]
            if i0 + 1 < ntiles:
                gens.append(tiles[1](i0 + 1))
            _drive(gens)
        S.barrier()
        S.emit()
    return S
```

```python
import numpy as np
from contextlib import ExitStack
import concourse.bass as bass
import concourse.mybir as mybir
from concourse.bass_utils import run_bass_kernel_spmd

F32 = mybir.dt.float32
BF16 = mybir.dt.bfloat16
AF = mybir.ActivationFunctionType
ALU = mybir.AluOpType
AX = mybir.AxisListType

D = 1024
DFF = 2816
EPS = 1e-6
NFC = DFF // 128
NKC = D // 128


class Res:
    __slots__ = ("name", "w", "r")

    def __init__(self, name):
        self.name = name
        self.w = None
        self.r = {}


class Ev:
    __slots__ = ("kind", "key", "op", "count")

    def __init__(self, kind, key, op=None, count=0):
        self.kind = kind
        self.key = key
        self.op = op
        self.count = count


class Op:
    __slots__ = ("fn", "deps", "sig", "count", "dma_key", "dma_n", "ev", "inc")

    def __init__(self, fn, deps):
        self.fn = fn
        self.deps = deps
        self.sig = False
        self.count = 0
        self.dma_key = None
        self.dma_n = 0
        self.ev = None


ENGS = ("sp", "act", "dve", "pool", "pe")
FUSE_WAITS = True


class Sched:
    DMA_SEMS = {}
    DMA_CNT = {}

    def __init__(self, nc, tag):
        self.nc = nc
        self.tag = tag
        self.ops = {e: [] for e in ENGS}
        self.dma_cnt = Sched.DMA_CNT
        self.nres = 0
        self.keymap = {}

    def res(self, name=None):
        self.nres += 1
        return Res(name or f"r{self.nres}")

    def _deps(self, eng, reads, writes):
        deps = []
        for r in reads:
            if r.w is not None:
                deps.append(r.w)
        for w in writes:
            if w.w is not None:
                deps.append(w.w)
            deps.extend(w.r.values())
        out = []
        seen = set()
        for d in deps:
            if id(d) in seen:
                continue
            seen.add(id(d))
            if d.kind == "e" and d.key == "pe" and eng == "pe":
                continue
            out.append(d)
        return out

    def op(self, eng, fn, reads=(), writes=()):
        o = Op(fn, self._deps(eng, reads, writes))
        ev = Ev("e", eng, op=o)
        o.ev = ev
        self.ops[eng].append(o)
        for r in reads:
            r.r[("e", eng)] = ev
        for w in writes:
            w.w = ev
            w.r = {}
        return o

    def dma(self, queue, fn, n, key, reads=(), writes=(), inc=16):
        if key not in self.keymap:
            self.keymap[key] = f"k{len(self.keymap)}"
        key = self.keymap[key]
        o = Op(fn, self._deps(queue, reads, writes))
        o.dma_key = key
        o.dma_n = n
        o.inc = inc
        c = self.dma_cnt.get(key, 0) + n * inc
        self.dma_cnt[key] = c
        ev = Ev("d", key, op=o, count=c)
        o.ev = ev
        self.ops[queue].append(o)
        for r in reads:
            r.r[("d", key)] = ev
        for w in writes:
            w.w = ev
            w.r = {}
        return o

    def barrier(self):
        evs = []
        for e in ENGS:
            for o in reversed(self.ops[e]):
                if o.dma_key is None and o.fn is not None:
                    evs.append(o.ev)
                    break
        lastd = {}
        for e in ENGS:
            for o in self.ops[e]:
                if o.dma_key is not None:
                    lastd[o.dma_key] = o.ev
        evs.extend(lastd.values())
        for e in ENGS:
            deps = [d for d in evs if not (d.kind == "e" and d.key == e)]
            o = Op(None, deps)
            o.ev = Ev("e", e, op=o)
            self.ops[e].append(o)

    def emit(self):
        nc = self.nc
        for e in ENGS:
            for o in self.ops[e]:
                for d in o.deps:
                    if d.kind == "e":
                        d.op.sig = True
        sems = {}
        for e in ENGS:
            c = 0
            for o in self.ops[e]:
                if o.dma_key is None and o.sig:
                    assert o.fn is not None
                    c += 1
                    o.count = c
            sems[("e", e)] = nc.alloc_semaphore(f"{self.tag}_e_{e}")
        for k in self.dma_cnt:
            if k not in Sched.DMA_SEMS:
                Sched.DMA_SEMS[k] = nc.alloc_semaphore(f"d_{k}")
            sems[("d", k)] = Sched.DMA_SEMS[k]
        self.sems = sems

        def run(eng_name, eng):
            waited = {}
            for o in self.ops[eng_name]:
                need = {}
                for d in o.deps:
                    k = (d.kind, d.key)
                    val = d.op.count if d.kind == "e" else d.count
                    assert val > 0, (eng_name, d.kind, d.key)
                    if waited.get(k, 0) >= val:
                        continue
                    waited[k] = val
                    need[k] = max(need.get(k, 0), val)
                need = list(need.items())
                fuse = None
                if FUSE_WAITS and need and o.fn is not None and o.dma_key is None:
                    fuse = need.pop()
                for k, val in need:
                    eng.wait_ge(sems[k], val)
                if o.fn is None:
                    continue
                ins = o.fn(eng)
                if fuse is not None:
                    ins._wait_ge(sems[fuse[0]], fuse[1])
                if o.dma_key is not None:
                    assert len(ins) == o.dma_n
                    for i in ins:
                        i.then_inc(sems[("d", o.dma_key)], o.inc)
                elif o.sig:
                    ins.then_inc(sems[("e", eng_name)], 1)

        with nc.Block() as block:
            @block.sync
            def _(e):
                run("sp", e)

            @block.scalar
            def _(e):
                run("act", e)

            @block.vector
            def _(e):
                run("dve", e)

            @block.gpsimd
            def _(e):
                run("pool", e)

            @block.tensor
            def _(e):
                run("pe", e)

    def release(self):
        for s in self.sems.values():
            self.nc.release_semaphore(s)


def load_weight_bf16(S, nc, w_dram, w_sb, rows_chunks, cols, gain_sb, stage, stage_res, w_res, qi=[0]):
    CW = 512
    engs = ("act", "dve", "act", "dve", "pool")
    for kc in range(rows_chunks):
        for c0 in range(0, cols, CW):
            cw = min(CW, cols - c0)
            n = qi[0]
            i = n % len(stage)
            qi[0] += 1
            st, sr = stage[i], stage_res[i]
            src = w_dram[kc * 128:(kc + 1) * 128, c0:c0 + cw]
            S.dma("sp" if n % 2 == 0 else "act", (lambda e, st=st, src=src, cw=cw: [e.dma_start(out=st[:, 0:cw], in_=src)]),
                  1, f"wst{i}", writes=[sr])
            dst = w_sb[:, kc, c0:c0 + cw]
            eng = engs[n % len(engs)]
            wr = Res("wchunk")
            if gain_sb is not None:
                g = gain_sb[:, kc:kc + 1]
                if eng == "act":
                    fn = (lambda e, dst=dst, st=st, cw=cw, g=g: e.activation(dst, st[:, 0:cw], AF.Copy, scale=g))
                elif eng == "dve":
                    fn = (lambda e, dst=dst, st=st, cw=cw, g=g: e.tensor_scalar(dst, st[:, 0:cw], g, None, ALU.mult))
                else:
                    fn = (lambda e, dst=dst, st=st, cw=cw, g=g: e.tensor_scalar(dst, st[:, 0:cw], g, 0.0, ALU.mult, ALU.add))
            else:
                if eng == "act":
                    fn = (lambda e, dst=dst, st=st, cw=cw: e.copy(dst, st[:, 0:cw]))
                else:
                    fn = (lambda e, dst=dst, st=st, cw=cw: e.tensor_copy(dst, st[:, 0:cw]))
            S.op(eng, fn, reads=[sr], writes=[wr])
            w_res.append(wr)


def ffn_phase(nc, tag, x_d, out_d, ntiles, gain_d, wg_d, wu_d, wd_d, ident_d):
    S = Sched(nc, tag)
    with ExitStack() as es:
        def sb(name, shape, dt):
            return es.enter_context(nc.sbuf_tensor(f"{tag}_{name}", shape, dt))

        def ps(name, shape, dt):
            return es.enter_context(nc.psum_tensor(f"{tag}_{name}", shape, dt))

        wg = sb("wg", [128, NKC, DFF], BF16)
        wu = sb("wu", [128, NKC, DFF], BF16)
        wd = sb("wd", [128, NFC, D], BF16)
        gain = sb("gain", [128, NKC], F32)
        ident = sb("ident", [128, 128], BF16)
        xt0 = sb("xt0", [128, 4, D], F32)
        xt1 = sb("xt1", [128, 4, D], F32)
        stg = [xt1[:, j_, h_ * 512:(h_ + 1) * 512] for j_ in range(4) for h_ in range(2)]
        st0 = stg[0]
        xn4 = sb("xn4", [128, 4, D], BF16)
        xnT = sb("xnT", [128, NKC, 512], BF16)
        actb = sb("act", [128, NFC, 512], BF16)
        sg = sb("sg", [128, 2, 512], BF16)
        stat = sb("stat", [128, 16], F32)
        junk = sb("junk", [128, D], BF16)
        pg0 = ps("pg0", [128, 512], F32)
        pg1 = ps("pg1", [128, 512], F32)
        pu0 = ps("pu0", [128, 512], F32)
        pu1 = ps("pu1", [128, 512], F32)
        po0 = ps("po0", [128, 512], F32)
        po1 = ps("po1", [128, 512], F32)
        pt0 = ps("pt0", [128, 1024], BF16)
        pt1 = ps("pt1", [128, 1024], BF16)
        R = S.res
        r_gain, r_ident, r_w = R("gain"), R("ident"), R("w")
        r_xt = [[R(f"xt{i}_{j}") for j in range(4)] for i in range(2)]
        r_st = [R(f"stg{i}") for i in range(8)]
        S.dma("sp", lambda e: [e.dma_start(out=gain[:], in_=gain_d)], 1, "c0", writes=[r_gain])
        S.dma("sp", lambda e: [e.dma_start(out=st0[:, 0:128], in_=ident_d)], 1, "wst0", writes=[r_st[0]])
        S.op("pool", lambda e: e.tensor_copy(ident[:], st0[:, 0:128]), reads=[r_st[0]], writes=[r_ident])
        stat_dummy = None
        qi = [1]
        for en_ in ("pool", "dve", "act"):
            S.op(en_, (lambda e, en_=en_: (e.copy(stat[:, 8:9], gain[:, 0:1]) if en_ == "act" else e.tensor_copy(stat[:, 9 if en_ == "dve" else 10:10 if en_ == "dve" else 11], gain[:, 0:1]))),
                 reads=[r_gain], writes=[R("dummy")])
        l_wg, l_wu, l_wd = [], [], []
        load_weight_bf16(S, nc, wg_d, wg, NKC, DFF, gain, stg, r_st, l_wg, qi)
        load_weight_bf16(S, nc, wu_d, wu, NKC, DFF, gain, stg, r_st, l_wu, qi)
        load_weight_bf16(S, nc, wd_d, wd, NFC, D, None, stg, r_st, l_wd, qi)
        r_wg, r_wu, r_wd = R("wg"), R("wu"), R("wd")
        S.op("pool", lambda e: e.memset(stat[:, 12:13], 0.0), reads=l_wg, writes=[r_wg])
        S.op("pool", lambda e: e.memset(stat[:, 13:14], 0.0), reads=l_wu, writes=[r_wu])
        S.op("pool", lambda e: e.memset(stat[:, 14:15], 0.0), reads=l_wd, writes=[r_wd] + r_st + r_xt[1])
        xts = [xt0, xt1]
        r_xn4 = [R(f"xn{j}") for j in range(4)]
        r_xnT, r_act = [R(f"xnT{k}") for k in range(NKC)], [R(f"act{f}") for f in range(NFC)]
        r_sg = [R("sg0"), R("sg1")]
        r_stat = [R(f"stat{j}") for j in range(4)]
        r_junk = R("junk")
        pgs, pus, pos, pts = [pg0, pg1], [pu0, pu1], [po0, po1], [pt0, pt1]
        r_pg, r_pu = [R("pg0"), R("pg1")], [R("pu0"), R("pu1")]
        r_po, r_pt = [R("po0"), R("po1")], [R("pt0"), R("pt1")]

        def load_tile(i):
            b = i % 2
            for j in range(4):
                src = x_d[i * 512 + j * 128: i * 512 + (j + 1) * 128, :]
                dst = xts[b][:, j, :]
                S.dma("sp", (lambda e, dst=dst, src=src: [e.dma_start(out=dst, in_=src)]), 1,
                      f"xt{b}_{j}", writes=[r_xt[b][j]])

        load_tile(0)
        nt_ctr = [0]
        for i in range(ntiles):
            b = i % 2
            xt = xts[b]
            if i + 1 < ntiles:
                load_tile(i + 1)
            norm_transpose4(S, xt, r_xt[b], stat, r_stat, junk, xn4, r_xn4, pts, r_pt, ident, r_ident, xnT, r_xnT)
            for f in range(NFC):
                pb = f % 2
                for kc in range(NKC):
                    S.op("pe", (lambda e, pb=pb, f=f, kc=kc: e.matmul(
                        pgs[pb][:], wg[:, kc, f * 128:(f + 1) * 128], xnT[:, kc, :],
                        start=(kc == 0), stop=(kc == NKC - 1))),
                        reads=[r_wg, r_xnT[kc]], writes=[r_pg[pb]])
                for kc in range(NKC):
                    S.op("pe", (lambda e, pb=pb, f=f, kc=kc: e.matmul(
                        pus[pb][:], wu[:, kc, f * 128:(f + 1) * 128], xnT[:, kc, :],
                        start=(kc == 0), stop=(kc == NKC - 1))),
                        reads=[r_wu, r_xnT[kc]], writes=[r_pu[pb]])
                S.op("act", (lambda e, pb=pb: e.activation(sg[:, pb, :], pgs[pb][:], AF.Silu)),
                     reads=[r_pg[pb]], writes=[r_sg[pb]])
                S.op("dve", (lambda e, pb=pb, f=f: e.tensor_tensor(actb[:, f, :], sg[:, pb, :], pus[pb][:], ALU.mult)),
                     reads=[r_sg[pb], r_pu[pb]], writes=[r_act[f]])
            for j in range(4):
                for hh in range(2):
                    pb = (j * 2 + hh) % 2
                    for f in range(NFC):
                        S.op("pe", (lambda e, pb=pb, f=f, j=j, hh=hh: e.matmul(
                            pos[pb][:], actb[:, f, j * 128:(j + 1) * 128], wd[:, f, hh * 512:(hh + 1) * 512],
                            start=(f == 0), stop=(f == NFC - 1))),
                            reads=[r_wd, r_act[f]], writes=[r_po[pb]])
                    dst = xt[:, j, hh * 512:(hh + 1) * 512]
                    S.op("dve", (lambda e, dst=dst, pb=pb: e.scalar_tensor_tensor(
                        dst, pos[pb][:], 0.5, dst, ALU.mult, ALU.add)),
                        reads=[r_po[pb], r_xt[b][j]], writes=[r_xt[b][j]])
                dstd = out_d[i * 512 + j * 128: i * 512 + (j + 1) * 128, :]
                src = xt[:, j, :]
                S.dma("pool", (lambda e, dstd=dstd, src=src: [e.dma_start(out=dstd, in_=src)]), 1,
                      f"xo{b}_{j}", reads=[r_xt[b][j]])
        S.barrier()
        S.emit()
    return S


class Ctx:
    def __init__(self, nc, tag, es):
        self.nc, self.tag, self.es = nc, tag, es

    def sb(self, name, shape, dt):
        return self.es.enter_context(self.nc.sbuf_tensor(f"{self.tag}_{name}", shape, dt))

    def ps(self, name, shape, dt=F32):
        return self.es.enter_context(self.nc.psum_tensor(f"{self.tag}_{name}", shape, dt))


def load_w(S, w_dram, w_sb, nrc, cols, gain_sb, st, r_st, r_w, qi, rows_last=128):
    for rc in range(nrc):
        for c0 in range(0, cols, 512):
            cw = min(512, cols - c0)
            i = qi[0] % 2
            qi[0] += 1
            stt, sr = st[i], r_st[i]
            src = w_dram[rc * 128:(rc + 1) * 128, c0:c0 + cw]
            S.dma("sp", (lambda e, stt=stt, src=src, cw=cw: [e.dma_start(out=stt[:, 0:cw], in_=src)]),
                  1, f"wst{i}", writes=[sr])
            dst = w_sb[:, rc, c0:c0 + cw]
            if gain_sb is not None:
                g = gain_sb[:, rc:rc + 1]
                S.op("pool", (lambda e, dst=dst, stt=stt, cw=cw, g=g:
                              e.tensor_scalar(dst, stt[:, 0:cw], g, 0.0, ALU.mult, ALU.add)),
                     reads=[sr], writes=[r_w])
            else:
                S.op("pool", (lambda e, dst=dst, stt=stt, cw=cw: e.tensor_copy(dst, stt[:, 0:cw])),
                     reads=[sr], writes=[r_w])


def norm_transpose(S, xt, r_xt_j, j, stat, r_stat, junk, r_junk, xn, r_xn, pt, r_pt, ident, r_ident,
                   xnT, r_xnT, nfeat=D):
    nkc = nfeat // 128
    xj = xt[:, j, :]
    ss = stat[:, j:j + 1]
    rs = stat[:, 4 + j:5 + j]
    S.op("act", (lambda e: e.activation(junk[:, 0:nfeat], xj, AF.Square, accum_out=ss)),
         reads=[r_xt_j], writes=[r_junk, r_stat[j]])
    S.op("act", (lambda e: e.activation(rs, ss, AF.Sqrt, bias=EPS, scale=1.0 / nfeat)),
         reads=[r_stat[j]], writes=[r_stat[j]])
    S.op("dve", (lambda e: e.reciprocal(rs, rs)), reads=[r_stat[j]], writes=[r_stat[j]])
    S.op("dve", (lambda e: e.tensor_scalar(xn[:, 0:nfeat], xj, rs, None, ALU.mult)),
         reads=[r_xt_j, r_stat[j]], writes=[r_xn])
    for kc in range(nkc):
        S.op("pe", (lambda e, kc=kc: e.transpose(pt[:, kc * 128:(kc + 1) * 128],
                                                 xn[:, kc * 128:(kc + 1) * 128], ident[:])),
             reads=[r_xn, r_ident], writes=[r_pt])
    dst = xnT[:, 0:nkc, j * 128:(j + 1) * 128]
    src = pt[:, 0:nkc * 128].rearrange("p (k t) -> p k t", k=nkc)
    S.op("act", (lambda e: e.copy(dst, src)), reads=[r_pt], writes=r_xnT)


def norm_transpose4(S, xt, r_xt, stat, r_stat, junk, xn4, r_xn, pts, r_pts, ident, r_ident, xnT, r_xnT, nfeat=D):
    nkc = nfeat // 128
    r_j = [Res("junk") for _ in range(4)]
    for j in range(4):
        S.op("act", (lambda e, j=j: e.activation(junk[:, 0:nfeat], xt[:, j, :], AF.Square, accum_out=stat[:, j:j + 1])),
             reads=[r_xt[j]], writes=[r_j[j], r_stat[j]])
    for j in range(4):
        S.op("act", (lambda e, j=j: e.activation(stat[:, 4 + j:5 + j], stat[:, j:j + 1], AF.Sqrt, bias=EPS, scale=1.0 / nfeat)),
             reads=[r_stat[j]], writes=[r_stat[j]])
    for j in range(4):
        S.op("dve", (lambda e, j=j: e.reciprocal(stat[:, 4 + j:5 + j], stat[:, 4 + j:5 + j])), reads=[r_stat[j]], writes=[r_stat[j]])
    for j in range(4):
        S.op("dve" if j % 2 == 0 else "pool",
             (lambda e, j=j: e.tensor_scalar(xn4[:, j, 0:nfeat], xt[:, j, :], stat[:, 4 + j:5 + j], 0.0, ALU.mult, ALU.add)),
             reads=[r_xt[j], r_stat[j]], writes=[r_xn[j]])
    for j in range(4):
        pt, r_pt = pts[j % 2], r_pts[j % 2]
        for kc in range(nkc):
            S.op("pe", (lambda e, kc=kc, j=j, pt=pt: e.transpose(pt[:, kc * 128:(kc + 1) * 128],
                                                               xn4[:, j, kc * 128:(kc + 1) * 128], ident[:])),
                 reads=[r_xn[j], r_ident], writes=[r_pt])
        dst = xnT[:, 0:nkc, j * 128:(j + 1) * 128]
        src = pt[:, 0:nkc * 128].rearrange("p (k t) -> p k t", k=nkc)
        S.op("act", (lambda e, dst=dst, src=src: e.copy(dst, src)), reads=[r_pt], writes=r_xnT)


C_CQ, C_CKV, C_KR, C_HQ, C_HI, C_HFF, C_HFB, C_HG, C_KRS = 0, 384, 640, 672, 1184, 1696, 2208, 2720, 3232
WIN_COLS = 3264
UQ_COLS = 768 + 256


def mixer_in_phase(nc, tag, ntok, h1_d, cst, w, scr):
    S = Sched(nc, tag)
    ntiles = ntok // 512
    with ExitStack() as es:
        C = Ctx(nc, tag, es)
        R = S.res
        win = C.sb("win", [128, NKC, WIN_COLS], BF16)
        wuq = C.sb("wuq", [128, 3, UQ_COLS], BF16)
        wuk = C.sb("wuk", [128, 2, 512], BF16)
        wuv = C.sb("wuv", [128, 2, 512], BF16)
        gains = C.sb("gains", [128, 16], F32)
        lbt = C.sb("lbt", [128, 16], F32)
        lb = C.sb("lb", [128, 8], F32)
        oml = C.sb("oml", [128, 8], F32)
        ident = C.sb("ident", [128, 128], BF16)
        ones = C.sb("ones", [128, 128], BF16)
        rmask = C.sb("rmask", [128, 512], F32)
        mF = C.sb("mF", [128, 128], F32)
        mB = C.sb("mB", [128, 128], F32)
        ht = C.sb("ht", [128, 4, D], F32)
        xn = C.sb("xn", [128, D], BF16)
        junk = C.sb("junk", [128, D], BF16)
        stat = C.sb("stat", [128, 16], F32)
        xnT = C.sb("xnT", [128, NKC, 512], BF16)
        cqT = C.sb("cqT", [128, 3, 512], BF16)
        ckvT = C.sb("ckvT", [128, 2, 512], BF16)
        sqq = C.sb("sqq", [128, 2, 512], BF16)
        sqkv = C.sb("sqkv", [128, 2, 512], BF16)
        rsq = C.sb("rsq", [128, 512], F32)
        rskv = C.sb("rskv", [128, 512], F32)
        rstok = C.sb("rstok", [128, 8], F32)
        tct = C.sb("tct", [128, 512], F32)
        tst = C.sb("tst", [128, 512], F32)
        t1a = C.sb("t1a", [128, 512], F32)
        t2a = C.sb("t2a", [128, 512], F32)
        t1 = [t1a, t1a]
        t2 = [t2a, t2a]
        qout = C.sb("qout", [128, 8, 512], BF16)
        knT = C.sb("knT", [128, 4, 512], BF16)
        krp = C.sb("krp", [128, 512], BF16)
        vt = C.sb("vt", [128, 4, 512], BF16)
        qh = C.sb("qh", [128, 4, 512], F32)
        hA = [C.sb(f"hA{i}", [128, 512], F32) for i in range(2)]
        hB = [C.sb(f"hB{i}", [128, 512], F32) for i in range(2)]
        hC = [C.sb(f"hC{i}", [128, 512], F32) for i in range(2)]
        hE1 = [C.sb(f"hE1{i}", [128, 512], F32) for i in range(2)]
        hE2 = [C.sb(f"hE2{i}", [128, 512], F32) for i in range(2)]
        st = [hA[0], hB[0]]
        qpo = C.sb("qpo", [128, 8, 512], BF16)
        kpo = C.sb("kpo", [128, 8, 512], BF16)
        kppo = C.sb("kppo", [128, 8, 512], BF16)
        dco = C.sb("dco", [128, 8, 16], F32)
        vht = C.sb("vht", [128, 4, 512], BF16)
        ght = C.sb("ght", [128, 4, 512], BF16)
        ato = C.sb("ato", [128, 4, 8, 128], BF16)
        kto = C.sb("kto", [128, 4, 8, 128], BF16)
        pt = C.ps("pt", [128, 1024], BF16)
        pm = [C.ps(f"pm{i}", [128, 512]) for i in range(4)]
        pn = C.ps("pn", [128, 512])
        pa = C.ps("pa", [128, 512])
        ptk = C.ps("ptk", [128, 1024], BF16)

        r_c = R("consts")
        r_ident, r_ones = R("ident"), R("ones")

        def cdma(dst, src):
            S.dma("sp", (lambda e: [e.dma_start(out=dst, in_=src)]), 1, "c", writes=[r_c])
        cdma(gains[:, 0:8], cst["mix_norm"])
        cdma(gains[:, 8:11], cst["q_norm"])
        cdma(gains[:, 11:13], cst["kv_norm"])
        cdma(lbt[:], cst["hg_lb"])
        cdma(rmask[:], cst["rmask"])
        cdma(mF[:], cst["maskF"])
        cdma(mB[:], cst["maskB"])
        cdma(ht[:, 0, 0:128], cst["ident"])
        S.op("pool", lambda e: e.tensor_copy(ident[:], ht[:, 0, 0:128]), reads=[r_c], writes=[r_ident])
        S.op("pool", lambda e: e.memset(ones[:], 1.0), writes=[r_ones])
        lv = lbt[:].rearrange("p (d l h) -> p d l h", d=2, l=2)
        lb3 = lb[:].rearrange("p (d h) -> p d h", d=2)
        oml3 = oml[:].rearrange("p (d h) -> p d h", d=2)
        r_lb = R("lb")
        S.op("dve", lambda e: e.tensor_tensor(lb3, lv[:, :, 0, :], lv[:, :, 1, :], ALU.subtract), reads=[r_c], writes=[r_lb])
        S.op("act", lambda e: e.activation(oml[:], lb[:], AF.Sigmoid, scale=-1.0), reads=[r_lb], writes=[R("oml")])
        S.op("act", lambda e: e.activation(lb[:], lb[:], AF.Sigmoid), reads=[r_lb], writes=[r_lb])
        r_w = R("w")
        r_hA, r_hB = [R("hA0"), R("hA1")], [R("hB0"), R("hB1")]
        r_st = [r_hA[0], r_hB[0]]
        S.op("pool", lambda e: e.tensor_copy(stat[:, 15:16], gains[:, 0:1]), reads=[r_c], writes=[R("d")])
        qi = [0]
        load_w(S, w["w_in"], win, NKC, WIN_COLS, gains[:, 0:8], st, r_st, r_w, qi)
        load_w(S, w["w_uq"], wuq, 3, UQ_COLS, gains[:, 8:11], st, r_st, r_w, qi)
        load_w(S, w["w_uk"], wuk, 2, 512, gains[:, 11:13], st, r_st, r_w, qi)
        load_w(S, w["w_uv"], wuv, 2, 512, gains[:, 11:13], st, r_st, r_w, qi)

        r_ht = [R(f"ht{j}") for j in range(4)]
        r_stat = [R(f"stat{j}") for j in range(4)]
        r_junk, r_xn, r_pt = R("junk"), R("xn"), R("pt")
        r_xnT = [R(f"xnT{k}") for k in range(NKC)]
        r_pm = [R(f"pm{i}") for i in range(4)]
        r_pn, r_pa, r_ptk = R("pn"), R("pa"), R("ptk")
        r_cqT, r_ckvT, r_sqq, r_sqkv = R("cqT"), R("ckvT"), [R("sqq0"), R("sqq1")], R("sqkv")
        r_rsq, r_rskv, r_rstok = R("rsq"), R("rskv"), R("rstok")
        r_tab = R("tab")
        r_t1a, r_t2a = R("t1a"), R("t2a")
        r_t1, r_t2 = [r_t1a, r_t1a], [r_t2a, r_t2a]
        r_qout, r_knT, r_krp, r_vt, r_qh = R("qout"), R("knT"), R("krp"), R("vt"), R("qh")
        r_hC = [R("hC0"), R("hC1")]
        r_hE1, r_hE2 = [R("hE10"), R("hE11")], [R("hE20"), R("hE21")]
        r_qpo, r_kpo, r_kppo, r_dco = R("qpo"), R("kpo"), R("kppo"), R("dco")
        r_vht, r_ght, r_ato, r_kto = R("vht"), R("ght"), R("ato"), R("kto")
        S.op("pool", lambda e: e.memset(tct[:], 1.0), writes=[r_tab])
        S.op("pool", lambda e: e.memset(tst[:], 0.0), writes=[r_tab])
        pmi = [0]

        def nextpm():
            i = pmi[0] % 4
            pmi[0] += 1
            return pm[i], r_pm[i]

        def mm_fm(ps, r_ps, wsb, c0, m, xT, r_x, nkc, out_p0=0):
            for kc in range(nkc):
                S.op("pe", (lambda e, kc=kc: e.matmul(ps[out_p0:out_p0 + m, :], wsb[:, kc, c0:c0 + m], xT[:, kc, :],
                                                      start=(kc == 0), stop=(kc == nkc - 1))),
                     reads=[r_w] + r_x, writes=[r_ps])

        def mm_tm(ps, r_ps, xT, r_x, j, wsb, c0, n, nkc):
            for kc in range(nkc):
                S.op("pe", (lambda e, kc=kc: e.matmul(ps[:, 0:n], xT[:, kc, j * 128:(j + 1) * 128], wsb[:, kc, c0:c0 + n],
                                                      start=(kc == 0), stop=(kc == nkc - 1))),
                     reads=[r_w] + r_x, writes=[r_ps])

        for i in range(ntiles):
            t0 = i * 512
            S.dma("sp", (lambda e, t0=t0: [e.dma_start(out=ht[:, j, :], in_=h1_d[t0 + j * 128:t0 + (j + 1) * 128, :])
                                          for j in range(4)]), 4, "ht", writes=r_ht)
            S.dma("sp", (lambda e, t0=t0: [e.dma_start(out=tct[64:96, :], in_=cst["rope_c"][:, t0:t0 + 512]),
                                          e.dma_start(out=tst[64:96, :], in_=cst["rope_s"][:, t0:t0 + 512])]),
                  2, "tab", writes=[r_tab])
            for j in range(4):
                norm_transpose(S, ht, r_ht[j], j, stat, r_stat, junk, r_junk, xn, r_xn, pt, r_pt, ident, r_ident,
                               xnT, r_xnT)
            for c in range(3):
                ps, rp = nextpm()
                mm_fm(ps, rp, win, C_CQ + c * 128, 128, xnT, r_xnT, NKC)
                S.op("act", (lambda e, ps=ps, c=c: e.copy(cqT[:, c, :], ps[:])), reads=[rp], writes=[r_cqT])
                S.op("act", (lambda e, ps=ps, c=c: e.activation(sqq[:, c % 2, :], ps[:], AF.Square)),
                     reads=[rp], writes=[r_sqq[c % 2]])
                S.op("pe", (lambda e, c=c: e.matmul(pn[:], ones[:], sqq[:, c % 2, :], start=(c == 0), stop=(c == 2))),
                     reads=[r_ones, r_sqq[c % 2]], writes=[r_pn])
            S.op("act", lambda e: e.activation(rsq[:], pn[:], AF.Sqrt, bias=EPS, scale=1.0 / 384), reads=[r_pn], writes=[r_rsq])
            S.op("dve", lambda e: e.reciprocal(rsq[:], rsq[:]), reads=[r_rsq], writes=[r_rsq])
            for c in range(2):
                ps, rp = nextpm()
                mm_fm(ps, rp, win, C_CKV + c * 128, 128, xnT, r_xnT, NKC)
                S.op("act", (lambda e, ps=ps, c=c: e.copy(ckvT[:, c, :], ps[:])), reads=[rp], writes=[r_ckvT])
                S.op("act", (lambda e, ps=ps, c=c: e.activation(sqkv[:, c, :], ps[:], AF.Square)),
                     reads=[rp], writes=[r_sqkv])
            for c in range(2):
                S.op("pe", (lambda e, c=c: e.matmul(pn[:], ones[:], sqkv[:, c, :], start=(c == 0), stop=(c == 1))),
                     reads=[r_ones, r_sqkv], writes=[r_pn])
            S.op("act", lambda e: e.activation(rskv[:], pn[:], AF.Sqrt, bias=EPS, scale=1.0 / 256), reads=[r_pn], writes=[r_rskv])
            S.op("dve", lambda e: e.reciprocal(rskv[:], rskv[:]), reads=[r_rskv], writes=[r_rskv])
            for j in range(4):
                for c in range(2):
                    S.op("pe", (lambda e, j=j, c=c: e.matmul(pa[:, j:j + 1], sqkv[:, c, j * 128:(j + 1) * 128], ones[:, 0:1],
                                                             start=(c == 0), stop=(c == 1))),
                         reads=[r_ones, r_sqkv], writes=[r_pa])
            S.op("act", lambda e: e.activation(rstok[:, 0:4], pa[:, 0:4], AF.Sqrt, bias=EPS, scale=1.0 / 256),
                 reads=[r_pa], writes=[r_rstok])
            S.op("dve", lambda e: e.reciprocal(rstok[:, 0:4], rstok[:, 0:4]), reads=[r_rstok], writes=[r_rstok])
            ps, rp = nextpm()
            mm_fm(ps, rp, win, C_KR, 32, xnT, r_xnT, NKC, out_p0=64)
            ps2, rp2 = nextpm()
            mm_fm(ps2, rp2, win, C_KRS, 32, xnT, r_xnT, NKC, out_p0=64)
            S.op("dve", (lambda e, ps=ps: e.tensor_tensor(t1[0][64:96, :], ps[64:96, :], tct[64:96, :], ALU.mult)),
                 reads=[rp, r_tab], writes=[r_t1[0]])
            S.op("dve", (lambda e, ps2=ps2: e.tensor_tensor(t2[0][64:96, :], ps2[64:96, :], tst[64:96, :], ALU.mult)),
                 reads=[rp2, r_tab], writes=[r_t2[0]])
            S.op("pool", lambda e: e.tensor_tensor(krp[64:96, :], t1[0][64:96, :], t2[0][64:96, :], ALU.add),
                 reads=[r_t1[0], r_t2[0]], writes=[r_krp])
            S.dma("pool", (lambda e, t0=t0: [e.dma_start(out=scr["KTr"][:, t0:t0 + 512], in_=krp[64:96, :])]), 1, "s_krp",
                  reads=[r_krp])
            for h in range(8):
                b = h % 2
                ps, rp = nextpm()
                mm_fm(ps, rp, wuq, h * 96, 96, cqT, [r_cqT], 3)
                ps2, rp2 = nextpm()
                mm_fm(ps2, rp2, wuq, 768 + h * 32, 32, cqT, [r_cqT], 3, out_p0=64)
                S.op("dve", (lambda e, ps=ps, b=b: e.tensor_tensor(t1[b][0:96, :], ps[0:96, :], tct[0:96, :], ALU.mult)),
                     reads=[rp, r_tab], writes=[r_t1[b]])
                S.op("dve", (lambda e, ps2=ps2, b=b: e.tensor_tensor(t2[b][64:96, :], ps2[64:96, :], tst[64:96, :], ALU.mult)),
                     reads=[rp2, r_tab], writes=[r_t2[b]])
                S.op("pool", (lambda e, b=b: e.tensor_tensor(t1[b][64:96, :], t1[b][64:96, :], t2[b][64:96, :], ALU.add)),
                     reads=[r_t2[b]], writes=[r_t1[b]])
                S.op("pool", (lambda e, b=b, h=h: e.tensor_tensor(qout[0:96, h, :], t1[b][0:96, :], rsq[0:96, :], ALU.mult)),
                     reads=[r_t1[b], r_rsq], writes=[r_qout])
            S.dma("pool", (lambda e, t0=t0: [e.dma_start(out=scr["QT"][h, :, t0:t0 + 512], in_=qout[0:96, h, :])
                                            for h in range(8)]), 8, "s_q", reads=[r_qout])
            for a in range(4):
                ps, rp = nextpm()
                mm_fm(ps, rp, wuk, a * 128, 128, ckvT, [r_ckvT], 2)
                S.op("dve", (lambda e, ps=ps, a=a: e.tensor_tensor(knT[:, a, :], ps[:], rskv[:], ALU.mult)),
                     reads=[rp, r_rskv], writes=[r_knT])
            S.dma("pool", (lambda e, t0=t0: [e.dma_start(out=scr["KTn"][a * 128:(a + 1) * 128, t0:t0 + 512], in_=knT[:, a, :])
                                            for a in range(4)]), 4, "s_kn", reads=[r_knT])
            for j in range(4):
                ps, rp = nextpm()
                mm_tm(ps, rp, ckvT, [r_ckvT], j, wuv, 0, 512, 2)
                S.op("act", (lambda e, ps=ps, j=j: e.activation(vt[:, j, :], ps[:], AF.Copy, scale=rstok[:, j:j + 1])),
                     reads=[rp, r_rstok], writes=[r_vt])
            S.dma("pool", (lambda e, t0=t0: [e.dma_start(
                out=scr["VA"][:, t0 + j * 128:t0 + (j + 1) * 128, :].rearrange("h t d -> t h d"),
                in_=vt[:, j, :].rearrange("t (h d) -> t h d", h=8)) for j in range(4)]), 4, "s_v", reads=[r_vt])
            for h in range(4):
                ps, rp = nextpm()
                mm_fm(ps, rp, win, C_HQ + h * 128, 128, xnT, r_xnT, NKC)
                S.op("act", (lambda e, ps=ps, h=h: e.activation(qh[:, h, :], ps[:], AF.Silu)), reads=[rp], writes=[r_qh])
            for j in range(4):
                ps, rp = nextpm()
                mm_tm(ps, rp, xnT, r_xnT, j, win, C_HI, 512, NKC)
                S.op("act", (lambda e, ps=ps, j=j: e.copy(vht[:, j, :], ps[:])), reads=[rp], writes=[r_vht])
                ps, rp = nextpm()
                mm_tm(ps, rp, xnT, r_xnT, j, win, C_HG, 512, NKC)
                S.op("act", (lambda e, ps=ps, j=j: e.activation(ght[:, j, :], ps[:], AF.Silu)), reads=[rp], writes=[r_ght])
            S.dma("pool", (lambda e, t0=t0: [
                e.dma_start(out=scr["VH"][t0:t0 + 512, :].rearrange("(j p) c -> p j c", p=128), in_=vht[:]),
                e.dma_start(out=scr["GH"][t0:t0 + 512, :].rearrange("(j p) c -> p j c", p=128), in_=ght[:])]),
                2, "s_vg", reads=[r_vht, r_ght])
            for d in range(2):
                for h in range(4):
                    hd = d * 4 + h
                    b = hd % 2
                    A, B, Cc, E1, E2 = hA[b], hB[b], hC[b], hE1[b], hE2[b]
                    rA, rB, rC, rE1, rE2 = r_hA[b], r_hB[b], r_hC[b], r_hE1[b], r_hE2[b]
                    ps, rp = nextpm()
                    mm_fm(ps, rp, win, (C_HFF if d == 0 else C_HFB) + h * 128, 128, xnT, r_xnT, NKC)
                    lbs, omls = lb[:, hd:hd + 1], oml[:, hd:hd + 1]
                    S.op("act", (lambda e, ps=ps, A=A: e.activation(A[:], ps[:], AF.Sigmoid)), reads=[rp], writes=[rA])
                    S.op("act", (lambda e, ps=ps, B=B: e.activation(B[:], ps[:], AF.Sigmoid, scale=-1.0)), reads=[rp], writes=[rB])
                    S.op("pool", (lambda e, B=B, omls=omls: e.tensor_scalar(B[:], B[:], omls, 0.0, ALU.mult, ALU.add)),
                         reads=[r_lb], writes=[rB])
                    S.op("dve", (lambda e, A=A, omls=omls, lbs=lbs: e.tensor_scalar(A[:], A[:], omls, lbs, ALU.mult, ALU.add)),
                         reads=[r_lb], writes=[rA])
                    S.op("act", (lambda e, A=A: e.activation(A[:], A[:], AF.Ln)), writes=[rA])
                    S.op("dve", (lambda e, A=A, Cc=Cc: e.tensor_tensor_scan(Cc[:], rmask[:], A[:], 0.0, ALU.mult, ALU.add)),
                         reads=[rA, r_c], writes=[rC])
                    Cv = Cc[:].rearrange("p (c t) -> p c t", t=32)
                    Av = A[:].rearrange("p (c t) -> p c t", t=32)
                    if d == 0:
                        bsrc, rb = Cc, rC
                        dcol = 31
                    else:
                        S.op("pool", (lambda e, A=A, Cc=Cc: e.tensor_tensor(A[:], A[:], Cc[:], ALU.subtract)),
                             reads=[rC], writes=[rA])
                        S.op("pool", (lambda e, Av=Av, Cv=Cv: e.tensor_tensor(Av, Av, Cv[:, :, 31:32].broadcast_to([128, 16, 32]), ALU.add)),
                             reads=[rC], writes=[rA])
                        bsrc, rb = A, rA
                        dcol = 0
                    S.op("act", (lambda e, E1=E1, bsrc=bsrc: e.activation(E1[:], bsrc[:], AF.Exp)), reads=[rb], writes=[rE1])
                    S.op("act", (lambda e, E2=E2, bsrc=bsrc: e.activation(E2[:], bsrc[:], AF.Exp, scale=-1.0)), reads=[rb], writes=[rE2])
                    S.op("pool", (lambda e, E1=E1, h=h, hd=hd: e.tensor_tensor(qpo[:, hd, :], qh[:, h, :], E1[:], ALU.mult)),
                         reads=[rE1, r_qh], writes=[r_qpo])
                    S.op("dve", (lambda e, E2=E2, B=B: e.tensor_tensor(E2[:], E2[:], B[:], ALU.mult)), reads=[rB], writes=[rE2])
                    S.op("act", (lambda e, E2=E2, hd=hd: e.copy(kpo[:, hd, :], E2[:])), reads=[rE2], writes=[r_kpo])
                    E1v = E1[:].rearrange("p (c t) -> p c t", t=32)
                    E2v = E2[:].rearrange("p (c t) -> p c t", t=32)
                    S.op("dve", (lambda e, E1v=E1v, hd=hd, dcol=dcol: e.tensor_copy(dco[:, hd, :], E1v[:, :, dcol])),
                         reads=[rE1], writes=[r_dco])
                    kv = kppo[:, hd, :].rearrange("p (c t) -> p c t", t=32)
                    S.op("pool", (lambda e, E1v=E1v, E2v=E2v, kv=kv, dcol=dcol: e.tensor_tensor(
                        kv, E2v, E1v[:, :, dcol:dcol + 1].broadcast_to([128, 16, 32]), ALU.mult)),
                        reads=[rE1, rE2], writes=[r_kppo])
            nch = ntok // 32
            S.dma("pool", (lambda e, t0=t0, i=i: [
                e.dma_start(out=scr["QP"][:, :, t0:t0 + 512].rearrange("h p t -> p h t"), in_=qpo[:]),
                e.dma_start(out=scr["DC"][:, :, i * 16:(i + 1) * 16].rearrange("h p c -> p h c"), in_=dco[:])]),
                2, "s_qp", reads=[r_qpo, r_dco])
            for j in range(4):
                for d in range(2):
                    for h in range(4):
                        hd = d * 4 + h
                        S.op("pe", (lambda e, j=j, hd=hd, h=h: e.matmul(
                            pa[:, h * 128:(h + 1) * 128], kpo[:, hd, j * 128:(j + 1) * 128], qpo[:, hd, j * 128:(j + 1) * 128],
                            start=True, stop=True)), reads=[r_kpo, r_qpo], writes=[r_pa])
                    msk = (mF if d == 0 else mB)
                    S.op("dve", (lambda e, j=j, d=d, msk=msk: e.tensor_tensor(
                        ato[:, j, d * 4:(d + 1) * 4, :], pa[:].rearrange("p (h t) -> p h t", h=4),
                        msk[:].rearrange("p (o t) -> p o t", o=1).broadcast_to([128, 4, 128]), ALU.mult)),
                        reads=[r_pa, r_c], writes=[r_ato])
                for hd in range(8):
                    S.op("pe", (lambda e, j=j, hd=hd: e.transpose(ptk[:, hd * 128:(hd + 1) * 128],
                                                                   kppo[:, hd, j * 128:(j + 1) * 128], ident[:])),
                         reads=[r_kppo, r_ident], writes=[r_ptk])
                S.op("act", (lambda e, j=j: e.copy(kto[:, j, :, :], ptk[:].rearrange("p (h k) -> p h k", h=8))),
                     reads=[r_ptk], writes=[r_kto])
            S.dma("pool", (lambda e, t0=t0: [
                e.dma_start(out=scr["AT"][t0:t0 + 512, :, :].rearrange("(j p) h t -> p j h t", p=128), in_=ato[:]),
                e.dma_start(out=scr["KPT"][t0:t0 + 512, :, :].rearrange("(j p) h k -> p j h k", p=128), in_=kto[:])]),
                2, "s_at", reads=[r_ato, r_kto])
        S.barrier()
        S.emit()
    return S


def attn_phase(nc, tag, seqs, scr, xchg=None):
    S = Sched(nc, tag)
    SKMAX = max(sum(p[3] for p in sq["kp"]) for sq in seqs)
    SL = max(sq["nq"] for sq in seqs)
    scale = 96.0 ** -0.5
    with ExitStack() as es:
        C = Ctx(nc, tag, es)
        R = S.res
        kt = [C.sb(f"kt{i}", [128, SKMAX], BF16) for i in range(2)]
        vt = [C.sb(f"vt{i}", [128, SKMAX // 128, 65], BF16) for i in range(2)]
        qt = [C.sb(f"qt{i}", [128, SL], BF16) for i in range(2)]
        pT = [C.sb(f"pT{i}", [128, 1024], BF16) for i in range(3)]
        onesf = C.sb("onesf", [128, 64], F32)
        rl = C.sb("rl", [128, 512], F32)
        osb = C.sb("osb", [128, 512], F32)
        obf = [C.sb(f"obf{i}", [128, 512], BF16) for i in range(2)]
        psS = [C.ps(f"psS{i}", [128, 1024]) for i in range(2)]
        psO = [C.ps(f"psO{i}", [128, 512]) for i in range(2)]
        psB = C.ps("psB", [128, 512])
        r_kt, r_vt, r_qt = [R(), R()], [R(), R()], [R(), R()]
        r_pT, r_psS, r_psO = [R(), R(), R()], [R(), R(), R()], [R(), R()]
        r_ones, r_rl, r_osb, r_obf, r_psB = R(), R(), R(), [R(), R()], R()
        S.op("pool", lambda e: e.memset(onesf[:], 1.0), writes=[r_ones])
        for i in range(2):
            S.op("pool", (lambda e, i=i: e.memset(vt[i][:, :, 64:65], 1.0)), writes=[r_vt[i]])
        r_gath = R()
        r_g2, r_g3 = R(), R()
        r_gs = []
        if xchg is not None:
            n0 = xchg["n0"]
            r_xin = R()
            S.dma("sp", (lambda e: [e.dma_start(out=xchg["XK_in"][a_], in_=scr["KTn"][a_ * 128:(a_ + 1) * 128, 0:n0]) for a_ in range(4)]
                         + [e.dma_start(out=xchg["XR_in"][0:32, :], in_=scr["KTr"][:, 0:n0])]
                         + [e.dma_start(out=xchg["XV_in"][a_].rearrange("(hh t) d -> hh t d", hh=2), in_=scr["VA"][2 * a_:2 * a_ + 2, 0:n0, :])
                            for a_ in range(4)]), 9, "xin", writes=[r_xin])
            r_gs = [r_gath, r_g2, r_g3] + [R() for _ in range(6)]
            ccl = [(xchg["XK_in"][a_], xchg["XK_out"][a_]) for a_ in range(4)] + [(xchg["XR_in"], xchg["XR_out"])] + \
                  [(xchg["XV_in"][a_], xchg["XV_out"][a_]) for a_ in range(4)]
            for ci_, (cin, cout) in enumerate(ccl):
                S.dma("pool", (lambda e, cin=cin, cout=cout: [e.collective_compute(
                    "AllGather", ALU.bypass, replica_groups=xchg["groups"], ins=[cin.opt()], outs=[cout.opt()])]), 1, f"cc{ci_}",
                    reads=[r_xin], writes=[r_gs[ci_]], inc=1)
        heads = []
        for sq in seqs:
            for h in range(8):
                heads.append((sq, h))
        units = []
        obc = [0]
        for hi, (sq, h) in enumerate(heads):
            SK = sum(p[3] for p in sq["kp"])
            for qb in range(sq["nq"] // 512):
                for kc in range(SK // 256):
                    units.append((hi, qb, kc, SK // 256, obc[0] % 2, qb == sq["nq"] // 512 - 1))
                obc[0] += 1
        loaded = [-1]
        pending = []

        def load_head(hi):
            if hi >= len(heads) or hi <= loaded[0]:
                return
            loaded[0] = hi
            sq, h = heads[hi]
            b = hi % 2
            off = 0
            lst = []
            for (ktn, ktr, va, n) in sq["kp"]:
                lst.append((kt[b][0:64, off:off + n], ktn(h)))
                lst.append((kt[b][64:96, off:off + n], ktr))
                off += n
            dep = r_gs if sq.get("gathered") else []
            S.dma("sp", (lambda e, lst=lst: [e.dma_start(out=o, in_=i_) for o, i_ in lst]), len(lst), f"kt{b}", reads=dep, writes=[r_kt[b]])
            off = 0
            lst2 = []
            for (ktn, ktr, va, n) in sq["kp"]:
                lst2.append((vt[b][:, off // 128:(off + n) // 128, 0:64], va(h).rearrange("(c p) d -> p c d", p=128)))
                off += n
            S.dma("sp", (lambda e, lst2=lst2: [e.dma_start(out=o, in_=i_) for o, i_ in lst2]), len(lst2), f"vt{b}", reads=dep, writes=[r_vt[b]])
            q0, nq = sq["q0"], sq["nq"]
            S.dma("sp", (lambda e, b=b, h=h, q0=q0, nq=nq: [e.dma_start(out=qt[b][0:96, 0:nq], in_=scr["QT"][h, :, q0:q0 + nq])]), 1,
                  f"qt{b}", writes=[r_qt[b]])

        def qk(u):
            hi, qb, kc, nkc, ob, lastq = units[u]
            b = hi % 2
            r = u % 2
            r3 = u % 3
            for t in range(2):
                S.op("pe", (lambda e, t=t: e.matmul(psS[r][:, t * 512:(t + 1) * 512], kt[b][0:96, (2 * kc + t) * 128:(2 * kc + t + 1) * 128],
                                                    qt[b][0:96, qb * 512:(qb + 1) * 512], start=True, stop=True)),
                     reads=[r_kt[b], r_qt[b]], writes=[r_psS[r]])
            S.op("act", (lambda e: e.activation(pT[r3][:], psS[r][:], AF.Exp, scale=scale)), reads=[r_psS[r]], writes=[r_pT[r3]])

        def pv(u):
            hi, qb, kc, nkc, ob, lastq = units[u]
            b = hi % 2
            r3 = u % 3
            for t in range(2):
                S.op("pe", (lambda e, t=t: e.matmul(psO[ob][0:65, :], vt[b][:, 2 * kc + t, 0:65], pT[r3][:, t * 512:(t + 1) * 512],
                                                    start=(kc == 0 and t == 0), stop=(kc == nkc - 1 and t == 1))),
                     reads=[r_vt[b], r_pT[r3]], writes=[r_psO[ob]])
            if kc == nkc - 1:
                sq, h = heads[hi]
                S.op("dve", (lambda e: e.reciprocal(rl[64:65, :], psO[ob][64:65, :])), reads=[r_psO[ob]], writes=[r_rl])
                S.op("dve", (lambda e: e.tensor_copy(osb[0:64, :], psO[ob][0:64, :])), reads=[r_psO[ob]], writes=[r_osb])
                t0 = sq["q0"] + qb * 512

                def fin():
                    S.op("pe", (lambda e: e.matmul(psB[0:64, :], onesf[64:65, 0:64], rl[64:65, :], start=True, stop=True)),
                         reads=[r_ones, r_rl], writes=[r_psB])
                    S.op("dve", (lambda e: e.tensor_tensor(obf[ob][0:64, :], osb[0:64, :], psB[0:64, :], ALU.mult)),
                         reads=[r_osb, r_psB], writes=[r_obf[ob]])
                    S.dma("pool", (lambda e: [e.dma_start(out=scr["MIXT"][h * 64:(h + 1) * 64, t0:t0 + 512], in_=obf[ob][0:64, :])]), 1,
                          f"so{ob}", reads=[r_obf[ob]])
                pending.append([4, fin])

        load_head(0)
        load_head(1)
        n = len(units)
        qk(0)
        for u in range(n):
            if u + 1 < n:
                qk(u + 1)
            for pnd in list(pending):
                pnd[0] -= 1
                if pnd[0] <= 0:
                    pnd[1]()
                    pending.remove(pnd)
            pv(u)
            hi, qb, kc, nkc, ob_, lastq = units[u]
            if lastq and kc == nkc - 1:
                load_head(hi + 2)
        for pnd in pending:
            pnd[1]()
        S.barrier()
        S.emit()
    return S


def scan_phase(nc, tag, seq_lens, ntok, scr, xchg=None, cst=None):
    S = Sched(nc, tag)
    nch = ntok // 32
    with ExitStack() as es:
        C = Ctx(nc, tag, es)
        R = S.res
        NR = 3
        qpb = [[C.sb(f"qpb{d}{i}", [128, 4, 128], BF16) for i in range(NR)] for d in range(2)]
        atb = [[C.sb(f"atb{d}{i}", [128, 4, 128], BF16) for i in range(NR)] for d in range(2)]
        kpb = [[C.sb(f"kpb{d}{i}", [128, 4, 128], BF16) for i in range(NR)] for d in range(2)]
        vb = [[C.sb(f"vb{d}{i}", [128, 512], BF16) for i in range(NR)] for d in range(2)]
        r_ld = [[R() for i in range(NR)] for d in range(2)]
        dct = C.sb("dct", [128, 8, nch], F32)
        zer = C.sb("zer", [128, 512], BF16)
        SstAll = C.sb("SstAll", [128, 8, 129], F32)
        Sst = [SstAll[:, hd, 0:128] for hd in range(8)]
        Sbf = [C.sb(f"Sbf{hd}", [128, 128], BF16) for hd in range(8)]
        r_S = [R() for hd in range(8)]
        r_Sb = [R() for hd in range(8)]
        ot = [[C.sb(f"ot{d}{i}", [128, 512], F32) for i in range(2)] for d in range(2)]
        r_ot = [[R() for i in range(2)] for d in range(2)]
        psO = [[C.ps(f"psO{d}{i}", [128, 512]) for i in range(2)] for d in range(2)]
        r_psO = [[R() for i in range(2)] for d in range(2)]
        psU = [C.ps(f"psU{i}", [128, 512]) for i in range(4)]
        r_psU = [R() for i in range(4)]
        r_dc, r_z = R(), R()
        S.dma("sp", (lambda e: [e.dma_start(out=dct[:, hd, :], in_=scr["DC"][hd, :, :]) for hd in range(8)]), 8, "dc", writes=[r_dc])
        S.op("pool", lambda e: e.memset(zer[:], 0.0), writes=[r_z])
        ui = [0]
        offs = []
        a = 0
        for n in seq_lens:
            offs.append(a)
            a += n

        def run_seq(s0, SLs, mode, zero_init=True):
            NB = SLs // 128
            if zero_init:
                for hd in range(8):
                    S.op("pool", (lambda e, hd=hd: e.memset(Sst[hd], 0.0)), writes=[r_S[hd]])
                    S.op("pool", (lambda e, hd=hd: e.memset(Sbf[hd][:], 0.0)), writes=[r_Sb[hd]])

            def load(step):
                if step >= NB:
                    return
                for d in range(2):
                    blk = step if d == 0 else NB - 1 - step
                    t0 = s0 + blk * 128
                    i = step % NR
                    if mode == "full":
                        S.dma("sp", (lambda e, d=d, i=i, t0=t0: [
                            e.dma_start(out=qpb[d][i][:], in_=scr["QP"][d * 4:(d + 1) * 4, :, t0:t0 + 128].rearrange("h p t -> p h t")),
                            e.dma_start(out=atb[d][i][:], in_=scr["AT"][t0:t0 + 128, d * 4:(d + 1) * 4, :]),
                            e.dma_start(out=kpb[d][i][:], in_=scr["KPT"][t0:t0 + 128, d * 4:(d + 1) * 4, :]),
                            e.dma_start(out=vb[d][i][:], in_=scr["VH"][t0:t0 + 128, :])]), 4, f"ld{d}{i}", writes=[r_ld[d][i]])
                    else:
                        S.dma("sp", (lambda e, d=d, i=i, t0=t0: [
                            e.dma_start(out=kpb[d][i][:], in_=scr["KPT"][t0:t0 + 128, d * 4:(d + 1) * 4, :]),
                            e.dma_start(out=vb[d][i][:], in_=scr["VH"][t0:t0 + 128, :])]), 2, f"ld{d}{i}", writes=[r_ld[d][i]])
            load(0)
            load(1)
            for step in range(NB):
                load(step + 2)
                i = step % NR
                ob = step % 2
                if mode == "full":
                    for d in range(2):
                        S.op("pe", (lambda e, d=d, ob=ob: e.matmul(psO[d][ob][:], zer[:, 0:128], zer[:], start=True, stop=False,
                                                                   skip_group_check=True)), reads=[r_z], writes=[r_psO[d][ob]])
                        for h in range(4):
                            S.op("pe", (lambda e, d=d, ob=ob, h=h, i=i: e.matmul(
                                psO[d][ob][:, h * 128:(h + 1) * 128], atb[d][i][:, h, :], vb[d][i][:, h * 128:(h + 1) * 128],
                                start=False, stop=False, skip_group_check=True)), reads=[r_ld[d][i]], writes=[r_psO[d][ob]])
                for ci in range(4):
                    for d in range(2):
                        blk = step if d == 0 else NB - 1 - step
                        c = ci if d == 0 else 3 - ci
                        gch = (s0 + blk * 128) // 32 + c
                        for h in range(4):
                            hd = d * 4 + h
                            if mode == "full":
                                S.op("pe", (lambda e, d=d, ob=ob, h=h, i=i, c=c, hd=hd: e.matmul(
                                    psO[d][ob][32 * c:32 * c + 32, h * 128:(h + 1) * 128], qpb[d][i][:, h, 32 * c:32 * c + 32], Sbf[hd][:],
                                    start=False, stop=(ci == 3), skip_group_check=True, tile_position=(0, 32 * c))),
                                    reads=[r_ld[d][i], r_Sb[hd]], writes=[r_psO[d][ob]])
                            pu = ui[0] % 4
                            ui[0] += 1
                            S.op("pe", (lambda e, d=d, h=h, i=i, c=c, pu=pu: e.matmul(
                                psU[pu][:, 0:128], kpb[d][i][32 * c:32 * c + 32, h, :], vb[d][i][32 * c:32 * c + 32, h * 128:(h + 1) * 128],
                                start=True, stop=True, tile_position=(32 * c, 0))),
                                reads=[r_ld[d][i]], writes=[r_psU[pu]])
                            S.op("dve", (lambda e, hd=hd, pu=pu, gch=gch: e.scalar_tensor_tensor(
                                Sst[hd], Sst[hd], dct[:, hd, gch:gch + 1], psU[pu][:, 0:128], ALU.mult, ALU.add)),
                                reads=[r_psU[pu], r_dc], writes=[r_S[hd]])
                            if mode == "full":
                                S.op("act", (lambda e, hd=hd: e.copy(Sbf[hd][:], Sst[hd])), reads=[r_S[hd]], writes=[r_Sb[hd]])
                if mode == "full":
                    for d in range(2):
                        blk = step if d == 0 else NB - 1 - step
                        t0 = s0 + blk * 128
                        S.op("dve" if d == 0 else "act",
                             (lambda e, d=d, ob=ob: (e.tensor_copy(ot[d][ob][:], psO[d][ob][:]) if d == 0
                                                     else e.copy(ot[d][ob][:], psO[d][ob][:]))),
                             reads=[r_psO[d][ob]], writes=[r_ot[d][ob]])
                        S.dma("pool", (lambda e, d=d, ob=ob, t0=t0: [e.dma_start(out=scr["OF"][d, t0:t0 + 128, :], in_=ot[d][ob][:])]), 1,
                              f"so{d}{ob}", reads=[r_ot[d][ob]])

        if xchg is None:
            for s0, n in zip(offs, seq_lens):
                run_seq(s0, n, "full")
        else:
            n0 = seq_lens[0]
            G = C.sb("G", [128, 4, 8, 129], F32)
            rkm = C.sb("rkm", [128, 8], F32)
            tmp = C.sb("tmp", [128, 128], F32)
            r_G, r_rk, r_tmp, r_xs = R(), R(), R(), R()
            S.dma("sp", lambda e: [e.dma_start(out=rkm[:], in_=cst["rankmask"])], 1, "rk", writes=[r_rk])
            run_seq(offs[0], n0, "state")
            for hd in range(8):
                S.op("dve", (lambda e, hd=hd: e.tensor_reduce(SstAll[:, hd, 128:129], dct[:, hd, offs[0] // 32:(offs[0] + n0) // 32],
                                                              AX.X, ALU.mult)), reads=[r_dc], writes=[r_S[hd]])
            S.dma("pool", (lambda e: [e.dma_start(out=xchg["XS_in"].rearrange("(h p) c -> p h c", p=128), in_=SstAll[:])]), 1, "xs",
                  reads=r_S, writes=[r_xs])
            S.dma("pool", (lambda e: [e.collective_compute("AllGather", ALU.bypass, replica_groups=xchg["groups"],
                                                           ins=[xchg["XS_in"].opt()], outs=[xchg["XS_out"].opt()])]), 1, "ccs", reads=[r_xs], writes=[r_G], inc=1)
            for s0, n in list(zip(offs, seq_lens))[1:]:
                run_seq(s0, n, "full")
            S.dma("sp", (lambda e: [e.dma_start(out=G[:], in_=xchg["XS_out"].rearrange("(r h p) c -> p r h c", r=4, h=8))]), 1, "g",
                  reads=[r_G], writes=[r_G])
            for hd in range(8):
                S.op("pool", (lambda e, hd=hd: e.memset(Sst[hd], 0.0)), writes=[r_S[hd]])
                order = range(4) if hd < 4 else range(3, -1, -1)
                for i in order:
                    mcol = rkm[:, (0 if hd < 4 else 4) + i:(0 if hd < 4 else 4) + i + 1]
                    S.op("dve", (lambda e, hd=hd, i=i: e.scalar_tensor_tensor(tmp[:], Sst[hd], G[:, i, hd, 128:129], G[:, i, hd, 0:128],
                                                                            ALU.mult, ALU.add)), reads=[r_G, r_S[hd]], writes=[r_tmp])
                    S.op("dve", (lambda e, hd=hd: e.tensor_tensor(tmp[:], tmp[:], Sst[hd], ALU.subtract)), reads=[r_S[hd]], writes=[r_tmp])
                    S.op("dve", (lambda e, hd=hd, mcol=mcol: e.scalar_tensor_tensor(Sst[hd], tmp[:], mcol, Sst[hd], ALU.mult, ALU.add)),
                         reads=[r_tmp, r_rk], writes=[r_S[hd]])
                S.op("act", (lambda e, hd=hd: e.copy(Sbf[hd][:], Sst[hd])), reads=[r_S[hd]], writes=[r_Sb[hd]])
            run_seq(offs[0], n0, "full", zero_init=False)
        S.barrier()
        S.emit()
    return S


def outproj_phase(nc, tag, ntok, h1_d, h2_d, cst, w, scr):
    S = Sched(nc, tag)
    ntiles = ntok // 512
    with ExitStack() as es:
        C = Ctx(nc, tag, es)
        R = S.res
        wo = C.sb("wo", [128, 8, D], BF16)
        st = [C.sb("st0", [128, 512], F32), C.sb("st1", [128, 512], F32)]
        ident = C.sb("ident", [128, 128], BF16)
        onb = C.sb("onb", [128, 512], F32)
        ht = [C.sb(f"ht{i}", [128, 4, D], F32) for i in range(2)]
        mT = [C.sb(f"mT{i}", [128, 8, 512], BF16) for i in range(2)]
        of = [C.sb(f"of{i}", [128, 4, 512], F32) for i in range(2)]
        ob = [C.sb(f"ob{i}", [128, 4, 512], F32) for i in range(2)]
        gh = [C.sb(f"gh{i}", [128, 4, 512], BF16) for i in range(2)]
        osum4 = C.sb("osum4", [128, 4, 512], F32)
        junk = C.sb("junk", [128, 128], BF16)
        stat4 = C.sb("stat4", [128, 40], F32)
        mh4 = C.sb("mh4", [128, 4, 512], BF16)
        pt = C.ps("pt", [128, 1024], BF16)
        pt2 = C.ps("pt2", [128, 1024], BF16)
        r_pts = [R(), R()]
        po = [C.ps(f"po{i}", [128, 512]) for i in range(2)]
        r_w, r_st, r_c, r_ident = R(), [R(), R()], R(), R()
        S.dma("sp", lambda e: [e.dma_start(out=onb[:], in_=cst["hg_norm_b"]), e.dma_start(out=st[0][:, 0:128], in_=cst["ident"])],
              2, "c", writes=[r_c, r_st[0]])
        S.op("pool", lambda e: e.tensor_copy(ident[:], st[0][:, 0:128]), reads=[r_st[0]], writes=[r_ident])
        qi = [1]
        load_w(S, w["w_o"], wo, 8, D, None, st, r_st, r_w, qi)
        r_ld = [R(), R()]
        r_ht = [[R() for j in range(4)] for i in range(2)]
        r_mT = [[R() for j in range(4)] for i in range(2)]
        r_osum, r_junk, r_stat, r_mh, r_pt, r_po = R(), R(), R(), R(), R(), [R(), R()]

        def load(i):
            if i >= ntiles:
                return
            b = i % 2
            t0 = i * 512
            S.dma("sp", (lambda e: [e.dma_start(out=ht[b][:, j, :], in_=h1_d[t0 + j * 128:t0 + (j + 1) * 128, :]) for j in range(4)]),
                  4, f"ht{b}", writes=r_ht[b])
            S.dma("sp", (lambda e: [
                e.dma_start(out=mT[b][:, 0:4, :], in_=scr["MIXT"][0:512, t0:t0 + 512].rearrange("(c p) t -> p c t", p=128)),
                e.dma_start(out=of[b][:], in_=scr["OF"][0, t0:t0 + 512, :].rearrange("(j p) c -> p j c", p=128)),
                e.dma_start(out=ob[b][:], in_=scr["OF"][1, t0:t0 + 512, :].rearrange("(j p) c -> p j c", p=128)),
                e.dma_start(out=gh[b][:], in_=scr["GH"][t0:t0 + 512, :].rearrange("(j p) c -> p j c", p=128))]),
                4, f"ld{b}", writes=[r_ld[b]] + r_mT[b])
        load(0)

        def body(i):
            b = i % 2
            load(i + 1)
            t0 = i * 512
            r_os = [R() for j in range(4)]
            r_mhj = [R() for j in range(4)]
            r_stj = [R() for j in range(4)]
            r_jk = [Res("junk") for _ in range(16)]
            for j in range(4):
                S.op("dve" if j % 2 == 0 else "pool",
                     (lambda e, j=j: e.tensor_tensor(osum4[:, j, :], of[b][:, j, :], ob[b][:, j, :], ALU.add)),
                     reads=[r_ld[b], r_osum], writes=[r_os[j]])
            for j in range(4):
                for h in range(4):
                    S.op("act", (lambda e, h=h, j=j: e.activation(junk[:], osum4[:, j, h * 128:(h + 1) * 128], AF.Square,
                                                                  accum_out=stat4[:, j * 8 + h:j * 8 + h + 1])),
                         reads=[r_os[j]], writes=[r_jk[j * 4 + h], r_stj[j]])
            for j in range(4):
                S.op("act", (lambda e, j=j: e.activation(stat4[:, j * 8 + 4:j * 8 + 8], stat4[:, j * 8:j * 8 + 4], AF.Sqrt, bias=EPS, scale=1.0 / 128)),
                     reads=[r_stj[j]], writes=[r_stj[j]])
            for j in range(4):
                S.op("dve", (lambda e, j=j: e.reciprocal(stat4[:, j * 8 + 4:j * 8 + 8], stat4[:, j * 8 + 4:j * 8 + 8])), reads=[r_stj[j]], writes=[r_stj[j]])
            for j in range(4):
                ov = osum4[:, j, :].rearrange("p (h v) -> p h v", h=4)
                S.op("dve", (lambda e, ov=ov, j=j: e.tensor_tensor(
                    ov, ov, stat4[:, j * 8 + 4:j * 8 + 8].rearrange("p (h o) -> p h o", o=1).broadcast_to([128, 4, 128]), ALU.mult)),
                    reads=[r_stj[j]], writes=[r_os[j]])
            for j in range(4):
                S.op("pool", (lambda e, j=j: e.tensor_tensor(osum4[:, j, :], osum4[:, j, :], onb[:], ALU.mult)), reads=[r_c], writes=[r_os[j]])
            for j in range(4):
                S.op("dve", (lambda e, j=j: e.tensor_tensor(mh4[:, j, :], osum4[:, j, :], gh[b][:, j, :], ALU.mult)),
                     reads=[r_os[j], r_ld[b], r_mh], writes=[r_mhj[j]])
            for j in range(4):
                ptj, r_ptj = [pt, pt2][j % 2], r_pts[j % 2]
                for c in range(4):
                    S.op("pe", (lambda e, c=c, j=j, ptj=ptj: e.transpose(ptj[:, c * 128:(c + 1) * 128], mh4[:, j, c * 128:(c + 1) * 128], ident[:])),
                         reads=[r_mhj[j], r_ident], writes=[r_ptj])
                S.op("act", (lambda e, j=j, ptj=ptj: e.copy(mT[b][:, 4:8, j * 128:(j + 1) * 128],
                                                            ptj[:, 0:512].rearrange("p (k t) -> p k t", k=4))),
                     reads=[r_ptj], writes=[r_mT[b][j]])
            S.op("pool", (lambda e: e.memset(stat4[:, 32:33], 0.0)), writes=r_os + r_mhj + [r_osum, r_mh])
            for j in range(4):
                for hh in range(2):
                    pb = (j * 2 + hh) % 2
                    for kc in range(8):
                        S.op("pe", (lambda e, j=j, hh=hh, kc=kc, pb=pb: e.matmul(
                            po[pb][:], mT[b][:, kc, j * 128:(j + 1) * 128], wo[:, kc, hh * 512:(hh + 1) * 512],
                            start=(kc == 0), stop=(kc == 7))), reads=[r_w, r_mT[b][j]], writes=[r_po[pb]])
                    dst = ht[b][:, j, hh * 512:(hh + 1) * 512]
                    S.op("dve", (lambda e, dst=dst, pb=pb: e.tensor_tensor(dst, dst, po[pb][:], ALU.add)),
                         reads=[r_po[pb]], writes=[r_ht[b][j]])
                S.dma("pool", (lambda e, j=j: [e.dma_start(out=h2_d[t0 + j * 128:t0 + (j + 1) * 128, :], in_=ht[b][:, j, :])]), 1,
                      f"so{b}{j}", reads=[r_ht[b][j]])
        for i in range(ntiles):
            body(i)
        S.barrier()
        S.emit()
    return S


def ple_phase(nc, tag, ntok, h3_d, p_d, y_d, cst, w):
    S = Sched(nc, tag)
    ntiles = ntok // 512
    with ExitStack() as es:
        C = Ctx(nc, tag, es)
        R = S.res
        wg = C.sb("wg", [128, 8, D], BF16)
        wp = C.sb("wp", [128, 2, D], BF16)
        st = [C.sb("st0", [128, 512], F32), C.sb("st1", [128, 512], F32)]
        ident = C.sb("ident", [128, 128], BF16)
        gain = C.sb("gain", [128, 8], F32)
        fnb = C.sb("fnb", [128, D], F32)
        ht = [C.sb(f"ht{i}", [128, 4, D], F32) for i in range(2)]
        ptl = [C.sb(f"ptl{i}", [128, 4, 256], F32) for i in range(2)]
        xn4 = C.sb("xn4", [128, 4, D], BF16)
        junk = C.sb("junk", [128, D], BF16)
        stat = C.sb("stat", [128, 16], F32)
        xnT = C.sb("xnT", [128, 8, 512], BF16)
        pb16 = C.sb("pb16", [128, 4, 256], BF16)
        pT = C.sb("pT", [128, 2, 512], BF16)
        gsb = [C.sb(f"gsb{i}", [128, 512], F32) for i in range(2)]
        pt = C.ps("pt", [128, 1024], BF16)
        pt2 = C.ps("pt2", [128, 1024], BF16)
        pg = [C.ps(f"pg{i}", [128, 512]) for i in range(2)]
        pp = [C.ps(f"pp{i}", [128, 512]) for i in range(2)]
        r_w, r_st, r_c, r_ident = R(), [R(), R()], R(), R()
        S.dma("sp", lambda e: [e.dma_start(out=gain[:], in_=cst["ple_norm"]), e.dma_start(out=fnb[:], in_=cst["final_norm_b"]),
                               e.dma_start(out=st[0][:, 0:128], in_=cst["ident"])], 3, "c", writes=[r_c, r_st[0]])
        S.op("pool", lambda e: e.tensor_copy(ident[:], st[0][:, 0:128]), reads=[r_st[0]], writes=[r_ident])
        S.op("pool", lambda e: e.tensor_copy(stat[:, 15:16], gain[:, 0:1]), reads=[r_c], writes=[R()])
        qi = [1]
        load_w(S, w["w_ple_gate"], wg, 8, D, gain, st, r_st, r_w, qi)
        load_w(S, w["w_ple_proj"], wp, 2, D, None, st, r_st, r_w, qi)
        r_ht = [[R() for j in range(4)] for i in range(2)]
        r_pl = [R(), R()]
        r_stat = [R() for j in range(4)]
        r_junk, r_pt = R(), R()
        r_xn4 = [R() for j in range(4)]
        r_pts = [R(), R()]
        r_xnT = [R() for k in range(8)]
        r_pb16, r_pT, r_gsb, r_pg, r_pp = [R() for j in range(4)], R(), [R(), R()], [R(), R()], [R(), R()]

        def load(i):
            if i >= ntiles:
                return
            b = i % 2
            t0 = i * 512
            S.dma("sp", (lambda e: [e.dma_start(out=ht[b][:, j, :], in_=h3_d[t0 + j * 128:t0 + (j + 1) * 128, :]) for j in range(4)]),
                  4, f"ht{b}", writes=r_ht[b])
            S.dma("sp", (lambda e: [e.dma_start(out=ptl[b][:], in_=p_d[t0:t0 + 512, :].rearrange("(j p) c -> p j c", p=128))]),
                  1, f"pl{b}", writes=[r_pl[b]])
        load(0)

        def body(i):
            b = i % 2
            load(i + 1)
            t0 = i * 512
            norm_transpose4(S, ht[b], r_ht[b], stat, r_stat, junk, xn4, r_xn4, [pt, pt2], r_pts, ident, r_ident, xnT, r_xnT)
            for j in range(4):
                S.op("pool", (lambda e, j=j: e.tensor_copy(pb16[:, j, :], ptl[b][:, j, :])), reads=[r_pl[b]], writes=[r_pb16[j]])
            for j in range(4):
                ptj, r_ptj = [pt, pt2][j % 2], r_pts[j % 2]
                for c in range(2):
                    S.op("pe", (lambda e, c=c, j=j, ptj=ptj: e.transpose(ptj[:, c * 128:(c + 1) * 128], pb16[:, j, c * 128:(c + 1) * 128], ident[:])),
                         reads=[r_pb16[j], r_ident], writes=[r_ptj])
                S.op("act", (lambda e, j=j, ptj=ptj: e.copy(pT[:, :, j * 128:(j + 1) * 128], ptj[:, 0:256].rearrange("p (k t) -> p k t", k=2))),
                     reads=[r_ptj], writes=[r_pT])
            for j in range(4):
                for hh in range(2):
                    k2 = (j * 2 + hh) % 2
                    for kc in range(8):
                        S.op("pe", (lambda e, j=j, hh=hh, kc=kc, k2=k2: e.matmul(
                            pg[k2][:], xnT[:, kc, j * 128:(j + 1) * 128], wg[:, kc, hh * 512:(hh + 1) * 512],
                            start=(kc == 0), stop=(kc == 7))), reads=[r_w] + r_xnT, writes=[r_pg[k2]])
                    for kc in range(2):
                        S.op("pe", (lambda e, j=j, hh=hh, kc=kc, k2=k2: e.matmul(
                            pp[k2][:], pT[:, kc, j * 128:(j + 1) * 128], wp[:, kc, hh * 512:(hh + 1) * 512],
                            start=(kc == 0), stop=(kc == 1))), reads=[r_w, r_pT], writes=[r_pp[k2]])
                    S.op("act", (lambda e, k2=k2: e.activation(gsb[k2][:], pg[k2][:], AF.Sigmoid)), reads=[r_pg[k2]], writes=[r_gsb[k2]])
                    S.op("dve", (lambda e, k2=k2: e.tensor_tensor(gsb[k2][:], gsb[k2][:], pp[k2][:], ALU.mult)),
                         reads=[r_pp[k2]], writes=[r_gsb[k2]])
                    dst = ht[b][:, j, hh * 512:(hh + 1) * 512]
                    S.op("pool", (lambda e, dst=dst, k2=k2: e.tensor_tensor(dst, dst, gsb[k2][:], ALU.add)),
                         reads=[r_gsb[k2]], writes=[r_ht[b][j]])
            r_j2 = [Res("junk") for _ in range(4)]
            for j in range(4):
                S.op("act", (lambda e, j=j: e.activation(junk[:], ht[b][:, j, :], AF.Square, accum_out=stat[:, 8 + j:9 + j])),
                     reads=[r_ht[b][j]], writes=[r_j2[j], r_stat[j]])
            for j in range(4):
                S.op("act", (lambda e, j=j: e.activation(stat[:, 8 + j:9 + j], stat[:, 8 + j:9 + j], AF.Sqrt, bias=EPS, scale=1.0 / D)),
                     writes=[r_stat[j]])
            for j in range(4):
                S.op("dve", (lambda e, j=j: e.reciprocal(stat[:, 8 + j:9 + j], stat[:, 8 + j:9 + j])), writes=[r_stat[j]])
            for j in range(4):
                S.op("dve", (lambda e, j=j: e.scalar_tensor_tensor(ht[b][:, j, :], ht[b][:, j, :], stat[:, 8 + j:9 + j], fnb[:], ALU.mult, ALU.mult)),
                     reads=[r_stat[j], r_c], writes=[r_ht[b][j]])
                S.dma("pool", (lambda e, j=j: [e.dma_start(out=y_d[t0 + j * 128:t0 + (j + 1) * 128, :], in_=ht[b][:, j, :])]), 1,
                      f"so{b}{j}", reads=[r_ht[b][j]])
        for i in range(ntiles):
            body(i)
        S.barrier()
        S.emit()
    return S


CONST_SHAPES = {
    "ident": [128, 128], "rmask": [128, 512], "maskF": [128, 128], "maskB": [128, 128],
    "ffn1_norm": [128, 8], "mix_norm": [128, 8], "q_norm": [128, 3], "kv_norm": [128, 2], "hg_lb": [128, 16],
    "hg_norm_b": [128, 512], "ffn2_norm": [128, 8], "ple_norm": [128, 8], "final_norm_b": [128, D],
}
W_SHAPES = {
    "ffn1_wg": [D, DFF], "ffn1_wu": [D, DFF], "ffn1_wd": [DFF, D], "w_in": [D, WIN_COLS], "w_uq": [384, UQ_COLS],
    "w_uk": [256, 512], "w_uv": [256, 512], "w_o": [D, D], "ffn2_wg": [D, DFF], "ffn2_wu": [D, DFF], "ffn2_wd": [DFF, D],
    "w_ple_gate": [D, D], "w_ple_proj": [256, D],
}


def build_program(seq_lens, dbg=(), phases=None, gather=False):
    Sched.DMA_SEMS = {}
    Sched.DMA_CNT = {}
    nc = bass.Bass("TRN2", target_bir_lowering=False)
    ntok = sum(seq_lens)
    ntiles = ntok // 512

    def inp(name, shape, dt=F32):
        return nc.dram_tensor(name, shape, dt, kind="ExternalInput").ap()

    def scratch(name, shape, dt):
        kind = "ExternalOutput" if name in dbg else "Internal"
        return nc.dram_tensor(name, shape, dt, kind=kind).ap()

    x = inp("x", [ntok, D])
    p = inp("p", [ntok, 256])
    cst = {k: inp(k, v) for k, v in CONST_SHAPES.items()}
    cst["rope_c"] = inp("rope_c", [32, ntok])
    cst["rope_s"] = inp("rope_s", [32, ntok])
    w = {k: inp(k, v) for k, v in W_SHAPES.items()}
    y = nc.dram_tensor("y", [ntok, D], F32, kind="ExternalOutput").ap()
    h1 = scratch("h1", [ntok, D], F32)
    h2 = scratch("h2", [ntok, D], F32)
    h3 = scratch("h3", [ntok, D], F32)
    scr = {
        "QT": scratch("QT", [8, 96, ntok], BF16), "KTn": scratch("KTn", [512, ntok], BF16),
        "KTr": scratch("KTr", [32, ntok], BF16), "VA": scratch("VA", [8, ntok, 64], BF16),
        "QP": scratch("QP", [8, 128, ntok], BF16), "DC": scratch("DC", [8, 128, ntok // 32], F32),
        "VH": scratch("VH", [ntok, 512], BF16), "GH": scratch("GH", [ntok, 512], BF16),
        "AT": scratch("AT", [ntok, 8, 128], BF16), "KPT": scratch("KPT", [ntok, 8, 128], BF16),
        "MIXT": scratch("MIXT", [512, ntok], BF16), "OF": scratch("OF", [2, ntok, 512], F32),
    }
    def on(k):
        return phases is None or k in phases
    if on("f1"):
        ffn_phase(nc, "f1", x, h1, ntiles, cst["ffn1_norm"], w["ffn1_wg"], w["ffn1_wu"], w["ffn1_wd"], cst["ident"])
    if on("mi"):
        mixer_phase2(nc, "ma", "a", ntok, h1, cst, w, scr)
        mixer_phase2(nc, "mb", "b", ntok, h1, cst, w, scr)
    seqs = []
    a = 0
    for SLs in seq_lens:
        b = a + SLs
        seqs.append(dict(q0=a, nq=SLs, kp=[((lambda h, a=a, b=b: scr["KTn"][h * 64:(h + 1) * 64, a:b]), scr["KTr"][:, a:b],
                                            (lambda h, a=a, b=b: scr["VA"][h, a:b, :]), SLs)]))
        a = b
    xchg = None
    if gather:
        n0 = seq_lens[0]
        cst["rankmask"] = inp("rankmask", [128, 8])
        xchg = dict(n0=n0, groups=[[0, 1, 2, 3], [4, 5, 6, 7]],
                    XK_in=[scratch(f"XK_in{a_}", [128, n0], BF16) for a_ in range(4)],
                    XK_out=[scratch(f"XK_out{a_}", [4 * 128, n0], BF16) for a_ in range(4)],
                    XR_in=scratch("XR_in", [128, n0], BF16), XR_out=scratch("XR_out", [4 * 128, n0], BF16),
                    XV_in=[scratch(f"XV_in{a_}", [2 * n0, 64], BF16) for a_ in range(4)],
                    XV_out=[scratch(f"XV_out{a_}", [4 * 2 * n0, 64], BF16) for a_ in range(4)],
                    XS_in=scratch("XS_in", [1024, 129], F32), XS_out=scratch("XS_out", [4096, 129], F32))
        kp = []
        for r in range(4):
            kp.append(((lambda h, r=r: xchg["XK_out"][h // 2][r * 128 + (h % 2) * 64:r * 128 + (h % 2) * 64 + 64, :]),
                       xchg["XR_out"][r * 128:r * 128 + 32, :],
                       (lambda h, r=r: xchg["XV_out"][h // 2].rearrange("(r hh t) d -> r hh t d", r=4, hh=2)[r, h % 2]),
                       n0))
        seqs[0] = dict(q0=0, nq=n0, kp=kp, gathered=True)
        seqs = seqs[1:] + seqs[0:1]
    if on("at"):
        attn_phase(nc, "at", seqs, scr, xchg=xchg)
    if on("sc"):
        scan_phase(nc, "sc", list(seq_lens), ntok, scr, xchg=xchg, cst=cst)
    if on("op"):
        outproj_phase(nc, "op", ntok, h1, h2, cst, w, scr)
    if on("f2"):
        ffn_phase(nc, "f2", h2, h3, ntiles, cst["ffn2_norm"], w["ffn2_wg"], w["ffn2_wu"], w["ffn2_wd"], cst["ident"])
    if on("pl"):
        ple_phase(nc, "pl", ntok, h3, p, y, cst, w)
    return nc


def _lay(v, nch):
    return np.ascontiguousarray(np.asarray(v, np.float32).reshape(nch, 128).T)


def host_consts(inputs):
    f = np.float32
    c = {}
    c["ident"] = np.eye(128, dtype=f)
    rm = np.ones((128, 512), f)
    rm[:, 0::32] = 0.0
    c["rmask"] = rm
    idx = np.arange(128)
    same = (idx[:, None] // 32) == (idx[None, :] // 32)
    c["maskF"] = (same & (idx[:, None] <= idx[None, :])).astype(f)
    c["maskB"] = (same & (idx[:, None] >= idx[None, :])).astype(f)
    c["ffn1_norm"] = _lay(inputs["ffn1_norm"][0], 8)
    c["mix_norm"] = _lay(inputs["mix_norm"][0], 8)
    c["q_norm"] = _lay(inputs["q_norm"][0], 3)
    c["kv_norm"] = _lay(inputs["kv_norm"][0], 2)
    lb = np.asarray(inputs["hg_lb"], f).reshape(2, 2, 4, 128)
    c["hg_lb"] = np.ascontiguousarray(lb.transpose(3, 0, 1, 2).reshape(128, 16))
    c["hg_norm_b"] = np.ascontiguousarray(np.broadcast_to(np.asarray(inputs["hg_norm"][0], f)[None, :], (128, 512)))
    c["ffn2_norm"] = _lay(inputs["ffn2_norm"][0], 8)
    c["ple_norm"] = _lay(inputs["ple_norm"][0], 8)
    c["final_norm_b"] = np.ascontiguousarray(np.broadcast_to(np.asarray(inputs["final_norm"], f)[None, :], (128, D)))
    wts = {}
    for k in ("ffn1_wg", "ffn1_wu", "ffn1_wd", "w_uk", "w_uv", "w_o", "ffn2_wg", "ffn2_wu", "ffn2_wd", "w_ple_gate", "w_ple_proj"):
        wts[k] = np.ascontiguousarray(np.asarray(inputs[k][0], f))
    win = np.asarray(inputs["w_in"][0], f)
    wts["w_in"] = np.ascontiguousarray(np.concatenate([win, win[:, 656:672], win[:, 640:656]], axis=1))
    wq = np.asarray(inputs["w_uq"][0], f)
    sw = [np.concatenate([wq[:, h * 96 + 80:h * 96 + 96], wq[:, h * 96 + 64:h * 96 + 80]], axis=1) for h in range(8)]
    wts["w_uq"] = np.ascontiguousarray(np.concatenate([wq] + sw, axis=1))
    return c, wts


def rope_tables(pos):
    inv = np.exp(np.arange(0, 32, 2, dtype=np.float32) * np.float32(-np.log(10000.0) / 32)).astype(np.float32)
    ang = (pos.astype(np.float32)[None, :] * inv[:, None]).astype(np.float32)
    cs, sn = np.cos(ang).astype(np.float32), np.sin(ang).astype(np.float32)
    return np.ascontiguousarray(np.concatenate([cs, cs], 0)), np.ascontiguousarray(np.concatenate([-sn, sn], 0))


_PROG = {}


def run_balanced(inputs):
    xp = np.asarray(inputs["x_prompt"], np.float32)
    xs = np.asarray(inputs["x_sample"], np.float32)
    pp = np.asarray(inputs["p_prompt"], np.float32)[0]
    psm = np.asarray(inputs["p_sample"], np.float32)[0]
    SP, SS = xp.shape[1], xs.shape[1]
    Q = SP // 4
    lens = (Q, SS, SS)
    c, wts = host_consts(inputs)
    key = ("bal", lens)
    if key not in _PROG:
        _PROG[key] = build_program(lens, gather=True)
    nc = _PROG[key]
    in_maps = []
    for core in range(8):
        pb, r = core // 4, core % 4
        im = {"x": np.ascontiguousarray(np.concatenate([xp[pb, r * Q:(r + 1) * Q], xs[2 * core], xs[2 * core + 1]], axis=0)),
              "p": np.ascontiguousarray(np.concatenate([pp[pb, r * Q:(r + 1) * Q], psm[2 * core], psm[2 * core + 1]], axis=0))}
        pos = np.concatenate([np.arange(r * Q, (r + 1) * Q, dtype=np.float32), np.arange(SS, dtype=np.float32),
                              np.arange(SS, dtype=np.float32)])
        im["rope_c"], im["rope_s"] = rope_tables(pos)
        rk = np.zeros((128, 8), np.float32)
        for i in range(4):
            rk[:, i] = 1.0 if i < r else 0.0
            rk[:, 4 + i] = 1.0 if i > r else 0.0
        im["rankmask"] = rk
        im.update(c)
        im.update(wts)
        in_maps.append(im)
    res = run_bass_kernel_spmd(nc, in_maps, core_ids=list(range(8)))
    y_prompt = np.empty((2, SP, D), np.float32)
    y_sample = np.empty((16, SS, D), np.float32)
    for core in range(8):
        pb, r = core // 4, core % 4
        y = np.asarray(res.results[core]["y"])
        y_prompt[pb, r * Q:(r + 1) * Q] = y[0:Q]
        y_sample[2 * core] = y[Q:Q + SS]
        y_sample[2 * core + 1] = y[Q + SS:]
    return (y_prompt, y_sample)


def kernel(**inputs):
    return run_balanced(inputs)


def _roll(make_gen, n, stagger):
    active = []
    nxt = 0
    since = 10 ** 9
    while nxt < n or active:
        if nxt < n and len(active) < 2 and (not active or since >= stagger):
            active.append(make_gen(nxt))
            nxt += 1
            since = 0
        for g in list(active):
            try:
                next(g)
            except StopIteration:
                active.remove(g)
        since += 1


def _drive(gens):
    alive = list(gens)
    while alive:
        for g in list(alive):
            try:
                next(g)
            except StopIteration:
                alive.remove(g)


def mixer_phase2(nc, tag, part, ntok, h1_d, cst, w, scr):
    S = Sched(nc, tag)
    ntiles = ntok // 512
    with ExitStack() as es:
        C = Ctx(nc, tag, es)
        R = S.res
        gains = C.sb("gains", [128, 16], F32)
        ident = C.sb("ident", [128, 128], BF16)
        ht = C.sb("ht", [128, 4, D], F32)
        xn4 = C.sb("xn4", [128, 4, D], BF16)
        junk = C.sb("junk", [128, D], BF16)
        stat = C.sb("stat", [128, 16], F32)
        pt = C.ps("pt", [128, 1024], BF16)
        pt2 = C.ps("pt2", [128, 1024], BF16)
        pm = [C.ps(f"pm{i}", [128, 512]) for i in range(3)]
        pa = C.ps("pa", [128, 512])
        r_c, r_ident, r_w = R(), R(), R()
        r_ht = [R() for j in range(4)]
        r_stat = [R() for j in range(4)]
        r_xn4 = [R() for j in range(4)]
        r_pts = [R(), R()]
        r_pm = [R() for i in range(3)]
        r_pa = R()
        st = [ht[:, 0, 0:512], ht[:, 1, 0:512]]
        r_st = [r_ht[0], r_ht[1]]
        pmi = [0]

        def nextpm():
            i = pmi[0] % 3
            pmi[0] += 1
            return pm[i], r_pm[i]

        def cdma(dst, src):
            S.dma("sp", (lambda e: [e.dma_start(out=dst, in_=src)]), 1, "c", writes=[r_c])

        def mm_fm(ps, r_ps, wsb, c0, m, xT, r_x, nkc, out_p0=0):
            for kc in range(nkc):
                S.op("pe", (lambda e, kc=kc: e.matmul(ps[out_p0:out_p0 + m, :], wsb[:, kc, c0:c0 + m], xT[:, kc, :],
                                                      start=(kc == 0), stop=(kc == nkc - 1))),
                     reads=[r_w] + r_x, writes=[r_ps])

        def mm_tm(ps, r_ps, xT, r_x, j, wsb, c0, n, nkc):
            for kc in range(nkc):
                S.op("pe", (lambda e, kc=kc: e.matmul(ps[:, 0:n], xT[:, kc, j * 128:(j + 1) * 128], wsb[:, kc, c0:c0 + n],
                                                      start=(kc == 0), stop=(kc == nkc - 1))),
                     reads=[r_w] + r_x, writes=[r_ps])

        cdma(gains[:, 0:8], cst["mix_norm"])
        cdma(gains[:, 8:11], cst["q_norm"])
        cdma(gains[:, 11:13], cst["kv_norm"])
        S.dma("sp", (lambda e: [e.dma_start(out=ht[:, 2, 0:128], in_=cst["ident"])]), 1, "c2", writes=[r_ht[2]])
        S.op("pool", lambda e: e.tensor_copy(ident[:], ht[:, 2, 0:128]), reads=[r_ht[2]], writes=[r_ident])
        S.op("pool", lambda e: e.tensor_copy(stat[:, 15:16], gains[:, 0:1]), reads=[r_c], writes=[R()])
        qi = [0]

        def head(i, xnT, r_xnT):
            t0 = i * 512
            S.dma("sp", (lambda e: [e.dma_start(out=ht[:, j, :], in_=h1_d[t0 + j * 128:t0 + (j + 1) * 128, :])
                                    for j in range(4)]), 4, "ht", writes=r_ht)
            norm_transpose4(S, ht, r_ht, stat, r_stat, junk, xn4, r_xn4, [pt, pt2], r_pts, ident, r_ident, xnT, r_xnT)

        if part == "a":
            wa = C.sb("wa", [128, NKC, 704], BF16)
            wuq = C.sb("wuq", [128, 3, UQ_COLS], BF16)
            wuk = C.sb("wuk", [128, 2, 512], BF16)
            wuv = C.sb("wuv", [128, 2, 512], BF16)
            ones = C.sb("ones", [128, 128], BF16)
            r_ones = R()
            S.op("pool", lambda e: e.memset(ones[:], 1.0), writes=[r_ones])
            load_w(S, w["w_in"][:, 0:672], wa[:, :, 0:672], NKC, 672, gains[:, 0:8], st, r_st, r_w, qi)
            load_w(S, w["w_in"][:, C_KRS:C_KRS + 32], wa[:, :, 672:704], NKC, 32, gains[:, 0:8], st, r_st, r_w, qi)
            load_w(S, w["w_uq"], wuq, 3, UQ_COLS, gains[:, 8:11], st, r_st, r_w, qi)
            load_w(S, w["w_uk"], wuk, 2, 512, gains[:, 11:13], st, r_st, r_w, qi)
            load_w(S, w["w_uv"], wuv, 2, 512, gains[:, 11:13], st, r_st, r_w, qi)

            def make_set(k):
                pn = C.ps(f"pn{k}", [128, 512])
                r_pn = R()
                xnT = C.sb(f"xnT{k}", [128, NKC, 512], BF16)
                cqT = C.sb(f"cqT{k}", [128, 3, 512], BF16)
                ckvT = C.sb(f"ckvT{k}", [128, 2, 512], BF16)
                sqq = C.sb(f"sqq{k}", [128, 2, 512], BF16)
                sqkv = C.sb(f"sqkv{k}", [128, 2, 512], BF16)
                rsq = C.sb(f"rsq{k}", [128, 512], F32)
                rskv = C.sb(f"rskv{k}", [128, 512], F32)
                rstok = C.sb(f"rstok{k}", [128, 8], F32)
                tct = C.sb(f"tct{k}", [128, 512], F32)
                tst = C.sb(f"tst{k}", [128, 512], F32)
                t1 = [C.sb(f"t1{k}{i}", [128, 512], F32) for i in range(2)]
                t2 = C.sb(f"t2{k}", [128, 512], F32)
                qout = C.sb(f"qout{k}", [128, 8, 512], BF16)
                knT = C.sb(f"knT{k}", [128, 4, 512], BF16)
                krp = C.sb(f"krp{k}", [128, 512], BF16)
                vt = C.sb(f"vt{k}", [128, 4, 512], BF16)
                r_xnT = [R() for _ in range(NKC)]
                r_cqT, r_ckvT, r_sqq, r_sqkv = R(), R(), [R(), R()], R()
                r_rsq, r_rskv, r_rstok, r_tab = R(), R(), R(), R()
                r_t1, r_t2 = [R(), R()], R()
                r_qout, r_knT, r_krp, r_vt = R(), R(), R(), R()
                S.op("pool", lambda e: e.memset(tct[:], 1.0), writes=[r_tab])
                S.op("pool", lambda e: e.memset(tst[:], 0.0), writes=[r_tab])

                def tile(i):
                    t0 = i * 512
                    S.dma("sp", (lambda e: [e.dma_start(out=tct[64:96, :], in_=cst["rope_c"][:, t0:t0 + 512]),
                                            e.dma_start(out=tst[64:96, :], in_=cst["rope_s"][:, t0:t0 + 512])]),
                          2, f"tab{k}", writes=[r_tab])
                    head(i, xnT, r_xnT)
                    yield
                    for c in range(3):
                        ps, rp = nextpm()
                        mm_fm(ps, rp, wa, C_CQ + c * 128, 128, xnT, r_xnT, NKC)
                        S.op("act", (lambda e, ps=ps, c=c: e.copy(cqT[:, c, :], ps[:])), reads=[rp], writes=[r_cqT])
                        S.op("act", (lambda e, ps=ps, c=c: e.activation(sqq[:, c % 2, :], ps[:], AF.Square)),
                             reads=[rp], writes=[r_sqq[c % 2]])
                        S.op("pe", (lambda e, c=c: e.matmul(pn[:], ones[:], sqq[:, c % 2, :], start=(c == 0), stop=(c == 2))),
                             reads=[r_ones, r_sqq[c % 2]], writes=[r_pn])
                        yield
                    S.op("act", lambda e: e.activation(rsq[:], pn[:], AF.Sqrt, bias=EPS, scale=1.0 / 384), reads=[r_pn], writes=[r_rsq])
                    S.op("dve", lambda e: e.reciprocal(rsq[:], rsq[:]), reads=[r_rsq], writes=[r_rsq])
                    yield
                    for c in range(2):
                        ps, rp = nextpm()
                        mm_fm(ps, rp, wa, C_CKV + c * 128, 128, xnT, r_xnT, NKC)
                        S.op("act", (lambda e, ps=ps, c=c: e.copy(ckvT[:, c, :], ps[:])), reads=[rp], writes=[r_ckvT])
                        S.op("act", (lambda e, ps=ps, c=c: e.activation(sqkv[:, c, :], ps[:], AF.Square)),
                             reads=[rp], writes=[r_sqkv])
                        yield
                    for c in range(2):
                        S.op("pe", (lambda e, c=c: e.matmul(pn[:], ones[:], sqkv[:, c, :], start=(c == 0), stop=(c == 1))),
                             reads=[r_ones, r_sqkv], writes=[r_pn])
                    S.op("act", lambda e: e.activation(rskv[:], pn[:], AF.Sqrt, bias=EPS, scale=1.0 / 256), reads=[r_pn], writes=[r_rskv])
                    S.op("dve", lambda e: e.reciprocal(rskv[:], rskv[:]), reads=[r_rskv], writes=[r_rskv])
                    for j in range(4):
                        for c in range(2):
                            S.op("pe", (lambda e, j=j, c=c: e.matmul(pa[:, j:j + 1], sqkv[:, c, j * 128:(j + 1) * 128], ones[:, 0:1],
                                                                     start=(c == 0), stop=(c == 1))),
                                 reads=[r_ones, r_sqkv], writes=[r_pa])
                    S.op("act", lambda e: e.activation(rstok[:, 0:4], pa[:, 0:4], AF.Sqrt, bias=EPS, scale=1.0 / 256),
                         reads=[r_pa], writes=[r_rstok])
                    S.op("dve", lambda e: e.reciprocal(rstok[:, 0:4], rstok[:, 0:4]), reads=[r_rstok], writes=[r_rstok])
                    yield
                    ps, rp = nextpm()
                    mm_fm(ps, rp, wa, C_KR, 32, xnT, r_xnT, NKC, out_p0=64)
                    ps2, rp2 = nextpm()
                    mm_fm(ps2, rp2, wa, 672, 32, xnT, r_xnT, NKC, out_p0=64)
                    S.op("dve", (lambda e, ps=ps: e.tensor_tensor(t1[0][64:96, :], ps[64:96, :], tct[64:96, :], ALU.mult)),
                         reads=[rp, r_tab], writes=[r_t1[0]])
                    S.op("dve", (lambda e, ps2=ps2: e.tensor_tensor(t2[64:96, :], ps2[64:96, :], tst[64:96, :], ALU.mult)),
                         reads=[rp2, r_tab], writes=[r_t2])
                    S.op("pool", lambda e: e.tensor_tensor(krp[64:96, :], t1[0][64:96, :], t2[64:96, :], ALU.add),
                         reads=[r_t1[0], r_t2], writes=[r_krp])
                    S.dma("pool", (lambda e: [e.dma_start(out=scr["KTr"][:, t0:t0 + 512], in_=krp[64:96, :])]), 1, f"s_krp{k}",
                          reads=[r_krp])
                    yield
                    for h in range(8):
                        b = h % 2
                        ps, rp = nextpm()
                        mm_fm(ps, rp, wuq, h * 96, 96, cqT, [r_cqT], 3)
                        ps2, rp2 = nextpm()
                        mm_fm(ps2, rp2, wuq, 768 + h * 32, 32, cqT, [r_cqT], 3, out_p0=64)
                        S.op("dve", (lambda e, ps=ps, b=b: e.tensor_tensor(t1[b][0:96, :], ps[0:96, :], tct[0:96, :], ALU.mult)),
                             reads=[rp, r_tab], writes=[r_t1[b]])
                        S.op("dve", (lambda e, ps2=ps2: e.tensor_tensor(t2[64:96, :], ps2[64:96, :], tst[64:96, :], ALU.mult)),
                             reads=[rp2, r_tab], writes=[r_t2])
                        S.op("pool", (lambda e, b=b: e.tensor_tensor(t1[b][64:96, :], t1[b][64:96, :], t2[64:96, :], ALU.add)),
                             reads=[r_t2], writes=[r_t1[b]])
                        S.op("pool", (lambda e, b=b, h=h: e.tensor_tensor(qout[0:96, h, :], t1[b][0:96, :], rsq[0:96, :], ALU.mult)),
                             reads=[r_t1[b], r_rsq], writes=[r_qout])
                        yield
                    S.dma("pool", (lambda e: [e.dma_start(out=scr["QT"][h, :, t0:t0 + 512], in_=qout[0:96, h, :])
                                              for h in range(8)]), 8, f"s_q{k}", reads=[r_qout])
                    for a in range(4):
                        ps, rp = nextpm()
                        mm_fm(ps, rp, wuk, a * 128, 128, ckvT, [r_ckvT], 2)
                        S.op("dve", (lambda e, ps=ps, a=a: e.tensor_tensor(knT[:, a, :], ps[:], rskv[:], ALU.mult)),
                             reads=[rp, r_rskv], writes=[r_knT])
                        yield
                    S.dma("pool", (lambda e: [e.dma_start(out=scr["KTn"][a * 128:(a + 1) * 128, t0:t0 + 512], in_=knT[:, a, :])
                                              for a in range(4)]), 4, f"s_kn{k}", reads=[r_knT])
                    for j in range(4):
                        ps, rp = nextpm()
                        mm_tm(ps, rp, ckvT, [r_ckvT], j, wuv, 0, 512, 2)
                        S.op("act", (lambda e, ps=ps, j=j: e.activation(vt[:, j, :], ps[:], AF.Copy, scale=rstok[:, j:j + 1])),
                             reads=[rp, r_rstok], writes=[r_vt])
                        yield
                    S.dma("pool", (lambda e: [e.dma_start(
                        out=scr["VA"][:, t0 + j * 128:t0 + (j + 1) * 128, :].rearrange("h t d -> t h d"),
                        in_=vt[:, j, :].rearrange("t (h d) -> t h d", h=8)) for j in range(4)]), 4, f"s_v{k}", reads=[r_vt])
                return tile
        else:
            wb = C.sb("wb", [128, NKC, 2560], BF16)
            lbt = C.sb("lbt", [128, 16], F32)
            lb = C.sb("lb", [128, 8], F32)
            oml = C.sb("oml", [128, 8], F32)
            rmask = C.sb("rmask", [128, 512], F32)
            mF = C.sb("mF", [128, 128], F32)
            mB = C.sb("mB", [128, 128], F32)
            ato = C.sb("ato", [128, 4, 8, 128], BF16)
            kto = C.sb("kto", [128, 4, 8, 128], BF16)
            ptk = C.ps("ptk", [128, 1024], BF16)
            r_ptk, r_ato, r_kto, r_lb = R(), R(), R(), R()
            cdma(lbt[:], cst["hg_lb"])
            cdma(rmask[:], cst["rmask"])
            cdma(mF[:], cst["maskF"])
            cdma(mB[:], cst["maskB"])
            lv = lbt[:].rearrange("p (d l h) -> p d l h", d=2, l=2)
            lb3 = lb[:].rearrange("p (d h) -> p d h", d=2)
            S.op("dve", lambda e: e.tensor_tensor(lb3, lv[:, :, 0, :], lv[:, :, 1, :], ALU.subtract), reads=[r_c], writes=[r_lb])
            S.op("act", lambda e: e.activation(oml[:], lb[:], AF.Sigmoid, scale=-1.0), reads=[r_lb], writes=[R()])
            S.op("act", lambda e: e.activation(lb[:], lb[:], AF.Sigmoid), reads=[r_lb], writes=[r_lb])
            c1 = C.sb("c1", [128, 8], F32)
            c0 = C.sb("c0", [128, 8], F32)
            c1n = C.sb("c1n", [128, 8], F32)
            S.op("dve", lambda e: e.tensor_scalar(c1[:], oml[:], 0.5, None, ALU.mult), reads=[r_lb], writes=[r_lb])
            S.op("dve", lambda e: e.tensor_tensor(c0[:], lb[:], c1[:], ALU.add), reads=[r_lb], writes=[r_lb])
            S.op("dve", lambda e: e.tensor_scalar(c1n[:], c1[:], -1.0, None, ALU.mult), reads=[r_lb], writes=[r_lb])
            load_w(S, w["w_in"][:, 672:3232], wb, NKC, 2560, gains[:, 0:8], st, r_st, r_w, qi)
            B_HQ, B_HI, B_HFF, B_HFB, B_HG = 0, 512, 1024, 1536, 2048

            def make_set(k):
                xnT = C.sb(f"xnT{k}", [128, NKC, 512], BF16)
                qh = C.sb(f"qh{k}", [128, 4, 512], F32)
                A = C.sb(f"hA{k}", [128, 512], F32)
                B = C.sb(f"hB{k}", [128, 512], F32)
                Cc = C.sb(f"hC{k}", [128, 512], F32)
                E1 = C.sb(f"hE1{k}", [128, 512], F32)
                E2 = C.sb(f"hE2{k}", [128, 512], F32)
                qpo = C.sb(f"qpo{k}", [128, 8, 512], BF16)
                kpo = C.sb(f"kpo{k}", [128, 8, 512], BF16)
                kppo = C.sb(f"kppo{k}", [128, 8, 512], BF16)
                dco = C.sb(f"dco{k}", [128, 8, 16], F32)
                vht = C.sb(f"vht{k}", [128, 4, 512], BF16)
                ght = C.sb(f"ght{k}", [128, 4, 512], BF16)
                r_xnT = [R() for _ in range(NKC)]
                r_qh, rA, rB, rC, rE1, rE2 = R(), R(), R(), R(), R(), R()
                r_qpo, r_kpo, r_kppo, r_dco, r_vht, r_ght = R(), R(), R(), R(), R(), R()

                def tile(i):
                    t0 = i * 512
                    head(i, xnT, r_xnT)
                    yield
                    for h in range(4):
                        ps, rp = nextpm()
                        mm_fm(ps, rp, wb, B_HQ + h * 128, 128, xnT, r_xnT, NKC)
                        S.op("act", (lambda e, ps=ps, h=h: e.activation(qh[:, h, :], ps[:], AF.Silu)), reads=[rp], writes=[r_qh])
                        yield
                    for j in range(4):
                        ps, rp = nextpm()
                        mm_tm(ps, rp, xnT, r_xnT, j, wb, B_HI, 512, NKC)
                        S.op("act", (lambda e, ps=ps, j=j: e.copy(vht[:, j, :], ps[:])), reads=[rp], writes=[r_vht])
                        yield
                        ps, rp = nextpm()
                        mm_tm(ps, rp, xnT, r_xnT, j, wb, B_HG, 512, NKC)
                        S.op("act", (lambda e, ps=ps, j=j: e.activation(ght[:, j, :], ps[:], AF.Silu)), reads=[rp], writes=[r_ght])
                        yield
                    S.dma("pool", (lambda e: [
                        e.dma_start(out=scr["VH"][t0:t0 + 512, :].rearrange("(j p) c -> p j c", p=128), in_=vht[:]),
                        e.dma_start(out=scr["GH"][t0:t0 + 512, :].rearrange("(j p) c -> p j c", p=128), in_=ght[:])]),
                        2, f"s_vg{k}", reads=[r_vht, r_ght])
                    for d in range(2):
                        for h in range(4):
                            hd = d * 4 + h
                            ps, rp = nextpm()
                            mm_fm(ps, rp, wb, (B_HFF if d == 0 else B_HFB) + h * 128, 128, xnT, r_xnT, NKC)
                            c0s, c1s, c1ns = c0[:, hd:hd + 1], c1[:, hd:hd + 1], c1n[:, hd:hd + 1]
                            S.op("act", (lambda e, ps=ps: e.activation(A[:], ps[:], AF.Tanh, scale=0.5)), reads=[rp], writes=[rA])
                            yield
                            S.op("pool", (lambda e, c1s=c1s, c1ns=c1ns: e.tensor_scalar(B[:], A[:], c1ns, c1s, ALU.mult, ALU.add)),
                                 reads=[r_lb, rA], writes=[rB])
                            S.op("dve", (lambda e, c0s=c0s, c1s=c1s: e.tensor_scalar(A[:], A[:], c1s, c0s, ALU.mult, ALU.add)),
                                 reads=[r_lb, rB], writes=[rA])
                            S.op("act", (lambda e: e.activation(A[:], A[:], AF.Ln)), writes=[rA])
                            yield
                            S.op("dve", (lambda e: e.tensor_tensor_scan(Cc[:], rmask[:], A[:], 0.0, ALU.mult, ALU.add)),
                                 reads=[rA, r_c], writes=[rC])
                            Cv = Cc[:].rearrange("p (c t) -> p c t", t=32)
                            Av = A[:].rearrange("p (c t) -> p c t", t=32)
                            if d == 0:
                                bsrc, rb, dcol = Cc, rC, 31
                            else:
                                S.op("pool", (lambda e: e.tensor_tensor(A[:], A[:], Cc[:], ALU.subtract)), reads=[rC], writes=[rA])
                                S.op("pool", (lambda e, Av=Av, Cv=Cv: e.tensor_tensor(
                                    Av, Av, Cv[:, :, 31:32].broadcast_to([128, 16, 32]), ALU.add)), reads=[rC], writes=[rA])
                                bsrc, rb, dcol = A, rA, 0
                            yield
                            S.op("act", (lambda e, bsrc=bsrc: e.activation(E1[:], bsrc[:], AF.Exp)), reads=[rb], writes=[rE1])
                            S.op("act", (lambda e, bsrc=bsrc: e.activation(E2[:], bsrc[:], AF.Exp, scale=-1.0)), reads=[rb], writes=[rE2])
                            yield
                            S.op("pool", (lambda e, h=h, hd=hd: e.tensor_tensor(qpo[:, hd, :], qh[:, h, :], E1[:], ALU.mult)),
                                 reads=[rE1, r_qh], writes=[r_qpo])
                            S.op("dve", (lambda e, hd=hd: e.tensor_tensor(kpo[:, hd, :], E2[:], B[:], ALU.mult)), reads=[rB, rE2], writes=[r_kpo])
                            yield
                            E1v = E1[:].rearrange("p (c t) -> p c t", t=32)
                            E2v = kpo[:, hd, :].rearrange("p (c t) -> p c t", t=32)
                            S.op("dve", (lambda e, E1v=E1v, hd=hd, dcol=dcol: e.tensor_copy(dco[:, hd, :], E1v[:, :, dcol])),
                                 reads=[rE1], writes=[r_dco])
                            kv = kppo[:, hd, :].rearrange("p (c t) -> p c t", t=32)
                            S.op("pool", (lambda e, E1v=E1v, E2v=E2v, kv=kv, dcol=dcol: e.tensor_tensor(
                                kv, E2v, E1v[:, :, dcol:dcol + 1].broadcast_to([128, 16, 32]), ALU.mult)),
                                reads=[rE1, r_kpo], writes=[r_kppo])
                            yield
                    S.dma("pool", (lambda e: [
                        e.dma_start(out=scr["QP"][:, :, t0:t0 + 512].rearrange("h p t -> p h t"), in_=qpo[:]),
                        e.dma_start(out=scr["DC"][:, :, i * 16:(i + 1) * 16].rearrange("h p c -> p h c"), in_=dco[:])]),
                        2, f"s_qp{k}", reads=[r_qpo, r_dco])
                    for j in range(4):
                        for d in range(2):
                            for h in range(4):
                                hd = d * 4 + h
                                S.op("pe", (lambda e, j=j, hd=hd, h=h: e.matmul(
                                    pa[:, h * 128:(h + 1) * 128], kpo[:, hd, j * 128:(j + 1) * 128], qpo[:, hd, j * 128:(j + 1) * 128],
                                    start=True, stop=True)), reads=[r_kpo, r_qpo], writes=[r_pa])
                            msk = (mF if d == 0 else mB)
                            S.op("dve", (lambda e, j=j, d=d, msk=msk: e.tensor_tensor(
                                ato[:, j, d * 4:(d + 1) * 4, :], pa[:].rearrange("p (h t) -> p h t", h=4),
                                msk[:].rearrange("p (o t) -> p o t", o=1).broadcast_to([128, 4, 128]), ALU.mult)),
                                reads=[r_pa, r_c], writes=[r_ato])
                        for hd in range(8):
                            S.op("pe", (lambda e, j=j, hd=hd: e.transpose(ptk[:, hd * 128:(hd + 1) * 128],
                                                                           kppo[:, hd, j * 128:(j + 1) * 128], ident[:])),
                                 reads=[r_kppo, r_ident], writes=[r_ptk])
                        S.op("act", (lambda e, j=j: e.copy(kto[:, j, :, :], ptk[:].rearrange("p (h k) -> p h k", h=8))),
                             reads=[r_ptk], writes=[r_kto])
                    S.dma("pool", (lambda e: [
                        e.dma_start(out=scr["AT"][t0:t0 + 512, :, :].rearrange("(j p) h t -> p j h t", p=128), in_=ato[:]),
                        e.dma_start(out=scr["KPT"][t0:t0 + 512, :, :].rearrange("(j p) h k -> p j h k", p=128), in_=kto[:])]),
                        2, "s_at", reads=[r_ato, r_kto])
                return tile

        tiles = [make_set(0), make_set(1)]
        _roll(lambda i: tiles[i % 2](i), ntiles, 12 if part == "a" else 30)
        S.barrier()
        S.emit()
    return S
```

```python
import numpy as np
from contextlib import ExitStack
import concourse.bass as bass
import concourse.mybir as mybir
from concourse.bass_utils import run_bass_kernel_spmd

F32 = mybir.dt.float32
BF16 = mybir.dt.bfloat16
AF = mybir.ActivationFunctionType
ALU = mybir.AluOpType
AX = mybir.AxisListType

D = 1024
DFF = 2816
EPS = 1e-6
NFC = DFF // 128
NKC = D // 128


class Res:
    __slots__ = ("name", "w", "r")

    def __init__(self, name):
        self.name = name
        self.w = None
        self.r = {}


class Ev:
    __slots__ = ("kind", "key", "op", "count")

    def __init__(self, kind, key, op=None, count=0):
        self.kind = kind
        self.key = key
        self.op = op
        self.count = count


class Op:
    __slots__ = ("fn", "deps", "sig", "count", "dma_key", "dma_n", "ev", "inc")

    def __init__(self, fn, deps):
        self.fn = fn
        self.deps = deps
        self.sig = False
        self.count = 0
        self.dma_key = None
        self.dma_n = 0
        self.ev = None


ENGS = ("sp", "act", "dve", "pool", "pe")
FUSE_WAITS = True


class Sched:
    DMA_SEMS = {}
    DMA_CNT = {}

    def __init__(self, nc, tag):
        self.nc = nc
        self.tag = tag
        self.ops = {e: [] for e in ENGS}
        self.dma_cnt = Sched.DMA_CNT
        self.nres = 0
        self.keymap = {}

    def res(self, name=None):
        self.nres += 1
        return Res(name or f"r{self.nres}")

    def _deps(self, eng, reads, writes):
        deps = []
        for r in reads:
            if r.w is not None:
                deps.append(r.w)
        for w in writes:
            if w.w is not None:
                deps.append(w.w)
            deps.extend(w.r.values())
        out = []
        seen = set()
        for d in deps:
            if id(d) in seen:
                continue
            seen.add(id(d))
            if d.kind == "e" and d.key == "pe" and eng == "pe":
                continue
            out.append(d)
        return out

    def op(self, eng, fn, reads=(), writes=()):
        o = Op(fn, self._deps(eng, reads, writes))
        ev = Ev("e", eng, op=o)
        o.ev = ev
        self.ops[eng].append(o)
        for r in reads:
            r.r[("e", eng)] = ev
        for w in writes:
            w.w = ev
            w.r = {}
        return o

    def dma(self, queue, fn, n, key, reads=(), writes=(), inc=16):
        if key not in self.keymap:
            self.keymap[key] = f"k{len(self.keymap)}"
        key = self.keymap[key]
        o = Op(fn, self._deps(queue, reads, writes))
        o.dma_key = key
        o.dma_n = n
        o.inc = inc
        c = self.dma_cnt.get(key, 0) + n * inc
        self.dma_cnt[key] = c
        ev = Ev("d", key, op=o, count=c)
        o.ev = ev
        self.ops[queue].append(o)
        for r in reads:
            r.r[("d", key)] = ev
        for w in writes:
            w.w = ev
            w.r = {}
        return o

    def barrier(self):
        evs = []
        for e in ENGS:
            for o in reversed(self.ops[e]):
                if o.dma_key is None and o.fn is not None:
                    evs.append(o.ev)
                    break
        lastd = {}
        for e in ENGS:
            for o in self.ops[e]:
                if o.dma_key is not None:
                    lastd[o.dma_key] = o.ev
        evs.extend(lastd.values())
        for e in ENGS:
            deps = [d for d in evs if not (d.kind == "e" and d.key == e)]
            o = Op(None, deps)
            o.ev = Ev("e", e, op=o)
            self.ops[e].append(o)

    def emit(self):
        nc = self.nc
        for e in ENGS:
            for o in self.ops[e]:
                for d in o.deps:
                    if d.kind == "e":
                        d.op.sig = True
        sems = {}
        for e in ENGS:
            c = 0
            for o in self.ops[e]:
                if o.dma_key is None and o.sig:
                    assert o.fn is not None
                    c += 1
                    o.count = c
            sems[("e", e)] = nc.alloc_semaphore(f"{self.tag}_e_{e}")
        for k in self.dma_cnt:
            if k not in Sched.DMA_SEMS:
                Sched.DMA_SEMS[k] = nc.alloc_semaphore(f"d_{k}")
            sems[("d", k)] = Sched.DMA_SEMS[k]
        self.sems = sems

        def run(eng_name, eng):
            waited = {}
            for o in self.ops[eng_name]:
                need = {}
                for d in o.deps:
                    k = (d.kind, d.key)
                    val = d.op.count if d.kind == "e" else d.count
                    assert val > 0, (eng_name, d.kind, d.key)
                    if waited.get(k, 0) >= val:
                        continue
                    waited[k] = val
                    need[k] = max(need.get(k, 0), val)
                need = list(need.items())
                fuse = None
                if FUSE_WAITS and need and o.fn is not None and o.dma_key is None:
                    fuse = need.pop()
                for k, val in need:
                    eng.wait_ge(sems[k], val)
                if o.fn is None:
                    continue
                ins = o.fn(eng)
                if fuse is not None:
                    ins._wait_ge(sems[fuse[0]], fuse[1])
                if o.dma_key is not None:
                    assert len(ins) == o.dma_n
                    for i in ins:
                        i.then_inc(sems[("d", o.dma_key)], o.inc)
                elif o.sig:
                    ins.then_inc(sems[("e", eng_name)], 1)

        with nc.Block() as block:
            @block.sync
            def _(e):
                run("sp", e)

            @block.scalar
            def _(e):
                run("act", e)

            @block.vector
            def _(e):
                run("dve", e)

            @block.gpsimd
            def _(e):
                run("pool", e)

            @block.tensor
            def _(e):
                run("pe", e)

    def release(self):
        for s in self.sems.values():
            self.nc.release_semaphore(s)


def load_weight_bf16(S, nc, w_dram, w_sb, rows_chunks, cols, gain_sb, stage, stage_res, w_res, qi=[0]):
    CW = 512
    engs = ("act", "dve", "act", "dve", "pool")
    for kc in range(rows_chunks):
        for c0 in range(0, cols, CW):
            cw = min(CW, cols - c0)
            n = qi[0]
            i = n % len(stage)
            qi[0] += 1
            st, sr = stage[i], stage_res[i]
            src = w_dram[kc * 128:(kc + 1) * 128, c0:c0 + cw]
            S.dma("sp" if n % 2 == 0 else "act", (lambda e, st=st, src=src, cw=cw: [e.dma_start(out=st[:, 0:cw], in_=src)]),
                  1, f"wst{i}", writes=[sr])
            dst = w_sb[:, kc, c0:c0 + cw]
            eng = engs[n % len(engs)]
            wr = Res("wchunk")
            if gain_sb is not None:
                g = gain_sb[:, kc:kc + 1]
                if eng == "act":
                    fn = (lambda e, dst=dst, st=st, cw=cw, g=g: e.activation(dst, st[:, 0:cw], AF.Copy, scale=g))
                elif eng == "dve":
                    fn = (lambda e, dst=dst, st=st, cw=cw, g=g: e.tensor_scalar(dst, st[:, 0:cw], g, None, ALU.mult))
                else:
                    fn = (lambda e, dst=dst, st=st, cw=cw, g=g: e.tensor_scalar(dst, st[:, 0:cw], g, 0.0, ALU.mult, ALU.add))
            else:
                if eng == "act":
                    fn = (lambda e, dst=dst, st=st, cw=cw: e.copy(dst, st[:, 0:cw]))
                else:
                    fn = (lambda e, dst=dst, st=st, cw=cw: e.tensor_copy(dst, st[:, 0:cw]))
            S.op(eng, fn, reads=[sr], writes=[wr])
            w_res.append(wr)


def ffn_phase(nc, tag, x_d, out_d, ntiles, gain_d, wg_d, wu_d, wd_d, ident_d):
    S = Sched(nc, tag)
    with ExitStack() as es:
        def sb(name, shape, dt):
            return es.enter_context(nc.sbuf_tensor(f"{tag}_{name}", shape, dt))

        def ps(name, shape, dt):
            return es.enter_context(nc.psum_tensor(f"{tag}_{name}", shape, dt))

        wg = sb("wg", [128, NKC, DFF], BF16)
        wu = sb("wu", [128, NKC, DFF], BF16)
        wd = sb("wd", [128, NFC, D], BF16)
        gain = sb("gain", [128, NKC], F32)
        ident = sb("ident", [128, 128], BF16)
        xt0 = sb("xt0", [128, 4, D], F32)
        xt1 = sb("xt1", [128, 4, D], F32)
        stg = [xt1[:, j_, h_ * 512:(h_ + 1) * 512] for j_ in range(4) for h_ in range(2)]
        st0 = stg[0]
        xn4 = sb("xn4", [128, 4, D], BF16)
        xnT = sb("xnT", [128, NKC, 512], BF16)
        actb = sb("act", [128, NFC, 512], BF16)
        sg = sb("sg", [128, 2, 512], BF16)
        stat = sb("stat", [128, 16], F32)
        junk = sb("junk", [128, D], BF16)
        pg0 = ps("pg0", [128, 512], F32)
        pg1 = ps("pg1", [128, 512], F32)
        pu0 = ps("pu0", [128, 512], F32)
        pu1 = ps("pu1", [128, 512], F32)
        po0 = ps("po0", [128, 512], F32)
        po1 = ps("po1", [128, 512], F32)
        pt0 = ps("pt0", [128, 1024], BF16)
        pt1 = ps("pt1", [128, 1024], BF16)
        R = S.res
        r_gain, r_ident, r_w = R("gain"), R("ident"), R("w")
        r_xt = [[R(f"xt{i}_{j}") for j in range(4)] for i in range(2)]
        r_st = [R(f"stg{i}") for i in range(8)]
        S.dma("sp", lambda e: [e.dma_start(out=gain[:], in_=gain_d)], 1, "c0", writes=[r_gain])
        S.dma("sp", lambda e: [e.dma_start(out=st0[:, 0:128], in_=ident_d)], 1, "wst0", writes=[r_st[0]])
        S.op("pool", lambda e: e.tensor_copy(ident[:], st0[:, 0:128]), reads=[r_st[0]], writes=[r_ident])
        stat_dummy = None
        qi = [1]
        for en_ in ("pool", "dve", "act"):
            S.op(en_, (lambda e, en_=en_: (e.copy(stat[:, 8:9], gain[:, 0:1]) if en_ == "act" else e.tensor_copy(stat[:, 9 if en_ == "dve" else 10:10 if en_ == "dve" else 11], gain[:, 0:1]))),
                 reads=[r_gain], writes=[R("dummy")])
        l_wg, l_wu, l_wd = [], [], []
        load_weight_bf16(S, nc, wg_d, wg, NKC, DFF, gain, stg, r_st, l_wg, qi)
        load_weight_bf16(S, nc, wu_d, wu, NKC, DFF, gain, stg, r_st, l_wu, qi)
        load_weight_bf16(S, nc, wd_d, wd, NFC, D, None, stg, r_st, l_wd, qi)
        r_wg, r_wu, r_wd = R("wg"), R("wu"), R("wd")
        S.op("pool", lambda e: e.memset(stat[:, 12:13], 0.0), reads=l_wg, writes=[r_wg])
        S.op("pool", lambda e: e.memset(stat[:, 13:14], 0.0), reads=l_wu, writes=[r_wu])
        S.op("pool", lambda e: e.memset(stat[:, 14:15], 0.0), reads=l_wd, writes=[r_wd] + r_st + r_xt[1])
        xts = [xt0, xt1]
        r_xn4 = [R(f"xn{j}") for j in range(4)]
        r_xnT, r_act = [R(f"xnT{k}") for k in range(NKC)], [R(f"act{f}") for f in range(NFC)]
        r_sg = [R("sg0"), R("sg1")]
        r_stat = [R(f"stat{j}") for j in range(4)]
        r_junk = R("junk")
        pgs, pus, pos, pts = [pg0, pg1], [pu0, pu1], [po0, po1], [pt0, pt1]
        r_pg, r_pu = [R("pg0"), R("pg1")], [R("pu0"), R("pu1")]
        r_po, r_pt = [R("po0"), R("po1")], [R("pt0"), R("pt1")]

        def load_tile(i):
            b = i % 2
            for j in range(4):
                src = x_d[i * 512 + j * 128: i * 512 + (j + 1) * 128, :]
                dst = xts[b][:, j, :]
                S.dma("sp", (lambda e, dst=dst, src=src: [e.dma_start(out=dst, in_=src)]), 1,
                      f"xt{b}_{j}", writes=[r_xt[b][j]])

        load_tile(0)
        nt_ctr = [0]
        for i in range(ntiles):
            b = i % 2
            xt = xts[b]
            if i + 1 < ntiles:
                load_tile(i + 1)
            norm_transpose4(S, xt, r_xt[b], stat, r_stat, junk, xn4, r_xn4, pts, r_pt, ident, r_ident, xnT, r_xnT)
            for f in range(NFC):
                pb = f % 2
                for kc in range(NKC):
                    S.op("pe", (lambda e, pb=pb, f=f, kc=kc: e.matmul(
                        pgs[pb][:], wg[:, kc, f * 128:(f + 1) * 128], xnT[:, kc, :],
                        start=(kc == 0), stop=(kc == NKC - 1))),
                        reads=[r_wg, r_xnT[kc]], writes=[r_pg[pb]])
                for kc in range(NKC):
                    S.op("pe", (lambda e, pb=pb, f=f, kc=kc: e.matmul(
                        pus[pb][:], wu[:, kc, f * 128:(f + 1) * 128], xnT[:, kc, :],
                        start=(kc == 0), stop=(kc == NKC - 1))),
                        reads=[r_wu, r_xnT[kc]], writes=[r_pu[pb]])
                S.op("act", (lambda e, pb=pb: e.activation(sg[:, pb, :], pgs[pb][:], AF.Silu)),
                     reads=[r_pg[pb]], writes=[r_sg[pb]])
                S.op("dve", (lambda e, pb=pb, f=f: e.tensor_tensor(actb[:, f, :], sg[:, pb, :], pus[pb][:], ALU.mult)),
                     reads=[r_sg[pb], r_pu[pb]], writes=[r_act[f]])
            for j in range(4):
                for hh in range(2):
                    pb = (j * 2 + hh) % 2
                    for f in range(NFC):
                        S.op("pe", (lambda e, pb=pb, f=f, j=j, hh=hh: e.matmul(
                            pos[pb][:], actb[:, f, j * 128:(j + 1) * 128], wd[:, f, hh * 512:(hh + 1) * 512],
                            start=(f == 0), stop=(f == NFC - 1))),
                            reads=[r_wd, r_act[f]], writes=[r_po[pb]])
                    dst = xt[:, j, hh * 512:(hh + 1) * 512]
                    S.op("dve", (lambda e, dst=dst, pb=pb: e.scalar_tensor_tensor(
                        dst, pos[pb][:], 0.5, dst, ALU.mult, ALU.add)),
                        reads=[r_po[pb], r_xt[b][j]], writes=[r_xt[b][j]])
                dstd = out_d[i * 512 + j * 128: i * 512 + (j + 1) * 128, :]
                src = xt[:, j, :]
                S.dma("pool", (lambda e, dstd=dstd, src=src: [e.dma_start(out=dstd, in_=src)]), 1,
                      f"xo{b}_{j}", reads=[r_xt[b][j]])
        S.barrier()
        S.emit()
    return S


class Ctx:
    def __init__(self, nc, tag, es):
        self.nc, self.tag, self.es = nc, tag, es

    def sb(self, name, shape, dt):
        return self.es.enter_context(self.nc.sbuf_tensor(f"{self.tag}_{name}", shape, dt))

    def ps(self, name, shape, dt=F32):
        return self.es.enter_context(self.nc.psum_tensor(f"{self.tag}_{name}", shape, dt))


def load_w(S, w_dram, w_sb, nrc, cols, gain_sb, st, r_st, r_w, qi, rows_last=128):
    for rc in range(nrc):
        for c0 in range(0, cols, 512):
            cw = min(512, cols - c0)
            i = qi[0] % 2
            qi[0] += 1
            stt, sr = st[i], r_st[i]
            src = w_dram[rc * 128:(rc + 1) * 128, c0:c0 + cw]
            S.dma("sp", (lambda e, stt=stt, src=src, cw=cw: [e.dma_start(out=stt[:, 0:cw], in_=src)]),
                  1, f"wst{i}", writes=[sr])
            dst = w_sb[:, rc, c0:c0 + cw]
            if gain_sb is not None:
                g = gain_sb[:, rc:rc + 1]
                S.op("pool", (lambda e, dst=dst, stt=stt, cw=cw, g=g:
                              e.tensor_scalar(dst, stt[:, 0:cw], g, 0.0, ALU.mult, ALU.add)),
                     reads=[sr], writes=[r_w])
            else:
                S.op("pool", (lambda e, dst=dst, stt=stt, cw=cw: e.tensor_copy(dst, stt[:, 0:cw])),
                     reads=[sr], writes=[r_w])


def norm_transpose(S, xt, r_xt_j, j, stat, r_stat, junk, r_junk, xn, r_xn, pt, r_pt, ident, r_ident,
                   xnT, r_xnT, nfeat=D):
    nkc = nfeat // 128
    xj = xt[:, j, :]
    ss = stat[:, j:j + 1]
    rs = stat[:, 4 + j:5 + j]
    S.op("act", (lambda e: e.activation(junk[:, 0:nfeat], xj, AF.Square, accum_out=ss)),
         reads=[r_xt_j], writes=[r_junk, r_stat[j]])
    S.op("act", (lambda e: e.activation(rs, ss, AF.Sqrt, bias=EPS, scale=1.0 / nfeat)),
         reads=[r_stat[j]], writes=[r_stat[j]])
    S.op("dve", (lambda e: e.reciprocal(rs, rs)), reads=[r_stat[j]], writes=[r_stat[j]])
    S.op("dve", (lambda e: e.tensor_scalar(xn[:, 0:nfeat], xj, rs, None, ALU.mult)),
         reads=[r_xt_j, r_stat[j]], writes=[r_xn])
    for kc in range(nkc):
        S.op("pe", (lambda e, kc=kc: e.transpose(pt[:, kc * 128:(kc + 1) * 128],
                                                 xn[:, kc * 128:(kc + 1) * 128], ident[:])),
             reads=[r_xn, r_ident], writes=[r_pt])
    dst = xnT[:, 0:nkc, j * 128:(j + 1) * 128]
    src = pt[:, 0:nkc * 128].rearrange("p (k t) -> p k t", k=nkc)
    S.op("act", (lambda e: e.copy(dst, src)), reads=[r_pt], writes=r_xnT)


def norm_transpose4(S, xt, r_xt, stat, r_stat, junk, xn4, r_xn, pts, r_pts, ident, r_ident, xnT, r_xnT, nfeat=D):
    nkc = nfeat // 128
    for j in range(4):
        S.op("act", (lambda e, j=j: e.activation(xn4[:, j, 0:nfeat], xt[:, j, :], AF.Square, accum_out=stat[:, j:j + 1])),
             reads=[r_xt[j]], writes=[r_xn[j], r_stat[j]])
    for j in range(4):
        S.op("act", (lambda e, j=j: e.activation(stat[:, 4 + j:5 + j], stat[:, j:j + 1], AF.Sqrt, bias=EPS, scale=1.0 / nfeat)),
             reads=[r_stat[j]], writes=[r_stat[j]])
    for j in range(4):
        S.op("dve", (lambda e, j=j: e.reciprocal(stat[:, 4 + j:5 + j], stat[:, 4 + j:5 + j])), reads=[r_stat[j]], writes=[r_stat[j]])
    for j in range(4):
        S.op("dve" if j % 2 == 0 else "pool",
             (lambda e, j=j: e.tensor_scalar(xn4[:, j, 0:nfeat], xt[:, j, :], stat[:, 4 + j:5 + j], 0.0, ALU.mult, ALU.add)),
             reads=[r_xt[j], r_stat[j]], writes=[r_xn[j]])
    for j in range(4):
        pt, r_pt = pts[j % 2], r_pts[j % 2]
        for kc in range(nkc):
            S.op("pe", (lambda e, kc=kc, j=j, pt=pt: e.transpose(pt[:, kc * 128:(kc + 1) * 128],
                                                               xn4[:, j, kc * 128:(kc + 1) * 128], ident[:])),
                 reads=[r_xn[j], r_ident], writes=[r_pt])
        dst = xnT[:, 0:nkc, j * 128:(j + 1) * 128]
        src = pt[:, 0:nkc * 128].rearrange("p (k t) -> p k t", k=nkc)
        S.op("act", (lambda e, dst=dst, src=src: e.copy(dst, src)), reads=[r_pt], writes=r_xnT)


C_CQ, C_CKV, C_KR, C_HQ, C_HI, C_HFF, C_HFB, C_HG, C_KRS = 0, 384, 640, 672, 1184, 1696, 2208, 2720, 3232
WIN_COLS = 3264
UQ_COLS = 768 + 256


def mixer_in_phase(nc, tag, ntok, h1_d, cst, w, scr):
    S = Sched(nc, tag)
    ntiles = ntok // 512
    with ExitStack() as es:
        C = Ctx(nc, tag, es)
        R = S.res
        win = C.sb("win", [128, NKC, WIN_COLS], BF16)
        wuq = C.sb("wuq", [128, 3, UQ_COLS], BF16)
        wuk = C.sb("wuk", [128, 2, 512], BF16)
        wuv = C.sb("wuv", [128, 2, 512], BF16)
        gains = C.sb("gains", [128, 16], F32)
        lbt = C.sb("lbt", [128, 16], F32)
        lb = C.sb("lb", [128, 8], F32)
        oml = C.sb("oml", [128, 8], F32)
        ident = C.sb("ident", [128, 128], BF16)
        ones = C.sb("ones", [128, 128], BF16)
        rmask = C.sb("rmask", [128, 512], F32)
        mF = C.sb("mF", [128, 128], F32)
        mB = C.sb("mB", [128, 128], F32)
        ht = C.sb("ht", [128, 4, D], F32)
        xn = C.sb("xn", [128, D], BF16)
        junk = C.sb("junk", [128, D], BF16)
        stat = C.sb("stat", [128, 16], F32)
        xnT = C.sb("xnT", [128, NKC, 512], BF16)
        cqT = C.sb("cqT", [128, 3, 512], BF16)
        ckvT = C.sb("ckvT", [128, 2, 512], BF16)
        sqq = C.sb("sqq", [128, 2, 512], BF16)
        sqkv = C.sb("sqkv", [128, 2, 512], BF16)
        rsq = C.sb("rsq", [128, 512], F32)
        rskv = C.sb("rskv", [128, 512], F32)
        rstok = C.sb("rstok", [128, 8], F32)
        tct = C.sb("tct", [128, 512], F32)
        tst = C.sb("tst", [128, 512], F32)
        t1a = C.sb("t1a", [128, 512], F32)
        t2a = C.sb("t2a", [128, 512], F32)
        t1 = [t1a, t1a]
        t2 = [t2a, t2a]
        qout = C.sb("qout", [128, 8, 512], BF16)
        knT = C.sb("knT", [128, 4, 512], BF16)
        krp = C.sb("krp", [128, 512], BF16)
        vt = C.sb("vt", [128, 4, 512], BF16)
        qh = C.sb("qh", [128, 4, 512], F32)
        hA = [C.sb(f"hA{i}", [128, 512], F32) for i in range(2)]
        hB = [C.sb(f"hB{i}", [128, 512], F32) for i in range(2)]
        hC = [C.sb(f"hC{i}", [128, 512], F32) for i in range(2)]
        hE1 = [C.sb(f"hE1{i}", [128, 512], F32) for i in range(2)]
        hE2 = [C.sb(f"hE2{i}", [128, 512], F32) for i in range(2)]
        st = [hA[0], hB[0]]
        qpo = C.sb("qpo", [128, 8, 512], BF16)
        kpo = C.sb("kpo", [128, 8, 512], BF16)
        kppo = C.sb("kppo", [128, 8, 512], BF16)
        dco = C.sb("dco", [128, 8, 16], F32)
        vht = C.sb("vht", [128, 4, 512], BF16)
        ght = C.sb("ght", [128, 4, 512], BF16)
        ato = C.sb("ato", [128, 4, 8, 128], BF16)
        kto = C.sb("kto", [128, 4, 8, 128], BF16)
        pt = C.ps("pt", [128, 1024], BF16)
        pm = [C.ps(f"pm{i}", [128, 512]) for i in range(4)]
        pn = C.ps("pn", [128, 512])
        pa = C.ps("pa", [128, 512])
        ptk = C.ps("ptk", [128, 1024], BF16)

        r_c = R("consts")
        r_ident, r_ones = R("ident"), R("ones")

        def cdma(dst, src):
            S.dma("sp", (lambda e: [e.dma_start(out=dst, in_=src)]), 1, "c", writes=[r_c])
        cdma(gains[:, 0:8], cst["mix_norm"])
        cdma(gains[:, 8:11], cst["q_norm"])
        cdma(gains[:, 11:13], cst["kv_norm"])
        cdma(lbt[:], cst["hg_lb"])
        cdma(rmask[:], cst["rmask"])
        cdma(mF[:], cst["maskF"])
        cdma(mB[:], cst["maskB"])
        cdma(ht[:, 0, 0:128], cst["ident"])
        S.op("pool", lambda e: e.tensor_copy(ident[:], ht[:, 0, 0:128]), reads=[r_c], writes=[r_ident])
        S.op("pool", lambda e: e.memset(ones[:], 1.0), writes=[r_ones])
        lv = lbt[:].rearrange("p (d l h) -> p d l h", d=2, l=2)
        lb3 = lb[:].rearrange("p (d h) -> p d h", d=2)
        oml3 = oml[:].rearrange("p (d h) -> p d h", d=2)
        r_lb = R("lb")
        S.op("dve", lambda e: e.tensor_tensor(lb3, lv[:, :, 0, :], lv[:, :, 1, :], ALU.subtract), reads=[r_c], writes=[r_lb])
        S.op("act", lambda e: e.activation(oml[:], lb[:], AF.Sigmoid, scale=-1.0), reads=[r_lb], writes=[R("oml")])
        S.op("act", lambda e: e.activation(lb[:], lb[:], AF.Sigmoid), reads=[r_lb], writes=[r_lb])
        r_w = R("w")
        r_hA, r_hB = [R("hA0"), R("hA1")], [R("hB0"), R("hB1")]
        r_st = [r_hA[0], r_hB[0]]
        S.op("pool", lambda e: e.tensor_copy(stat[:, 15:16], gains[:, 0:1]), reads=[r_c], writes=[R("d")])
        qi = [0]
        load_w(S, w["w_in"], win, NKC, WIN_COLS, gains[:, 0:8], st, r_st, r_w, qi)
        load_w(S, w["w_uq"], wuq, 3, UQ_COLS, gains[:, 8:11], st, r_st, r_w, qi)
        load_w(S, w["w_uk"], wuk, 2, 512, gains[:, 11:13], st, r_st, r_w, qi)
        load_w(S, w["w_uv"], wuv, 2, 512, gains[:, 11:13], st, r_st, r_w, qi)

        r_ht = [R(f"ht{j}") for j in range(4)]
        r_stat = [R(f"stat{j}") for j in range(4)]
        r_junk, r_xn, r_pt = R("junk"), R("xn"), R("pt")
        r_xnT = [R(f"xnT{k}") for k in range(NKC)]
        r_pm = [R(f"pm{i}") for i in range(4)]
        r_pn, r_pa, r_ptk = R("pn"), R("pa"), R("ptk")
        r_cqT, r_ckvT, r_sqq, r_sqkv = R("cqT"), R("ckvT"), [R("sqq0"), R("sqq1")], R("sqkv")
        r_rsq, r_rskv, r_rstok = R("rsq"), R("rskv"), R("rstok")
        r_tab = R("tab")
        r_t1a, r_t2a = R("t1a"), R("t2a")
        r_t1, r_t2 = [r_t1a, r_t1a], [r_t2a, r_t2a]
        r_qout, r_knT, r_krp, r_vt, r_qh = R("qout"), R("knT"), R("krp"), R("vt"), R("qh")
        r_hC = [R("hC0"), R("hC1")]
        r_hE1, r_hE2 = [R("hE10"), R("hE11")], [R("hE20"), R("hE21")]
        r_qpo, r_kpo, r_kppo, r_dco = R("qpo"), R("kpo"), R("kppo"), R("dco")
        r_vht, r_ght, r_ato, r_kto = R("vht"), R("ght"), R("ato"), R("kto")
        S.op("pool", lambda e: e.memset(tct[:], 1.0), writes=[r_tab])
        S.op("pool", lambda e: e.memset(tst[:], 0.0), writes=[r_tab])
        pmi = [0]

        def nextpm():
            i = pmi[0] % 4
            pmi[0] += 1
            return pm[i], r_pm[i]

        def mm_fm(ps, r_ps, wsb, c0, m, xT, r_x, nkc, out_p0=0):
            for kc in range(nkc):
                S.op("pe", (lambda e, kc=kc: e.matmul(ps[out_p0:out_p0 + m, :], wsb[:, kc, c0:c0 + m], xT[:, kc, :],
                                                      start=(kc == 0), stop=(kc == nkc - 1))),
                     reads=[r_w] + r_x, writes=[r_ps])

        def mm_tm(ps, r_ps, xT, r_x, j, wsb, c0, n, nkc):
            for kc in range(nkc):
                S.op("pe", (lambda e, kc=kc: e.matmul(ps[:, 0:n], xT[:, kc, j * 128:(j + 1) * 128], wsb[:, kc, c0:c0 + n],
                                                      start=(kc == 0), stop=(kc == nkc - 1))),
                     reads=[r_w] + r_x, writes=[r_ps])

        for i in range(ntiles):
            t0 = i * 512
            S.dma("sp", (lambda e, t0=t0: [e.dma_start(out=ht[:, j, :], in_=h1_d[t0 + j * 128:t0 + (j + 1) * 128, :])
                                          for j in range(4)]), 4, "ht", writes=r_ht)
            S.dma("sp", (lambda e, t0=t0: [e.dma_start(out=tct[64:96, :], in_=cst["rope_c"][:, t0:t0 + 512]),
                                          e.dma_start(out=tst[64:96, :], in_=cst["rope_s"][:, t0:t0 + 512])]),
                  2, "tab", writes=[r_tab])
            for j in range(4):
                norm_transpose(S, ht, r_ht[j], j, stat, r_stat, junk, r_junk, xn, r_xn, pt, r_pt, ident, r_ident,
                               xnT, r_xnT)
            for c in range(3):
                ps, rp = nextpm()
                mm_fm(ps, rp, win, C_CQ + c * 128, 128, xnT, r_xnT, NKC)
                S.op("act", (lambda e, ps=ps, c=c: e.copy(cqT[:, c, :], ps[:])), reads=[rp], writes=[r_cqT])
                S.op("act", (lambda e, ps=ps, c=c: e.activation(sqq[:, c % 2, :], ps[:], AF.Square)),
                     reads=[rp], writes=[r_sqq[c % 2]])
                S.op("pe", (lambda e, c=c: e.matmul(pn[:], ones[:], sqq[:, c % 2, :], start=(c == 0), stop=(c == 2))),
                     reads=[r_ones, r_sqq[c % 2]], writes=[r_pn])
            S.op("act", lambda e: e.activation(rsq[:], pn[:], AF.Sqrt, bias=EPS, scale=1.0 / 384), reads=[r_pn], writes=[r_rsq])
            S.op("dve", lambda e: e.reciprocal(rsq[:], rsq[:]), reads=[r_rsq], writes=[r_rsq])
            for c in range(2):
                ps, rp = nextpm()
                mm_fm(ps, rp, win, C_CKV + c * 128, 128, xnT, r_xnT, NKC)
                S.op("act", (lambda e, ps=ps, c=c: e.copy(ckvT[:, c, :], ps[:])), reads=[rp], writes=[r_ckvT])
                S.op("act", (lambda e, ps=ps, c=c: e.activation(sqkv[:, c, :], ps[:], AF.Square)),
                     reads=[rp], writes=[r_sqkv])
            for c in range(2):
                S.op("pe", (lambda e, c=c: e.matmul(pn[:], ones[:], sqkv[:, c, :], start=(c == 0), stop=(c == 1))),
                     reads=[r_ones, r_sqkv], writes=[r_pn])
            S.op("act", lambda e: e.activation(rskv[:], pn[:], AF.Sqrt, bias=EPS, scale=1.0 / 256), reads=[r_pn], writes=[r_rskv])
            S.op("dve", lambda e: e.reciprocal(rskv[:], rskv[:]), reads=[r_rskv], writes=[r_rskv])
            for j in range(4):
                for c in range(2):
                    S.op("pe", (lambda e, j=j, c=c: e.matmul(pa[:, j:j + 1], sqkv[:, c, j * 128:(j + 1) * 128], ones[:, 0:1],
                                                             start=(c == 0), stop=(c == 1))),
                         reads=[r_ones, r_sqkv], writes=[r_pa])
            S.op("act", lambda e: e.activation(rstok[:, 0:4], pa[:, 0:4], AF.Sqrt, bias=EPS, scale=1.0 / 256),
                 reads=[r_pa], writes=[r_rstok])
            S.op("dve", lambda e: e.reciprocal(rstok[:, 0:4], rstok[:, 0:4]), reads=[r_rstok], writes=[r_rstok])
            ps, rp = nextpm()
            mm_fm(ps, rp, win, C_KR, 32, xnT, r_xnT, NKC, out_p0=64)
            ps2, rp2 = nextpm()
            mm_fm(ps2, rp2, win, C_KRS, 32, xnT, r_xnT, NKC, out_p0=64)
            S.op("dve", (lambda e, ps=ps: e.tensor_tensor(t1[0][64:96, :], ps[64:96, :], tct[64:96, :], ALU.mult)),
                 reads=[rp, r_tab], writes=[r_t1[0]])
            S.op("dve", (lambda e, ps2=ps2: e.tensor_tensor(t2[0][64:96, :], ps2[64:96, :], tst[64:96, :], ALU.mult)),
                 reads=[rp2, r_tab], writes=[r_t2[0]])
            S.op("pool", lambda e: e.tensor_tensor(krp[64:96, :], t1[0][64:96, :], t2[0][64:96, :], ALU.add),
                 reads=[r_t1[0], r_t2[0]], writes=[r_krp])
            S.dma("pool", (lambda e, t0=t0: [e.dma_start(out=scr["KTr"][:, t0:t0 + 512], in_=krp[64:96, :])]), 1, "s_krp",
                  reads=[r_krp])
            for h in range(8):
                b = h % 2
                ps, rp = nextpm()
                mm_fm(ps, rp, wuq, h * 96, 96, cqT, [r_cqT], 3)
                ps2, rp2 = nextpm()
                mm_fm(ps2, rp2, wuq, 768 + h * 32, 32, cqT, [r_cqT], 3, out_p0=64)
                S.op("dve", (lambda e, ps=ps, b=b: e.tensor_tensor(t1[b][0:96, :], ps[0:96, :], tct[0:96, :], ALU.mult)),
                     reads=[rp, r_tab], writes=[r_t1[b]])
                S.op("dve", (lambda e, ps2=ps2, b=b: e.tensor_tensor(t2[b][64:96, :], ps2[64:96, :], tst[64:96, :], ALU.mult)),
                     reads=[rp2, r_tab], writes=[r_t2[b]])
                S.op("pool", (lambda e, b=b: e.tensor_tensor(t1[b][64:96, :], t1[b][64:96, :], t2[b][64:96, :], ALU.add)),
                     reads=[r_t2[b]], writes=[r_t1[b]])
                S.op("pool", (lambda e, b=b, h=h: e.tensor_tensor(qout[0:96, h, :], t1[b][0:96, :], rsq[0:96, :], ALU.mult)),
                     reads=[r_t1[b], r_rsq], writes=[r_qout])
            S.dma("pool", (lambda e, t0=t0: [e.dma_start(out=scr["QT"][h, :, t0:t0 + 512], in_=qout[0:96, h, :])
                                            for h in range(8)]), 8, "s_q", reads=[r_qout])
            for a in range(4):
                ps, rp = nextpm()
                mm_fm(ps, rp, wuk, a * 128, 128, ckvT, [r_ckvT], 2)
                S.op("dve", (lambda e, ps=ps, a=a: e.tensor_tensor(knT[:, a, :], ps[:], rskv[:], ALU.mult)),
                     reads=[rp, r_rskv], writes=[r_knT])
            S.dma("pool", (lambda e, t0=t0: [e.dma_start(out=scr["KTn"][a * 128:(a + 1) * 128, t0:t0 + 512], in_=knT[:, a, :])
                                            for a in range(4)]), 4, "s_kn", reads=[r_knT])
            for j in range(4):
                ps, rp = nextpm()
                mm_tm(ps, rp, ckvT, [r_ckvT], j, wuv, 0, 512, 2)
                S.op("act", (lambda e, ps=ps, j=j: e.activation(vt[:, j, :], ps[:], AF.Copy, scale=rstok[:, j:j + 1])),
                     reads=[rp, r_rstok], writes=[r_vt])
            S.dma("pool", (lambda e, t0=t0: [e.dma_start(
                out=scr["VA"][:, t0 + j * 128:t0 + (j + 1) * 128, :].rearrange("h t d -> t h d"),
                in_=vt[:, j, :].rearrange("t (h d) -> t h d", h=8)) for j in range(4)]), 4, "s_v", reads=[r_vt])
            for h in range(4):
                ps, rp = nextpm()
                mm_fm(ps, rp, win, C_HQ + h * 128, 128, xnT, r_xnT, NKC)
                S.op("act", (lambda e, ps=ps, h=h: e.activation(qh[:, h, :], ps[:], AF.Silu)), reads=[rp], writes=[r_qh])
            for j in range(4):
                ps, rp = nextpm()
                mm_tm(ps, rp, xnT, r_xnT, j, win, C_HI, 512, NKC)
                S.op("act", (lambda e, ps=ps, j=j: e.copy(vht[:, j, :], ps[:])), reads=[rp], writes=[r_vht])
                ps, rp = nextpm()
                mm_tm(ps, rp, xnT, r_xnT, j, win, C_HG, 512, NKC)
                S.op("act", (lambda e, ps=ps, j=j: e.activation(ght[:, j, :], ps[:], AF.Silu)), reads=[rp], writes=[r_ght])
            S.dma("pool", (lambda e, t0=t0: [
                e.dma_start(out=scr["VH"][t0:t0 + 512, :].rearrange("(j p) c -> p j c", p=128), in_=vht[:]),
                e.dma_start(out=scr["GH"][t0:t0 + 512, :].rearrange("(j p) c -> p j c", p=128), in_=ght[:])]),
                2, "s_vg", reads=[r_vht, r_ght])
            for d in range(2):
                for h in range(4):
                    hd = d * 4 + h
                    b = hd % 2
                    A, B, Cc, E1, E2 = hA[b], hB[b], hC[b], hE1[b], hE2[b]
                    rA, rB, rC, rE1, rE2 = r_hA[b], r_hB[b], r_hC[b], r_hE1[b], r_hE2[b]
                    ps, rp = nextpm()
                    mm_fm(ps, rp, win, (C_HFF if d == 0 else C_HFB) + h * 128, 128, xnT, r_xnT, NKC)
                    lbs, omls = lb[:, hd:hd + 1], oml[:, hd:hd + 1]
                    S.op("act", (lambda e, ps=ps, A=A: e.activation(A[:], ps[:], AF.Sigmoid)), reads=[rp], writes=[rA])
                    S.op("act", (lambda e, ps=ps, B=B: e.activation(B[:], ps[:], AF.Sigmoid, scale=-1.0)), reads=[rp], writes=[rB])
                    S.op("pool", (lambda e, B=B, omls=omls: e.tensor_scalar(B[:], B[:], omls, 0.0, ALU.mult, ALU.add)),
                         reads=[r_lb], writes=[rB])
                    S.op("dve", (lambda e, A=A, omls=omls, lbs=lbs: e.tensor_scalar(A[:], A[:], omls, lbs, ALU.mult, ALU.add)),
                         reads=[r_lb], writes=[rA])
                    S.op("act", (lambda e, A=A: e.activation(A[:], A[:], AF.Ln)), writes=[rA])
                    S.op("dve", (lambda e, A=A, Cc=Cc: e.tensor_tensor_scan(Cc[:], rmask[:], A[:], 0.0, ALU.mult, ALU.add)),
                         reads=[rA, r_c], writes=[rC])
                    Cv = Cc[:].rearrange("p (c t) -> p c t", t=32)
                    Av = A[:].rearrange("p (c t) -> p c t", t=32)
                    if d == 0:
                        bsrc, rb = Cc, rC
                        dcol = 31
                    else:
                        S.op("pool", (lambda e, A=A, Cc=Cc: e.tensor_tensor(A[:], A[:], Cc[:], ALU.subtract)),
                             reads=[rC], writes=[rA])
                        S.op("pool", (lambda e, Av=Av, Cv=Cv: e.tensor_tensor(Av, Av, Cv[:, :, 31:32].broadcast_to([128, 16, 32]), ALU.add)),
                             reads=[rC], writes=[rA])
                        bsrc, rb = A, rA
                        dcol = 0
                    S.op("act", (lambda e, E1=E1, bsrc=bsrc: e.activation(E1[:], bsrc[:], AF.Exp)), reads=[rb], writes=[rE1])
                    S.op("act", (lambda e, E2=E2, bsrc=bsrc: e.activation(E2[:], bsrc[:], AF.Exp, scale=-1.0)), reads=[rb], writes=[rE2])
                    S.op("pool", (lambda e, E1=E1, h=h, hd=hd: e.tensor_tensor(qpo[:, hd, :], qh[:, h, :], E1[:], ALU.mult)),
                         reads=[rE1, r_qh], writes=[r_qpo])
                    S.op("dve", (lambda e, E2=E2, B=B: e.tensor_tensor(E2[:], E2[:], B[:], ALU.mult)), reads=[rB], writes=[rE2])
                    S.op("act", (lambda e, E2=E2, hd=hd: e.copy(kpo[:, hd, :], E2[:])), reads=[rE2], writes=[r_kpo])
                    E1v = E1[:].rearrange("p (c t) -> p c t", t=32)
                    E2v = E2[:].rearrange("p (c t) -> p c t", t=32)
                    S.op("dve", (lambda e, E1v=E1v, hd=hd, dcol=dcol: e.tensor_copy(dco[:, hd, :], E1v[:, :, dcol])),
                         reads=[rE1], writes=[r_dco])
                    kv = kppo[:, hd, :].rearrange("p (c t) -> p c t", t=32)
                    S.op("pool", (lambda e, E1v=E1v, E2v=E2v, kv=kv, dcol=dcol: e.tensor_tensor(
                        kv, E2v, E1v[:, :, dcol:dcol + 1].broadcast_to([128, 16, 32]), ALU.mult)),
                        reads=[rE1, rE2], writes=[r_kppo])
            nch = ntok // 32
            S.dma("pool", (lambda e, t0=t0, i=i: [
                e.dma_start(out=scr["QP"][:, :, t0:t0 + 512].rearrange("h p t -> p h t"), in_=qpo[:]),
                e.dma_start(out=scr["DC"][:, :, i * 16:(i + 1) * 16].rearrange("h p c -> p h c"), in_=dco[:])]),
                2, "s_qp", reads=[r_qpo, r_dco])
            for j in range(4):
                for d in range(2):
                    for h in range(4):
                        hd = d * 4 + h
                        S.op("pe", (lambda e, j=j, hd=hd, h=h: e.matmul(
                            pa[:, h * 128:(h + 1) * 128], kpo[:, hd, j * 128:(j + 1) * 128], qpo[:, hd, j * 128:(j + 1) * 128],
                            start=True, stop=True)), reads=[r_kpo, r_qpo], writes=[r_pa])
                    msk = (mF if d == 0 else mB)
                    S.op("dve", (lambda e, j=j, d=d, msk=msk: e.tensor_tensor(
                        ato[:, j, d * 4:(d + 1) * 4, :], pa[:].rearrange("p (h t) -> p h t", h=4),
                        msk[:].rearrange("p (o t) -> p o t", o=1).broadcast_to([128, 4, 128]), ALU.mult)),
                        reads=[r_pa, r_c], writes=[r_ato])
                for hd in range(8):
                    S.op("pe", (lambda e, j=j, hd=hd: e.transpose(ptk[:, hd * 128:(hd + 1) * 128],
                                                                   kppo[:, hd, j * 128:(j + 1) * 128], ident[:])),
                         reads=[r_kppo, r_ident], writes=[r_ptk])
                S.op("act", (lambda e, j=j: e.copy(kto[:, j, :, :], ptk[:].rearrange("p (h k) -> p h k", h=8))),
                     reads=[r_ptk], writes=[r_kto])
            S.dma("pool", (lambda e, t0=t0: [
                e.dma_start(out=scr["AT"][t0:t0 + 512, :, :].rearrange("(j p) h t -> p j h t", p=128), in_=ato[:]),
                e.dma_start(out=scr["KPT"][t0:t0 + 512, :, :].rearrange("(j p) h k -> p j h k", p=128), in_=kto[:])]),
                2, "s_at", reads=[r_ato, r_kto])
        S.barrier()
        S.emit()
    return S


def attn_phase(nc, tag, seqs, scr, xchg=None):
    S = Sched(nc, tag)
    SKMAX = max(sum(p[3] for p in sq["kp"]) for sq in seqs)
    SL = max(sq["nq"] for sq in seqs)
    scale = 96.0 ** -0.5
    with ExitStack() as es:
        C = Ctx(nc, tag, es)
        R = S.res
        kt = [C.sb(f"kt{i}", [128, SKMAX], BF16) for i in range(2)]
        vt = [C.sb(f"vt{i}", [128, SKMAX // 128, 65], BF16) for i in range(2)]
        qt = [C.sb(f"qt{i}", [128, SL], BF16) for i in range(2)]
        pT = [C.sb(f"pT{i}", [128, 1024], BF16) for i in range(3)]
        onesf = C.sb("onesf", [128, 64], F32)
        rl = C.sb("rl", [128, 512], F32)
        osb = C.sb("osb", [128, 512], F32)
        obf = [C.sb(f"obf{i}", [128, 512], BF16) for i in range(2)]
        psS = [C.ps(f"psS{i}", [128, 1024]) for i in range(2)]
        psO = [C.ps(f"psO{i}", [128, 512]) for i in range(2)]
        psB = C.ps("psB", [128, 512])
        r_kt, r_vt, r_qt = [R(), R()], [R(), R()], [R(), R()]
        r_pT, r_psS, r_psO = [R(), R(), R()], [R(), R(), R()], [R(), R()]
        r_ones, r_rl, r_osb, r_obf, r_psB = R(), R(), R(), [R(), R()], R()
        S.op("pool", lambda e: e.memset(onesf[:], 1.0), writes=[r_ones])
        for i in range(2):
            S.op("pool", (lambda e, i=i: e.memset(vt[i][:, :, 64:65], 1.0)), writes=[r_vt[i]])
        r_gath = R()
        r_g2, r_g3 = R(), R()
        r_gs = []
        if xchg is not None:
            n0 = xchg["n0"]
            r_xin = R()
            S.dma("sp", (lambda e: [e.dma_start(out=xchg["XK_in"][a_], in_=scr["KTn"][a_ * 128:(a_ + 1) * 128, 0:n0]) for a_ in range(4)]
                         + [e.dma_start(out=xchg["XR_in"][0:32, :], in_=scr["KTr"][:, 0:n0])]
                         + [e.dma_start(out=xchg["XV_in"][a_].rearrange("(hh t) d -> hh t d", hh=2), in_=scr["VA"][2 * a_:2 * a_ + 2, 0:n0, :])
                            for a_ in range(4)]), 9, "xin", writes=[r_xin])
            r_gs = [r_gath, r_g2, r_g3] + [R() for _ in range(6)]
            ccl = [(xchg["XK_in"][a_], xchg["XK_out"][a_]) for a_ in range(4)] + [(xchg["XR_in"], xchg["XR_out"])] + \
                  [(xchg["XV_in"][a_], xchg["XV_out"][a_]) for a_ in range(4)]
            for ci_, (cin, cout) in enumerate(ccl):
                S.dma("pool", (lambda e, cin=cin, cout=cout: [e.collective_compute(
                    "AllGather", ALU.bypass, replica_groups=xchg["groups"], ins=[cin.opt()], outs=[cout.opt()])]), 1, f"cc{ci_}",
                    reads=[r_xin], writes=[r_gs[ci_]], inc=1)
        heads = []
        for sq in seqs:
            for h in range(8):
                heads.append((sq, h))
        units = []
        obc = [0]
        for hi, (sq, h) in enumerate(heads):
            SK = sum(p[3] for p in sq["kp"])
            for qb in range(sq["nq"] // 512):
                for kc in range(SK // 256):
                    units.append((hi, qb, kc, SK // 256, obc[0] % 2, qb == sq["nq"] // 512 - 1))
                obc[0] += 1
        loaded = [-1]
        pending = []

        def load_head(hi):
            if hi >= len(heads) or hi <= loaded[0]:
                return
            loaded[0] = hi
            sq, h = heads[hi]
            b = hi % 2
            off = 0
            lst = []
            for (ktn, ktr, va, n) in sq["kp"]:
                lst.append((kt[b][0:64, off:off + n], ktn(h)))
                lst.append((kt[b][64:96, off:off + n], ktr))
                off += n
            dep = r_gs if sq.get("gathered") else []
            S.dma("sp", (lambda e, lst=lst: [e.dma_start(out=o, in_=i_) for o, i_ in lst]), len(lst), f"kt{b}", reads=dep, writes=[r_kt[b]])
            off = 0
            lst2 = []
            for (ktn, ktr, va, n) in sq["kp"]:
                lst2.append((vt[b][:, off // 128:(off + n) // 128, 0:64], va(h).rearrange("(c p) d -> p c d", p=128)))
                off += n
            S.dma("sp", (lambda e, lst2=lst2: [e.dma_start(out=o, in_=i_) for o, i_ in lst2]), len(lst2), f"vt{b}", reads=dep, writes=[r_vt[b]])
            q0, nq = sq["q0"], sq["nq"]
            S.dma("sp", (lambda e, b=b, h=h, q0=q0, nq=nq: [e.dma_start(out=qt[b][0:96, 0:nq], in_=scr["QT"][h, :, q0:q0 + nq])]), 1,
                  f"qt{b}", writes=[r_qt[b]])

        def qk(u):
            hi, qb, kc, nkc, ob, lastq = units[u]
            b = hi % 2
            r = u % 2
            r3 = u % 3
            for t in range(2):
                S.op("pe", (lambda e, t=t: e.matmul(psS[r][:, t * 512:(t + 1) * 512], kt[b][0:96, (2 * kc + t) * 128:(2 * kc + t + 1) * 128],
                                                    qt[b][0:96, qb * 512:(qb + 1) * 512], start=True, stop=True)),
                     reads=[r_kt[b], r_qt[b]], writes=[r_psS[r]])
            S.op("act", (lambda e: e.activation(pT[r3][:], psS[r][:], AF.Exp, scale=scale)), reads=[r_psS[r]], writes=[r_pT[r3]])

        def pv(u):
            hi, qb, kc, nkc, ob, lastq = units[u]
            b = hi % 2
            r3 = u % 3
            for t in range(2):
                S.op("pe", (lambda e, t=t: e.matmul(psO[ob][0:65, :], vt[b][:, 2 * kc + t, 0:65], pT[r3][:, t * 512:(t + 1) * 512],
                                                    start=(kc == 0 and t == 0), stop=(kc == nkc - 1 and t == 1))),
                     reads=[r_vt[b], r_pT[r3]], writes=[r_psO[ob]])
            if kc == nkc - 1:
                sq, h = heads[hi]
                S.op("dve", (lambda e: e.reciprocal(rl[64:65, :], psO[ob][64:65, :])), reads=[r_psO[ob]], writes=[r_rl])
                S.op("dve", (lambda e: e.tensor_copy(osb[0:64, :], psO[ob][0:64, :])), reads=[r_psO[ob]], writes=[r_osb])
                t0 = sq["q0"] + qb * 512

                def fin():
                    S.op("pe", (lambda e: e.matmul(psB[0:64, :], onesf[64:65, 0:64], rl[64:65, :], start=True, stop=True)),
                         reads=[r_ones, r_rl], writes=[r_psB])
                    S.op("dve", (lambda e: e.tensor_tensor(obf[ob][0:64, :], osb[0:64, :], psB[0:64, :], ALU.mult)),
                         reads=[r_osb, r_psB], writes=[r_obf[ob]])
                    S.dma("pool", (lambda e: [e.dma_start(out=scr["MIXT"][h * 64:(h + 1) * 64, t0:t0 + 512], in_=obf[ob][0:64, :])]), 1,
                          f"so{ob}", reads=[r_obf[ob]])
                pending.append([4, fin])

        load_head(0)
        load_head(1)
        n = len(units)
        qk(0)
        for u in range(n):
            if u + 1 < n:
                qk(u + 1)
            for pnd in list(pending):
                pnd[0] -= 1
                if pnd[0] <= 0:
                    pnd[1]()
                    pending.remove(pnd)
            pv(u)
            hi, qb, kc, nkc, ob_, lastq = units[u]
            if lastq and kc == nkc - 1:
                load_head(hi + 2)
        for pnd in pending:
            pnd[1]()
        S.barrier()
        S.emit()
    return S


def scan_phase(nc, tag, seq_lens, ntok, scr, xchg=None, cst=None):
    S = Sched(nc, tag)
    nch = ntok // 32
    with ExitStack() as es:
        C = Ctx(nc, tag, es)
        R = S.res
        NR = 3
        qpb = [[C.sb(f"qpb{d}{i}", [128, 4, 128], BF16) for i in range(NR)] for d in range(2)]
        atb = [[C.sb(f"atb{d}{i}", [128, 4, 128], BF16) for i in range(NR)] for d in range(2)]
        kpb = [[C.sb(f"kpb{d}{i}", [128, 4, 128], BF16) for i in range(NR)] for d in range(2)]
        vb = [[C.sb(f"vb{d}{i}", [128, 512], BF16) for i in range(NR)] for d in range(2)]
        r_ld = [[R() for i in range(NR)] for d in range(2)]
        dct = C.sb("dct", [128, 8, nch], F32)
        zer = C.sb("zer", [128, 512], BF16)
        SstAll = C.sb("SstAll", [128, 8, 129], F32)
        Sst = [SstAll[:, hd, 0:128] for hd in range(8)]
        Sbf = [C.sb(f"Sbf{hd}", [128, 128], BF16) for hd in range(8)]
        r_S = [R() for hd in range(8)]
        r_Sb = [R() for hd in range(8)]
        ot = [[C.sb(f"ot{d}{i}", [128, 512], F32) for i in range(2)] for d in range(2)]
        r_ot = [[R() for i in range(2)] for d in range(2)]
        psO = [[C.ps(f"psO{d}{i}", [128, 512]) for i in range(2)] for d in range(2)]
        r_psO = [[R() for i in range(2)] for d in range(2)]
        psU = [C.ps(f"psU{i}", [128, 512]) for i in range(4)]
        r_psU = [R() for i in range(4)]
        r_dc, r_z = R(), R()
        S.dma("sp", (lambda e: [e.dma_start(out=dct[:, hd, :], in_=scr["DC"][hd, :, :]) for hd in range(8)]), 8, "dc", writes=[r_dc])
        S.op("pool", lambda e: e.memset(zer[:], 0.0), writes=[r_z])
        ui = [0]
        offs = []
        a = 0
        for n in seq_lens:
            offs.append(a)
            a += n

        def run_seq(s0, SLs, mode, zero_init=True):
            NB = SLs // 128
            if zero_init:
                for hd in range(8):
                    S.op("pool", (lambda e, hd=hd: e.memset(Sst[hd], 0.0)), writes=[r_S[hd]])
                    S.op("pool", (lambda e, hd=hd: e.memset(Sbf[hd][:], 0.0)), writes=[r_Sb[hd]])

            def load(step):
                if step >= NB:
                    return
                for d in range(2):
                    blk = step if d == 0 else NB - 1 - step
                    t0 = s0 + blk * 128
                    i = step % NR
                    if mode == "full":
                        S.dma("sp", (lambda e, d=d, i=i, t0=t0: [
                            e.dma_start(out=qpb[d][i][:], in_=scr["QP"][d * 4:(d + 1) * 4, :, t0:t0 + 128].rearrange("h p t -> p h t")),
                            e.dma_start(out=atb[d][i][:], in_=scr["AT"][t0:t0 + 128, d * 4:(d + 1) * 4, :]),
                            e.dma_start(out=kpb[d][i][:], in_=scr["KPT"][t0:t0 + 128, d * 4:(d + 1) * 4, :]),
                            e.dma_start(out=vb[d][i][:], in_=scr["VH"][t0:t0 + 128, :])]), 4, f"ld{d}{i}", writes=[r_ld[d][i]])
                    else:
                        S.dma("sp", (lambda e, d=d, i=i, t0=t0: [
                            e.dma_start(out=kpb[d][i][:], in_=scr["KPT"][t0:t0 + 128, d * 4:(d + 1) * 4, :]),
                            e.dma_start(out=vb[d][i][:], in_=scr["VH"][t0:t0 + 128, :])]), 2, f"ld{d}{i}", writes=[r_ld[d][i]])
            load(0)
            load(1)
            for step in range(NB):
                load(step + 2)
                i = step % NR
                ob = step % 2
                if mode == "full":
                    for d in range(2):
                        S.op("pe", (lambda e, d=d, ob=ob: e.matmul(psO[d][ob][:], zer[:, 0:128], zer[:], start=True, stop=False,
                                                                   skip_group_check=True)), reads=[r_z], writes=[r_psO[d][ob]])
                        for h in range(4):
                            S.op("pe", (lambda e, d=d, ob=ob, h=h, i=i: e.matmul(
                                psO[d][ob][:, h * 128:(h + 1) * 128], atb[d][i][:, h, :], vb[d][i][:, h * 128:(h + 1) * 128],
                                start=False, stop=False, skip_group_check=True)), reads=[r_ld[d][i]], writes=[r_psO[d][ob]])
                for ci in range(4):
                    for d in range(2):
                        blk = step if d == 0 else NB - 1 - step
                        c = ci if d == 0 else 3 - ci
                        gch = (s0 + blk * 128) // 32 + c
                        for h in range(4):
                            hd = d * 4 + h
                            if mode == "full":
                                S.op("pe", (lambda e, d=d, ob=ob, h=h, i=i, c=c, hd=hd: e.matmul(
                                    psO[d][ob][32 * c:32 * c + 32, h * 128:(h + 1) * 128], qpb[d][i][:, h, 32 * c:32 * c + 32], Sbf[hd][:],
                                    start=False, stop=(ci == 3), skip_group_check=True, tile_position=(0, 32 * c))),
                                    reads=[r_ld[d][i], r_Sb[hd]], writes=[r_psO[d][ob]])
                            pu = ui[0] % 4
                            ui[0] += 1
                            S.op("pe", (lambda e, d=d, h=h, i=i, c=c, pu=pu: e.matmul(
                                psU[pu][:, 0:128], kpb[d][i][32 * c:32 * c + 32, h, :], vb[d][i][32 * c:32 * c + 32, h * 128:(h + 1) * 128],
                                start=True, stop=True, tile_position=(32 * c, 0))),
                                reads=[r_ld[d][i]], writes=[r_psU[pu]])
                            S.op("dve", (lambda e, hd=hd, pu=pu, gch=gch: e.scalar_tensor_tensor(
                                Sst[hd], Sst[hd], dct[:, hd, gch:gch + 1], psU[pu][:, 0:128], ALU.mult, ALU.add)),
                                reads=[r_psU[pu], r_dc], writes=[r_S[hd]])
                            if mode == "full":
                                S.op("act", (lambda e, hd=hd: e.copy(Sbf[hd][:], Sst[hd])), reads=[r_S[hd]], writes=[r_Sb[hd]])
                if mode == "full":
                    for d in range(2):
                        blk = step if d == 0 else NB - 1 - step
                        t0 = s0 + blk * 128
                        S.op("dve" if d == 0 else "act",
                             (lambda e, d=d, ob=ob: (e.tensor_copy(ot[d][ob][:], psO[d][ob][:]) if d == 0
                                                     else e.copy(ot[d][ob][:], psO[d][ob][:]))),
                             reads=[r_psO[d][ob]], writes=[r_ot[d][ob]])
                        S.dma("pool", (lambda e, d=d, ob=ob, t0=t0: [e.dma_start(out=scr["OF"][d, t0:t0 + 128, :], in_=ot[d][ob][:])]), 1,
                              f"so{d}{ob}", reads=[r_ot[d][ob]])

        if xchg is None:
            for s0, n in zip(offs, seq_lens):
                run_seq(s0, n, "full")
        else:
            n0 = seq_lens[0]
            G = C.sb("G", [128, 4, 8, 129], F32)
            rkm = C.sb("rkm", [128, 8], F32)
            tmp = C.sb("tmp", [128, 128], F32)
            r_G, r_rk, r_tmp, r_xs = R(), R(), R(), R()
            S.dma("sp", lambda e: [e.dma_start(out=rkm[:], in_=cst["rankmask"])], 1, "rk", writes=[r_rk])
            run_seq(offs[0], n0, "state")
            for hd in range(8):
                S.op("dve", (lambda e, hd=hd: e.tensor_reduce(SstAll[:, hd, 128:129], dct[:, hd, offs[0] // 32:(offs[0] + n0) // 32],
                                                              AX.X, ALU.mult)), reads=[r_dc], writes=[r_S[hd]])
            S.dma("pool", (lambda e: [e.dma_start(out=xchg["XS_in"].rearrange("(h p) c -> p h c", p=128), in_=SstAll[:])]), 1, "xs",
                  reads=r_S, writes=[r_xs])
            S.dma("pool", (lambda e: [e.collective_compute("AllGather", ALU.bypass, replica_groups=xchg["groups"],
                                                           ins=[xchg["XS_in"].opt()], outs=[xchg["XS_out"].opt()])]), 1, "ccs", reads=[r_xs], writes=[r_G], inc=1)
            for s0, n in list(zip(offs, seq_lens))[1:]:
                run_seq(s0, n, "full")
            S.dma("sp", (lambda e: [e.dma_start(out=G[:], in_=xchg["XS_out"].rearrange("(r h p) c -> p r h c", r=4, h=8))]), 1, "g",
                  reads=[r_G], writes=[r_G])
            for hd in range(8):
                S.op("pool", (lambda e, hd=hd: e.memset(Sst[hd], 0.0)), writes=[r_S[hd]])
                order = range(4) if hd < 4 else range(3, -1, -1)
                for i in order:
                    mcol = rkm[:, (0 if hd < 4 else 4) + i:(0 if hd < 4 else 4) + i + 1]
                    S.op("dve", (lambda e, hd=hd, i=i: e.scalar_tensor_tensor(tmp[:], Sst[hd], G[:, i, hd, 128:129], G[:, i, hd, 0:128],
                                                                            ALU.mult, ALU.add)), reads=[r_G, r_S[hd]], writes=[r_tmp])
                    S.op("dve", (lambda e, hd=hd: e.tensor_tensor(tmp[:], tmp[:], Sst[hd], ALU.subtract)), reads=[r_S[hd]], writes=[r_tmp])
                    S.op("dve", (lambda e, hd=hd, mcol=mcol: e.scalar_tensor_tensor(Sst[hd], tmp[:], mcol, Sst[hd], ALU.mult, ALU.add)),
                         reads=[r_tmp, r_rk], writes=[r_S[hd]])
                S.op("act", (lambda e, hd=hd: e.copy(Sbf[hd][:], Sst[hd])), reads=[r_S[hd]], writes=[r_Sb[hd]])
            run_seq(offs[0], n0, "full", zero_init=False)
        S.barrier()
        S.emit()
    return S


def outproj_phase(nc, tag, ntok, h1_d, h2_d, cst, w, scr):
    S = Sched(nc, tag)
    ntiles = ntok // 512
    with ExitStack() as es:
        C = Ctx(nc, tag, es)
        R = S.res
        wo = C.sb("wo", [128, 8, D], BF16)
        st = [C.sb("st0", [128, 512], F32), C.sb("st1", [128, 512], F32)]
        ident = C.sb("ident", [128, 128], BF16)
        onb = C.sb("onb", [128, 512], F32)
        ht = [C.sb(f"ht{i}", [128, 4, D], F32) for i in range(2)]
        mT = [C.sb(f"mT{i}", [128, 8, 512], BF16) for i in range(2)]
        of = [C.sb(f"of{i}", [128, 4, 512], F32) for i in range(2)]
        ob = [C.sb(f"ob{i}", [128, 4, 512], F32) for i in range(2)]
        gh = [C.sb(f"gh{i}", [128, 4, 512], BF16) for i in range(2)]
        osum4 = C.sb("osum4", [128, 4, 512], F32)
        junk = C.sb("junk", [128, 128], BF16)
        stat4 = C.sb("stat4", [128, 40], F32)
        mh4 = C.sb("mh4", [128, 4, 512], BF16)
        pt = C.ps("pt", [128, 1024], BF16)
        pt2 = C.ps("pt2", [128, 1024], BF16)
        r_pts = [R(), R()]
        po = [C.ps(f"po{i}", [128, 512]) for i in range(2)]
        r_w, r_st, r_c, r_ident = R(), [R(), R()], R(), R()
        S.dma("sp", lambda e: [e.dma_start(out=onb[:], in_=cst["hg_norm_b"]), e.dma_start(out=st[0][:, 0:128], in_=cst["ident"])],
              2, "c", writes=[r_c, r_st[0]])
        S.op("pool", lambda e: e.tensor_copy(ident[:], st[0][:, 0:128]), reads=[r_st[0]], writes=[r_ident])
        qi = [1]
        load_w(S, w["w_o"], wo, 8, D, None, st, r_st, r_w, qi)
        r_ld = [R(), R()]
        r_ht = [[R() for j in range(4)] for i in range(2)]
        r_mT = [[R() for j in range(4)] for i in range(2)]
        r_osum, r_junk, r_stat, r_mh, r_pt, r_po = R(), R(), R(), R(), R(), [R(), R()]

        def load(i):
            if i >= ntiles:
                return
            b = i % 2
            t0 = i * 512
            S.dma("sp", (lambda e: [e.dma_start(out=ht[b][:, j, :], in_=h1_d[t0 + j * 128:t0 + (j + 1) * 128, :]) for j in range(4)]),
                  4, f"ht{b}", writes=r_ht[b])
            S.dma("sp", (lambda e: [
                e.dma_start(out=mT[b][:, 0:4, :], in_=scr["MIXT"][0:512, t0:t0 + 512].rearrange("(c p) t -> p c t", p=128)),
                e.dma_start(out=of[b][:], in_=scr["OF"][0, t0:t0 + 512, :].rearrange("(j p) c -> p j c", p=128)),
                e.dma_start(out=ob[b][:], in_=scr["OF"][1, t0:t0 + 512, :].rearrange("(j p) c -> p j c", p=128)),
                e.dma_start(out=gh[b][:], in_=scr["GH"][t0:t0 + 512, :].rearrange("(j p) c -> p j c", p=128))]),
                4, f"ld{b}", writes=[r_ld[b]] + r_mT[b])
        load(0)

        def body(i):
            b = i % 2
            load(i + 1)
            t0 = i * 512
            r_os = [R() for j in range(4)]
            r_mhj = [R() for j in range(4)]
            r_stj = [R() for j in range(4)]
            r_jk = [Res("junk") for _ in range(16)]
            for j in range(4):
                S.op("dve" if j % 2 == 0 else "pool",
                     (lambda e, j=j: e.tensor_tensor(osum4[:, j, :], of[b][:, j, :], ob[b][:, j, :], ALU.add)),
                     reads=[r_ld[b], r_osum], writes=[r_os[j]])
            for j in range(4):
                for h in range(4):
                    S.op("act", (lambda e, h=h, j=j: e.activation(mh4[:, j, h * 128:(h + 1) * 128], osum4[:, j, h * 128:(h + 1) * 128], AF.Square,
                                                                  accum_out=stat4[:, j * 8 + h:j * 8 + h + 1])),
                         reads=[r_os[j], r_mh], writes=[r_mhj[j], r_stj[j]])
            for j in range(4):
                S.op("act", (lambda e, j=j: e.activation(stat4[:, j * 8 + 4:j * 8 + 8], stat4[:, j * 8:j * 8 + 4], AF.Sqrt, bias=EPS, scale=1.0 / 128)),
                     reads=[r_stj[j]], writes=[r_stj[j]])
            for j in range(4):
                S.op("dve", (lambda e, j=j: e.reciprocal(stat4[:, j * 8 + 4:j * 8 + 8], stat4[:, j * 8 + 4:j * 8 + 8])), reads=[r_stj[j]], writes=[r_stj[j]])
            for j in range(4):
                ov = osum4[:, j, :].rearrange("p (h v) -> p h v", h=4)
                S.op("dve", (lambda e, ov=ov, j=j: e.tensor_tensor(
                    ov, ov, stat4[:, j * 8 + 4:j * 8 + 8].rearrange("p (h o) -> p h o", o=1).broadcast_to([128, 4, 128]), ALU.mult)),
                    reads=[r_stj[j]], writes=[r_os[j]])
            for j in range(4):
                S.op("pool", (lambda e, j=j: e.tensor_tensor(osum4[:, j, :], osum4[:, j, :], onb[:], ALU.mult)), reads=[r_c], writes=[r_os[j]])
            for j in range(4):
                S.op("dve", (lambda e, j=j: e.tensor_tensor(mh4[:, j, :], osum4[:, j, :], gh[b][:, j, :], ALU.mult)),
                     reads=[r_os[j], r_ld[b], r_mh], writes=[r_mhj[j]])
            for j in range(4):
                ptj, r_ptj = [pt, pt2][j % 2], r_pts[j % 2]
                for c in range(4):
                    S.op("pe", (lambda e, c=c, j=j, ptj=ptj: e.transpose(ptj[:, c * 128:(c + 1) * 128], mh4[:, j, c * 128:(c + 1) * 128], ident[:])),
                         reads=[r_mhj[j], r_ident], writes=[r_ptj])
                S.op("act", (lambda e, j=j, ptj=ptj: e.copy(mT[b][:, 4:8, j * 128:(j + 1) * 128],
                                                            ptj[:, 0:512].rearrange("p (k t) -> p k t", k=4))),
                     reads=[r_ptj], writes=[r_mT[b][j]])
            S.op("pool", (lambda e: e.memset(stat4[:, 32:33], 0.0)), writes=r_os + r_mhj + [r_osum, r_mh])
            for j in range(4):
                for hh in range(2):
                    pb = (j * 2 + hh) % 2
                    for kc in range(8):
                        S.op("pe", (lambda e, j=j, hh=hh, kc=kc, pb=pb: e.matmul(
                            po[pb][:], mT[b][:, kc, j * 128:(j + 1) * 128], wo[:, kc, hh * 512:(hh + 1) * 512],
                            start=(kc == 0), stop=(kc == 7))), reads=[r_w, r_mT[b][j]], writes=[r_po[pb]])
                    dst = ht[b][:, j, hh * 512:(hh + 1) * 512]
                    S.op("dve", (lambda e, dst=dst, pb=pb: e.tensor_tensor(dst, dst, po[pb][:], ALU.add)),
                         reads=[r_po[pb]], writes=[r_ht[b][j]])
                S.dma("pool", (lambda e, j=j: [e.dma_start(out=h2_d[t0 + j * 128:t0 + (j + 1) * 128, :], in_=ht[b][:, j, :])]), 1,
                      f"so{b}{j}", reads=[r_ht[b][j]])
        for i in range(ntiles):
            body(i)
        S.barrier()
        S.emit()
    return S


def ple_phase(nc, tag, ntok, h3_d, p_d, y_d, cst, w):
    S = Sched(nc, tag)
    ntiles = ntok // 512
    with ExitStack() as es:
        C = Ctx(nc, tag, es)
        R = S.res
        wg = C.sb("wg", [128, 8, D], BF16)
        wp = C.sb("wp", [128, 2, D], BF16)
        st = [C.sb("st0", [128, 512], F32), C.sb("st1", [128, 512], F32)]
        ident = C.sb("ident", [128, 128], BF16)
        gain = C.sb("gain", [128, 8], F32)
        fnb = C.sb("fnb", [128, D], F32)
        ht = [C.sb(f"ht{i}", [128, 4, D], F32) for i in range(2)]
        ptl = [C.sb(f"ptl{i}", [128, 4, 256], F32) for i in range(2)]
        xn4 = C.sb("xn4", [128, 4, D], BF16)
        junk = C.sb("junk", [128, D], BF16)
        stat = C.sb("stat", [128, 16], F32)
        xnT = C.sb("xnT", [128, 8, 512], BF16)
        pb16 = C.sb("pb16", [128, 4, 256], BF16)
        pT = C.sb("pT", [128, 2, 512], BF16)
        gsb = [C.sb(f"gsb{i}", [128, 512], F32) for i in range(2)]
        pt = C.ps("pt", [128, 1024], BF16)
        pt2 = C.ps("pt2", [128, 1024], BF16)
        pg = [C.ps(f"pg{i}", [128, 512]) for i in range(2)]
        pp = [C.ps(f"pp{i}", [128, 512]) for i in range(2)]
        r_w, r_st, r_c, r_ident = R(), [R(), R()], R(), R()
        S.dma("sp", lambda e: [e.dma_start(out=gain[:], in_=cst["ple_norm"]), e.dma_start(out=fnb[:], in_=cst["final_norm_b"]),
                               e.dma_start(out=st[0][:, 0:128], in_=cst["ident"])], 3, "c", writes=[r_c, r_st[0]])
        S.op("pool", lambda e: e.tensor_copy(ident[:], st[0][:, 0:128]), reads=[r_st[0]], writes=[r_ident])
        S.op("pool", lambda e: e.tensor_copy(stat[:, 15:16], gain[:, 0:1]), reads=[r_c], writes=[R()])
        qi = [1]
        load_w(S, w["w_ple_gate"], wg, 8, D, gain, st, r_st, r_w, qi)
        load_w(S, w["w_ple_proj"], wp, 2, D, None, st, r_st, r_w, qi)
        r_ht = [[R() for j in range(4)] for i in range(2)]
        r_pl = [R(), R()]
        r_stat = [R() for j in range(4)]
        r_junk, r_pt = R(), R()
        r_xn4 = [R() for j in range(4)]
        r_pts = [R(), R()]
        r_xnT = [R() for k in range(8)]
        r_pb16, r_pT, r_gsb, r_pg, r_pp = [R() for j in range(4)], R(), [R(), R()], [R(), R()], [R(), R()]

        def load(i):
            if i >= ntiles:
                return
            b = i % 2
            t0 = i * 512
            S.dma("sp", (lambda e: [e.dma_start(out=ht[b][:, j, :], in_=h3_d[t0 + j * 128:t0 + (j + 1) * 128, :]) for j in range(4)]),
                  4, f"ht{b}", writes=r_ht[b])
            S.dma("sp", (lambda e: [e.dma_start(out=ptl[b][:], in_=p_d[t0:t0 + 512, :].rearrange("(j p) c -> p j c", p=128))]),
                  1, f"pl{b}", writes=[r_pl[b]])
        load(0)

        def body(i):
            b = i % 2
            load(i + 1)
            t0 = i * 512
            norm_transpose4(S, ht[b], r_ht[b], stat, r_stat, junk, xn4, r_xn4, [pt, pt2], r_pts, ident, r_ident, xnT, r_xnT)
            for j in range(4):
                S.op("pool", (lambda e, j=j: e.tensor_copy(pb16[:, j, :], ptl[b][:, j, :])), reads=[r_pl[b]], writes=[r_pb16[j]])
            for j in range(4):
                ptj, r_ptj = [pt, pt2][j % 2], r_pts[j % 2]
                for c in range(2):
                    S.op("pe", (lambda e, c=c, j=j, ptj=ptj: e.transpose(ptj[:, c * 128:(c + 1) * 128], pb16[:, j, c * 128:(c + 1) * 128], ident[:])),
                         reads=[r_pb16[j], r_ident], writes=[r_ptj])
                S.op("act", (lambda e, j=j, ptj=ptj: e.copy(pT[:, :, j * 128:(j + 1) * 128], ptj[:, 0:256].rearrange("p (k t) -> p k t", k=2))),
                     reads=[r_ptj], writes=[r_pT])
            for j in range(4):
                for hh in range(2):
                    k2 = (j * 2 + hh) % 2
                    for kc in range(8):
                        S.op("pe", (lambda e, j=j, hh=hh, kc=kc, k2=k2: e.matmul(
                            pg[k2][:], xnT[:, kc, j * 128:(j + 1) * 128], wg[:, kc, hh * 512:(hh + 1) * 512],
                            start=(kc == 0), stop=(kc == 7))), reads=[r_w] + r_xnT, writes=[r_pg[k2]])
                    for kc in range(2):
                        S.op("pe", (lambda e, j=j, hh=hh, kc=kc, k2=k2: e.matmul(
                            pp[k2][:], pT[:, kc, j * 128:(j + 1) * 128], wp[:, kc, hh * 512:(hh + 1) * 512],
                            start=(kc == 0), stop=(kc == 1))), reads=[r_w, r_pT], writes=[r_pp[k2]])
                    S.op("act", (lambda e, k2=k2: e.activation(gsb[k2][:], pg[k2][:], AF.Sigmoid)), reads=[r_pg[k2]], writes=[r_gsb[k2]])
                    S.op("dve", (lambda e, k2=k2: e.tensor_tensor(gsb[k2][:], gsb[k2][:], pp[k2][:], ALU.mult)),
                         reads=[r_pp[k2]], writes=[r_gsb[k2]])
                    dst = ht[b][:, j, hh * 512:(hh + 1) * 512]
                    S.op("pool", (lambda e, dst=dst, k2=k2: e.tensor_tensor(dst, dst, gsb[k2][:], ALU.add)),
                         reads=[r_gsb[k2]], writes=[r_ht[b][j]])
            for j in range(4):
                S.op("act", (lambda e, j=j: e.activation(xn4[:, j, :], ht[b][:, j, :], AF.Square, accum_out=stat[:, 8 + j:9 + j])),
                     reads=[r_ht[b][j]], writes=[r_xn4[j], r_stat[j]])
            for j in range(4):
                S.op("act", (lambda e, j=j: e.activation(stat[:, 8 + j:9 + j], stat[:, 8 + j:9 + j], AF.Sqrt, bias=EPS, scale=1.0 / D)),
                     writes=[r_stat[j]])
            for j in range(4):
                S.op("dve", (lambda e, j=j: e.reciprocal(stat[:, 8 + j:9 + j], stat[:, 8 + j:9 + j])), writes=[r_stat[j]])
            for j in range(4):
                S.op("dve", (lambda e, j=j: e.scalar_tensor_tensor(ht[b][:, j, :], ht[b][:, j, :], stat[:, 8 + j:9 + j], fnb[:], ALU.mult, ALU.mult)),
                     reads=[r_stat[j], r_c], writes=[r_ht[b][j]])
                S.dma("pool", (lambda e, j=j: [e.dma_start(out=y_d[t0 + j * 128:t0 + (j + 1) * 128, :], in_=ht[b][:, j, :])]), 1,
                      f"so{b}{j}", reads=[r_ht[b][j]])
        for i in range(ntiles):
            body(i)
        S.barrier()
        S.emit()
    return S


CONST_SHAPES = {
    "ident": [128, 128], "rmask": [128, 512], "maskF": [128, 128], "maskB": [128, 128],
    "ffn1_norm": [128, 8], "mix_norm": [128, 8], "q_norm": [128, 3], "kv_norm": [128, 2], "hg_lb": [128, 16],
    "hg_norm_b": [128, 512], "ffn2_norm": [128, 8], "ple_norm": [128, 8], "final_norm_b": [128, D],
}
W_SHAPES = {
    "ffn1_wg": [D, DFF], "ffn1_wu": [D, DFF], "ffn1_wd": [DFF, D], "w_in": [D, WIN_COLS], "w_uq": [384, UQ_COLS],
    "w_uk": [256, 512], "w_uv": [256, 512], "w_o": [D, D], "ffn2_wg": [D, DFF], "ffn2_wu": [D, DFF], "ffn2_wd": [DFF, D],
    "w_ple_gate": [D, D], "w_ple_proj": [256, D],
}


def build_program(seq_lens, dbg=(), phases=None, gather=False):
    Sched.DMA_SEMS = {}
    Sched.DMA_CNT = {}
    nc = bass.Bass("TRN2", target_bir_lowering=False)
    ntok = sum(seq_lens)
    ntiles = ntok // 512

    def inp(name, shape, dt=F32):
        return nc.dram_tensor(name, shape, dt, kind="ExternalInput").ap()

    def scratch(name, shape, dt):
        kind = "ExternalOutput" if name in dbg else "Internal"
        return nc.dram_tensor(name, shape, dt, kind=kind).ap()

    x = inp("x", [ntok, D])
    p = inp("p", [ntok, 256])
    cst = {k: inp(k, v) for k, v in CONST_SHAPES.items()}
    cst["rope_c"] = inp("rope_c", [32, ntok])
    cst["rope_s"] = inp("rope_s", [32, ntok])
    w = {k: inp(k, v) for k, v in W_SHAPES.items()}
    y = nc.dram_tensor("y", [ntok, D], F32, kind="ExternalOutput").ap()
    h1 = scratch("h1", [ntok, D], F32)
    h2 = scratch("h2", [ntok, D], F32)
    h3 = scratch("h3", [ntok, D], F32)
    scr = {
        "QT": scratch("QT", [8, 96, ntok], BF16), "KTn": scratch("KTn", [512, ntok], BF16),
        "KTr": scratch("KTr", [32, ntok], BF16), "VA": scratch("VA", [8, ntok, 64], BF16),
        "QP": scratch("QP", [8, 128, ntok], BF16), "DC": scratch("DC", [8, 128, ntok // 32], F32),
        "VH": scratch("VH", [ntok, 512], BF16), "GH": scratch("GH", [ntok, 512], BF16),
        "AT": scratch("AT", [ntok, 8, 128], BF16), "KPT": scratch("KPT", [ntok, 8, 128], BF16),
        "MIXT": scratch("MIXT", [512, ntok], BF16), "OF": scratch("OF", [2, ntok, 512], F32),
    }
    def on(k):
        return phases is None or k in phases
    if on("f1"):
        ffn_phase(nc, "f1", x, h1, ntiles, cst["ffn1_norm"], w["ffn1_wg"], w["ffn1_wu"], w["ffn1_wd"], cst["ident"])
    if on("mi"):
        mixer_phase2(nc, "ma", "a", ntok, h1, cst, w, scr)
        mixer_phase2(nc, "mb", "b", ntok, h1, cst, w, scr)
    seqs = []
    a = 0
    for SLs in seq_lens:
        b = a + SLs
        seqs.append(dict(q0=a, nq=SLs, kp=[((lambda h, a=a, b=b: scr["KTn"][h * 64:(h + 1) * 64, a:b]), scr["KTr"][:, a:b],
                                            (lambda h, a=a, b=b: scr["VA"][h, a:b, :]), SLs)]))
        a = b
    xchg = None
    if gather:
        n0 = seq_lens[0]
        cst["rankmask"] = inp("rankmask", [128, 8])
        xchg = dict(n0=n0, groups=[[0, 1, 2, 3], [4, 5, 6, 7]],
                    XK_in=[scratch(f"XK_in{a_}", [128, n0], BF16) for a_ in range(4)],
                    XK_out=[scratch(f"XK_out{a_}", [4 * 128, n0], BF16) for a_ in range(4)],
                    XR_in=scratch("XR_in", [128, n0], BF16), XR_out=scratch("XR_out", [4 * 128, n0], BF16),
                    XV_in=[scratch(f"XV_in{a_}", [2 * n0, 64], BF16) for a_ in range(4)],
                    XV_out=[scratch(f"XV_out{a_}", [4 * 2 * n0, 64], BF16) for a_ in range(4)],
                    XS_in=scratch("XS_in", [1024, 129], F32), XS_out=scratch("XS_out", [4096, 129], F32))
        kp = []
        for r in range(4):
            kp.append(((lambda h, r=r: xchg["XK_out"][h // 2][r * 128 + (h % 2) * 64:r * 128 + (h % 2) * 64 + 64, :]),
                       xchg["XR_out"][r * 128:r * 128 + 32, :],
                       (lambda h, r=r: xchg["XV_out"][h // 2].rearrange("(r hh t) d -> r hh t d", r=4, hh=2)[r, h % 2]),
                       n0))
        seqs[0] = dict(q0=0, nq=n0, kp=kp, gathered=True)
        seqs = seqs[1:] + seqs[0:1]
    if on("at"):
        attn_phase(nc, "at", seqs, scr, xchg=xchg)
    if on("sc"):
        scan_phase(nc, "sc", list(seq_lens), ntok, scr, xchg=xchg, cst=cst)
    if on("op"):
        outproj_phase(nc, "op", ntok, h1, h2, cst, w, scr)
    if on("f2"):
        ffn_phase(nc, "f2", h2, h3, ntiles, cst["ffn2_norm"], w["ffn2_wg"], w["ffn2_wu"], w["ffn2_wd"], cst["ident"])
    if on("pl"):
        ple_phase(nc, "pl", ntok, h3, p, y, cst, w)
    return nc


def _lay(v, nch):
    return np.ascontiguousarray(np.asarray(v, np.float32).reshape(nch, 128).T)


def host_consts(inputs):
    f = np.float32
    c = {}
    c["ident"] = np.eye(128, dtype=f)
    rm = np.ones((128, 512), f)
    rm[:, 0::32] = 0.0
    c["rmask"] = rm
    idx = np.arange(128)
    same = (idx[:, None] // 32) == (idx[None, :] // 32)
    c["maskF"] = (same & (idx[:, None] <= idx[None, :])).astype(f)
    c["maskB"] = (same & (idx[:, None] >= idx[None, :])).astype(f)
    c["ffn1_norm"] = _lay(inputs["ffn1_norm"][0], 8)
    c["mix_norm"] = _lay(inputs["mix_norm"][0], 8)
    c["q_norm"] = _lay(inputs["q_norm"][0], 3)
    c["kv_norm"] = _lay(inputs["kv_norm"][0], 2)
    lb = np.asarray(inputs["hg_lb"], f).reshape(2, 2, 4, 128)
    c["hg_lb"] = np.ascontiguousarray(lb.transpose(3, 0, 1, 2).reshape(128, 16))
    c["hg_norm_b"] = np.ascontiguousarray(np.broadcast_to(np.asarray(inputs["hg_norm"][0], f)[None, :], (128, 512)))
    c["ffn2_norm"] = _lay(inputs["ffn2_norm"][0], 8)
    c["ple_norm"] = _lay(inputs["ple_norm"][0], 8)
    c["final_norm_b"] = np.ascontiguousarray(np.broadcast_to(np.asarray(inputs["final_norm"], f)[None, :], (128, D)))
    wts = {}
    for k in ("ffn1_wg", "ffn1_wu", "ffn1_wd", "w_uk", "w_uv", "w_o", "ffn2_wg", "ffn2_wu", "ffn2_wd", "w_ple_gate", "w_ple_proj"):
        wts[k] = np.ascontiguousarray(np.asarray(inputs[k][0], f))
    win = np.asarray(inputs["w_in"][0], f)
    wts["w_in"] = np.ascontiguousarray(np.concatenate([win, win[:, 656:672], win[:, 640:656]], axis=1))
    wq = np.asarray(inputs["w_uq"][0], f)
    sw = [np.concatenate([wq[:, h * 96 + 80:h * 96 + 96], wq[:, h * 96 + 64:h * 96 + 80]], axis=1) for h in range(8)]
    wts["w_uq"] = np.ascontiguousarray(np.concatenate([wq] + sw, axis=1))
    return c, wts


def rope_tables(pos):
    inv = np.exp(np.arange(0, 32, 2, dtype=np.float32) * np.float32(-np.log(10000.0) / 32)).astype(np.float32)
    ang = (pos.astype(np.float32)[None, :] * inv[:, None]).astype(np.float32)
    cs, sn = np.cos(ang).astype(np.float32), np.sin(ang).astype(np.float32)
    return np.ascontiguousarray(np.concatenate([cs, cs], 0)), np.ascontiguousarray(np.concatenate([-sn, sn], 0))


_PROG = {}


def run_balanced(inputs):
    xp = np.asarray(inputs["x_prompt"], np.float32)
    xs = np.asarray(inputs["x_sample"], np.float32)
    pp = np.asarray(inputs["p_prompt"], np.float32)[0]
    psm = np.asarray(inputs["p_sample"], np.float32)[0]
    SP, SS = xp.shape[1], xs.shape[1]
    Q = SP // 4
    lens = (Q, SS, SS)
    c, wts = host_consts(inputs)
    key = ("bal", lens)
    if key not in _PROG:
        _PROG[key] = build_program(lens, gather=True)
    nc = _PROG[key]
    in_maps = []
    for core in range(8):
        pb, r = core // 4, core % 4
        im = {"x": np.ascontiguousarray(np.concatenate([xp[pb, r * Q:(r + 1) * Q], xs[2 * core], xs[2 * core + 1]], axis=0)),
              "p": np.ascontiguousarray(np.concatenate([pp[pb, r * Q:(r + 1) * Q], psm[2 * core], psm[2 * core + 1]], axis=0))}
        pos = np.concatenate([np.arange(r * Q, (r + 1) * Q, dtype=np.float32), np.arange(SS, dtype=np.float32),
                              np.arange(SS, dtype=np.float32)])
        im["rope_c"], im["rope_s"] = rope_tables(pos)
        rk = np.zeros((128, 8), np.float32)
        for i in range(4):
            rk[:, i] = 1.0 if i < r else 0.0
            rk[:, 4 + i] = 1.0 if i > r else 0.0
        im["rankmask"] = rk
        im.update(c)
        im.update(wts)
        in_maps.append(im)
    res = run_bass_kernel_spmd(nc, in_maps, core_ids=list(range(8)))
    y_prompt = np.empty((2, SP, D), np.float32)
    y_sample = np.empty((16, SS, D), np.float32)
    for core in range(8):
        pb, r = core // 4, core % 4
        y = np.asarray(res.results[core]["y"])
        y_prompt[pb, r * Q:(r + 1) * Q] = y[0:Q]
        y_sample[2 * core] = y[Q:Q + SS]
        y_sample[2 * core + 1] = y[Q + SS:]
    return (y_prompt, y_sample)


def kernel(**inputs):
    return run_balanced(inputs)


def _roll(make_gen, n, stagger):
    active = []
    nxt = 0
    since = 10 ** 9
    while nxt < n or active:
        if nxt < n and len(active) < 2 and (not active or since >= stagger):
            active.append(make_gen(nxt))
            nxt += 1
            since = 0
        for g in list(active):
            try:
                next(g)
            except StopIteration:
                active.remove(g)
        since += 1


def _drive(gens):
    alive = list(gens)
    while alive:
        for g in list(alive):
            try:
                next(g)
            except StopIteration:
                alive.remove(g)


def mixer_phase2(nc, tag, part, ntok, h1_d, cst, w, scr):
    S = Sched(nc, tag)
    ntiles = ntok // 512
    with ExitStack() as es:
        C = Ctx(nc, tag, es)
        R = S.res
        gains = C.sb("gains", [128, 16], F32)
        ident = C.sb("ident", [128, 128], BF16)
        ht = C.sb("ht", [128, 4, D], F32)
        xn4 = C.sb("xn4", [128, 4, D], BF16)
        junk = C.sb("junk", [128, D], BF16)
        stat = C.sb("stat", [128, 16], F32)
        pt = C.ps("pt", [128, 1024], BF16)
        pt2 = C.ps("pt2", [128, 1024], BF16)
        pm = [C.ps(f"pm{i}", [128, 512]) for i in range(3)]
        pa = C.ps("pa", [128, 512])
        r_c, r_ident, r_w = R(), R(), R()
        r_ht = [R() for j in range(4)]
        r_stat = [R() for j in range(4)]
        r_xn4 = [R() for j in range(4)]
        r_pts = [R(), R()]
        r_pm = [R() for i in range(3)]
        r_pa = R()
        st = [ht[:, 0, 0:512], ht[:, 1, 0:512]]
        r_st = [r_ht[0], r_ht[1]]
        pmi = [0]

        def nextpm():
            i = pmi[0] % 3
            pmi[0] += 1
            return pm[i], r_pm[i]

        def cdma(dst, src):
            S.dma("sp", (lambda e: [e.dma_start(out=dst, in_=src)]), 1, "c", writes=[r_c])

        def mm_fm(ps, r_ps, wsb, c0, m, xT, r_x, nkc, out_p0=0):
            for kc in range(nkc):
                S.op("pe", (lambda e, kc=kc: e.matmul(ps[out_p0:out_p0 + m, :], wsb[:, kc, c0:c0 + m], xT[:, kc, :],
                                                      start=(kc == 0), stop=(kc == nkc - 1))),
                     reads=[r_w] + r_x, writes=[r_ps])

        def mm_tm(ps, r_ps, xT, r_x, j, wsb, c0, n, nkc):
            for kc in range(nkc):
                S.op("pe", (lambda e, kc=kc: e.matmul(ps[:, 0:n], xT[:, kc, j * 128:(j + 1) * 128], wsb[:, kc, c0:c0 + n],
                                                      start=(kc == 0), stop=(kc == nkc - 1))),
                     reads=[r_w] + r_x, writes=[r_ps])

        cdma(gains[:, 0:8], cst["mix_norm"])
        cdma(gains[:, 8:11], cst["q_norm"])
        cdma(gains[:, 11:13], cst["kv_norm"])
        S.dma("sp", (lambda e: [e.dma_start(out=ht[:, 2, 0:128], in_=cst["ident"])]), 1, "c2", writes=[r_ht[2]])
        S.op("pool", lambda e: e.tensor_copy(ident[:], ht[:, 2, 0:128]), reads=[r_ht[2]], writes=[r_ident])
        S.op("pool", lambda e: e.tensor_copy(stat[:, 15:16], gains[:, 0:1]), reads=[r_c], writes=[R()])
        qi = [0]

        def head(i, xnT, r_xnT):
            t0 = i * 512
            S.dma("sp", (lambda e: [e.dma_start(out=ht[:, j, :], in_=h1_d[t0 + j * 128:t0 + (j + 1) * 128, :])
                                    for j in range(4)]), 4, "ht", writes=r_ht)
            norm_transpose4(S, ht, r_ht, stat, r_stat, junk, xn4, r_xn4, [pt, pt2], r_pts, ident, r_ident, xnT, r_xnT)

        if part == "a":
            wa = C.sb("wa", [128, NKC, 704], BF16)
            wuq = C.sb("wuq", [128, 3, UQ_COLS], BF16)
            wuk = C.sb("wuk", [128, 2, 512], BF16)
            wuv = C.sb("wuv", [128, 2, 512], BF16)
            ones = C.sb("ones", [128, 128], BF16)
            r_ones = R()
            S.op("pool", lambda e: e.memset(ones[:], 1.0), writes=[r_ones])
            load_w(S, w["w_in"][:, 0:672], wa[:, :, 0:672], NKC, 672, gains[:, 0:8], st, r_st, r_w, qi)
            load_w(S, w["w_in"][:, C_KRS:C_KRS + 32], wa[:, :, 672:704], NKC, 32, gains[:, 0:8], st, r_st, r_w, qi)
            load_w(S, w["w_uq"], wuq, 3, UQ_COLS, gains[:, 8:11], st, r_st, r_w, qi)
            load_w(S, w["w_uk"], wuk, 2, 512, gains[:, 11:13], st, r_st, r_w, qi)
            load_w(S, w["w_uv"], wuv, 2, 512, gains[:, 11:13], st, r_st, r_w, qi)

            def make_set(k):
                pn = C.ps(f"pn{k}", [128, 512])
                r_pn = R()
                xnT = C.sb(f"xnT{k}", [128, NKC, 512], BF16)
                cqT = C.sb(f"cqT{k}", [128, 3, 512], BF16)
                ckvT = C.sb(f"ckvT{k}", [128, 2, 512], BF16)
                sqq = C.sb(f"sqq{k}", [128, 2, 512], BF16)
                sqkv = C.sb(f"sqkv{k}", [128, 2, 512], BF16)
                rsq = C.sb(f"rsq{k}", [128, 512], F32)
                rskv = C.sb(f"rskv{k}", [128, 512], F32)
                rstok = C.sb(f"rstok{k}", [128, 8], F32)
                tct = C.sb(f"tct{k}", [128, 512], F32)
                tst = C.sb(f"tst{k}", [128, 512], F32)
                t1 = [C.sb(f"t1{k}{i}", [128, 512], F32) for i in range(2)]
                t2 = C.sb(f"t2{k}", [128, 512], F32)
                qout = C.sb(f"qout{k}", [128, 8, 512], BF16)
                knT = C.sb(f"knT{k}", [128, 4, 512], BF16)
                krp = C.sb(f"krp{k}", [128, 512], BF16)
                vt = C.sb(f"vt{k}", [128, 4, 512], BF16)
                r_xnT = [R() for _ in range(NKC)]
                r_cqT, r_ckvT, r_sqq, r_sqkv = R(), R(), [R(), R()], R()
                r_rsq, r_rskv, r_rstok, r_tab = R(), R(), R(), R()
                r_t1, r_t2 = [R(), R()], R()
                r_qout, r_knT, r_krp, r_vt = R(), R(), R(), R()
                S.op("pool", lambda e: e.memset(tct[:], 1.0), writes=[r_tab])
                S.op("pool", lambda e: e.memset(tst[:], 0.0), writes=[r_tab])

                def tile(i):
                    t0 = i * 512
                    S.dma("sp", (lambda e: [e.dma_start(out=tct[64:96, :], in_=cst["rope_c"][:, t0:t0 + 512]),
                                            e.dma_start(out=tst[64:96, :], in_=cst["rope_s"][:, t0:t0 + 512])]),
                          2, f"tab{k}", writes=[r_tab])
                    head(i, xnT, r_xnT)
                    yield
                    for c in range(3):
                        ps, rp = nextpm()
                        mm_fm(ps, rp, wa, C_CQ + c * 128, 128, xnT, r_xnT, NKC)
                        S.op("act", (lambda e, ps=ps, c=c: e.copy(cqT[:, c, :], ps[:])), reads=[rp], writes=[r_cqT])
                        S.op("act", (lambda e, ps=ps, c=c: e.activation(sqq[:, c % 2, :], ps[:], AF.Square)),
                             reads=[rp], writes=[r_sqq[c % 2]])
                        S.op("pe", (lambda e, c=c: e.matmul(pn[:], ones[:], sqq[:, c % 2, :], start=(c == 0), stop=(c == 2))),
                             reads=[r_ones, r_sqq[c % 2]], writes=[r_pn])
                        yield
                    S.op("act", lambda e: e.activation(rsq[:], pn[:], AF.Sqrt, bias=EPS, scale=1.0 / 384), reads=[r_pn], writes=[r_rsq])
                    S.op("dve", lambda e: e.reciprocal(rsq[:], rsq[:]), reads=[r_rsq], writes=[r_rsq])
                    yield
                    for c in range(2):
                        ps, rp = nextpm()
                        mm_fm(ps, rp, wa, C_CKV + c * 128, 128, xnT, r_xnT, NKC)
                        S.op("act", (lambda e, ps=ps, c=c: e.copy(ckvT[:, c, :], ps[:])), reads=[rp], writes=[r_ckvT])
                        S.op("act", (lambda e, ps=ps, c=c: e.activation(sqkv[:, c, :], ps[:], AF.Square)),
                             reads=[rp], writes=[r_sqkv])
                        yield
                    for c in range(2):
                        S.op("pe", (lambda e, c=c: e.matmul(pn[:], ones[:], sqkv[:, c, :], start=(c == 0), stop=(c == 1))),
                             reads=[r_ones, r_sqkv], writes=[r_pn])
                    S.op("act", lambda e: e.activation(rskv[:], pn[:], AF.Sqrt, bias=EPS, scale=1.0 / 256), reads=[r_pn], writes=[r_rskv])
                    S.op("dve", lambda e: e.reciprocal(rskv[:], rskv[:]), reads=[r_rskv], writes=[r_rskv])
                    for j in range(4):
                        for c in range(2):
                            S.op("pe", (lambda e, j=j, c=c: e.matmul(pa[:, j:j + 1], sqkv[:, c, j * 128:(j + 1) * 128], ones[:, 0:1],
                                                                     start=(c == 0), stop=(c == 1))),
                                 reads=[r_ones, r_sqkv], writes=[r_pa])
                    S.op("act", lambda e: e.activation(rstok[:, 0:4], pa[:, 0:4], AF.Sqrt, bias=EPS, scale=1.0 / 256),
                         reads=[r_pa], writes=[r_rstok])
                    S.op("dve", lambda e: e.reciprocal(rstok[:, 0:4], rstok[:, 0:4]), reads=[r_rstok], writes=[r_rstok])
                    yield
                    ps, rp = nextpm()
                    mm_fm(ps, rp, wa, C_KR, 32, xnT, r_xnT, NKC, out_p0=64)
                    ps2, rp2 = nextpm()
                    mm_fm(ps2, rp2, wa, 672, 32, xnT, r_xnT, NKC, out_p0=64)
                    S.op("dve", (lambda e, ps=ps: e.tensor_tensor(t1[0][64:96, :], ps[64:96, :], tct[64:96, :], ALU.mult)),
                         reads=[rp, r_tab], writes=[r_t1[0]])
                    S.op("dve", (lambda e, ps2=ps2: e.tensor_tensor(t2[64:96, :], ps2[64:96, :], tst[64:96, :], ALU.mult)),
                         reads=[rp2, r_tab], writes=[r_t2])
                    S.op("pool", lambda e: e.tensor_tensor(krp[64:96, :], t1[0][64:96, :], t2[64:96, :], ALU.add),
                         reads=[r_t1[0], r_t2], writes=[r_krp])
                    S.dma("pool", (lambda e: [e.dma_start(out=scr["KTr"][:, t0:t0 + 512], in_=krp[64:96, :])]), 1, f"s_krp{k}",
                          reads=[r_krp])
                    yield
                    for h in range(8):
                        b = h % 2
                        ps, rp = nextpm()
                        mm_fm(ps, rp, wuq, h * 96, 96, cqT, [r_cqT], 3)
                        ps2, rp2 = nextpm()
                        mm_fm(ps2, rp2, wuq, 768 + h * 32, 32, cqT, [r_cqT], 3, out_p0=64)
                        S.op("dve", (lambda e, ps=ps, b=b: e.tensor_tensor(t1[b][0:96, :], ps[0:96, :], tct[0:96, :], ALU.mult)),
                             reads=[rp, r_tab], writes=[r_t1[b]])
                        S.op("dve", (lambda e, ps2=ps2: e.tensor_tensor(t2[64:96, :], ps2[64:96, :], tst[64:96, :], ALU.mult)),
                             reads=[rp2, r_tab], writes=[r_t2])
                        S.op("pool", (lambda e, b=b: e.tensor_tensor(t1[b][64:96, :], t1[b][64:96, :], t2[64:96, :], ALU.add)),
                             reads=[r_t2], writes=[r_t1[b]])
                        S.op("pool", (lambda e, b=b, h=h: e.tensor_tensor(qout[0:96, h, :], t1[b][0:96, :], rsq[0:96, :], ALU.mult)),
                             reads=[r_t1[b], r_rsq], writes=[r_qout])
                        yield
                    S.dma("pool", (lambda e: [e.dma_start(out=scr["QT"][h, :, t0:t0 + 512], in_=qout[0:96, h, :])
                                              for h in range(8)]), 8, f"s_q{k}", reads=[r_qout])
                    for a in range(4):
                        ps, rp = nextpm()
                        mm_fm(ps, rp, wuk, a * 128, 128, ckvT, [r_ckvT], 2)
                        S.op("dve", (lambda e, ps=ps, a=a: e.tensor_tensor(knT[:, a, :], ps[:], rskv[:], ALU.mult)),
                             reads=[rp, r_rskv], writes=[r_knT])
                        yield
                    S.dma("pool", (lambda e: [e.dma_start(out=scr["KTn"][a * 128:(a + 1) * 128, t0:t0 + 512], in_=knT[:, a, :])
                                              for a in range(4)]), 4, f"s_kn{k}", reads=[r_knT])
                    for j in range(4):
                        ps, rp = nextpm()
                        mm_tm(ps, rp, ckvT, [r_ckvT], j, wuv, 0, 512, 2)
                        S.op("act", (lambda e, ps=ps, j=j: e.activation(vt[:, j, :], ps[:], AF.Copy, scale=rstok[:, j:j + 1])),
                             reads=[rp, r_rstok], writes=[r_vt])
                        yield
                    S.dma("pool", (lambda e: [e.dma_start(
                        out=scr["VA"][:, t0 + j * 128:t0 + (j + 1) * 128, :].rearrange("h t d -> t h d"),
                        in_=vt[:, j, :].rearrange("t (h d) -> t h d", h=8)) for j in range(4)]), 4, f"s_v{k}", reads=[r_vt])
                return tile
        else:
            wb = C.sb("wb", [128, NKC, 2560], BF16)
            lbt = C.sb("lbt", [128, 16], F32)
            lb = C.sb("lb", [128, 8], F32)
            oml = C.sb("oml", [128, 8], F32)
            rmask = C.sb("rmask", [128, 512], F32)
            mF = C.sb("mF", [128, 128], F32)
            mB = C.sb("mB", [128, 128], F32)
            ato = C.sb("ato", [128, 4, 8, 128], BF16)
            kto = C.sb("kto", [128, 4, 8, 128], BF16)
            ptk = C.ps("ptk", [128, 1024], BF16)
            r_ptk, r_ato, r_kto, r_lb = R(), R(), R(), R()
            cdma(lbt[:], cst["hg_lb"])
            cdma(rmask[:], cst["rmask"])
            cdma(mF[:], cst["maskF"])
            cdma(mB[:], cst["maskB"])
            lv = lbt[:].rearrange("p (d l h) -> p d l h", d=2, l=2)
            lb3 = lb[:].rearrange("p (d h) -> p d h", d=2)
            S.op("dve", lambda e: e.tensor_tensor(lb3, lv[:, :, 0, :], lv[:, :, 1, :], ALU.subtract), reads=[r_c], writes=[r_lb])
            S.op("act", lambda e: e.activation(oml[:], lb[:], AF.Sigmoid, scale=-1.0), reads=[r_lb], writes=[R()])
            S.op("act", lambda e: e.activation(lb[:], lb[:], AF.Sigmoid), reads=[r_lb], writes=[r_lb])
            c1 = C.sb("c1", [128, 8], F32)
            c0 = C.sb("c0", [128, 8], F32)
            c1n = C.sb("c1n", [128, 8], F32)
            S.op("dve", lambda e: e.tensor_scalar(c1[:], oml[:], 0.5, None, ALU.mult), reads=[r_lb], writes=[r_lb])
            S.op("dve", lambda e: e.tensor_tensor(c0[:], lb[:], c1[:], ALU.add), reads=[r_lb], writes=[r_lb])
            S.op("dve", lambda e: e.tensor_scalar(c1n[:], c1[:], -1.0, None, ALU.mult), reads=[r_lb], writes=[r_lb])
            load_w(S, w["w_in"][:, 672:3232], wb, NKC, 2560, gains[:, 0:8], st, r_st, r_w, qi)
            B_HQ, B_HI, B_HFF, B_HFB, B_HG = 0, 512, 1024, 1536, 2048

            def make_set(k):
                xnT = C.sb(f"xnT{k}", [128, NKC, 512], BF16)
                qh = C.sb(f"qh{k}", [128, 4, 512], F32)
                A = C.sb(f"hA{k}", [128, 512], F32)
                B = C.sb(f"hB{k}", [128, 512], F32)
                Cc = C.sb(f"hC{k}", [128, 512], F32)
                E1 = C.sb(f"hE1{k}", [128, 512], F32)
                E2 = C.sb(f"hE2{k}", [128, 512], F32)
                qpo = C.sb(f"qpo{k}", [128, 8, 512], BF16)
                kpo = C.sb(f"kpo{k}", [128, 8, 512], BF16)
                kppo = C.sb(f"kppo{k}", [128, 8, 512], BF16)
                dco = C.sb(f"dco{k}", [128, 8, 16], F32)
                vht = C.sb(f"vht{k}", [128, 4, 512], BF16)
                ght = C.sb(f"ght{k}", [128, 4, 512], BF16)
                r_xnT = [R() for _ in range(NKC)]
                r_qh, rA, rB, rC, rE1, rE2 = R(), R(), R(), R(), R(), R()
                r_qpo, r_kpo, r_kppo, r_dco, r_vht, r_ght = R(), R(), R(), R(), R(), R()

                def tile(i):
                    t0 = i * 512
                    head(i, xnT, r_xnT)
                    yield
                    for h in range(4):
                        ps, rp = nextpm()
                        mm_fm(ps, rp, wb, B_HQ + h * 128, 128, xnT, r_xnT, NKC)
                        S.op("act", (lambda e, ps=ps, h=h: e.activation(qh[:, h, :], ps[:], AF.Silu)), reads=[rp], writes=[r_qh])
                        yield
                    for j in range(4):
                        ps, rp = nextpm()
                        mm_tm(ps, rp, xnT, r_xnT, j, wb, B_HI, 512, NKC)
                        S.op("dve", (lambda e, ps=ps, j=j: e.tensor_copy(vht[:, j, :], ps[:])), reads=[rp], writes=[r_vht])
                        yield
                        ps, rp = nextpm()
                        mm_tm(ps, rp, xnT, r_xnT, j, wb, B_HG, 512, NKC)
                        S.op("act", (lambda e, ps=ps, j=j: e.activation(ght[:, j, :], ps[:], AF.Silu)), reads=[rp], writes=[r_ght])
                        yield
                    S.dma("pool", (lambda e: [
                        e.dma_start(out=scr["VH"][t0:t0 + 512, :].rearrange("(j p) c -> p j c", p=128), in_=vht[:]),
                        e.dma_start(out=scr["GH"][t0:t0 + 512, :].rearrange("(j p) c -> p j c", p=128), in_=ght[:])]),
                        2, f"s_vg{k}", reads=[r_vht, r_ght])
                    for d in range(2):
                        for h in range(4):
                            hd = d * 4 + h
                            ps, rp = nextpm()
                            mm_fm(ps, rp, wb, (B_HFF if d == 0 else B_HFB) + h * 128, 128, xnT, r_xnT, NKC)
                            c0s, c1s, c1ns = c0[:, hd:hd + 1], c1[:, hd:hd + 1], c1n[:, hd:hd + 1]
                            S.op("act", (lambda e, ps=ps: e.activation(A[:], ps[:], AF.Tanh, scale=0.5)), reads=[rp], writes=[rA])
                            yield
                            S.op("pool", (lambda e, c1s=c1s, c1ns=c1ns: e.tensor_scalar(B[:], A[:], c1ns, c1s, ALU.mult, ALU.add)),
                                 reads=[r_lb, rA], writes=[rB])
                            S.op("dve", (lambda e, c0s=c0s, c1s=c1s: e.tensor_scalar(A[:], A[:], c1s, c0s, ALU.mult, ALU.add)),
                                 reads=[r_lb, rB], writes=[rA])
                            S.op("act", (lambda e: e.activation(A[:], A[:], AF.Ln)), writes=[rA])
                            yield
                            S.op("dve", (lambda e: e.tensor_tensor_scan(Cc[:], rmask[:], A[:], 0.0, ALU.mult, ALU.add)),
                                 reads=[rA, r_c], writes=[rC])
                            Cv = Cc[:].rearrange("p (c t) -> p c t", t=32)
                            Av = A[:].rearrange("p (c t) -> p c t", t=32)
                            if d == 0:
                                bsrc, rb, dcol = Cc, rC, 31
                            else:
                                S.op("pool", (lambda e: e.tensor_tensor(A[:], A[:], Cc[:], ALU.subtract)), reads=[rC], writes=[rA])
                                S.op("pool", (lambda e, Av=Av, Cv=Cv: e.tensor_tensor(
                                    Av, Av, Cv[:, :, 31:32].broadcast_to([128, 16, 32]), ALU.add)), reads=[rC], writes=[rA])
                                bsrc, rb, dcol = A, rA, 0
                            yield
                            S.op("act", (lambda e, bsrc=bsrc: e.activation(E1[:], bsrc[:], AF.Exp)), reads=[rb], writes=[rE1])
                            S.op("act", (lambda e, bsrc=bsrc: e.activation(E2[:], bsrc[:], AF.Exp, scale=-1.0)), reads=[rb], writes=[rE2])
                            yield
                            S.op("pool", (lambda e, h=h, hd=hd: e.tensor_tensor(qpo[:, hd, :], qh[:, h, :], E1[:], ALU.mult)),
                                 reads=[rE1, r_qh], writes=[r_qpo])
                            S.op("dve", (lambda e, hd=hd: e.tensor_tensor(kpo[:, hd, :], E2[:], B[:], ALU.mult)), reads=[rB, rE2], writes=[r_kpo])
                            yield
                            E1v = E1[:].rearrange("p (c t) -> p c t", t=32)
                            E2v = kpo[:, hd, :].rearrange("p (c t) -> p c t", t=32)
                            S.op("dve", (lambda e, E1v=E1v, hd=hd, dcol=dcol: e.tensor_copy(dco[:, hd, :], E1v[:, :, dcol])),
                                 reads=[rE1], writes=[r_dco])
                            kv = kppo[:, hd, :].rearrange("p (c t) -> p c t", t=32)
                            S.op("pool", (lambda e, E1v=E1v, E2v=E2v, kv=kv, dcol=dcol: e.tensor_tensor(
                                kv, E2v, E1v[:, :, dcol:dcol + 1].broadcast_to([128, 16, 32]), ALU.mult)),
                                reads=[rE1, r_kpo], writes=[r_kppo])
                            yield
                    S.dma("pool", (lambda e: [
                        e.dma_start(out=scr["QP"][:, :, t0:t0 + 512].rearrange("h p t -> p h t"), in_=qpo[:]),
                        e.dma_start(out=scr["DC"][:, :, i * 16:(i + 1) * 16].rearrange("h p c -> p h c"), in_=dco[:])]),
                        2, f"s_qp{k}", reads=[r_qpo, r_dco])
                    for j in range(4):
                        for d in range(2):
                            for h in range(4):
                                hd = d * 4 + h
                                S.op("pe", (lambda e, j=j, hd=hd, h=h: e.matmul(
                                    pa[:, h * 128:(h + 1) * 128], kpo[:, hd, j * 128:(j + 1) * 128], qpo[:, hd, j * 128:(j + 1) * 128],
                                    start=True, stop=True)), reads=[r_kpo, r_qpo], writes=[r_pa])
                            msk = (mF if d == 0 else mB)
                            S.op("dve", (lambda e, j=j, d=d, msk=msk: e.tensor_tensor(
                                ato[:, j, d * 4:(d + 1) * 4, :], pa[:].rearrange("p (h t) -> p h t", h=4),
                                msk[:].rearrange("p (o t) -> p o t", o=1).broadcast_to([128, 4, 128]), ALU.mult)),
                                reads=[r_pa, r_c], writes=[r_ato])
                        for hd in range(8):
                            S.op("pe", (lambda e, j=j, hd=hd: e.transpose(ptk[:, hd * 128:(hd + 1) * 128],
                                                                           kppo[:, hd, j * 128:(j + 1) * 128], ident[:])),
                                 reads=[r_kppo, r_ident], writes=[r_ptk])
                        S.op("dve", (lambda e, j=j: e.tensor_copy(kto[:, j, :, :], ptk[:].rearrange("p (h k) -> p h k", h=8))),
                             reads=[r_ptk], writes=[r_kto])
                    S.dma("pool", (lambda e: [
                        e.dma_start(out=scr["AT"][t0:t0 + 512, :, :].rearrange("(j p) h t -> p j h t", p=128), in_=ato[:]),
                        e.dma_start(out=scr["KPT"][t0:t0 + 512, :, :].rearrange("(j p) h k -> p j h k", p=128), in_=kto[:])]),
                        2, "s_at", reads=[r_ato, r_kto])
                return tile

        tiles = [make_set(0), make_set(1)]
        _roll(lambda i: tiles[i % 2](i), ntiles, 12 if part == "a" else 30)
        S.barrier()
        S.emit()
    return S
```

```python
import numpy as np
from contextlib import ExitStack
import concourse.bass as bass
import concourse.mybir as mybir
from concourse.bass_utils import run_bass_kernel_spmd

F32 = mybir.dt.float32
BF16 = mybir.dt.bfloat16
AF = mybir.ActivationFunctionType
ALU = mybir.AluOpType
AX = mybir.AxisListType

D = 1024
DFF = 2816
EPS = 1e-6
NFC = DFF // 128
NKC = D // 128


class Res:
    __slots__ = ("name", "w", "r")

    def __init__(self, name):
        self.name = name
        self.w = None
        self.r = {}


class Ev:
    __slots__ = ("kind", "key", "op", "count")

    def __init__(self, kind, key, op=None, count=0):
        self.kind = kind
        self.key = key
        self.op = op
        self.count = count


class Op:
    __slots__ = ("fn", "deps", "sig", "count", "dma_key", "dma_n", "ev", "inc")

    def __init__(self, fn, deps):
        self.fn = fn
        self.deps = deps
        self.sig = False
        self.count = 0
        self.dma_key = None
        self.dma_n = 0
        self.ev = None


ENGS = ("sp", "act", "dve", "pool", "pe")
FUSE_WAITS = True


class Sched:
    DMA_SEMS = {}
    DMA_CNT = {}

    def __init__(self, nc, tag):
        self.nc = nc
        self.tag = tag
        self.ops = {e: [] for e in ENGS}
        self.dma_cnt = Sched.DMA_CNT
        self.nres = 0
        self.keymap = {}

    def res(self, name=None):
        self.nres += 1
        return Res(name or f"r{self.nres}")

    def _deps(self, eng, reads, writes):
        deps = []
        for r in reads:
            if r.w is not None:
                deps.append(r.w)
        for w in writes:
            if w.w is not None:
                deps.append(w.w)
            deps.extend(w.r.values())
        out = []
        seen = set()
        for d in deps:
            if id(d) in seen:
                continue
            seen.add(id(d))
            if d.kind == "e" and d.key == "pe" and eng == "pe":
                continue
            out.append(d)
        return out

    def op(self, eng, fn, reads=(), writes=()):
        o = Op(fn, self._deps(eng, reads, writes))
        ev = Ev("e", eng, op=o)
        o.ev = ev
        self.ops[eng].append(o)
        for r in reads:
            r.r[("e", eng)] = ev
        for w in writes:
            w.w = ev
            w.r = {}
        return o

    def dma(self, queue, fn, n, key, reads=(), writes=(), inc=16):
        pre = "p" if queue == "pool" else "k"
        mk = (pre, key)
        if mk not in self.keymap:
            self.keymap[mk] = f"{pre}{sum(1 for q_ in self.keymap if q_[0] == pre)}"
        key = self.keymap[mk]
        o = Op(fn, self._deps(queue, reads, writes))
        o.dma_key = key
        o.dma_n = n
        o.inc = inc
        c = self.dma_cnt.get(key, 0) + n * inc
        self.dma_cnt[key] = c
        ev = Ev("d", key, op=o, count=c)
        o.ev = ev
        self.ops[queue].append(o)
        for r in reads:
            r.r[("d", key)] = ev
        for w in writes:
            w.w = ev
            w.r = {}
        return o

    def barrier(self):
        evs = []
        for e in ENGS:
            for o in reversed(self.ops[e]):
                if o.dma_key is None and o.fn is not None:
                    evs.append(o.ev)
                    break
        lastd = {}
        for e in ENGS:
            for o in self.ops[e]:
                if o.dma_key is not None:
                    lastd[o.dma_key] = o.ev
        evs.extend(lastd.values())
        for e in ENGS:
            deps = [d for d in evs if not (d.kind == "e" and d.key == e)]
            o = Op(None, deps)
            o.ev = Ev("e", e, op=o)
            self.ops[e].append(o)

    def emit(self):
        nc = self.nc
        for e in ENGS:
            for o in self.ops[e]:
                for d in o.deps:
                    if d.kind == "e":
                        d.op.sig = True
        sems = {}
        for e in ENGS:
            c = 0
            for o in self.ops[e]:
                if o.dma_key is None and o.sig:
                    assert o.fn is not None
                    c += 1
                    o.count = c
            sems[("e", e)] = nc.alloc_semaphore(f"{self.tag}_e_{e}")
        for k in self.dma_cnt:
            if k not in Sched.DMA_SEMS:
                Sched.DMA_SEMS[k] = nc.alloc_semaphore(f"d_{k}")
            sems[("d", k)] = Sched.DMA_SEMS[k]
        self.sems = sems

        def run(eng_name, eng):
            waited = {}
            for o in self.ops[eng_name]:
                need = {}
                for d in o.deps:
                    k = (d.kind, d.key)
                    val = d.op.count if d.kind == "e" else d.count
                    assert val > 0, (eng_name, d.kind, d.key)
                    if waited.get(k, 0) >= val:
                        continue
                    waited[k] = val
                    need[k] = max(need.get(k, 0), val)
                need = list(need.items())
                fuse = None
                if FUSE_WAITS and need and o.fn is not None and o.dma_key is None:
                    fuse = need.pop()
                for k, val in need:
                    eng.wait_ge(sems[k], val)
                if o.fn is None:
                    continue
                ins = o.fn(eng)
                if fuse is not None:
                    ins._wait_ge(sems[fuse[0]], fuse[1])
                if o.dma_key is not None:
                    assert len(ins) == o.dma_n
                    for i in ins:
                        i.then_inc(sems[("d", o.dma_key)], o.inc)
                elif o.sig:
                    ins.then_inc(sems[("e", eng_name)], 1)

        with nc.Block() as block:
            @block.sync
            def _(e):
                run("sp", e)

            @block.scalar
            def _(e):
                run("act", e)

            @block.vector
            def _(e):
                run("dve", e)

            @block.gpsimd
            def _(e):
                run("pool", e)

            @block.tensor
            def _(e):
                run("pe", e)

    def release(self):
        for s in self.sems.values():
            self.nc.release_semaphore(s)


def load_weight_bf16(S, nc, w_dram, w_sb, rows_chunks, cols, gain_sb, stage, stage_res, w_res, qi=[0]):
    CW = 512
    engs = ("act", "dve", "act", "dve", "pool")
    for kc in range(rows_chunks):
        for c0 in range(0, cols, CW):
            cw = min(CW, cols - c0)
            n = qi[0]
            i = n % len(stage)
            qi[0] += 1
            st, sr = stage[i], stage_res[i]
            src = w_dram[kc * 128:(kc + 1) * 128, c0:c0 + cw]
            S.dma("sp" if n % 2 == 0 else "act", (lambda e, st=st, src=src, cw=cw: [e.dma_start(out=st[:, 0:cw], in_=src)]),
                  1, f"wst{i}", writes=[sr])
            dst = w_sb[:, kc, c0:c0 + cw]
            eng = engs[n % len(engs)]
            wr = Res("wchunk")
            if gain_sb is not None:
                g = gain_sb[:, kc:kc + 1]
                if eng == "act":
                    fn = (lambda e, dst=dst, st=st, cw=cw, g=g: e.activation(dst, st[:, 0:cw], AF.Copy, scale=g))
                elif eng == "dve":
                    fn = (lambda e, dst=dst, st=st, cw=cw, g=g: e.tensor_scalar(dst, st[:, 0:cw], g, None, ALU.mult))
                else:
                    fn = (lambda e, dst=dst, st=st, cw=cw, g=g: e.tensor_scalar(dst, st[:, 0:cw], g, 0.0, ALU.mult, ALU.add))
            else:
                if eng == "act":
                    fn = (lambda e, dst=dst, st=st, cw=cw: e.copy(dst, st[:, 0:cw]))
                else:
                    fn = (lambda e, dst=dst, st=st, cw=cw: e.tensor_copy(dst, st[:, 0:cw]))
            S.op(eng, fn, reads=[sr], writes=[wr])
            w_res.append(wr)


def ffn_phase(nc, tag, x_d, out_d, ntiles, gain_d, wg_d, wu_d, wd_d, ident_d):
    S = Sched(nc, tag)
    with ExitStack() as es:
        def sb(name, shape, dt):
            return es.enter_context(nc.sbuf_tensor(f"{tag}_{name}", shape, dt))

        def ps(name, shape, dt):
            return es.enter_context(nc.psum_tensor(f"{tag}_{name}", shape, dt))

        wg = sb("wg", [128, NKC, DFF], BF16)
        wu = sb("wu", [128, NKC, DFF], BF16)
        wd = sb("wd", [128, NFC, D], BF16)
        gain = sb("gain", [128, NKC], F32)
        ident = sb("ident", [128, 128], BF16)
        xt0 = sb("xt0", [128, 4, D], F32)
        xt1 = sb("xt1", [128, 4, D], F32)
        stg = [xt1[:, j_, h_ * 512:(h_ + 1) * 512] for j_ in range(4) for h_ in range(2)]
        st0 = stg[0]
        xn4 = sb("xn4", [128, 4, D], BF16)
        xnT = sb("xnT", [128, NKC, 512], BF16)
        actb = sb("act", [128, NFC, 512], BF16)
        sg = sb("sg", [128, 2, 512], BF16)
        stat = sb("stat", [128, 16], F32)
        junk = sb("junk", [128, D], BF16)
        pg0 = ps("pg0", [128, 512], F32)
        pg1 = ps("pg1", [128, 512], F32)
        pu0 = ps("pu0", [128, 512], F32)
        pu1 = ps("pu1", [128, 512], F32)
        po0 = ps("po0", [128, 512], F32)
        po1 = ps("po1", [128, 512], F32)
        pt0 = ps("pt0", [128, 1024], BF16)
        pt1 = ps("pt1", [128, 1024], BF16)
        R = S.res
        r_gain, r_ident, r_w = R("gain"), R("ident"), R("w")
        r_xt = [[R(f"xt{i}_{j}") for j in range(4)] for i in range(2)]
        r_st = [R(f"stg{i}") for i in range(8)]
        S.dma("sp", lambda e: [e.dma_start(out=gain[:], in_=gain_d)], 1, "c0", writes=[r_gain])
        S.dma("sp", lambda e: [e.dma_start(out=st0[:, 0:128], in_=ident_d)], 1, "wst0", writes=[r_st[0]])
        S.op("pool", lambda e: e.tensor_copy(ident[:], st0[:, 0:128]), reads=[r_st[0]], writes=[r_ident])
        stat_dummy = None
        qi = [1]
        for en_ in ("pool", "dve", "act"):
            S.op(en_, (lambda e, en_=en_: (e.copy(stat[:, 8:9], gain[:, 0:1]) if en_ == "act" else e.tensor_copy(stat[:, 9 if en_ == "dve" else 10:10 if en_ == "dve" else 11], gain[:, 0:1]))),
                 reads=[r_gain], writes=[R("dummy")])
        l_wg, l_wu, l_wd = [], [], []
        load_weight_bf16(S, nc, wg_d, wg, NKC, DFF, gain, stg, r_st, l_wg, qi)
        load_weight_bf16(S, nc, wu_d, wu, NKC, DFF, gain, stg, r_st, l_wu, qi)
        load_weight_bf16(S, nc, wd_d, wd, NFC, D, None, stg, r_st, l_wd, qi)
        r_wg, r_wu, r_wd = R("wg"), R("wu"), R("wd")
        S.op("pool", lambda e: e.memset(stat[:, 12:13], 0.0), reads=l_wg, writes=[r_wg])
        S.op("pool", lambda e: e.memset(stat[:, 13:14], 0.0), reads=l_wu, writes=[r_wu])
        S.op("pool", lambda e: e.memset(stat[:, 14:15], 0.0), reads=l_wd, writes=[r_wd] + r_st + r_xt[1])
        xts = [xt0, xt1]
        r_xn4 = [R(f"xn{j}") for j in range(4)]
        r_xnT, r_act = [R(f"xnT{k}") for k in range(NKC)], [R(f"act{f}") for f in range(NFC)]
        r_sg = [R("sg0"), R("sg1")]
        r_stat = [R(f"stat{j}") for j in range(4)]
        r_junk = R("junk")
        pgs, pus, pos, pts = [pg0, pg1], [pu0, pu1], [po0, po1], [pt0, pt1]
        r_pg, r_pu = [R("pg0"), R("pg1")], [R("pu0"), R("pu1")]
        r_po, r_pt = [R("po0"), R("po1")], [R("pt0"), R("pt1")]

        def load_tile(i):
            b = i % 2
            for j in range(4):
                src = x_d[i * 512 + j * 128: i * 512 + (j + 1) * 128, :]
                dst = xts[b][:, j, :]
                S.dma("sp", (lambda e, dst=dst, src=src: [e.dma_start(out=dst, in_=src)]), 1,
                      f"xt{b}_{j}", writes=[r_xt[b][j]])

        load_tile(0)
        nt_ctr = [0]
        for i in range(ntiles):
            b = i % 2
            xt = xts[b]
            if i + 1 < ntiles:
                load_tile(i + 1)
            norm_transpose4(S, xt, r_xt[b], stat, r_stat, junk, xn4, r_xn4, pts, r_pt, ident, r_ident, xnT, r_xnT)
            for f in range(NFC):
                pb = f % 2
                for kc in range(NKC):
                    S.op("pe", (lambda e, pb=pb, f=f, kc=kc: e.matmul(
                        pgs[pb][:], wg[:, kc, f * 128:(f + 1) * 128], xnT[:, kc, :],
                        start=(kc == 0), stop=(kc == NKC - 1))),
                        reads=[r_wg, r_xnT[kc]], writes=[r_pg[pb]])
                for kc in range(NKC):
                    S.op("pe", (lambda e, pb=pb, f=f, kc=kc: e.matmul(
                        pus[pb][:], wu[:, kc, f * 128:(f + 1) * 128], xnT[:, kc, :],
                        start=(kc == 0), stop=(kc == NKC - 1))),
                        reads=[r_wu, r_xnT[kc]], writes=[r_pu[pb]])
                S.op("act", (lambda e, pb=pb: e.activation(sg[:, pb, :], pgs[pb][:], AF.Silu)),
                     reads=[r_pg[pb]], writes=[r_sg[pb]])
                S.op("dve", (lambda e, pb=pb, f=f: e.tensor_tensor(actb[:, f, :], sg[:, pb, :], pus[pb][:], ALU.mult)),
                     reads=[r_sg[pb], r_pu[pb]], writes=[r_act[f]])
            for j in range(4):
                for hh in range(2):
                    pb = (j * 2 + hh) % 2
                    for f in range(NFC):
                        S.op("pe", (lambda e, pb=pb, f=f, j=j, hh=hh: e.matmul(
                            pos[pb][:], actb[:, f, j * 128:(j + 1) * 128], wd[:, f, hh * 512:(hh + 1) * 512],
                            start=(f == 0), stop=(f == NFC - 1))),
                            reads=[r_wd, r_act[f]], writes=[r_po[pb]])
                    dst = xt[:, j, hh * 512:(hh + 1) * 512]
                    S.op("dve", (lambda e, dst=dst, pb=pb: e.scalar_tensor_tensor(
                        dst, pos[pb][:], 0.5, dst, ALU.mult, ALU.add)),
                        reads=[r_po[pb], r_xt[b][j]], writes=[r_xt[b][j]])
                dstd = out_d[i * 512 + j * 128: i * 512 + (j + 1) * 128, :]
                src = xt[:, j, :]
                S.dma("pool", (lambda e, dstd=dstd, src=src: [e.dma_start(out=dstd, in_=src)]), 1,
                      f"xo{b}_{j}", reads=[r_xt[b][j]])
        S.barrier()
        S.emit()
    return S


class Ctx:
    def __init__(self, nc, tag, es):
        self.nc, self.tag, self.es = nc, tag, es

    def sb(self, name, shape, dt):
        return self.es.enter_context(self.nc.sbuf_tensor(f"{self.tag}_{name}", shape, dt))

    def ps(self, name, shape, dt=F32):
        return self.es.enter_context(self.nc.psum_tensor(f"{self.tag}_{name}", shape, dt))


def load_w(S, w_dram, w_sb, nrc, cols, gain_sb, st, r_st, r_w, qi, rows_last=128):
    for rc in range(nrc):
        for c0 in range(0, cols, 512):
            cw = min(512, cols - c0)
            i = qi[0] % 2
            qi[0] += 1
            stt, sr = st[i], r_st[i]
            src = w_dram[rc * 128:(rc + 1) * 128, c0:c0 + cw]
            S.dma("sp", (lambda e, stt=stt, src=src, cw=cw: [e.dma_start(out=stt[:, 0:cw], in_=src)]),
                  1, f"wst{i}", writes=[sr])
            dst = w_sb[:, rc, c0:c0 + cw]
            if gain_sb is not None:
                g = gain_sb[:, rc:rc + 1]
                S.op("pool", (lambda e, dst=dst, stt=stt, cw=cw, g=g:
                              e.tensor_scalar(dst, stt[:, 0:cw], g, 0.0, ALU.mult, ALU.add)),
                     reads=[sr], writes=[r_w])
            else:
                S.op("pool", (lambda e, dst=dst, stt=stt, cw=cw: e.tensor_copy(dst, stt[:, 0:cw])),
                     reads=[sr], writes=[r_w])


def norm_transpose(S, xt, r_xt_j, j, stat, r_stat, junk, r_junk, xn, r_xn, pt, r_pt, ident, r_ident,
                   xnT, r_xnT, nfeat=D):
    nkc = nfeat // 128
    xj = xt[:, j, :]
    ss = stat[:, j:j + 1]
    rs = stat[:, 4 + j:5 + j]
    S.op("act", (lambda e: e.activation(junk[:, 0:nfeat], xj, AF.Square, accum_out=ss)),
         reads=[r_xt_j], writes=[r_junk, r_stat[j]])
    S.op("act", (lambda e: e.activation(rs, ss, AF.Sqrt, bias=EPS, scale=1.0 / nfeat)),
         reads=[r_stat[j]], writes=[r_stat[j]])
    S.op("dve", (lambda e: e.reciprocal(rs, rs)), reads=[r_stat[j]], writes=[r_stat[j]])
    S.op("dve", (lambda e: e.tensor_scalar(xn[:, 0:nfeat], xj, rs, None, ALU.mult)),
         reads=[r_xt_j, r_stat[j]], writes=[r_xn])
    for kc in range(nkc):
        S.op("pe", (lambda e, kc=kc: e.transpose(pt[:, kc * 128:(kc + 1) * 128],
                                                 xn[:, kc * 128:(kc + 1) * 128], ident[:])),
             reads=[r_xn, r_ident], writes=[r_pt])
    dst = xnT[:, 0:nkc, j * 128:(j + 1) * 128]
    src = pt[:, 0:nkc * 128].rearrange("p (k t) -> p k t", k=nkc)
    S.op("act", (lambda e: e.copy(dst, src)), reads=[r_pt], writes=r_xnT)


def norm_transpose4(S, xt, r_xt, stat, r_stat, junk, xn4, r_xn, pts, r_pts, ident, r_ident, xnT, r_xnT, nfeat=D):
    nkc = nfeat // 128
    for j in range(4):
        S.op("act", (lambda e, j=j: e.activation(xn4[:, j, 0:nfeat], xt[:, j, :], AF.Square, accum_out=stat[:, j:j + 1])),
             reads=[r_xt[j]], writes=[r_xn[j], r_stat[j]])
    for j in range(4):
        S.op("act", (lambda e, j=j: e.activation(stat[:, 4 + j:5 + j], stat[:, j:j + 1], AF.Sqrt, bias=EPS, scale=1.0 / nfeat)),
             reads=[r_stat[j]], writes=[r_stat[j]])
    for j in range(4):
        S.op("dve", (lambda e, j=j: e.reciprocal(stat[:, 4 + j:5 + j], stat[:, 4 + j:5 + j])), reads=[r_stat[j]], writes=[r_stat[j]])
    for j in range(4):
        S.op("dve" if j % 2 == 0 else "pool",
             (lambda e, j=j: e.tensor_scalar(xn4[:, j, 0:nfeat], xt[:, j, :], stat[:, 4 + j:5 + j], 0.0, ALU.mult, ALU.add)),
             reads=[r_xt[j], r_stat[j]], writes=[r_xn[j]])
    for j in range(4):
        pt, r_pt = pts[j % 2], r_pts[j % 2]
        for kc in range(nkc):
            S.op("pe", (lambda e, kc=kc, j=j, pt=pt: e.transpose(pt[:, kc * 128:(kc + 1) * 128],
                                                               xn4[:, j, kc * 128:(kc + 1) * 128], ident[:])),
                 reads=[r_xn[j], r_ident], writes=[r_pt])
        dst = xnT[:, 0:nkc, j * 128:(j + 1) * 128]
        src = pt[:, 0:nkc * 128].rearrange("p (k t) -> p k t", k=nkc)
        S.op("act", (lambda e, dst=dst, src=src: e.copy(dst, src)), reads=[r_pt], writes=r_xnT)


C_CQ, C_CKV, C_KR, C_HQ, C_HI, C_HFF, C_HFB, C_HG, C_KRS = 0, 384, 640, 672, 1184, 1696, 2208, 2720, 3232
WIN_COLS = 3264
UQ_COLS = 768 + 256


def mixer_in_phase(nc, tag, ntok, h1_d, cst, w, scr):
    S = Sched(nc, tag)
    ntiles = ntok // 512
    with ExitStack() as es:
        C = Ctx(nc, tag, es)
        R = S.res
        win = C.sb("win", [128, NKC, WIN_COLS], BF16)
        wuq = C.sb("wuq", [128, 3, UQ_COLS], BF16)
        wuk = C.sb("wuk", [128, 2, 512], BF16)
        wuv = C.sb("wuv", [128, 2, 512], BF16)
        gains = C.sb("gains", [128, 16], F32)
        lbt = C.sb("lbt", [128, 16], F32)
        lb = C.sb("lb", [128, 8], F32)
        oml = C.sb("oml", [128, 8], F32)
        ident = C.sb("ident", [128, 128], BF16)
        ones = C.sb("ones", [128, 128], BF16)
        rmask = C.sb("rmask", [128, 512], F32)
        mF = C.sb("mF", [128, 128], F32)
        mB = C.sb("mB", [128, 128], F32)
        ht = C.sb("ht", [128, 4, D], F32)
        xn = C.sb("xn", [128, D], BF16)
        junk = C.sb("junk", [128, D], BF16)
        stat = C.sb("stat", [128, 16], F32)
        xnT = C.sb("xnT", [128, NKC, 512], BF16)
        cqT = C.sb("cqT", [128, 3, 512], BF16)
        ckvT = C.sb("ckvT", [128, 2, 512], BF16)
        sqq = C.sb("sqq", [128, 2, 512], BF16)
        sqkv = C.sb("sqkv", [128, 2, 512], BF16)
        rsq = C.sb("rsq", [128, 512], F32)
        rskv = C.sb("rskv", [128, 512], F32)
        rstok = C.sb("rstok", [128, 8], F32)
        tct = C.sb("tct", [128, 512], F32)
        tst = C.sb("tst", [128, 512], F32)
        t1a = C.sb("t1a", [128, 512], F32)
        t2a = C.sb("t2a", [128, 512], F32)
        t1 = [t1a, t1a]
        t2 = [t2a, t2a]
        qout = C.sb("qout", [128, 8, 512], BF16)
        knT = C.sb("knT", [128, 4, 512], BF16)
        krp = C.sb("krp", [128, 512], BF16)
        vt = C.sb("vt", [128, 4, 512], BF16)
        qh = C.sb("qh", [128, 4, 512], F32)
        hA = [C.sb(f"hA{i}", [128, 512], F32) for i in range(2)]
        hB = [C.sb(f"hB{i}", [128, 512], F32) for i in range(2)]
        hC = [C.sb(f"hC{i}", [128, 512], F32) for i in range(2)]
        hE1 = [C.sb(f"hE1{i}", [128, 512], F32) for i in range(2)]
        hE2 = [C.sb(f"hE2{i}", [128, 512], F32) for i in range(2)]
        st = [hA[0], hB[0]]
        qpo = C.sb("qpo", [128, 8, 512], BF16)
        kpo = C.sb("kpo", [128, 8, 512], BF16)
        kppo = C.sb("kppo", [128, 8, 512], BF16)
        dco = C.sb("dco", [128, 8, 16], F32)
        vht = C.sb("vht", [128, 4, 512], BF16)
        ght = C.sb("ght", [128, 4, 512], BF16)
        ato = C.sb("ato", [128, 4, 8, 128], BF16)
        kto = C.sb("kto", [128, 4, 8, 128], BF16)
        pt = C.ps("pt", [128, 1024], BF16)
        pm = [C.ps(f"pm{i}", [128, 512]) for i in range(4)]
        pn = C.ps("pn", [128, 512])
        pa = C.ps("pa", [128, 512])
        ptk = C.ps("ptk", [128, 1024], BF16)

        r_c = R("consts")
        r_ident, r_ones = R("ident"), R("ones")

        def cdma(dst, src):
            S.dma("sp", (lambda e: [e.dma_start(out=dst, in_=src)]), 1, "c", writes=[r_c])
        cdma(gains[:, 0:8], cst["mix_norm"])
        cdma(gains[:, 8:11], cst["q_norm"])
        cdma(gains[:, 11:13], cst["kv_norm"])
        cdma(lbt[:], cst["hg_lb"])
        cdma(rmask[:], cst["rmask"])
        cdma(mF[:], cst["maskF"])
        cdma(mB[:], cst["maskB"])
        cdma(ht[:, 0, 0:128], cst["ident"])
        S.op("pool", lambda e: e.tensor_copy(ident[:], ht[:, 0, 0:128]), reads=[r_c], writes=[r_ident])
        S.op("pool", lambda e: e.memset(ones[:], 1.0), writes=[r_ones])
        lv = lbt[:].rearrange("p (d l h) -> p d l h", d=2, l=2)
        lb3 = lb[:].rearrange("p (d h) -> p d h", d=2)
        oml3 = oml[:].rearrange("p (d h) -> p d h", d=2)
        r_lb = R("lb")
        S.op("dve", lambda e: e.tensor_tensor(lb3, lv[:, :, 0, :], lv[:, :, 1, :], ALU.subtract), reads=[r_c], writes=[r_lb])
        S.op("act", lambda e: e.activation(oml[:], lb[:], AF.Sigmoid, scale=-1.0), reads=[r_lb], writes=[R("oml")])
        S.op("act", lambda e: e.activation(lb[:], lb[:], AF.Sigmoid), reads=[r_lb], writes=[r_lb])
        r_w = R("w")
        r_hA, r_hB = [R("hA0"), R("hA1")], [R("hB0"), R("hB1")]
        r_st = [r_hA[0], r_hB[0]]
        S.op("pool", lambda e: e.tensor_copy(stat[:, 15:16], gains[:, 0:1]), reads=[r_c], writes=[R("d")])
        qi = [0]
        load_w(S, w["w_in"], win, NKC, WIN_COLS, gains[:, 0:8], st, r_st, r_w, qi)
        load_w(S, w["w_uq"], wuq, 3, UQ_COLS, gains[:, 8:11], st, r_st, r_w, qi)
        load_w(S, w["w_uk"], wuk, 2, 512, gains[:, 11:13], st, r_st, r_w, qi)
        load_w(S, w["w_uv"], wuv, 2, 512, gains[:, 11:13], st, r_st, r_w, qi)

        r_ht = [R(f"ht{j}") for j in range(4)]
        r_stat = [R(f"stat{j}") for j in range(4)]
        r_junk, r_xn, r_pt = R("junk"), R("xn"), R("pt")
        r_xnT = [R(f"xnT{k}") for k in range(NKC)]
        r_pm = [R(f"pm{i}") for i in range(4)]
        r_pn, r_pa, r_ptk = R("pn"), R("pa"), R("ptk")
        r_cqT, r_ckvT, r_sqq, r_sqkv = R("cqT"), R("ckvT"), [R("sqq0"), R("sqq1")], R("sqkv")
        r_rsq, r_rskv, r_rstok = R("rsq"), R("rskv"), R("rstok")
        r_tab = R("tab")
        r_t1a, r_t2a = R("t1a"), R("t2a")
        r_t1, r_t2 = [r_t1a, r_t1a], [r_t2a, r_t2a]
        r_qout, r_knT, r_krp, r_vt, r_qh = R("qout"), R("knT"), R("krp"), R("vt"), R("qh")
        r_hC = [R("hC0"), R("hC1")]
        r_hE1, r_hE2 = [R("hE10"), R("hE11")], [R("hE20"), R("hE21")]
        r_qpo, r_kpo, r_kppo, r_dco = R("qpo"), R("kpo"), R("kppo"), R("dco")
        r_vht, r_ght, r_ato, r_kto = R("vht"), R("ght"), R("ato"), R("kto")
        S.op("pool", lambda e: e.memset(tct[:], 1.0), writes=[r_tab])
        S.op("pool", lambda e: e.memset(tst[:], 0.0), writes=[r_tab])
        pmi = [0]

        def nextpm():
            i = pmi[0] % 4
            pmi[0] += 1
            return pm[i], r_pm[i]

        def mm_fm(ps, r_ps, wsb, c0, m, xT, r_x, nkc, out_p0=0):
            for kc in range(nkc):
                S.op("pe", (lambda e, kc=kc: e.matmul(ps[out_p0:out_p0 + m, :], wsb[:, kc, c0:c0 + m], xT[:, kc, :],
                                                      start=(kc == 0), stop=(kc == nkc - 1))),
                     reads=[r_w] + r_x, writes=[r_ps])

        def mm_tm(ps, r_ps, xT, r_x, j, wsb, c0, n, nkc):
            for kc in range(nkc):
                S.op("pe", (lambda e, kc=kc: e.matmul(ps[:, 0:n], xT[:, kc, j * 128:(j + 1) * 128], wsb[:, kc, c0:c0 + n],
                                                      start=(kc == 0), stop=(kc == nkc - 1))),
                     reads=[r_w] + r_x, writes=[r_ps])

        for i in range(ntiles):
            t0 = i * 512
            S.dma("sp", (lambda e, t0=t0: [e.dma_start(out=ht[:, j, :], in_=h1_d[t0 + j * 128:t0 + (j + 1) * 128, :])
                                          for j in range(4)]), 4, "ht", writes=r_ht)
            S.dma("sp", (lambda e, t0=t0: [e.dma_start(out=tct[64:96, :], in_=cst["rope_c"][:, t0:t0 + 512]),
                                          e.dma_start(out=tst[64:96, :], in_=cst["rope_s"][:, t0:t0 + 512])]),
                  2, "tab", writes=[r_tab])
            for j in range(4):
                norm_transpose(S, ht, r_ht[j], j, stat, r_stat, junk, r_junk, xn, r_xn, pt, r_pt, ident, r_ident,
                               xnT, r_xnT)
            for c in range(3):
                ps, rp = nextpm()
                mm_fm(ps, rp, win, C_CQ + c * 128, 128, xnT, r_xnT, NKC)
                S.op("act", (lambda e, ps=ps, c=c: e.copy(cqT[:, c, :], ps[:])), reads=[rp], writes=[r_cqT])
                S.op("act", (lambda e, ps=ps, c=c: e.activation(sqq[:, c % 2, :], ps[:], AF.Square)),
                     reads=[rp], writes=[r_sqq[c % 2]])
                S.op("pe", (lambda e, c=c: e.matmul(pn[:], ones[:], sqq[:, c % 2, :], start=(c == 0), stop=(c == 2))),
                     reads=[r_ones, r_sqq[c % 2]], writes=[r_pn])
            S.op("act", lambda e: e.activation(rsq[:], pn[:], AF.Sqrt, bias=EPS, scale=1.0 / 384), reads=[r_pn], writes=[r_rsq])
            S.op("dve", lambda e: e.reciprocal(rsq[:], rsq[:]), reads=[r_rsq], writes=[r_rsq])
            for c in range(2):
                ps, rp = nextpm()
                mm_fm(ps, rp, win, C_CKV + c * 128, 128, xnT, r_xnT, NKC)
                S.op("act", (lambda e, ps=ps, c=c: e.copy(ckvT[:, c, :], ps[:])), reads=[rp], writes=[r_ckvT])
                S.op("act", (lambda e, ps=ps, c=c: e.activation(sqkv[:, c, :], ps[:], AF.Square)),
                     reads=[rp], writes=[r_sqkv])
            for c in range(2):
                S.op("pe", (lambda e, c=c: e.matmul(pn[:], ones[:], sqkv[:, c, :], start=(c == 0), stop=(c == 1))),
                     reads=[r_ones, r_sqkv], writes=[r_pn])
            S.op("act", lambda e: e.activation(rskv[:], pn[:], AF.Sqrt, bias=EPS, scale=1.0 / 256), reads=[r_pn], writes=[r_rskv])
            S.op("dve", lambda e: e.reciprocal(rskv[:], rskv[:]), reads=[r_rskv], writes=[r_rskv])
            for j in range(4):
                for c in range(2):
                    S.op("pe", (lambda e, j=j, c=c: e.matmul(pa[:, j:j + 1], sqkv[:, c, j * 128:(j + 1) * 128], ones[:, 0:1],
                                                             start=(c == 0), stop=(c == 1))),
                         reads=[r_ones, r_sqkv], writes=[r_pa])
            S.op("act", lambda e: e.activation(rstok[:, 0:4], pa[:, 0:4], AF.Sqrt, bias=EPS, scale=1.0 / 256),
                 reads=[r_pa], writes=[r_rstok])
            S.op("dve", lambda e: e.reciprocal(rstok[:, 0:4], rstok[:, 0:4]), reads=[r_rstok], writes=[r_rstok])
            ps, rp = nextpm()
            mm_fm(ps, rp, win, C_KR, 32, xnT, r_xnT, NKC, out_p0=64)
            ps2, rp2 = nextpm()
            mm_fm(ps2, rp2, win, C_KRS, 32, xnT, r_xnT, NKC, out_p0=64)
            S.op("dve", (lambda e, ps=ps: e.tensor_tensor(t1[0][64:96, :], ps[64:96, :], tct[64:96, :], ALU.mult)),
                 reads=[rp, r_tab], writes=[r_t1[0]])
            S.op("dve", (lambda e, ps2=ps2: e.tensor_tensor(t2[0][64:96, :], ps2[64:96, :], tst[64:96, :], ALU.mult)),
                 reads=[rp2, r_tab], writes=[r_t2[0]])
            S.op("pool", lambda e: e.tensor_tensor(krp[64:96, :], t1[0][64:96, :], t2[0][64:96, :], ALU.add),
                 reads=[r_t1[0], r_t2[0]], writes=[r_krp])
            S.dma("pool", (lambda e, t0=t0: [e.dma_start(out=scr["KTr"][:, t0:t0 + 512], in_=krp[64:96, :])]), 1, "s_krp",
                  reads=[r_krp])
            for h in range(8):
                b = h % 2
                ps, rp = nextpm()
                mm_fm(ps, rp, wuq, h * 96, 96, cqT, [r_cqT], 3)
                ps2, rp2 = nextpm()
                mm_fm(ps2, rp2, wuq, 768 + h * 32, 32, cqT, [r_cqT], 3, out_p0=64)
                S.op("dve", (lambda e, ps=ps, b=b: e.tensor_tensor(t1[b][0:96, :], ps[0:96, :], tct[0:96, :], ALU.mult)),
                     reads=[rp, r_tab], writes=[r_t1[b]])
                S.op("dve", (lambda e, ps2=ps2, b=b: e.tensor_tensor(t2[b][64:96, :], ps2[64:96, :], tst[64:96, :], ALU.mult)),
                     reads=[rp2, r_tab], writes=[r_t2[b]])
                S.op("pool", (lambda e, b=b: e.tensor_tensor(t1[b][64:96, :], t1[b][64:96, :], t2[b][64:96, :], ALU.add)),
                     reads=[r_t2[b]], writes=[r_t1[b]])
                S.op("pool", (lambda e, b=b, h=h: e.tensor_tensor(qout[0:96, h, :], t1[b][0:96, :], rsq[0:96, :], ALU.mult)),
                     reads=[r_t1[b], r_rsq], writes=[r_qout])
            S.dma("pool", (lambda e, t0=t0: [e.dma_start(out=scr["QT"][h, :, t0:t0 + 512], in_=qout[0:96, h, :])
                                            for h in range(8)]), 8, "s_q", reads=[r_qout])
            for a in range(4):
                ps, rp = nextpm()
                mm_fm(ps, rp, wuk, a * 128, 128, ckvT, [r_ckvT], 2)
                S.op("dve", (lambda e, ps=ps, a=a: e.tensor_tensor(knT[:, a, :], ps[:], rskv[:], ALU.mult)),
                     reads=[rp, r_rskv], writes=[r_knT])
            S.dma("pool", (lambda e, t0=t0: [e.dma_start(out=scr["KTn"][a * 128:(a + 1) * 128, t0:t0 + 512], in_=knT[:, a, :])
                                            for a in range(4)]), 4, "s_kn", reads=[r_knT])
            for j in range(4):
                ps, rp = nextpm()
                mm_tm(ps, rp, ckvT, [r_ckvT], j, wuv, 0, 512, 2)
                S.op("act", (lambda e, ps=ps, j=j: e.activation(vt[:, j, :], ps[:], AF.Copy, scale=rstok[:, j:j + 1])),
                     reads=[rp, r_rstok], writes=[r_vt])
            S.dma("pool", (lambda e, t0=t0: [e.dma_start(
                out=scr["VA"][:, t0 + j * 128:t0 + (j + 1) * 128, :].rearrange("h t d -> t h d"),
                in_=vt[:, j, :].rearrange("t (h d) -> t h d", h=8)) for j in range(4)]), 4, "s_v", reads=[r_vt])
            for h in range(4):
                ps, rp = nextpm()
                mm_fm(ps, rp, win, C_HQ + h * 128, 128, xnT, r_xnT, NKC)
                S.op("act", (lambda e, ps=ps, h=h: e.activation(qh[:, h, :], ps[:], AF.Silu)), reads=[rp], writes=[r_qh])
            for j in range(4):
                ps, rp = nextpm()
                mm_tm(ps, rp, xnT, r_xnT, j, win, C_HI, 512, NKC)
                S.op("act", (lambda e, ps=ps, j=j: e.copy(vht[:, j, :], ps[:])), reads=[rp], writes=[r_vht])
                ps, rp = nextpm()
                mm_tm(ps, rp, xnT, r_xnT, j, win, C_HG, 512, NKC)
                S.op("act", (lambda e, ps=ps, j=j: e.activation(ght[:, j, :], ps[:], AF.Silu)), reads=[rp], writes=[r_ght])
            S.dma("pool", (lambda e, t0=t0: [
                e.dma_start(out=scr["VH"][t0:t0 + 512, :].rearrange("(j p) c -> p j c", p=128), in_=vht[:]),
                e.dma_start(out=scr["GH"][t0:t0 + 512, :].rearrange("(j p) c -> p j c", p=128), in_=ght[:])]),
                2, "s_vg", reads=[r_vht, r_ght])
            for d in range(2):
                for h in range(4):
                    hd = d * 4 + h
                    b = hd % 2
                    A, B, Cc, E1, E2 = hA[b], hB[b], hC[b], hE1[b], hE2[b]
                    rA, rB, rC, rE1, rE2 = r_hA[b], r_hB[b], r_hC[b], r_hE1[b], r_hE2[b]
                    ps, rp = nextpm()
                    mm_fm(ps, rp, win, (C_HFF if d == 0 else C_HFB) + h * 128, 128, xnT, r_xnT, NKC)
                    lbs, omls = lb[:, hd:hd + 1], oml[:, hd:hd + 1]
                    S.op("act", (lambda e, ps=ps, A=A: e.activation(A[:], ps[:], AF.Sigmoid)), reads=[rp], writes=[rA])
                    S.op("act", (lambda e, ps=ps, B=B: e.activation(B[:], ps[:], AF.Sigmoid, scale=-1.0)), reads=[rp], writes=[rB])
                    S.op("pool", (lambda e, B=B, omls=omls: e.tensor_scalar(B[:], B[:], omls, 0.0, ALU.mult, ALU.add)),
                         reads=[r_lb], writes=[rB])
                    S.op("dve", (lambda e, A=A, omls=omls, lbs=lbs: e.tensor_scalar(A[:], A[:], omls, lbs, ALU.mult, ALU.add)),
                         reads=[r_lb], writes=[rA])
                    S.op("act", (lambda e, A=A: e.activation(A[:], A[:], AF.Ln)), writes=[rA])
                    S.op("dve", (lambda e, A=A, Cc=Cc: e.tensor_tensor_scan(Cc[:], rmask[:], A[:], 0.0, ALU.mult, ALU.add)),
                         reads=[rA, r_c], writes=[rC])
                    Cv = Cc[:].rearrange("p (c t) -> p c t", t=32)
                    Av = A[:].rearrange("p (c t) -> p c t", t=32)
                    if d == 0:
                        bsrc, rb = Cc, rC
                        dcol = 31
                    else:
                        S.op("pool", (lambda e, A=A, Cc=Cc: e.tensor_tensor(A[:], A[:], Cc[:], ALU.subtract)),
                             reads=[rC], writes=[rA])
                        S.op("pool", (lambda e, Av=Av, Cv=Cv: e.tensor_tensor(Av, Av, Cv[:, :, 31:32].broadcast_to([128, 16, 32]), ALU.add)),
                             reads=[rC], writes=[rA])
                        bsrc, rb = A, rA
                        dcol = 0
                    S.op("act", (lambda e, E1=E1, bsrc=bsrc: e.activation(E1[:], bsrc[:], AF.Exp)), reads=[rb], writes=[rE1])
                    S.op("act", (lambda e, E2=E2, bsrc=bsrc: e.activation(E2[:], bsrc[:], AF.Exp, scale=-1.0)), reads=[rb], writes=[rE2])
                    S.op("pool", (lambda e, E1=E1, h=h, hd=hd: e.tensor_tensor(qpo[:, hd, :], qh[:, h, :], E1[:], ALU.mult)),
                         reads=[rE1, r_qh], writes=[r_qpo])
                    S.op("dve", (lambda e, E2=E2, B=B: e.tensor_tensor(E2[:], E2[:], B[:], ALU.mult)), reads=[rB], writes=[rE2])
                    S.op("act", (lambda e, E2=E2, hd=hd: e.copy(kpo[:, hd, :], E2[:])), reads=[rE2], writes=[r_kpo])
                    E1v = E1[:].rearrange("p (c t) -> p c t", t=32)
                    E2v = E2[:].rearrange("p (c t) -> p c t", t=32)
                    S.op("dve", (lambda e, E1v=E1v, hd=hd, dcol=dcol: e.tensor_copy(dco[:, hd, :], E1v[:, :, dcol])),
                         reads=[rE1], writes=[r_dco])
                    kv = kppo[:, hd, :].rearrange("p (c t) -> p c t", t=32)
                    S.op("pool", (lambda e, E1v=E1v, E2v=E2v, kv=kv, dcol=dcol: e.tensor_tensor(
                        kv, E2v, E1v[:, :, dcol:dcol + 1].broadcast_to([128, 16, 32]), ALU.mult)),
                        reads=[rE1, rE2], writes=[r_kppo])
            nch = ntok // 32
            S.dma("pool", (lambda e, t0=t0, i=i: [
                e.dma_start(out=scr["QP"][:, :, t0:t0 + 512].rearrange("h p t -> p h t"), in_=qpo[:]),
                e.dma_start(out=scr["DC"][:, :, i * 16:(i + 1) * 16].rearrange("h p c -> p h c"), in_=dco[:])]),
                2, "s_qp", reads=[r_qpo, r_dco])
            for j in range(4):
                for d in range(2):
                    for h in range(4):
                        hd = d * 4 + h
                        S.op("pe", (lambda e, j=j, hd=hd, h=h: e.matmul(
                            pa[:, h * 128:(h + 1) * 128], kpo[:, hd, j * 128:(j + 1) * 128], qpo[:, hd, j * 128:(j + 1) * 128],
                            start=True, stop=True)), reads=[r_kpo, r_qpo], writes=[r_pa])
                    msk = (mF if d == 0 else mB)
                    S.op("dve", (lambda e, j=j, d=d, msk=msk: e.tensor_tensor(
                        ato[:, j, d * 4:(d + 1) * 4, :], pa[:].rearrange("p (h t) -> p h t", h=4),
                        msk[:].rearrange("p (o t) -> p o t", o=1).broadcast_to([128, 4, 128]), ALU.mult)),
                        reads=[r_pa, r_c], writes=[r_ato])
                for hd in range(8):
                    S.op("pe", (lambda e, j=j, hd=hd: e.transpose(ptk[:, hd * 128:(hd + 1) * 128],
                                                                   kppo[:, hd, j * 128:(j + 1) * 128], ident[:])),
                         reads=[r_kppo, r_ident], writes=[r_ptk])
                S.op("act", (lambda e, j=j: e.copy(kto[:, j, :, :], ptk[:].rearrange("p (h k) -> p h k", h=8))),
                     reads=[r_ptk], writes=[r_kto])
            S.dma("pool", (lambda e, t0=t0: [
                e.dma_start(out=scr["AT"][t0:t0 + 512, :, :].rearrange("(j p) h t -> p j h t", p=128), in_=ato[:]),
                e.dma_start(out=scr["KPT"][t0:t0 + 512, :, :].rearrange("(j p) h k -> p j h k", p=128), in_=kto[:])]),
                2, "s_at", reads=[r_ato, r_kto])
        S.barrier()
        S.emit()
    return S


def attn_phase(nc, tag, seqs, scr, xchg=None):
    S = Sched(nc, tag)
    SKMAX = max(sum(p[3] for p in sq["kp"]) for sq in seqs)
    SL = max(sq["nq"] for sq in seqs)
    scale = 96.0 ** -0.5
    with ExitStack() as es:
        C = Ctx(nc, tag, es)
        R = S.res
        kt = [C.sb(f"kt{i}", [128, SKMAX], BF16) for i in range(2)]
        vt = [C.sb(f"vt{i}", [128, SKMAX // 128, 65], BF16) for i in range(2)]
        qt = [C.sb(f"qt{i}", [128, SL], BF16) for i in range(2)]
        pT = [C.sb(f"pT{i}", [128, 1024], BF16) for i in range(3)]
        onesf = C.sb("onesf", [128, 64], F32)
        rl = C.sb("rl", [128, 512], F32)
        osb = C.sb("osb", [128, 512], F32)
        obf = [C.sb(f"obf{i}", [128, 512], BF16) for i in range(2)]
        psS = [C.ps(f"psS{i}", [128, 1024]) for i in range(2)]
        psO = [C.ps(f"psO{i}", [128, 512]) for i in range(2)]
        psB = C.ps("psB", [128, 512])
        r_kt, r_vt, r_qt = [R(), R()], [R(), R()], [R(), R()]
        r_pT, r_psS, r_psO = [R(), R(), R()], [R(), R(), R()], [R(), R()]
        r_ones, r_rl, r_osb, r_obf, r_psB = R(), R(), R(), [R(), R()], R()
        S.op("pool", lambda e: e.memset(onesf[:], 1.0), writes=[r_ones])
        for i in range(2):
            S.op("pool", (lambda e, i=i: e.memset(vt[i][:, :, 64:65], 1.0)), writes=[r_vt[i]])
        r_gath = R()
        r_g2, r_g3 = R(), R()
        r_gs = []
        if xchg is not None:
            n0 = xchg["n0"]
            r_xin = R()
            S.dma("sp", (lambda e: [e.dma_start(out=xchg["XK_in"][a_], in_=scr["KTn"][a_ * 128:(a_ + 1) * 128, 0:n0]) for a_ in range(4)]
                         + [e.dma_start(out=xchg["XR_in"][0:32, :], in_=scr["KTr"][:, 0:n0])]
                         + [e.dma_start(out=xchg["XV_in"][a_].rearrange("(hh t) d -> hh t d", hh=2), in_=scr["VA"][2 * a_:2 * a_ + 2, 0:n0, :])
                            for a_ in range(4)]), 9, "xin", writes=[r_xin])
            r_gs = [r_gath, r_g2, r_g3] + [R() for _ in range(6)]
            ccl = [(xchg["XK_in"][a_], xchg["XK_out"][a_]) for a_ in range(4)] + [(xchg["XR_in"], xchg["XR_out"])] + \
                  [(xchg["XV_in"][a_], xchg["XV_out"][a_]) for a_ in range(4)]
            for ci_, (cin, cout) in enumerate(ccl):
                S.dma("pool", (lambda e, cin=cin, cout=cout: [e.collective_compute(
                    "AllGather", ALU.bypass, replica_groups=xchg["groups"], ins=[cin.opt()], outs=[cout.opt()])]), 1, f"cc{ci_}",
                    reads=[r_xin], writes=[r_gs[ci_]], inc=1)
        heads = []
        for sq in seqs:
            for h in range(8):
                heads.append((sq, h))
        units = []
        obc = [0]
        for hi, (sq, h) in enumerate(heads):
            SK = sum(p[3] for p in sq["kp"])
            for qb in range(sq["nq"] // 512):
                for kc in range(SK // 256):
                    units.append((hi, qb, kc, SK // 256, obc[0] % 2, qb == sq["nq"] // 512 - 1))
                obc[0] += 1
        loaded = [-1]
        pending = []

        def load_head(hi):
            if hi >= len(heads) or hi <= loaded[0]:
                return
            loaded[0] = hi
            sq, h = heads[hi]
            b = hi % 2
            off = 0
            lst = []
            for (ktn, ktr, va, n) in sq["kp"]:
                lst.append((kt[b][0:64, off:off + n], ktn(h)))
                lst.append((kt[b][64:96, off:off + n], ktr))
                off += n
            dep = r_gs if sq.get("gathered") else []
            S.dma("sp", (lambda e, lst=lst: [e.dma_start(out=o, in_=i_) for o, i_ in lst]), len(lst), f"kt{b}", reads=dep, writes=[r_kt[b]])
            off = 0
            lst2 = []
            for (ktn, ktr, va, n) in sq["kp"]:
                lst2.append((vt[b][:, off // 128:(off + n) // 128, 0:64], va(h).rearrange("(c p) d -> p c d", p=128)))
                off += n
            S.dma("sp", (lambda e, lst2=lst2: [e.dma_start(out=o, in_=i_) for o, i_ in lst2]), len(lst2), f"vt{b}", reads=dep, writes=[r_vt[b]])
            q0, nq = sq["q0"], sq["nq"]
            S.dma("sp", (lambda e, b=b, h=h, q0=q0, nq=nq: [e.dma_start(out=qt[b][0:96, 0:nq], in_=scr["QT"][h, :, q0:q0 + nq])]), 1,
                  f"qt{b}", writes=[r_qt[b]])

        def qk(u):
            hi, qb, kc, nkc, ob, lastq = units[u]
            b = hi % 2
            r = u % 2
            r3 = u % 3
            for t in range(2):
                S.op("pe", (lambda e, t=t: e.matmul(psS[r][:, t * 512:(t + 1) * 512], kt[b][0:96, (2 * kc + t) * 128:(2 * kc + t + 1) * 128],
                                                    qt[b][0:96, qb * 512:(qb + 1) * 512], start=True, stop=True)),
                     reads=[r_kt[b], r_qt[b]], writes=[r_psS[r]])
            S.op("act", (lambda e: e.activation(pT[r3][:], psS[r][:], AF.Exp, scale=scale)), reads=[r_psS[r]], writes=[r_pT[r3]])

        def pv(u):
            hi, qb, kc, nkc, ob, lastq = units[u]
            b = hi % 2
            r3 = u % 3
            for t in range(2):
                S.op("pe", (lambda e, t=t: e.matmul(psO[ob][0:65, :], vt[b][:, 2 * kc + t, 0:65], pT[r3][:, t * 512:(t + 1) * 512],
                                                    start=(kc == 0 and t == 0), stop=(kc == nkc - 1 and t == 1))),
                     reads=[r_vt[b], r_pT[r3]], writes=[r_psO[ob]])
            if kc == nkc - 1:
                sq, h = heads[hi]
                S.op("dve", (lambda e: e.reciprocal(rl[64:65, :], psO[ob][64:65, :])), reads=[r_psO[ob]], writes=[r_rl])
                S.op("dve", (lambda e: e.tensor_copy(osb[0:64, :], psO[ob][0:64, :])), reads=[r_psO[ob]], writes=[r_osb])
                t0 = sq["q0"] + qb * 512

                def fin():
                    S.op("pe", (lambda e: e.matmul(psB[0:64, :], onesf[64:65, 0:64], rl[64:65, :], start=True, stop=True)),
                         reads=[r_ones, r_rl], writes=[r_psB])
                    S.op("dve", (lambda e: e.tensor_tensor(obf[ob][0:64, :], osb[0:64, :], psB[0:64, :], ALU.mult)),
                         reads=[r_osb, r_psB], writes=[r_obf[ob]])
                    S.dma("pool", (lambda e: [e.dma_start(out=scr["MIXT"][h * 64:(h + 1) * 64, t0:t0 + 512], in_=obf[ob][0:64, :])]), 1,
                          f"so{ob}", reads=[r_obf[ob]])
                pending.append([4, fin])

        load_head(0)
        load_head(1)
        n = len(units)
        qk(0)
        for u in range(n):
            if u + 1 < n:
                qk(u + 1)
            for pnd in list(pending):
                pnd[0] -= 1
                if pnd[0] <= 0:
                    pnd[1]()
                    pending.remove(pnd)
            pv(u)
            hi, qb, kc, nkc, ob_, lastq = units[u]
            if lastq and kc == nkc - 1:
                load_head(hi + 2)
        for pnd in pending:
            pnd[1]()
        S.barrier()
        S.emit()
    return S


def scan_phase(nc, tag, seq_lens, ntok, scr, xchg=None, cst=None):
    S = Sched(nc, tag)
    nch = ntok // 32
    with ExitStack() as es:
        C = Ctx(nc, tag, es)
        R = S.res
        NR = 3
        qpb = [[C.sb(f"qpb{d}{i}", [128, 4, 128], BF16) for i in range(NR)] for d in range(2)]
        atb = [[C.sb(f"atb{d}{i}", [128, 4, 128], BF16) for i in range(NR)] for d in range(2)]
        kpb = [[C.sb(f"kpb{d}{i}", [128, 4, 128], BF16) for i in range(NR)] for d in range(2)]
        vb = [[C.sb(f"vb{d}{i}", [128, 512], BF16) for i in range(NR)] for d in range(2)]
        r_ld = [[R() for i in range(NR)] for d in range(2)]
        dct = C.sb("dct", [128, 8, nch], F32)
        zer = C.sb("zer", [128, 512], BF16)
        SstAll = C.sb("SstAll", [128, 8, 129], F32)
        Sst = [SstAll[:, hd, 0:128] for hd in range(8)]
        Sbf = [C.sb(f"Sbf{hd}", [128, 128], BF16) for hd in range(8)]
        r_S = [R() for hd in range(8)]
        r_Sb = [R() for hd in range(8)]
        ot = [[C.sb(f"ot{d}{i}", [128, 512], F32) for i in range(2)] for d in range(2)]
        r_ot = [[R() for i in range(2)] for d in range(2)]
        psO = [[C.ps(f"psO{d}{i}", [128, 512]) for i in range(2)] for d in range(2)]
        r_psO = [[R() for i in range(2)] for d in range(2)]
        psU = [C.ps(f"psU{i}", [128, 512]) for i in range(4)]
        r_psU = [R() for i in range(4)]
        r_dc, r_z = R(), R()
        S.dma("sp", (lambda e: [e.dma_start(out=dct[:, hd, :], in_=scr["DC"][hd, :, :]) for hd in range(8)]), 8, "dc", writes=[r_dc])
        S.op("pool", lambda e: e.memset(zer[:], 0.0), writes=[r_z])
        ui = [0]
        offs = []
        a = 0
        for n in seq_lens:
            offs.append(a)
            a += n

        def run_seq(s0, SLs, mode, zero_init=True):
            NB = SLs // 128
            if zero_init:
                for hd in range(8):
                    S.op("pool", (lambda e, hd=hd: e.memset(Sst[hd], 0.0)), writes=[r_S[hd]])
                    S.op("pool", (lambda e, hd=hd: e.memset(Sbf[hd][:], 0.0)), writes=[r_Sb[hd]])

            def load(step):
                if step >= NB:
                    return
                for d in range(2):
                    blk = step if d == 0 else NB - 1 - step
                    t0 = s0 + blk * 128
                    i = step % NR
                    if mode == "full":
                        S.dma("sp", (lambda e, d=d, i=i, t0=t0: [
                            e.dma_start(out=qpb[d][i][:], in_=scr["QP"][d * 4:(d + 1) * 4, :, t0:t0 + 128].rearrange("h p t -> p h t")),
                            e.dma_start(out=atb[d][i][:], in_=scr["AT"][t0:t0 + 128, d * 4:(d + 1) * 4, :]),
                            e.dma_start(out=kpb[d][i][:], in_=scr["KPT"][t0:t0 + 128, d * 4:(d + 1) * 4, :]),
                            e.dma_start(out=vb[d][i][:], in_=scr["VH"][t0:t0 + 128, :])]), 4, f"ld{d}{i}", writes=[r_ld[d][i]])
                    else:
                        S.dma("sp", (lambda e, d=d, i=i, t0=t0: [
                            e.dma_start(out=kpb[d][i][:], in_=scr["KPT"][t0:t0 + 128, d * 4:(d + 1) * 4, :]),
                            e.dma_start(out=vb[d][i][:], in_=scr["VH"][t0:t0 + 128, :])]), 2, f"ld{d}{i}", writes=[r_ld[d][i]])
            load(0)
            load(1)
            for step in range(NB):
                load(step + 2)
                i = step % NR
                ob = step % 2
                if mode == "full":
                    for d in range(2):
                        S.op("pe", (lambda e, d=d, ob=ob: e.matmul(psO[d][ob][:], zer[:, 0:128], zer[:], start=True, stop=False,
                                                                   skip_group_check=True)), reads=[r_z], writes=[r_psO[d][ob]])
                        for h in range(4):
                            S.op("pe", (lambda e, d=d, ob=ob, h=h, i=i: e.matmul(
                                psO[d][ob][:, h * 128:(h + 1) * 128], atb[d][i][:, h, :], vb[d][i][:, h * 128:(h + 1) * 128],
                                start=False, stop=False, skip_group_check=True)), reads=[r_ld[d][i]], writes=[r_psO[d][ob]])
                for ci in range(4):
                    for d in range(2):
                        blk = step if d == 0 else NB - 1 - step
                        c = ci if d == 0 else 3 - ci
                        gch = (s0 + blk * 128) // 32 + c
                        for h in range(4):
                            hd = d * 4 + h
                            if mode == "full":
                                S.op("pe", (lambda e, d=d, ob=ob, h=h, i=i, c=c, hd=hd: e.matmul(
                                    psO[d][ob][32 * c:32 * c + 32, h * 128:(h + 1) * 128], qpb[d][i][:, h, 32 * c:32 * c + 32], Sbf[hd][:],
                                    start=False, stop=(ci == 3), skip_group_check=True, tile_position=(0, 32 * c))),
                                    reads=[r_ld[d][i], r_Sb[hd]], writes=[r_psO[d][ob]])
                            pu = ui[0] % 4
                            ui[0] += 1
                            S.op("pe", (lambda e, d=d, h=h, i=i, c=c, pu=pu: e.matmul(
                                psU[pu][:, 0:128], kpb[d][i][32 * c:32 * c + 32, h, :], vb[d][i][32 * c:32 * c + 32, h * 128:(h + 1) * 128],
                                start=True, stop=True, tile_position=(32 * c, 0))),
                                reads=[r_ld[d][i]], writes=[r_psU[pu]])
                            S.op("dve", (lambda e, hd=hd, pu=pu, gch=gch: e.scalar_tensor_tensor(
                                Sst[hd], Sst[hd], dct[:, hd, gch:gch + 1], psU[pu][:, 0:128], ALU.mult, ALU.add)),
                                reads=[r_psU[pu], r_dc], writes=[r_S[hd]])
                            if mode == "full":
                                S.op("act", (lambda e, hd=hd: e.copy(Sbf[hd][:], Sst[hd])), reads=[r_S[hd]], writes=[r_Sb[hd]])
                if mode == "full":
                    for d in range(2):
                        blk = step if d == 0 else NB - 1 - step
                        t0 = s0 + blk * 128
                        S.op("dve" if d == 0 else "act",
                             (lambda e, d=d, ob=ob: (e.tensor_copy(ot[d][ob][:], psO[d][ob][:]) if d == 0
                                                     else e.copy(ot[d][ob][:], psO[d][ob][:]))),
                             reads=[r_psO[d][ob]], writes=[r_ot[d][ob]])
                        S.dma("pool", (lambda e, d=d, ob=ob, t0=t0: [e.dma_start(out=scr["OF"][d, t0:t0 + 128, :], in_=ot[d][ob][:])]), 1,
                              f"so{d}{ob}", reads=[r_ot[d][ob]])

        if xchg is None:
            for s0, n in zip(offs, seq_lens):
                run_seq(s0, n, "full")
        else:
            n0 = seq_lens[0]
            G = C.sb("G", [128, 4, 8, 129], F32)
            rkm = C.sb("rkm", [128, 8], F32)
            tmp = C.sb("tmp", [128, 128], F32)
            r_G, r_rk, r_tmp, r_xs = R(), R(), R(), R()
            S.dma("sp", lambda e: [e.dma_start(out=rkm[:], in_=cst["rankmask"])], 1, "rk", writes=[r_rk])
            run_seq(offs[0], n0, "state")
            for hd in range(8):
                S.op("dve", (lambda e, hd=hd: e.tensor_reduce(SstAll[:, hd, 128:129], dct[:, hd, offs[0] // 32:(offs[0] + n0) // 32],
                                                              AX.X, ALU.mult)), reads=[r_dc], writes=[r_S[hd]])
            S.dma("pool", (lambda e: [e.dma_start(out=xchg["XS_in"].rearrange("(h p) c -> p h c", p=128), in_=SstAll[:])]), 1, "xs",
                  reads=r_S, writes=[r_xs])
            S.dma("pool", (lambda e: [e.collective_compute("AllGather", ALU.bypass, replica_groups=xchg["groups"],
                                                           ins=[xchg["XS_in"].opt()], outs=[xchg["XS_out"].opt()])]), 1, "ccs", reads=[r_xs], writes=[r_G], inc=1)
            for s0, n in list(zip(offs, seq_lens))[1:]:
                run_seq(s0, n, "full")
            S.dma("sp", (lambda e: [e.dma_start(out=G[:], in_=xchg["XS_out"].rearrange("(r h p) c -> p r h c", r=4, h=8))]), 1, "g",
                  reads=[r_G], writes=[r_G])
            for hd in range(8):
                S.op("pool", (lambda e, hd=hd: e.memset(Sst[hd], 0.0)), writes=[r_S[hd]])
                order = range(4) if hd < 4 else range(3, -1, -1)
                for i in order:
                    mcol = rkm[:, (0 if hd < 4 else 4) + i:(0 if hd < 4 else 4) + i + 1]
                    S.op("dve", (lambda e, hd=hd, i=i: e.scalar_tensor_tensor(tmp[:], Sst[hd], G[:, i, hd, 128:129], G[:, i, hd, 0:128],
                                                                            ALU.mult, ALU.add)), reads=[r_G, r_S[hd]], writes=[r_tmp])
                    S.op("dve", (lambda e, hd=hd: e.tensor_tensor(tmp[:], tmp[:], Sst[hd], ALU.subtract)), reads=[r_S[hd]], writes=[r_tmp])
                    S.op("dve", (lambda e, hd=hd, mcol=mcol: e.scalar_tensor_tensor(Sst[hd], tmp[:], mcol, Sst[hd], ALU.mult, ALU.add)),
                         reads=[r_tmp, r_rk], writes=[r_S[hd]])
                S.op("act", (lambda e, hd=hd: e.copy(Sbf[hd][:], Sst[hd])), reads=[r_S[hd]], writes=[r_Sb[hd]])
            run_seq(offs[0], n0, "full", zero_init=False)
        S.barrier()
        S.emit()
    return S


def outproj_phase(nc, tag, ntok, h1_d, h2_d, cst, w, scr):
    S = Sched(nc, tag)
    ntiles = ntok // 512
    with ExitStack() as es:
        C = Ctx(nc, tag, es)
        R = S.res
        wo = C.sb("wo", [128, 8, D], BF16)
        st = [C.sb("st0", [128, 512], F32), C.sb("st1", [128, 512], F32)]
        ident = C.sb("ident", [128, 128], BF16)
        onb = C.sb("onb", [128, 512], F32)
        ht = [C.sb(f"ht{i}", [128, 4, D], F32) for i in range(2)]
        mT = [C.sb(f"mT{i}", [128, 8, 512], BF16) for i in range(2)]
        of = [C.sb(f"of{i}", [128, 4, 512], F32) for i in range(2)]
        ob = [C.sb(f"ob{i}", [128, 4, 512], F32) for i in range(2)]
        gh = [C.sb(f"gh{i}", [128, 4, 512], BF16) for i in range(2)]
        osum4 = C.sb("osum4", [128, 4, 512], F32)
        junk = C.sb("junk", [128, 128], BF16)
        stat4 = C.sb("stat4", [128, 40], F32)
        mh4 = C.sb("mh4", [128, 4, 512], BF16)
        pt = C.ps("pt", [128, 1024], BF16)
        pt2 = C.ps("pt2", [128, 1024], BF16)
        r_pts = [R(), R()]
        po = [C.ps(f"po{i}", [128, 512]) for i in range(2)]
        r_w, r_st, r_c, r_ident = R(), [R(), R()], R(), R()
        S.dma("sp", lambda e: [e.dma_start(out=onb[:], in_=cst["hg_norm_b"]), e.dma_start(out=st[0][:, 0:128], in_=cst["ident"])],
              2, "c", writes=[r_c, r_st[0]])
        S.op("pool", lambda e: e.tensor_copy(ident[:], st[0][:, 0:128]), reads=[r_st[0]], writes=[r_ident])
        qi = [1]
        load_w(S, w["w_o"], wo, 8, D, None, st, r_st, r_w, qi)
        r_ld = [R(), R()]
        r_ht = [[R() for j in range(4)] for i in range(2)]
        r_mT = [[R() for j in range(4)] for i in range(2)]
        r_osum, r_junk, r_stat, r_mh, r_pt, r_po = R(), R(), R(), R(), R(), [R(), R()]

        def load(i):
            if i >= ntiles:
                return
            b = i % 2
            t0 = i * 512
            S.dma("sp", (lambda e: [e.dma_start(out=ht[b][:, j, :], in_=h1_d[t0 + j * 128:t0 + (j + 1) * 128, :]) for j in range(4)]),
                  4, f"ht{b}", writes=r_ht[b])
            S.dma("sp", (lambda e: [
                e.dma_start(out=mT[b][:, 0:4, :], in_=scr["MIXT"][0:512, t0:t0 + 512].rearrange("(c p) t -> p c t", p=128)),
                e.dma_start(out=of[b][:], in_=scr["OF"][0, t0:t0 + 512, :].rearrange("(j p) c -> p j c", p=128)),
                e.dma_start(out=ob[b][:], in_=scr["OF"][1, t0:t0 + 512, :].rearrange("(j p) c -> p j c", p=128)),
                e.dma_start(out=gh[b][:], in_=scr["GH"][t0:t0 + 512, :].rearrange("(j p) c -> p j c", p=128))]),
                4, f"ld{b}", writes=[r_ld[b]] + r_mT[b])
        load(0)

        def body(i):
            b = i % 2
            load(i + 1)
            t0 = i * 512
            r_os = [R() for j in range(4)]
            r_mhj = [R() for j in range(4)]
            r_stj = [R() for j in range(4)]
            r_jk = [Res("junk") for _ in range(16)]
            for j in range(4):
                S.op("dve" if j % 2 == 0 else "pool",
                     (lambda e, j=j: e.tensor_tensor(osum4[:, j, :], of[b][:, j, :], ob[b][:, j, :], ALU.add)),
                     reads=[r_ld[b], r_osum], writes=[r_os[j]])
            for j in range(4):
                for h in range(4):
                    S.op("act", (lambda e, h=h, j=j: e.activation(mh4[:, j, h * 128:(h + 1) * 128], osum4[:, j, h * 128:(h + 1) * 128], AF.Square,
                                                                  accum_out=stat4[:, j * 8 + h:j * 8 + h + 1])),
                         reads=[r_os[j], r_mh], writes=[r_mhj[j], r_stj[j]])
            for j in range(4):
                S.op("act", (lambda e, j=j: e.activation(stat4[:, j * 8 + 4:j * 8 + 8], stat4[:, j * 8:j * 8 + 4], AF.Sqrt, bias=EPS, scale=1.0 / 128)),
                     reads=[r_stj[j]], writes=[r_stj[j]])
            for j in range(4):
                S.op("dve", (lambda e, j=j: e.reciprocal(stat4[:, j * 8 + 4:j * 8 + 8], stat4[:, j * 8 + 4:j * 8 + 8])), reads=[r_stj[j]], writes=[r_stj[j]])
            for j in range(4):
                ov = osum4[:, j, :].rearrange("p (h v) -> p h v", h=4)
                S.op("dve", (lambda e, ov=ov, j=j: e.tensor_tensor(
                    ov, ov, stat4[:, j * 8 + 4:j * 8 + 8].rearrange("p (h o) -> p h o", o=1).broadcast_to([128, 4, 128]), ALU.mult)),
                    reads=[r_stj[j]], writes=[r_os[j]])
            for j in range(4):
                S.op("pool", (lambda e, j=j: e.tensor_tensor(osum4[:, j, :], osum4[:, j, :], onb[:], ALU.mult)), reads=[r_c], writes=[r_os[j]])
            for j in range(4):
                S.op("dve", (lambda e, j=j: e.tensor_tensor(mh4[:, j, :], osum4[:, j, :], gh[b][:, j, :], ALU.mult)),
                     reads=[r_os[j], r_ld[b], r_mh], writes=[r_mhj[j]])
            for j in range(4):
                ptj, r_ptj = [pt, pt2][j % 2], r_pts[j % 2]
                for c in range(4):
                    S.op("pe", (lambda e, c=c, j=j, ptj=ptj: e.transpose(ptj[:, c * 128:(c + 1) * 128], mh4[:, j, c * 128:(c + 1) * 128], ident[:])),
                         reads=[r_mhj[j], r_ident], writes=[r_ptj])
                S.op("act", (lambda e, j=j, ptj=ptj: e.copy(mT[b][:, 4:8, j * 128:(j + 1) * 128],
                                                            ptj[:, 0:512].rearrange("p (k t) -> p k t", k=4))),
                     reads=[r_ptj], writes=[r_mT[b][j]])
            S.op("pool", (lambda e: e.memset(stat4[:, 32:33], 0.0)), writes=r_os + r_mhj + [r_osum, r_mh])
            for j in range(4):
                for hh in range(2):
                    pb = (j * 2 + hh) % 2
                    for kc in range(8):
                        S.op("pe", (lambda e, j=j, hh=hh, kc=kc, pb=pb: e.matmul(
                            po[pb][:], mT[b][:, kc, j * 128:(j + 1) * 128], wo[:, kc, hh * 512:(hh + 1) * 512],
                            start=(kc == 0), stop=(kc == 7))), reads=[r_w, r_mT[b][j]], writes=[r_po[pb]])
                    dst = ht[b][:, j, hh * 512:(hh + 1) * 512]
                    S.op("dve", (lambda e, dst=dst, pb=pb: e.tensor_tensor(dst, dst, po[pb][:], ALU.add)),
                         reads=[r_po[pb]], writes=[r_ht[b][j]])
                S.dma("pool", (lambda e, j=j: [e.dma_start(out=h2_d[t0 + j * 128:t0 + (j + 1) * 128, :], in_=ht[b][:, j, :])]), 1,
                      f"so{b}{j}", reads=[r_ht[b][j]])
        for i in range(ntiles):
            body(i)
        S.barrier()
        S.emit()
    return S


def ple_phase(nc, tag, ntok, h3_d, p_d, y_d, cst, w):
    S = Sched(nc, tag)
    ntiles = ntok // 512
    with ExitStack() as es:
        C = Ctx(nc, tag, es)
        R = S.res
        wg = C.sb("wg", [128, 8, D], BF16)
        wp = C.sb("wp", [128, 2, D], BF16)
        st = [C.sb("st0", [128, 512], F32), C.sb("st1", [128, 512], F32)]
        ident = C.sb("ident", [128, 128], BF16)
        gain = C.sb("gain", [128, 8], F32)
        fnb = C.sb("fnb", [128, D], F32)
        ht = [C.sb(f"ht{i}", [128, 4, D], F32) for i in range(2)]
        ptl = [C.sb(f"ptl{i}", [128, 4, 256], F32) for i in range(2)]
        xn4 = C.sb("xn4", [128, 4, D], BF16)
        junk = C.sb("junk", [128, D], BF16)
        stat = C.sb("stat", [128, 16], F32)
        xnT = C.sb("xnT", [128, 8, 512], BF16)
        pb16 = C.sb("pb16", [128, 4, 256], BF16)
        pT = C.sb("pT", [128, 2, 512], BF16)
        gsb = [C.sb(f"gsb{i}", [128, 512], F32) for i in range(2)]
        pt = C.ps("pt", [128, 1024], BF16)
        pt2 = C.ps("pt2", [128, 1024], BF16)
        pg = [C.ps(f"pg{i}", [128, 512]) for i in range(2)]
        pp = [C.ps(f"pp{i}", [128, 512]) for i in range(2)]
        r_w, r_st, r_c, r_ident = R(), [R(), R()], R(), R()
        S.dma("sp", lambda e: [e.dma_start(out=gain[:], in_=cst["ple_norm"]), e.dma_start(out=fnb[:], in_=cst["final_norm_b"]),
                               e.dma_start(out=st[0][:, 0:128], in_=cst["ident"])], 3, "c", writes=[r_c, r_st[0]])
        S.op("pool", lambda e: e.tensor_copy(ident[:], st[0][:, 0:128]), reads=[r_st[0]], writes=[r_ident])
        S.op("pool", lambda e: e.tensor_copy(stat[:, 15:16], gain[:, 0:1]), reads=[r_c], writes=[R()])
        qi = [1]
        load_w(S, w["w_ple_gate"], wg, 8, D, gain, st, r_st, r_w, qi)
        load_w(S, w["w_ple_proj"], wp, 2, D, None, st, r_st, r_w, qi)
        r_ht = [[R() for j in range(4)] for i in range(2)]
        r_pl = [R(), R()]
        r_stat = [R() for j in range(4)]
        r_junk, r_pt = R(), R()
        r_xn4 = [R() for j in range(4)]
        r_pts = [R(), R()]
        r_xnT = [R() for k in range(8)]
        r_pb16, r_pT, r_gsb, r_pg, r_pp = [R() for j in range(4)], R(), [R(), R()], [R(), R()], [R(), R()]

        def load(i):
            if i >= ntiles:
                return
            b = i % 2
            t0 = i * 512
            S.dma("sp", (lambda e: [e.dma_start(out=ht[b][:, j, :], in_=h3_d[t0 + j * 128:t0 + (j + 1) * 128, :]) for j in range(4)]),
                  4, f"ht{b}", writes=r_ht[b])
            S.dma("sp", (lambda e: [e.dma_start(out=ptl[b][:], in_=p_d[t0:t0 + 512, :].rearrange("(j p) c -> p j c", p=128))]),
                  1, f"pl{b}", writes=[r_pl[b]])
        load(0)

        def body(i):
            b = i % 2
            load(i + 1)
            t0 = i * 512
            norm_transpose4(S, ht[b], r_ht[b], stat, r_stat, junk, xn4, r_xn4, [pt, pt2], r_pts, ident, r_ident, xnT, r_xnT)
            for j in range(4):
                S.op("pool", (lambda e, j=j: e.tensor_copy(pb16[:, j, :], ptl[b][:, j, :])), reads=[r_pl[b]], writes=[r_pb16[j]])
            for j in range(4):
                ptj, r_ptj = [pt, pt2][j % 2], r_pts[j % 2]
                for c in range(2):
                    S.op("pe", (lambda e, c=c, j=j, ptj=ptj: e.transpose(ptj[:, c * 128:(c + 1) * 128], pb16[:, j, c * 128:(c + 1) * 128], ident[:])),
                         reads=[r_pb16[j], r_ident], writes=[r_ptj])
                S.op("act", (lambda e, j=j, ptj=ptj: e.copy(pT[:, :, j * 128:(j + 1) * 128], ptj[:, 0:256].rearrange("p (k t) -> p k t", k=2))),
                     reads=[r_ptj], writes=[r_pT])
            for j in range(4):
                for hh in range(2):
                    k2 = (j * 2 + hh) % 2
                    for kc in range(8):
                        S.op("pe", (lambda e, j=j, hh=hh, kc=kc, k2=k2: e.matmul(
                            pg[k2][:], xnT[:, kc, j * 128:(j + 1) * 128], wg[:, kc, hh * 512:(hh + 1) * 512],
                            start=(kc == 0), stop=(kc == 7))), reads=[r_w] + r_xnT, writes=[r_pg[k2]])
                    for kc in range(2):
                        S.op("pe", (lambda e, j=j, hh=hh, kc=kc, k2=k2: e.matmul(
                            pp[k2][:], pT[:, kc, j * 128:(j + 1) * 128], wp[:, kc, hh * 512:(hh + 1) * 512],
                            start=(kc == 0), stop=(kc == 1))), reads=[r_w, r_pT], writes=[r_pp[k2]])
                    S.op("act", (lambda e, k2=k2: e.activation(gsb[k2][:], pg[k2][:], AF.Sigmoid)), reads=[r_pg[k2]], writes=[r_gsb[k2]])
                    S.op("dve", (lambda e, k2=k2: e.tensor_tensor(gsb[k2][:], gsb[k2][:], pp[k2][:], ALU.mult)),
                         reads=[r_pp[k2]], writes=[r_gsb[k2]])
                    dst = ht[b][:, j, hh * 512:(hh + 1) * 512]
                    S.op("pool", (lambda e, dst=dst, k2=k2: e.tensor_tensor(dst, dst, gsb[k2][:], ALU.add)),
                         reads=[r_gsb[k2]], writes=[r_ht[b][j]])
            for j in range(4):
                S.op("act", (lambda e, j=j: e.activation(xn4[:, j, :], ht[b][:, j, :], AF.Square, accum_out=stat[:, 8 + j:9 + j])),
                     reads=[r_ht[b][j]], writes=[r_xn4[j], r_stat[j]])
            for j in range(4):
                S.op("act", (lambda e, j=j: e.activation(stat[:, 8 + j:9 + j], stat[:, 8 + j:9 + j], AF.Sqrt, bias=EPS, scale=1.0 / D)),
                     writes=[r_stat[j]])
            for j in range(4):
                S.op("dve", (lambda e, j=j: e.reciprocal(stat[:, 8 + j:9 + j], stat[:, 8 + j:9 + j])), writes=[r_stat[j]])
            for j in range(4):
                S.op("dve", (lambda e, j=j: e.scalar_tensor_tensor(ht[b][:, j, :], ht[b][:, j, :], stat[:, 8 + j:9 + j], fnb[:], ALU.mult, ALU.mult)),
                     reads=[r_stat[j], r_c], writes=[r_ht[b][j]])
                S.dma("pool", (lambda e, j=j: [e.dma_start(out=y_d[t0 + j * 128:t0 + (j + 1) * 128, :], in_=ht[b][:, j, :])]), 1,
                      f"so{b}{j}", reads=[r_ht[b][j]])
        for i in range(ntiles):
            body(i)
        S.barrier()
        S.emit()
    return S


CONST_SHAPES = {
    "ident": [128, 128], "rmask": [128, 512], "maskF": [128, 128], "maskB": [128, 128],
    "ffn1_norm": [128, 8], "mix_norm": [128, 8], "q_norm": [128, 3], "kv_norm": [128, 2], "hg_lb": [128, 16],
    "hg_norm_b": [128, 512], "ffn2_norm": [128, 8], "ple_norm": [128, 8], "final_norm_b": [128, D],
}
W_SHAPES = {
    "ffn1_wg": [D, DFF], "ffn1_wu": [D, DFF], "ffn1_wd": [DFF, D], "w_in": [D, WIN_COLS], "w_uq": [384, UQ_COLS],
    "w_uk": [256, 512], "w_uv": [256, 512], "w_o": [D, D], "ffn2_wg": [D, DFF], "ffn2_wu": [D, DFF], "ffn2_wd": [DFF, D],
    "w_ple_gate": [D, D], "w_ple_proj": [256, D],
}


def build_program(seq_lens, dbg=(), phases=None, gather=False):
    Sched.DMA_SEMS = {}
    Sched.DMA_CNT = {}
    nc = bass.Bass("TRN2", target_bir_lowering=False)
    ntok = sum(seq_lens)
    ntiles = ntok // 512

    def inp(name, shape, dt=F32):
        return nc.dram_tensor(name, shape, dt, kind="ExternalInput").ap()

    def scratch(name, shape, dt):
        kind = "ExternalOutput" if name in dbg else "Internal"
        return nc.dram_tensor(name, shape, dt, kind=kind).ap()

    x = inp("x", [ntok, D])
    p = inp("p", [ntok, 256])
    cst = {k: inp(k, v) for k, v in CONST_SHAPES.items()}
    cst["rope_c"] = inp("rope_c", [32, ntok])
    cst["rope_s"] = inp("rope_s", [32, ntok])
    w = {k: inp(k, v) for k, v in W_SHAPES.items()}
    y = nc.dram_tensor("y", [ntok, D], F32, kind="ExternalOutput").ap()
    h1 = scratch("h1", [ntok, D], F32)
    h2 = scratch("h2", [ntok, D], F32)
    h3 = scratch("h3", [ntok, D], F32)
    scr = {
        "QT": scratch("QT", [8, 96, ntok], BF16), "KTn": scratch("KTn", [512, ntok], BF16),
        "KTr": scratch("KTr", [32, ntok], BF16), "VA": scratch("VA", [8, ntok, 64], BF16),
        "QP": scratch("QP", [8, 128, ntok], BF16), "DC": scratch("DC", [8, 128, ntok // 32], F32),
        "VH": scratch("VH", [ntok, 512], BF16), "GH": scratch("GH", [ntok, 512], BF16),
        "AT": scratch("AT", [ntok, 8, 128], BF16), "KPT": scratch("KPT", [ntok, 8, 128], BF16),
        "MIXT": scratch("MIXT", [512, ntok], BF16), "OF": scratch("OF", [2, ntok, 512], F32),
    }
    def on(k):
        return phases is None or k in phases
    if on("f1"):
        ffn_phase(nc, "f1", x, h1, ntiles, cst["ffn1_norm"], w["ffn1_wg"], w["ffn1_wu"], w["ffn1_wd"], cst["ident"])
    if on("mi"):
        mixer_phase2(nc, "ma", "a", ntok, h1, cst, w, scr)
        mixer_phase2(nc, "mb", "b", ntok, h1, cst, w, scr)
    seqs = []
    a = 0
    for SLs in seq_lens:
        b = a + SLs
        seqs.append(dict(q0=a, nq=SLs, kp=[((lambda h, a=a, b=b: scr["KTn"][h * 64:(h + 1) * 64, a:b]), scr["KTr"][:, a:b],
                                            (lambda h, a=a, b=b: scr["VA"][h, a:b, :]), SLs)]))
        a = b
    xchg = None
    if gather:
        n0 = seq_lens[0]
        cst["rankmask"] = inp("rankmask", [128, 8])
        xchg = dict(n0=n0, groups=[[0, 1, 2, 3], [4, 5, 6, 7]],
                    XK_in=[scratch(f"XK_in{a_}", [128, n0], BF16) for a_ in range(4)],
                    XK_out=[scratch(f"XK_out{a_}", [4 * 128, n0], BF16) for a_ in range(4)],
                    XR_in=scratch("XR_in", [128, n0], BF16), XR_out=scratch("XR_out", [4 * 128, n0], BF16),
                    XV_in=[scratch(f"XV_in{a_}", [2 * n0, 64], BF16) for a_ in range(4)],
                    XV_out=[scratch(f"XV_out{a_}", [4 * 2 * n0, 64], BF16) for a_ in range(4)],
                    XS_in=scratch("XS_in", [1024, 129], F32), XS_out=scratch("XS_out", [4096, 129], F32))
        kp = []
        for r in range(4):
            kp.append(((lambda h, r=r: xchg["XK_out"][h // 2][r * 128 + (h % 2) * 64:r * 128 + (h % 2) * 64 + 64, :]),
                       xchg["XR_out"][r * 128:r * 128 + 32, :],
                       (lambda h, r=r: xchg["XV_out"][h // 2].rearrange("(r hh t) d -> r hh t d", r=4, hh=2)[r, h % 2]),
                       n0))
        seqs[0] = dict(q0=0, nq=n0, kp=kp, gathered=True)
        seqs = seqs[1:] + seqs[0:1]
    if on("at"):
        attn_phase(nc, "at", seqs, scr, xchg=xchg)
    if on("sc"):
        scan_phase(nc, "sc", list(seq_lens), ntok, scr, xchg=xchg, cst=cst)
    if on("op"):
        outproj_phase(nc, "op", ntok, h1, h2, cst, w, scr)
    if on("f2"):
        ffn_phase(nc, "f2", h2, h3, ntiles, cst["ffn2_norm"], w["ffn2_wg"], w["ffn2_wu"], w["ffn2_wd"], cst["ident"])
    if on("pl"):
        ple_phase(nc, "pl", ntok, h3, p, y, cst, w)
    return nc


def _lay(v, nch):
    return np.ascontiguousarray(np.asarray(v, np.float32).reshape(nch, 128).T)


def host_consts(inputs):
    f = np.float32
    c = {}
    c["ident"] = np.eye(128, dtype=f)
    rm = np.ones((128, 512), f)
    rm[:, 0::32] = 0.0
    c["rmask"] = rm
    idx = np.arange(128)
    same = (idx[:, None] // 32) == (idx[None, :] // 32)
    c["maskF"] = (same & (idx[:, None] <= idx[None, :])).astype(f)
    c["maskB"] = (same & (idx[:, None] >= idx[None, :])).astype(f)
    c["ffn1_norm"] = _lay(inputs["ffn1_norm"][0], 8)
    c["mix_norm"] = _lay(inputs["mix_norm"][0], 8)
    c["q_norm"] = _lay(inputs["q_norm"][0], 3)
    c["kv_norm"] = _lay(inputs["kv_norm"][0], 2)
    lb = np.asarray(inputs["hg_lb"], f).reshape(2, 2, 4, 128)
    c["hg_lb"] = np.ascontiguousarray(lb.transpose(3, 0, 1, 2).reshape(128, 16))
    c["hg_norm_b"] = np.ascontiguousarray(np.broadcast_to(np.asarray(inputs["hg_norm"][0], f)[None, :], (128, 512)))
    c["ffn2_norm"] = _lay(inputs["ffn2_norm"][0], 8)
    c["ple_norm"] = _lay(inputs["ple_norm"][0], 8)
    c["final_norm_b"] = np.ascontiguousarray(np.broadcast_to(np.asarray(inputs["final_norm"], f)[None, :], (128, D)))
    wts = {}
    for k in ("ffn1_wg", "ffn1_wu", "ffn1_wd", "w_uk", "w_uv", "w_o", "ffn2_wg", "ffn2_wu", "ffn2_wd", "w_ple_gate", "w_ple_proj"):
        wts[k] = np.ascontiguousarray(np.asarray(inputs[k][0], f))
    win = np.asarray(inputs["w_in"][0], f)
    wts["w_in"] = np.ascontiguousarray(np.concatenate([win, win[:, 656:672], win[:, 640:656]], axis=1))
    wq = np.asarray(inputs["w_uq"][0], f)
    sw = [np.concatenate([wq[:, h * 96 + 80:h * 96 + 96], wq[:, h * 96 + 64:h * 96 + 80]], axis=1) for h in range(8)]
    wts["w_uq"] = np.ascontiguousarray(np.concatenate([wq] + sw, axis=1))
    return c, wts


def rope_tables(pos):
    inv = np.exp(np.arange(0, 32, 2, dtype=np.float32) * np.float32(-np.log(10000.0) / 32)).astype(np.float32)
    ang = (pos.astype(np.float32)[None, :] * inv[:, None]).astype(np.float32)
    cs, sn = np.cos(ang).astype(np.float32), np.sin(ang).astype(np.float32)
    return np.ascontiguousarray(np.concatenate([cs, cs], 0)), np.ascontiguousarray(np.concatenate([-sn, sn], 0))


_PROG = {}


def run_balanced(inputs):
    xp = np.asarray(inputs["x_prompt"], np.float32)
    xs = np.asarray(inputs["x_sample"], np.float32)
    pp = np.asarray(inputs["p_prompt"], np.float32)[0]
    psm = np.asarray(inputs["p_sample"], np.float32)[0]
    SP, SS = xp.shape[1], xs.shape[1]
    Q = SP // 4
    lens = (Q, SS, SS)
    c, wts = host_consts(inputs)
    key = ("bal", lens)
    if key not in _PROG:
        _PROG[key] = build_program(lens, gather=True)
    nc = _PROG[key]
    in_maps = []
    for core in range(8):
        pb, r = core // 4, core % 4
        im = {"x": np.ascontiguousarray(np.concatenate([xp[pb, r * Q:(r + 1) * Q], xs[2 * core], xs[2 * core + 1]], axis=0)),
              "p": np.ascontiguousarray(np.concatenate([pp[pb, r * Q:(r + 1) * Q], psm[2 * core], psm[2 * core + 1]], axis=0))}
        pos = np.concatenate([np.arange(r * Q, (r + 1) * Q, dtype=np.float32), np.arange(SS, dtype=np.float32),
                              np.arange(SS, dtype=np.float32)])
        im["rope_c"], im["rope_s"] = rope_tables(pos)
        rk = np.zeros((128, 8), np.float32)
        for i in range(4):
            rk[:, i] = 1.0 if i < r else 0.0
            rk[:, 4 + i] = 1.0 if i > r else 0.0
        im["rankmask"] = rk
        im.update(c)
        im.update(wts)
        in_maps.append(im)
    res = run_bass_kernel_spmd(nc, in_maps, core_ids=list(range(8)))
    y_prompt = np.empty((2, SP, D), np.float32)
    y_sample = np.empty((16, SS, D), np.float32)
    for core in range(8):
        pb, r = core // 4, core % 4
        y = np.asarray(res.results[core]["y"])
        y_prompt[pb, r * Q:(r + 1) * Q] = y[0:Q]
        y_sample[2 * core] = y[Q:Q + SS]
        y_sample[2 * core + 1] = y[Q + SS:]
    return (y_prompt, y_sample)


def kernel(**inputs):
    return run_balanced(inputs)


def _roll(make_gen, n, stagger):
    active = []
    nxt = 0
    since = 10 ** 9
    while nxt < n or active:
        if nxt < n and len(active) < 2 and (not active or since >= stagger):
            active.append(make_gen(nxt))
            nxt += 1
            since = 0
        for g in list(active):
            try:
                next(g)
            except StopIteration:
                active.remove(g)
        since += 1


def _drive(gens):
    alive = list(gens)
    while alive:
        for g in list(alive):
            try:
                next(g)
            except StopIteration:
                alive.remove(g)


def mixer_phase2(nc, tag, part, ntok, h1_d, cst, w, scr):
    S = Sched(nc, tag)
    ntiles = ntok // 512
    with ExitStack() as es:
        C = Ctx(nc, tag, es)
        R = S.res
        gains = C.sb("gains", [128, 16], F32)
        ident = C.sb("ident", [128, 128], BF16)
        ht = C.sb("ht", [128, 4, D], F32)
        xn4 = C.sb("xn4", [128, 4, D], BF16)
        junk = C.sb("junk", [128, D], BF16)
        stat = C.sb("stat", [128, 16], F32)
        pt = C.ps("pt", [128, 1024], BF16)
        pt2 = C.ps("pt2", [128, 1024], BF16)
        pm = [C.ps(f"pm{i}", [128, 512]) for i in range(3)]
        pa = C.ps("pa", [128, 512])
        r_c, r_ident, r_w = R(), R(), R()
        r_ht = [R() for j in range(4)]
        r_stat = [R() for j in range(4)]
        r_xn4 = [R() for j in range(4)]
        r_pts = [R(), R()]
        r_pm = [R() for i in range(3)]
        r_pa = R()
        st = [ht[:, 0, 0:512], ht[:, 1, 0:512]]
        r_st = [r_ht[0], r_ht[1]]
        pmi = [0]

        def nextpm():
            i = pmi[0] % 3
            pmi[0] += 1
            return pm[i], r_pm[i]

        def cdma(dst, src):
            S.dma("sp", (lambda e: [e.dma_start(out=dst, in_=src)]), 1, "c", writes=[r_c])

        def mm_fm(ps, r_ps, wsb, c0, m, xT, r_x, nkc, out_p0=0):
            for kc in range(nkc):
                S.op("pe", (lambda e, kc=kc: e.matmul(ps[out_p0:out_p0 + m, :], wsb[:, kc, c0:c0 + m], xT[:, kc, :],
                                                      start=(kc == 0), stop=(kc == nkc - 1))),
                     reads=[r_w] + r_x, writes=[r_ps])

        def mm_tm(ps, r_ps, xT, r_x, j, wsb, c0, n, nkc):
            for kc in range(nkc):
                S.op("pe", (lambda e, kc=kc: e.matmul(ps[:, 0:n], xT[:, kc, j * 128:(j + 1) * 128], wsb[:, kc, c0:c0 + n],
                                                      start=(kc == 0), stop=(kc == nkc - 1))),
                     reads=[r_w] + r_x, writes=[r_ps])

        cdma(gains[:, 0:8], cst["mix_norm"])
        cdma(gains[:, 8:11], cst["q_norm"])
        cdma(gains[:, 11:13], cst["kv_norm"])
        S.dma("sp", (lambda e: [e.dma_start(out=ht[:, 2, 0:128], in_=cst["ident"])]), 1, "c2", writes=[r_ht[2]])
        S.op("pool", lambda e: e.tensor_copy(ident[:], ht[:, 2, 0:128]), reads=[r_ht[2]], writes=[r_ident])
        S.op("pool", lambda e: e.tensor_copy(stat[:, 15:16], gains[:, 0:1]), reads=[r_c], writes=[R()])
        qi = [0]

        def head(i, xnT, r_xnT):
            t0 = i * 512
            S.dma("sp", (lambda e: [e.dma_start(out=ht[:, j, :], in_=h1_d[t0 + j * 128:t0 + (j + 1) * 128, :])
                                    for j in range(4)]), 4, "ht", writes=r_ht)
            norm_transpose4(S, ht, r_ht, stat, r_stat, junk, xn4, r_xn4, [pt, pt2], r_pts, ident, r_ident, xnT, r_xnT)

        if part == "a":
            wa = C.sb("wa", [128, NKC, 704], BF16)
            wuq = C.sb("wuq", [128, 3, UQ_COLS], BF16)
            wuk = C.sb("wuk", [128, 2, 512], BF16)
            wuv = C.sb("wuv", [128, 2, 512], BF16)
            ones = C.sb("ones", [128, 128], BF16)
            r_ones = R()
            S.op("pool", lambda e: e.memset(ones[:], 1.0), writes=[r_ones])
            load_w(S, w["w_in"][:, 0:672], wa[:, :, 0:672], NKC, 672, gains[:, 0:8], st, r_st, r_w, qi)
            load_w(S, w["w_in"][:, C_KRS:C_KRS + 32], wa[:, :, 672:704], NKC, 32, gains[:, 0:8], st, r_st, r_w, qi)
            load_w(S, w["w_uq"], wuq, 3, UQ_COLS, gains[:, 8:11], st, r_st, r_w, qi)
            load_w(S, w["w_uk"], wuk, 2, 512, gains[:, 11:13], st, r_st, r_w, qi)
            load_w(S, w["w_uv"], wuv, 2, 512, gains[:, 11:13], st, r_st, r_w, qi)

            def make_set(k):
                pn = C.ps(f"pn{k}", [128, 512])
                r_pn = R()
                xnT = C.sb(f"xnT{k}", [128, NKC, 512], BF16)
                cqT = C.sb(f"cqT{k}", [128, 3, 512], BF16)
                ckvT = C.sb(f"ckvT{k}", [128, 2, 512], BF16)
                sqq = C.sb(f"sqq{k}", [128, 2, 512], BF16)
                sqkv = C.sb(f"sqkv{k}", [128, 2, 512], BF16)
                rsq = C.sb(f"rsq{k}", [128, 512], F32)
                rskv = C.sb(f"rskv{k}", [128, 512], F32)
                rstok = C.sb(f"rstok{k}", [128, 8], F32)
                tct = C.sb(f"tct{k}", [128, 512], F32)
                tst = C.sb(f"tst{k}", [128, 512], F32)
                t1 = [C.sb(f"t1{k}{i}", [128, 512], F32) for i in range(2)]
                t2 = C.sb(f"t2{k}", [128, 512], F32)
                qout = C.sb(f"qout{k}", [128, 8, 512], BF16)
                knT = C.sb(f"knT{k}", [128, 4, 512], BF16)
                krp = C.sb(f"krp{k}", [128, 512], BF16)
                vt = C.sb(f"vt{k}", [128, 4, 512], BF16)
                r_xnT = [R() for _ in range(NKC)]
                r_cqT, r_ckvT, r_sqq, r_sqkv = R(), R(), [R(), R()], R()
                r_rsq, r_rskv, r_rstok, r_tab = R(), R(), R(), R()
                r_t1, r_t2 = [R(), R()], R()
                r_qout, r_knT, r_krp, r_vt = R(), R(), R(), R()
                S.op("pool", lambda e: e.memset(tct[:], 1.0), writes=[r_tab])
                S.op("pool", lambda e: e.memset(tst[:], 0.0), writes=[r_tab])

                def tile(i):
                    t0 = i * 512
                    S.dma("sp", (lambda e: [e.dma_start(out=tct[64:96, :], in_=cst["rope_c"][:, t0:t0 + 512]),
                                            e.dma_start(out=tst[64:96, :], in_=cst["rope_s"][:, t0:t0 + 512])]),
                          2, f"tab{k}", writes=[r_tab])
                    head(i, xnT, r_xnT)
                    yield
                    for c in range(3):
                        ps, rp = nextpm()
                        mm_fm(ps, rp, wa, C_CQ + c * 128, 128, xnT, r_xnT, NKC)
                        S.op("act", (lambda e, ps=ps, c=c: e.copy(cqT[:, c, :], ps[:])), reads=[rp], writes=[r_cqT])
                        S.op("act", (lambda e, ps=ps, c=c: e.activation(sqq[:, c % 2, :], ps[:], AF.Square)),
                             reads=[rp], writes=[r_sqq[c % 2]])
                        S.op("pe", (lambda e, c=c: e.matmul(pn[:], ones[:], sqq[:, c % 2, :], start=(c == 0), stop=(c == 2))),
                             reads=[r_ones, r_sqq[c % 2]], writes=[r_pn])
                        yield
                    S.op("act", lambda e: e.activation(rsq[:], pn[:], AF.Sqrt, bias=EPS, scale=1.0 / 384), reads=[r_pn], writes=[r_rsq])
                    S.op("dve", lambda e: e.reciprocal(rsq[:], rsq[:]), reads=[r_rsq], writes=[r_rsq])
                    yield
                    for c in range(2):
                        ps, rp = nextpm()
                        mm_fm(ps, rp, wa, C_CKV + c * 128, 128, xnT, r_xnT, NKC)
                        S.op("act", (lambda e, ps=ps, c=c: e.copy(ckvT[:, c, :], ps[:])), reads=[rp], writes=[r_ckvT])
                        S.op("act", (lambda e, ps=ps, c=c: e.activation(sqkv[:, c, :], ps[:], AF.Square)),
                             reads=[rp], writes=[r_sqkv])
                        yield
                    for c in range(2):
                        S.op("pe", (lambda e, c=c: e.matmul(pn[:], ones[:], sqkv[:, c, :], start=(c == 0), stop=(c == 1))),
                             reads=[r_ones, r_sqkv], writes=[r_pn])
                    S.op("act", lambda e: e.activation(rskv[:], pn[:], AF.Sqrt, bias=EPS, scale=1.0 / 256), reads=[r_pn], writes=[r_rskv])
                    S.op("dve", lambda e: e.reciprocal(rskv[:], rskv[:]), reads=[r_rskv], writes=[r_rskv])
                    for j in range(4):
                        for c in range(2):
                            S.op("pe", (lambda e, j=j, c=c: e.matmul(pa[:, j:j + 1], sqkv[:, c, j * 128:(j + 1) * 128], ones[:, 0:1],
                                                                     start=(c == 0), stop=(c == 1))),
                                 reads=[r_ones, r_sqkv], writes=[r_pa])
                    S.op("act", lambda e: e.activation(rstok[:, 0:4], pa[:, 0:4], AF.Sqrt, bias=EPS, scale=1.0 / 256),
                         reads=[r_pa], writes=[r_rstok])
                    S.op("dve", lambda e: e.reciprocal(rstok[:, 0:4], rstok[:, 0:4]), reads=[r_rstok], writes=[r_rstok])
                    yield
                    ps, rp = nextpm()
                    mm_fm(ps, rp, wa, C_KR, 32, xnT, r_xnT, NKC, out_p0=64)
                    ps2, rp2 = nextpm()
                    mm_fm(ps2, rp2, wa, 672, 32, xnT, r_xnT, NKC, out_p0=64)
                    S.op("dve", (lambda e, ps=ps: e.tensor_tensor(t1[0][64:96, :], ps[64:96, :], tct[64:96, :], ALU.mult)),
                         reads=[rp, r_tab], writes=[r_t1[0]])
                    S.op("dve", (lambda e, ps2=ps2: e.tensor_tensor(t2[64:96, :], ps2[64:96, :], tst[64:96, :], ALU.mult)),
                         reads=[rp2, r_tab], writes=[r_t2])
                    S.op("pool", lambda e: e.tensor_tensor(krp[64:96, :], t1[0][64:96, :], t2[64:96, :], ALU.add),
                         reads=[r_t1[0], r_t2], writes=[r_krp])
                    S.dma("pool", (lambda e: [e.dma_start(out=scr["KTr"][:, t0:t0 + 512], in_=krp[64:96, :])]), 1, f"s_krp{k}",
                          reads=[r_krp])
                    yield
                    for h in range(8):
                        b = h % 2
                        ps, rp = nextpm()
                        mm_fm(ps, rp, wuq, h * 96, 96, cqT, [r_cqT], 3)
                        ps2, rp2 = nextpm()
                        mm_fm(ps2, rp2, wuq, 768 + h * 32, 32, cqT, [r_cqT], 3, out_p0=64)
                        S.op("dve", (lambda e, ps=ps, b=b: e.tensor_tensor(t1[b][0:96, :], ps[0:96, :], tct[0:96, :], ALU.mult)),
                             reads=[rp, r_tab], writes=[r_t1[b]])
                        S.op("dve", (lambda e, ps2=ps2: e.tensor_tensor(t2[64:96, :], ps2[64:96, :], tst[64:96, :], ALU.mult)),
                             reads=[rp2, r_tab], writes=[r_t2])
                        S.op("pool", (lambda e, b=b: e.tensor_tensor(t1[b][64:96, :], t1[b][64:96, :], t2[64:96, :], ALU.add)),
                             reads=[r_t2], writes=[r_t1[b]])
                        S.op("pool", (lambda e, b=b, h=h: e.tensor_tensor(qout[0:96, h, :], t1[b][0:96, :], rsq[0:96, :], ALU.mult)),
                             reads=[r_t1[b], r_rsq], writes=[r_qout])
                        yield
                    S.dma("pool", (lambda e: [e.dma_start(out=scr["QT"][h, :, t0:t0 + 512], in_=qout[0:96, h, :])
                                              for h in range(8)]), 8, f"s_q{k}", reads=[r_qout])
                    for a in range(4):
                        ps, rp = nextpm()
                        mm_fm(ps, rp, wuk, a * 128, 128, ckvT, [r_ckvT], 2)
                        S.op("dve", (lambda e, ps=ps, a=a: e.tensor_tensor(knT[:, a, :], ps[:], rskv[:], ALU.mult)),
                             reads=[rp, r_rskv], writes=[r_knT])
                        yield
                    S.dma("pool", (lambda e: [e.dma_start(out=scr["KTn"][a * 128:(a + 1) * 128, t0:t0 + 512], in_=knT[:, a, :])
                                              for a in range(4)]), 4, f"s_kn{k}", reads=[r_knT])
                    for j in range(4):
                        ps, rp = nextpm()
                        mm_tm(ps, rp, ckvT, [r_ckvT], j, wuv, 0, 512, 2)
                        S.op("act", (lambda e, ps=ps, j=j: e.activation(vt[:, j, :], ps[:], AF.Copy, scale=rstok[:, j:j + 1])),
                             reads=[rp, r_rstok], writes=[r_vt])
                        yield
                    S.dma("pool", (lambda e: [e.dma_start(
                        out=scr["VA"][:, t0 + j * 128:t0 + (j + 1) * 128, :].rearrange("h t d -> t h d"),
                        in_=vt[:, j, :].rearrange("t (h d) -> t h d", h=8)) for j in range(4)]), 4, f"s_v{k}", reads=[r_vt])
                return tile
        else:
            wb = C.sb("wb", [128, NKC, 2560], BF16)
            lbt = C.sb("lbt", [128, 16], F32)
            lb = C.sb("lb", [128, 8], F32)
            oml = C.sb("oml", [128, 8], F32)
            rmask = C.sb("rmask", [128, 512], F32)
            mF = C.sb("mF", [128, 128], F32)
            mB = C.sb("mB", [128, 128], F32)
            ato = C.sb("ato", [128, 4, 8, 128], BF16)
            kto = C.sb("kto", [128, 4, 8, 128], BF16)
            ptk = C.ps("ptk", [128, 1024], BF16)
            r_ptk, r_ato, r_kto, r_lb = R(), R(), R(), R()
            cdma(lbt[:], cst["hg_lb"])
            cdma(rmask[:], cst["rmask"])
            cdma(mF[:], cst["maskF"])
            cdma(mB[:], cst["maskB"])
            lv = lbt[:].rearrange("p (d l h) -> p d l h", d=2, l=2)
            lb3 = lb[:].rearrange("p (d h) -> p d h", d=2)
            S.op("dve", lambda e: e.tensor_tensor(lb3, lv[:, :, 0, :], lv[:, :, 1, :], ALU.subtract), reads=[r_c], writes=[r_lb])
            S.op("act", lambda e: e.activation(oml[:], lb[:], AF.Sigmoid, scale=-1.0), reads=[r_lb], writes=[R()])
            S.op("act", lambda e: e.activation(lb[:], lb[:], AF.Sigmoid), reads=[r_lb], writes=[r_lb])
            c1 = C.sb("c1", [128, 8], F32)
            c0 = C.sb("c0", [128, 8], F32)
            c1n = C.sb("c1n", [128, 8], F32)
            S.op("dve", lambda e: e.tensor_scalar(c1[:], oml[:], 0.5, None, ALU.mult), reads=[r_lb], writes=[r_lb])
            S.op("dve", lambda e: e.tensor_tensor(c0[:], lb[:], c1[:], ALU.add), reads=[r_lb], writes=[r_lb])
            S.op("dve", lambda e: e.tensor_scalar(c1n[:], c1[:], -1.0, None, ALU.mult), reads=[r_lb], writes=[r_lb])
            load_w(S, w["w_in"][:, 672:3232], wb, NKC, 2560, gains[:, 0:8], st, r_st, r_w, qi)
            B_HQ, B_HI, B_HFF, B_HFB, B_HG = 0, 512, 1024, 1536, 2048

            def make_set(k):
                xnT = C.sb(f"xnT{k}", [128, NKC, 512], BF16)
                qh = C.sb(f"qh{k}", [128, 4, 512], F32)
                A = C.sb(f"hA{k}", [128, 512], F32)
                B = C.sb(f"hB{k}", [128, 512], F32)
                Cc = C.sb(f"hC{k}", [128, 512], F32)
                E1 = C.sb(f"hE1{k}", [128, 512], F32)
                E2 = C.sb(f"hE2{k}", [128, 512], F32)
                qpo = C.sb(f"qpo{k}", [128, 8, 512], BF16)
                kpo = C.sb(f"kpo{k}", [128, 8, 512], BF16)
                kppo = C.sb(f"kppo{k}", [128, 8, 512], BF16)
                dco = C.sb(f"dco{k}", [128, 8, 16], F32)
                vht = C.sb(f"vht{k}", [128, 4, 512], BF16)
                ght = C.sb(f"ght{k}", [128, 4, 512], BF16)
                r_xnT = [R() for _ in range(NKC)]
                r_qh, rA, rB, rC, rE1, rE2 = R(), R(), R(), R(), R(), R()
                r_qpo, r_kpo, r_kppo, r_dco, r_vht, r_ght = R(), R(), R(), R(), R(), R()

                def tile(i):
                    t0 = i * 512
                    head(i, xnT, r_xnT)
                    yield
                    for h in range(4):
                        ps, rp = nextpm()
                        mm_fm(ps, rp, wb, B_HQ + h * 128, 128, xnT, r_xnT, NKC)
                        S.op("act", (lambda e, ps=ps, h=h: e.activation(qh[:, h, :], ps[:], AF.Silu)), reads=[rp], writes=[r_qh])
                        yield
                    for j in range(4):
                        ps, rp = nextpm()
                        mm_tm(ps, rp, xnT, r_xnT, j, wb, B_HI, 512, NKC)
                        S.op("dve", (lambda e, ps=ps, j=j: e.tensor_copy(vht[:, j, :], ps[:])), reads=[rp], writes=[r_vht])
                        yield
                        ps, rp = nextpm()
                        mm_tm(ps, rp, xnT, r_xnT, j, wb, B_HG, 512, NKC)
                        S.op("act", (lambda e, ps=ps, j=j: e.activation(ght[:, j, :], ps[:], AF.Silu)), reads=[rp], writes=[r_ght])
                        yield
                    S.dma("pool", (lambda e: [
                        e.dma_start(out=scr["VH"][t0:t0 + 512, :].rearrange("(j p) c -> p j c", p=128), in_=vht[:]),
                        e.dma_start(out=scr["GH"][t0:t0 + 512, :].rearrange("(j p) c -> p j c", p=128), in_=ght[:])]),
                        2, f"s_vg{k}", reads=[r_vht, r_ght])
                    for d in range(2):
                        for h in range(4):
                            hd = d * 4 + h
                            ps, rp = nextpm()
                            mm_fm(ps, rp, wb, (B_HFF if d == 0 else B_HFB) + h * 128, 128, xnT, r_xnT, NKC)
                            c0s, c1s, c1ns = c0[:, hd:hd + 1], c1[:, hd:hd + 1], c1n[:, hd:hd + 1]
                            S.op("act", (lambda e, ps=ps: e.activation(A[:], ps[:], AF.Tanh, scale=0.5)), reads=[rp], writes=[rA])
                            yield
                            S.op("pool", (lambda e, c1s=c1s, c1ns=c1ns: e.tensor_scalar(B[:], A[:], c1ns, c1s, ALU.mult, ALU.add)),
                                 reads=[r_lb, rA], writes=[rB])
                            S.op("dve", (lambda e, c0s=c0s, c1s=c1s: e.tensor_scalar(A[:], A[:], c1s, c0s, ALU.mult, ALU.add)),
                                 reads=[r_lb, rB], writes=[rA])
                            S.op("act", (lambda e: e.activation(A[:], A[:], AF.Ln)), writes=[rA])
                            yield
                            S.op("dve", (lambda e: e.tensor_tensor_scan(Cc[:], rmask[:], A[:], 0.0, ALU.mult, ALU.add)),
                                 reads=[rA, r_c], writes=[rC])
                            Cv = Cc[:].rearrange("p (c t) -> p c t", t=32)
                            Av = A[:].rearrange("p (c t) -> p c t", t=32)
                            if d == 0:
                                bsrc, rb, dcol = Cc, rC, 31
                            else:
                                S.op("pool", (lambda e: e.tensor_tensor(A[:], A[:], Cc[:], ALU.subtract)), reads=[rC], writes=[rA])
                                S.op("pool", (lambda e, Av=Av, Cv=Cv: e.tensor_tensor(
                                    Av, Av, Cv[:, :, 31:32].broadcast_to([128, 16, 32]), ALU.add)), reads=[rC], writes=[rA])
                                bsrc, rb, dcol = A, rA, 0
                            yield
                            S.op("act", (lambda e, bsrc=bsrc: e.activation(E1[:], bsrc[:], AF.Exp)), reads=[rb], writes=[rE1])
                            S.op("act", (lambda e, bsrc=bsrc: e.activation(E2[:], bsrc[:], AF.Exp, scale=-1.0)), reads=[rb], writes=[rE2])
                            yield
                            S.op("pool", (lambda e, h=h, hd=hd: e.tensor_tensor(qpo[:, hd, :], qh[:, h, :], E1[:], ALU.mult)),
                                 reads=[rE1, r_qh], writes=[r_qpo])
                            S.op("dve", (lambda e, hd=hd: e.tensor_tensor(kpo[:, hd, :], E2[:], B[:], ALU.mult)), reads=[rB, rE2], writes=[r_kpo])
                            yield
                            E1v = E1[:].rearrange("p (c t) -> p c t", t=32)
                            E2v = kpo[:, hd, :].rearrange("p (c t) -> p c t", t=32)
                            S.op("dve", (lambda e, E1v=E1v, hd=hd, dcol=dcol: e.tensor_copy(dco[:, hd, :], E1v[:, :, dcol])),
                                 reads=[rE1], writes=[r_dco])
                            kv = kppo[:, hd, :].rearrange("p (c t) -> p c t", t=32)
                            S.op("pool", (lambda e, E1v=E1v, E2v=E2v, kv=kv, dcol=dcol: e.tensor_tensor(
                                kv, E2v, E1v[:, :, dcol:dcol + 1].broadcast_to([128, 16, 32]), ALU.mult)),
                                reads=[rE1, r_kpo], writes=[r_kppo])
                            yield
                    S.dma("pool", (lambda e: [
                        e.dma_start(out=scr["QP"][:, :, t0:t0 + 512].rearrange("h p t -> p h t"), in_=qpo[:]),
                        e.dma_start(out=scr["DC"][:, :, i * 16:(i + 1) * 16].rearrange("h p c -> p h c"), in_=dco[:])]),
                        2, f"s_qp{k}", reads=[r_qpo, r_dco])
                    for j in range(4):
                        for d in range(2):
                            for h in range(4):
                                hd = d * 4 + h
                                S.op("pe", (lambda e, j=j, hd=hd, h=h: e.matmul(
                                    pa[:, h * 128:(h + 1) * 128], kpo[:, hd, j * 128:(j + 1) * 128], qpo[:, hd, j * 128:(j + 1) * 128],
                                    start=True, stop=True)), reads=[r_kpo, r_qpo], writes=[r_pa])
                            msk = (mF if d == 0 else mB)
                            S.op("dve", (lambda e, j=j, d=d, msk=msk: e.tensor_tensor(
                                ato[:, j, d * 4:(d + 1) * 4, :], pa[:].rearrange("p (h t) -> p h t", h=4),
                                msk[:].rearrange("p (o t) -> p o t", o=1).broadcast_to([128, 4, 128]), ALU.mult)),
                                reads=[r_pa, r_c], writes=[r_ato])
                        for hd in range(8):
                            S.op("pe", (lambda e, j=j, hd=hd: e.transpose(ptk[:, hd * 128:(hd + 1) * 128],
                                                                           kppo[:, hd, j * 128:(j + 1) * 128], ident[:])),
                                 reads=[r_kppo, r_ident], writes=[r_ptk])
                        S.op("dve", (lambda e, j=j: e.tensor_copy(kto[:, j, :, :], ptk[:].rearrange("p (h k) -> p h k", h=8))),
                             reads=[r_ptk], writes=[r_kto])
                    S.dma("pool", (lambda e: [
                        e.dma_start(out=scr["AT"][t0:t0 + 512, :, :].rearrange("(j p) h t -> p j h t", p=128), in_=ato[:]),
                        e.dma_start(out=scr["KPT"][t0:t0 + 512, :, :].rearrange("(j p) h k -> p j h k", p=128), in_=kto[:])]),
                        2, "s_at", reads=[r_ato, r_kto])
                return tile

        tiles = [make_set(0), make_set(1)]
        _roll(lambda i: tiles[i % 2](i), ntiles, 12 if part == "a" else 30)
        S.barrier()
        S.emit()
    return S
```
